# Optimizing a Trainium2 kernel written in Bass

```python
import math
import jax, jax.numpy as jnp
from jax import lax
import numpy as np

D_MODEL = 1024
BATCH = 8
SEQ = 2048
DEPTH = 2
DEC_BATCH = 128
DEC_SEQ = 8
PAST_LEN = 2048
PAGE_SIZE = 128

POOL_WIDTH = D_MODEL // 2
POOL_WINDOWS = (2, 4, 8, 16)
POOL_GROUP = POOL_WIDTH // len(POOL_WINDOWS)
POOL_BUF = max(POOL_WINDOWS) - 1
ATT_HEADS = 8
ATT_HEAD_DIM = 64
ATT_WIDTH = ATT_HEADS * ATT_HEAD_DIM
ATT_SCALE = ATT_HEAD_DIM ** -0.5
Q_BLOCK = 128
SSM_WIDTH = D_MODEL
SSM_HEAD_DIM = 64
SSM_HEADS = SSM_WIDTH // SSM_HEAD_DIM
SSM_GROUPS = 2
SSM_STATE = 128
CONV_WIDTH = 4
CONV_DIM = SSM_WIDTH + 2 * SSM_GROUPS * SSM_STATE
SSD_CHUNK = 128
N_BRANCH = 3
RMS_EPS = 1e-6
IN_SIZES = (POOL_WIDTH, POOL_WIDTH, ATT_WIDTH, ATT_WIDTH, ATT_WIDTH, ATT_HEADS, ATT_WIDTH,
            SSM_WIDTH, CONV_DIM, SSM_HEADS, N_BRANCH * D_MODEL)
D_IN = 2 * POOL_WIDTH + 4 * ATT_WIDTH + ATT_HEADS + SSM_WIDTH + CONV_DIM + SSM_HEADS + N_BRANCH * D_MODEL

kernel_name = 'pool_fox_ssd_gated_hybrid_step'


def rmsnorm(x, w):
    xf = x.astype(jnp.float32)
    y = xf * lax.rsqrt(jnp.mean(xf * xf, axis=-1, keepdims=True) + RMS_EPS)
    return (y * w.astype(jnp.float32)).astype(x.dtype)


def split_cols(h):
    parts, off = [], 0
    for n in IN_SIZES:
        parts.append(h[..., off:off + n])
        off += n
    return parts


def pool_mixer(u, buf, pos, w_pool, scale):
    bsz, L, _ = u.shape
    full = jnp.concatenate([buf.astype(u.dtype), u], axis=1)
    cs = jnp.cumsum(full.astype(jnp.float32), axis=1)
    cs0 = jnp.concatenate([jnp.zeros_like(cs[:, :1]), cs], axis=1)
    end = POOL_BUF + 1
    means = []
    for g, w in enumerate(POOL_WINDOWS):
        sl = slice(g * POOL_GROUP, (g + 1) * POOL_GROUP)
        wsum = cs0[:, end:end + L, sl] - cs0[:, end - w:end - w + L, sl]
        cnt = jnp.minimum(pos + 1, w).astype(jnp.float32)[None, :, None]
        means.append(wsum / cnt)
    mean = jnp.concatenate(means, axis=-1).astype(u.dtype)
    d = (mean - u).reshape(bsz, L, len(POOL_WINDOWS), POOL_GROUP)
    y = jnp.einsum('blgc,gcd->blgd', d, w_pool).reshape(bsz, L, POOL_WIDTH) * scale
    return y, full[:, -POOL_BUF:]


def fox_attend(q, k, v, cq, ck, qpos, kpos):
    s = jnp.einsum('bqhd,bkhd->bhqk', q, k).astype(jnp.float32) * ATT_SCALE
    s = s + jnp.swapaxes(cq, 1, 2)[..., None] - jnp.swapaxes(ck, 1, 2)[:, :, None, :]
    mask = kpos[None, :] <= qpos[:, None]
    p = jax.nn.softmax(jnp.where(mask, s, -jnp.inf), axis=-1)
    return jnp.einsum('bhqk,bkhd->bqhd', p.astype(v.dtype), v)


def fox_prompt(q, k, v, logf, pos):
    bsz, L, H, D = q.shape
    c = jnp.cumsum(logf, axis=1)
    nb = L // Q_BLOCK
    qb = jnp.swapaxes(q.reshape(bsz, nb, Q_BLOCK, H, D), 0, 1)
    cb = jnp.swapaxes(c.reshape(bsz, nb, Q_BLOCK, H), 0, 1)
    pb = pos.reshape(nb, Q_BLOCK)
    out = lax.map(lambda a: fox_attend(a[0], k, v, a[1], c, a[2], pos), (qb, cb, pb))
    return jnp.swapaxes(out, 0, 1).reshape(bsz, L, H, D)


def fox_sample(q, k, v, logf, k_past, v_past, logf_past, pos):
    L = q.shape[1]
    k_all = jnp.concatenate([k_past.astype(k.dtype), k], axis=1)
    v_all = jnp.concatenate([v_past.astype(v.dtype), v], axis=1)
    c_all = jnp.cumsum(jnp.concatenate([logf_past.astype(jnp.float32), logf], axis=1), axis=1)
    lk = k_all.shape[1]
    kpos = jnp.arange(lk, dtype=jnp.int32)
    return fox_attend(q, k_all, v_all, c_all[:, lk - L:], c_all, pos, kpos)


def causal_conv(xbc, buf, w, b):
    L = xbc.shape[1]
    full = jnp.concatenate([buf.astype(xbc.dtype), xbc], axis=1)
    y = b
    for j in range(CONV_WIDTH):
        y = y + full[:, j:j + L] * w[j]
    return jax.nn.silu(y), full[:, -(CONV_WIDTH - 1):]


def ssd_scan(x, dt, a, bm, cm, h0):
    bsz, L = x.shape[0], x.shape[1]
    chunk = SSD_CHUNK if L % SSD_CHUNK == 0 else L
    nc = L // chunk
    G, HG, P, N = SSM_GROUPS, SSM_HEADS // SSM_GROUPS, SSM_HEAD_DIM, SSM_STATE
    f32 = jnp.float32
    xc = x.astype(f32).reshape(bsz, nc, chunk, G, HG, P)
    dtc = dt.reshape(bsz, nc, chunk, G, HG)
    bc = bm.astype(f32).reshape(bsz, nc, chunk, G, N)
    cc = cm.astype(f32).reshape(bsz, nc, chunk, G, N)
    cum = jnp.cumsum(dtc * a.reshape(G, HG), axis=2)
    causal = jnp.tril(jnp.ones((chunk, chunk), dtype=bool))[:, :, None, None]
    seg = cum[:, :, :, None] - cum[:, :, None, :]
    decay = jnp.exp(jnp.where(causal, seg, -jnp.inf))
    cb = jnp.einsum('bctgn,bcsgn->bctsg', cc, bc)
    mix = cb[..., None] * decay * dtc[:, :, None]
    y = jnp.einsum('bctsgh,bcsghp->bctghp', mix, xc)
    xw = (jnp.exp(cum[:, :, -1:] - cum) * dtc)[..., None] * xc
    s_chunk = jnp.einsum('bcsgn,bcsghp->bcghpn', bc, xw)
    d_chunk = jnp.exp(cum[:, :, -1])

    def step(h, inp):
        s_c, d_c = inp
        return h * d_c[..., None, None] + s_c, h

    h_last, h_prev = lax.scan(step, h0.astype(f32).reshape(bsz, G, HG, P, N),
                              (jnp.moveaxis(s_chunk, 1, 0), jnp.moveaxis(d_chunk, 1, 0)))
    h_prev = jnp.moveaxis(h_prev, 0, 1)
    y = y + jnp.einsum('bctgn,bcghpn->bctghp', cc, h_prev) * jnp.exp(cum)[..., None]
    return (y.reshape(bsz, L, SSM_HEADS, P).astype(x.dtype),
            h_last.reshape(bsz, SSM_HEADS, P, N).astype(h0.dtype))


def gated_group_rmsnorm(y, z, w):
    bsz, L, C = y.shape
    g = (y * jax.nn.silu(z)).reshape(bsz, L, SSM_GROUPS, C // SSM_GROUPS)
    return rmsnorm(g, w.reshape(SSM_GROUPS, C // SSM_GROUPS)).reshape(bsz, L, C)


def mixer_layer(x, pos, norm_pre, w_in, pool_w, pool_scale, f_bias, conv_w, conv_b, dt_bias, a_log,
                d_skip, ssm_norm, w_branch_a, w_branch_b, w_branch_c, w_out, norm_post,
                pool_buf, conv_buf, ssm_h0, att_past):
    bsz, L, _ = x.shape
    hn = rmsnorm(x, norm_pre)
    u_a, z_a, q, k, v, f_lin, z_b, z_c, xbc, dt_raw, gates = split_cols(hn @ w_in)
    y_a, pool_new = pool_mixer(u_a, pool_buf, pos, pool_w, pool_scale)
    y_a = y_a * jax.nn.silu(z_a)
    q = q.reshape(bsz, L, ATT_HEADS, ATT_HEAD_DIM)
    k = k.reshape(bsz, L, ATT_HEADS, ATT_HEAD_DIM)
    v = v.reshape(bsz, L, ATT_HEADS, ATT_HEAD_DIM)
    logf = jax.nn.log_sigmoid(f_lin.astype(jnp.float32) + f_bias.astype(jnp.float32))
    if att_past is None:
        o_b = fox_prompt(q, k, v, logf, pos)
    else:
        o_b = fox_sample(q, k, v, logf, att_past[0], att_past[1], att_past[2], pos)
    y_b = o_b.reshape(bsz, L, ATT_WIDTH) * jax.nn.silu(z_b)
    xbc_c, conv_new = causal_conv(xbc, conv_buf, conv_w, conv_b)
    xs = xbc_c[..., :SSM_WIDTH].reshape(bsz, L, SSM_HEADS, SSM_HEAD_DIM)
    bm = xbc_c[..., SSM_WIDTH:SSM_WIDTH + SSM_GROUPS * SSM_STATE].reshape(bsz, L, SSM_GROUPS, SSM_STATE)
    cm = xbc_c[..., SSM_WIDTH + SSM_GROUPS * SSM_STATE:].reshape(bsz, L, SSM_GROUPS, SSM_STATE)
    dt = jax.nn.softplus(dt_raw.astype(jnp.float32) + dt_bias.astype(jnp.float32))
    a = -jnp.exp(a_log.astype(jnp.float32))
    y_c, h_new = ssd_scan(xs, dt, a, bm, cm, ssm_h0)
    y_c = (y_c + d_skip[:, None] * xs).reshape(bsz, L, SSM_WIDTH)
    y_c = gated_group_rmsnorm(y_c, z_c, ssm_norm)
    g = jax.nn.sigmoid(gates.reshape(bsz, L, N_BRANCH, D_MODEL))
    m = g[:, :, 0] * (y_a @ w_branch_a) + g[:, :, 1] * (y_b @ w_branch_b) + g[:, :, 2] * (y_c @ w_branch_c)
    out = x + rmsnorm(m @ w_out, norm_post)
    return out, (k, v, logf, pool_new, conv_new, h_new)


def setup_inputs(seed: int = 0) -> dict:
    key = jax.random.key(seed)
    ks = jax.random.split(key, 32)
    f32 = jnp.float32
    n_pages = PAST_LEN // PAGE_SIZE
    n_pool = (DEC_BATCH * n_pages * 5) // 4
    nrm = lambda k, s: jax.random.normal(k, s, f32)
    x_prompt = nrm(ks[0], (BATCH, SEQ, D_MODEL))
    x_sample = nrm(ks[1], (DEC_BATCH, DEC_SEQ, D_MODEL))
    cache_k = nrm(ks[2], (DEPTH, n_pool, PAGE_SIZE, ATT_HEADS, ATT_HEAD_DIM))
    cache_v = nrm(ks[3], (DEPTH, n_pool, PAGE_SIZE, ATT_HEADS, ATT_HEAD_DIM))
    cache_logf = jax.nn.log_sigmoid(2.0 + nrm(ks[4], (DEPTH, n_pool, PAGE_SIZE, ATT_HEADS)))
    state_pool = nrm(ks[5], (DEPTH, DEC_BATCH, POOL_BUF, POOL_WIDTH))
    state_conv = nrm(ks[6], (DEPTH, DEC_BATCH, CONV_WIDTH - 1, CONV_DIM))
    state_ssm = 0.5 * nrm(ks[7], (DEPTH, DEC_BATCH, SSM_HEADS, SSM_HEAD_DIM, SSM_STATE))
    perm = jax.random.permutation(ks[8], n_pool)
    page_table = perm[:DEC_BATCH * n_pages].reshape(DEC_BATCH, n_pages).astype(jnp.int32)
    norm_pre = 1.0 + 0.02 * nrm(ks[9], (DEPTH, D_MODEL))
    w_in = nrm(ks[10], (DEPTH, D_MODEL, D_IN)) * D_MODEL ** -0.5
    pool_w = nrm(ks[11], (DEPTH, len(POOL_WINDOWS), POOL_GROUP, POOL_GROUP)) * POOL_GROUP ** -0.5
    pool_scale = 1.0 + 0.02 * nrm(ks[12], (DEPTH, POOL_WIDTH))
    f_bias = jax.random.uniform(ks[13], (DEPTH, ATT_HEADS), f32, 1.0, 3.0)
    conv_w = nrm(ks[14], (DEPTH, CONV_WIDTH, CONV_DIM)) * CONV_WIDTH ** -0.5
    conv_b = 0.01 * nrm(ks[15], (DEPTH, CONV_DIM))
    dt0 = jnp.exp(jax.random.uniform(ks[16], (DEPTH, SSM_HEADS), f32, math.log(1e-3), math.log(1e-1)))
    dt_bias = dt0 + jnp.log(-jnp.expm1(-dt0))
    a_log = jnp.log(jax.random.uniform(ks[17], (DEPTH, SSM_HEADS), f32, 1.0, 16.0))
    d_skip = 1.0 + 0.01 * nrm(ks[18], (DEPTH, SSM_HEADS))
    ssm_norm = 1.0 + 0.02 * nrm(ks[19], (DEPTH, SSM_WIDTH))
    w_branch_a = nrm(ks[20], (DEPTH, POOL_WIDTH, D_MODEL)) * POOL_WIDTH ** -0.5
    w_branch_b = nrm(ks[21], (DEPTH, ATT_WIDTH, D_MODEL)) * ATT_WIDTH ** -0.5
    w_branch_c = nrm(ks[22], (DEPTH, SSM_WIDTH, D_MODEL)) * SSM_WIDTH ** -0.5
    w_out = nrm(ks[23], (DEPTH, D_MODEL, D_MODEL)) * D_MODEL ** -0.5
    norm_post = 1.0 + 0.02 * nrm(ks[24], (DEPTH, D_MODEL))
    return {'x_prompt': x_prompt, 'x_sample': x_sample, 'cache_k': cache_k, 'cache_v': cache_v,
            'cache_logf': cache_logf, 'state_pool': state_pool, 'state_conv': state_conv,
            'state_ssm': state_ssm, 'page_table': page_table, 'norm_pre': norm_pre, 'w_in': w_in,
            'pool_w': pool_w, 'pool_scale': pool_scale, 'f_bias': f_bias, 'conv_w': conv_w,
            'conv_b': conv_b, 'dt_bias': dt_bias, 'a_log': a_log, 'd_skip': d_skip,
            'ssm_norm': ssm_norm, 'w_branch_a': w_branch_a, 'w_branch_b': w_branch_b,
            'w_branch_c': w_branch_c, 'w_out': w_out, 'norm_post': norm_post}


def reference(x_prompt, x_sample, cache_k, cache_v, cache_logf, state_pool, state_conv, state_ssm,
              page_table, norm_pre, w_in, pool_w, pool_scale, f_bias, conv_w, conv_b, dt_bias, a_log,
              d_skip, ssm_norm, w_branch_a, w_branch_b, w_branch_c, w_out, norm_post):
    b_p, seq, _ = x_prompt.shape
    b_s, dec_seq, _ = x_sample.shape
    past_len = page_table.shape[1] * PAGE_SIZE
    pos_p = jnp.arange(seq, dtype=jnp.int32)
    pos_s = past_len + jnp.arange(dec_seq, dtype=jnp.int32)
    hp, hs = x_prompt, x_sample
    st_p, st_s = [], []
    for l in range(DEPTH):
        wts = (norm_pre[l], w_in[l], pool_w[l], pool_scale[l], f_bias[l], conv_w[l], conv_b[l],
               dt_bias[l], a_log[l], d_skip[l], ssm_norm[l], w_branch_a[l], w_branch_b[l],
               w_branch_c[l], w_out[l], norm_post[l])
        pool0 = jnp.zeros((b_p, POOL_BUF, POOL_WIDTH), hp.dtype)
        conv0 = jnp.zeros((b_p, CONV_WIDTH - 1, CONV_DIM), hp.dtype)
        ssm0 = jnp.zeros((b_p, SSM_HEADS, SSM_HEAD_DIM, SSM_STATE), state_ssm.dtype)
        hp, sp = mixer_layer(hp, pos_p, *wts, pool0, conv0, ssm0, None)
        k_past = cache_k[l][page_table].reshape(b_s, past_len, ATT_HEADS, ATT_HEAD_DIM)
        v_past = cache_v[l][page_table].reshape(b_s, past_len, ATT_HEADS, ATT_HEAD_DIM)
        lf_past = cache_logf[l][page_table].reshape(b_s, past_len, ATT_HEADS)
        hs, ss = mixer_layer(hs, pos_s, *wts, state_pool[l], state_conv[l], state_ssm[l],
                             (k_past, v_past, lf_past))
        st_p.append(sp)
        st_s.append(ss)
    k_p = jnp.stack([s[0] for s in st_p])
    v_p = jnp.stack([s[1] for s in st_p])
    lf_p = jnp.stack([s[2] for s in st_p])
    pool_p = jnp.stack([s[3] for s in st_p])
    conv_p = jnp.stack([s[4] for s in st_p])
    ssm_p = jnp.stack([s[5] for s in st_p])
    k_s = jnp.stack([s[0] for s in st_s])
    v_s = jnp.stack([s[1] for s in st_s])
    lf_s = jnp.stack([s[2] for s in st_s])
    pool_s = jnp.stack([s[3] for s in st_s])
    conv_s = jnp.stack([s[4] for s in st_s])
    ssm_s = jnp.stack([s[5] for s in st_s])
    return (hp, hs, k_p, v_p, lf_p, pool_p, conv_p, ssm_p, k_s, v_s, lf_s, pool_s, conv_s, ssm_s)
```

```python
import numpy as np
import concourse.bass as bass
import concourse.mybir as mybir
from concourse.bass_utils import run_bass_kernel_spmd
from contextlib import ExitStack

F32 = mybir.dt.float32
BF16 = mybir.dt.bfloat16
I32 = mybir.dt.int32
ALU = mybir.AluOpType
AF = mybir.ActivationFunctionType

DEPTH = 2
D = 1024
DIN = 8728
TP = 2048
NTOK = 2176
NT = 17
NSEQ = 16
NPG = 16
C_UA, C_ZA, C_Q, C_K, C_V, C_F, C_ZB, C_ZC, C_XBC, C_DT, C_G = (
    0, 512, 1024, 1536, 2048, 2560, 2568, 3080, 4104, 5640, 5656)
EPS = 1e-6
NEG = -30000.0


class Buf:
    __slots__ = ("name", "last_w", "reads", "dsem", "dcnt", "excl")

    def __init__(self, name, excl=False):
        self.name = name
        self.excl = excl
        self.last_w = None
        self.reads = {}
        self.dsem = None
        self.dcnt = 0


def bufs(name, n):
    return [Buf("%s%d" % (name, i)) for i in range(n)]


class Prog:
    ENGS = ("pe", "act", "dve", "pool", "sp")

    def __init__(self, nc, strict=True):
        self.nc = nc
        self.ops = {e: [] for e in self.ENGS}
        self.cnt = {e: 0 for e in self.ENGS}
        self.seen = {e: {} for e in self.ENGS}
        self.strict = strict
        self.ndsem = 0
        self.dsem_final = {}
        self.qhist = {e: [] for e in self.ENGS}
        self.dreg = {}
        self.st = ExitStack()

    def sb(self, name, shape, dtype):
        return self.st.enter_context(self.nc.sbuf_tensor(name, list(shape), dtype))

    def ps(self, name, shape, dtype):
        return self.st.enter_context(self.nc.psum_tensor(name, list(shape), dtype))

    def _deps(self, eng, reads, writes):
        need = {}
        for b in reads:
            t = b.last_w
            if t is not None and need.get(t[0], 0) < t[1]:
                need[t[0]] = t[1]
            if b.excl:
                for k, v in b.reads.items():
                    if k != eng and need.get(k, 0) < v:
                        need[k] = v
        for b in writes:
            t = b.last_w
            if t is not None and need.get(t[0], 0) < t[1]:
                need[t[0]] = t[1]
            for k, v in b.reads.items():
                if need.get(k, 0) < v:
                    need[k] = v
        waits = []
        seen = self.seen[eng]
        for k, v in need.items():
            if k == eng and (eng == "pe" or not self.strict):
                continue
            if seen.get(k, 0) >= v:
                continue
            seen[k] = v
            waits.append((k, v))
        return waits

    def _mark(self, tok, reads, writes):
        for b in reads:
            if b.reads.get(tok[0], 0) < tok[1]:
                b.reads[tok[0]] = tok[1]
        for b in writes:
            b.last_w = tok
            b.reads = {}

    def op(self, eng, fn, reads=(), writes=()):
        waits = self._deps(eng, reads, writes)
        self.cnt[eng] += 1
        tok = (eng, self.cnt[eng])
        self.ops[eng].append((waits, fn, (eng, 1)))
        self._mark(tok, reads, writes)

    def dma(self, q, fn, reads=(), writes=(), owner=None):
        if owner is None:
            owner = writes[0] if writes else reads[0]
        reg = self.dreg.get(owner.name)
        if reg is None:
            reg = ["d%d" % self.ndsem, 0]
            self.ndsem += 1
            self.dreg[owner.name] = reg
        owner.dsem = reg[0]
        waits = self._deps(q, reads, writes)
        qh = self.qhist[q]
        if len(qh) >= 24:
            k0, v0 = qh[-24]
            if self.seen[q].get(k0, 0) < v0:
                self.seen[q][k0] = v0
                waits.append((k0, v0))
        reg[1] += 1
        owner.dcnt = reg[1]
        tok = (owner.dsem, 16 * owner.dcnt)
        self.dsem_final[owner.dsem] = 16 * owner.dcnt
        self.ops[q].append((waits, fn, (owner.dsem, 16)))
        qh.append(tok)
        self._mark(tok, reads, writes)

    def barrier(self):
        state = dict(self.cnt)
        for e in self.ENGS:
            waits = []
            for k, v in list(state.items()) + list(self.dsem_final.items()):
                if k == e or v == 0:
                    continue
                if self.seen[e].get(k, 0) >= v:
                    continue
                self.seen[e][k] = v
                waits.append((k, v))
            if waits:
                self.ops[e].append((waits, None, None))

    def emit(self):
        nc = self.nc
        with self.st as st:
            sems = {}
            for e in self.ENGS:
                sems[e] = st.enter_context(nc.semaphore("s_" + e))
            for i in range(self.ndsem):
                sems["d%d" % i] = st.enter_context(nc.semaphore("sd%d" % i))
            block = st.enter_context(nc.Block())

            def run(ename, eng):
                for waits, fn, inc in self.ops[ename]:
                    for k, v in waits:
                        eng.wait_ge(sems[k], v)
                    if fn is None:
                        continue
                    ins = fn(eng)
                    ins.then_inc(sems[inc[0]], inc[1])
                if ename == "sp":
                    for k, v in self.dsem_final.items():
                        eng.wait_ge(sems[k], v)

            @block.tensor
            def _(e):
                run("pe", e)

            @block.scalar
            def _(e):
                run("act", e)

            @block.vector
            def _(e):
                run("dve", e)

            @block.gpsimd
            def _(e):
                run("pool", e)

            @block.sync
            def _(e):
                run("sp", e)


class Arena:
    def __init__(self, p, name, nbytes):
        self.t = p.sb(name, [128, nbytes // 4], F32)
        self.cap = nbytes // 4
        self.off = 0

    def reset(self, off=0):
        self.off = off

    def mark(self):
        return self.off

    def alloc(self, shape, dtype):
        P = shape[0]
        free = list(shape[1:])
        n = 1
        for s in free:
            n *= s
        esz = 4 if dtype in (F32, I32) else 2
        w = (n * esz + 3) // 4
        w = (w + 3) // 4 * 4
        assert self.off + w <= self.cap, ("arena overflow", self.off, w, self.cap)
        v = self.t[0:P, self.off:self.off + w]
        self.off += w
        if dtype != F32:
            v = v.bitcast(dtype)
        v = v[:, 0:n]
        if len(free) == 1:
            return v
        names = " ".join("abcd"[:len(free)])
        kw = {k: s for k, s in zip(names.split(), free)}
        return v.rearrange("p (%s) -> p %s" % (names, names), **kw)


def build(n_pool=2560, dbg=None, stop=None, strict=True):
    nc = bass.Bass("TRN2", target_bir_lowering=False)
    p = Prog(nc, strict=strict)

    def din(name, shape, dt=F32):
        return nc.dram_tensor(name, list(shape), dt, kind="ExternalInput").ap()

    def dout(name, shape, dt=F32):
        return nc.dram_tensor(name, list(shape), dt, kind="ExternalOutput").ap()

    xp = din("xp", [TP, D]); xs = din("xs", [128, D])
    ck = din("ck", [DEPTH, n_pool * 128, 512]); cv = din("cv", [DEPTH, n_pool * 128, 512])
    clf = din("clf", [DEPTH, n_pool, 1024])
    spool = din("spool", [DEPTH, NSEQ, 15, 512]); sconv = din("sconv", [DEPTH, NSEQ, 3, 1536])
    sssm = din("sssm", [DEPTH, NSEQ, 1024, 128])
    pt = din("pt", [NSEQ, NPG], I32)
    norm_pre = din("norm_pre", [DEPTH, D]); w_in = din("w_in", [DEPTH, D, DIN])
    pool_w = din("pool_w", [DEPTH, 4, 128, 128]); pool_scale = din("pool_scale", [DEPTH, 512])
    f_bias = din("f_bias", [DEPTH, 8]); conv_w = din("conv_w", [DEPTH, 4, 1536]); conv_b = din("conv_b", [DEPTH, 1536])
    dt_bias = din("dt_bias", [DEPTH, 16]); a_log = din("a_log", [DEPTH, 16]); d_skip = din("d_skip", [DEPTH, 16])
    ssm_norm = din("ssm_norm", [DEPTH, D])
    w_a = din("w_a", [DEPTH, 512, D]); w_b = din("w_b", [DEPTH, 512, D]); w_c = din("w_c", [DEPTH, D, D])
    w_out = din("w_out", [DEPTH, D, D]); norm_post = din("norm_post", [DEPTH, D])

    yp = dout("yp", [TP, D]); ys = dout("ys", [128, D])
    kp = dout("kp", [DEPTH, TP, 512]); vp = dout("vp", [DEPTH, TP, 512]); lfp = dout("lfp", [DEPTH, TP, 8])
    poolp = dout("poolp", [DEPTH, 15, 512]); convp = dout("convp", [DEPTH, 3, 1536]); ssmp = dout("ssmp", [DEPTH, 1024, 128])
    kso = dout("kso", [DEPTH, 128, 512]); vso = dout("vso", [DEPTH, 128, 512]); lfs = dout("lfs", [DEPTH, 128, 8])
    pools = dout("pools", [DEPTH, NSEQ, 15, 512]); convs = dout("convs", [DEPTH, NSEQ, 3, 1536])
    ssms = dout("ssms", [DEPTH, NSEQ, 1024, 128])
    x1 = nc.dram_tensor("x1", [NTOK, D], F32, kind="Internal").ap()
    Bx1 = bufs("x1_", NT)
    dbg_out = {}
    if dbg:
        for k, shp in dbg.items():
            dbg_out[k] = dout("dbg_" + k, shp, BF16 if k in ("hnT", "yT", "mT") else F32)

    hnT = p.sb("hnT", [128, 8, NTOK], BF16); BhnT = bufs("hnT", NT)
    yT = p.sb("yT", [128, 16, NTOK], BF16)
    ByT = [bufs("yTa", NT), bufs("yTb", NT), bufs("yTc", NT)]
    YOFF = (0, 4, 8)
    PS = [p.ps("ps%d" % i, [128, 512], F32)[:, :] for i in range(8)]
    BPS = [Buf("ps%d" % i, excl=True) for i in range(8)]
    cst = Arena(p, "cst", 15 * 1024)
    ar = Arena(p, "arena", 90 * 1024)

    Bc = Buf("consts")
    onesf = cst.alloc([128, 128], F32); identf = cst.alloc([128, 128], F32); trif = cst.alloc([128, 128], F32)
    zerof = cst.alloc([128, 128], F32)
    identb = cst.alloc([128, 128], BF16); onesb = cst.alloc([128, 128], BF16); maskb = cst.alloc([128, 128], BF16)
    sel127 = cst.alloc([128, 128], F32)
    blk = cst.alloc([128, 128], F32); btrif = cst.alloc([128, 128], F32)
    negm_p = cst.alloc([128, 128], F32); negm_s = cst.alloc([128, 128], F32)
    lastmask = cst.alloc([128, 16], F32); lastrow = cst.alloc([128, 1], F32); L2s = cst.alloc([128, 128], F32)
    Emat = cst.alloc([128, 128], F32)
    invc = cst.alloc([128, 4, 16], F32)
    iot = cst.alloc([128, 16], I32)
    iop = cst.alloc([128, 1], I32); iopf = cst.alloc([128, 1], F32)

    def cop(eng, fn):
        p.op(eng, fn, reads=[Bc], writes=[Bc])

    cop("pool", lambda e: e.memset(onesf, 1.0))
    cop("pool", lambda e: e.memset(zerof, 0.0))
    cop("pool", lambda e: e.affine_select(out=identf, in_=onesf, pattern=[[-1, 128]], compare_op=ALU.is_equal, fill=0.0, base=0, channel_multiplier=1))
    cop("pool", lambda e: e.affine_select(out=trif, in_=onesf, pattern=[[1, 128]], compare_op=ALU.is_ge, fill=0.0, base=0, channel_multiplier=-1))
    cop("pool", lambda e: e.affine_select(out=sel127, in_=onesf, pattern=[[0, 128]], compare_op=ALU.is_equal, fill=0.0, base=-127, channel_multiplier=1))
    cop("pool", lambda e: e.tensor_copy(out=identb, in_=identf))
    cop("pool", lambda e: e.tensor_copy(out=onesb, in_=onesf))
    cop("pool", lambda e: e.affine_select(out=negm_p, in_=zerof, pattern=[[1, 128]], compare_op=ALU.is_ge, fill=NEG, base=0, channel_multiplier=-1))
    cop("pool", lambda e: e.tensor_copy(out=maskb, in_=negm_p))
    cop("pool", lambda e: e.affine_select(out=Emat, in_=onesf, pattern=[[1, 128]], compare_op=ALU.is_ge, fill=0.0, base=0, channel_multiplier=-8))
    cop("pool", lambda e: e.affine_select(out=Emat, in_=Emat, pattern=[[-1, 128]], compare_op=ALU.is_ge, fill=0.0, base=7, channel_multiplier=8))
    cop("pe", lambda e: e.matmul(PS[0][:, 0:128], lhsT=Emat[0:16, :], rhs=Emat[0:16, :], start=True, stop=True))
    cop("dve", lambda e: e.tensor_copy(out=blk, in_=PS[0][:, 0:128]))
    cop("dve", lambda e: e.tensor_tensor(out=btrif, in0=blk, in1=trif, op=ALU.mult))
    cop("dve", lambda e: e.tensor_scalar(out=negm_s, in0=btrif, scalar1=-1.0, scalar2=-NEG, op0=ALU.add, op1=ALU.mult))
    cop("pool", lambda e: e.affine_select(out=lastmask, in_=onesf[:, 0:16], pattern=[[-8, 16]], compare_op=ALU.is_equal, fill=0.0, base=-7, channel_multiplier=1))
    cop("dve", lambda e: e.tensor_reduce(out=lastrow, in_=lastmask, axis=mybir.AxisListType.X, op=ALU.add))
    cop("dve", lambda e: e.tensor_scalar(out=L2s, in0=blk, scalar1=lastrow[:, 0:1], scalar2=None, op0=ALU.mult))
    cop("pool", lambda e: e.iota(iot, pattern=[[1, 16]], base=1, channel_multiplier=0))
    cop("pool", lambda e: e.iota(iop, pattern=[[0, 1]], base=0, channel_multiplier=1))
    cop("dve", lambda e: e.tensor_copy(out=iopf, in_=iop))
    for g in range(4):
        cop("dve", lambda e, g=g: e.tensor_copy(out=invc[:, g, :], in_=iot))
        cop("dve", lambda e, g=g: e.tensor_scalar(out=invc[:, g, :], in0=invc[:, g, :], scalar1=float(2 ** (g + 1)), scalar2=None, op0=ALU.min))
        cop("dve", lambda e, g=g: e.reciprocal(out=invc[:, g, :], in_=invc[:, g, :]))

    ptb = cst.alloc([128, 256], I32); ptf = cst.alloc([128, 256], F32); kidx = cst.alloc([128, 256], I32)
    lidx = cst.alloc([128, 2], I32)
    Bpt = Buf("pt")
    p.dma("sp", lambda e: e.dma_start(out=ptb, in_=pt.rearrange("a b -> (a b)").partition_broadcast(128)), writes=[Bpt])
    for h in range(2):
        p.dma("sp", lambda e, h=h: e.dma_start(out=lidx[:, h:h + 1], in_=pt[8 * h:8 * h + 8, :].rearrange("a (b o) -> (a b) o", o=1)), writes=[Bpt])
    p.op("dve", lambda e: e.tensor_copy(out=ptf, in_=ptb), reads=[Bpt, Bc], writes=[Bpt])
    p.op("dve", lambda e: e.tensor_scalar(out=ptf, in0=ptf, scalar1=128.0, scalar2=iopf[:, 0:1], op0=ALU.mult, op1=ALU.add), reads=[Bpt], writes=[Bpt])
    p.op("dve", lambda e: e.tensor_copy(out=kidx, in_=ptf), reads=[Bpt], writes=[Bpt])

    npre_bc = cst.alloc([128, D], F32); Bnpre = Buf("npre")
    par = cst.alloc([128, 64], F32); Bpar = Buf("par")

    def dbg_dump(key, src_ap, rbufs, eng="sp"):
        if key in dbg_out:
            p.dma(eng, lambda e: e.dma_start(out=dbg_out[key], in_=src_ap), reads=rbufs)

    def tile_bufs(blist, tok0, ntok):
        return blist[tok0 // 128:(tok0 + ntok + 127) // 128]

    def load_w(dst, src, wbuf, q="pool"):
        p.dma(q, lambda e: e.dma_start(out=dst, in_=src.rearrange("(kc p) n -> p kc n", p=128)), writes=[wbuf])

    def proj_fm(W, wbuf, c0, ncols, tok0, ntok, ps_ap, psbuf):
        for kc in range(8):
            p.op("pe", lambda e, kc=kc: e.matmul(ps_ap, lhsT=W[:, kc, c0:c0 + ncols], rhs=hnT[:, kc, tok0:tok0 + ntok],
                                                  start=(kc == 0), stop=(kc == 7)),
                 reads=[wbuf] + tile_bufs(BhnT, tok0, ntok), writes=[psbuf])

    def proj_tm(W, wbuf, c0, ncols, j, ps_ap, psbuf):
        for kc in range(8):
            p.op("pe", lambda e, kc=kc: e.matmul(ps_ap, lhsT=hnT[:, kc, j * 128:(j + 1) * 128], rhs=W[:, kc, c0:c0 + ncols],
                                                  start=(kc == 0), stop=(kc == 7)),
                 reads=[wbuf, BhnT[j]], writes=[psbuf])

    def norm_to_hnT(xt, Bxt, j, sq, Bsq, hb, Bhb, psi):
        p.op("pool", lambda e: e.memset(sq, 0.0), writes=[Bsq])
        p.op("act", lambda e: e.activation(out=hb, in_=xt, func=AF.Square, accum_out=sq[:, 0:1]), reads=[Bxt], writes=[Bsq, Bhb])
        p.op("act", lambda e: e.activation(out=sq[:, 1:2], in_=sq[:, 0:1], func=AF.Ln, bias=EPS, scale=1.0 / D), reads=[Bsq], writes=[Bsq])
        p.op("act", lambda e: e.activation(out=sq[:, 2:3], in_=sq[:, 1:2], func=AF.Exp, scale=-0.5), reads=[Bsq], writes=[Bsq])
        p.op("dve", lambda e: e.scalar_tensor_tensor(out=hb, in0=xt, scalar=sq[:, 2:3], in1=npre_bc, op0=ALU.mult, op1=ALU.mult),
             reads=[Bxt, Bsq, Bnpre], writes=[Bhb])
        pst = PS[psi].bitcast(BF16)
        for kc in range(8):
            p.op("pe", lambda e, kc=kc: e.transpose(pst[:, kc * 128:(kc + 1) * 128], hb[:, kc * 128:(kc + 1) * 128], identb),
                 reads=[Bhb, Bc], writes=[BPS[psi]])
        p.op("act", lambda e: e.copy(out=hnT[:, :, j * 128:(j + 1) * 128], in_=pst.rearrange("p (k t) -> p k t", k=8)),
             reads=[BPS[psi]], writes=[BhnT[j]])

    for l in range(DEPTH):
        if l == 0:
            p.dma("sp", lambda e: e.dma_start(out=npre_bc, in_=norm_pre[0].partition_broadcast(128)), writes=[Bnpre])
            p.barrier(); ar.reset()
            xt = [ar.alloc([128, D], F32) for _ in range(2)]; Bxt = bufs("xt", 2)
            hb = [ar.alloc([128, D], BF16) for _ in range(2)]; Bhb = bufs("hb", 2)
            sq = [ar.alloc([128, 4], F32) for _ in range(2)]; Bsq = bufs("sq", 2)
            for j in range(NT):
                k = j % 2
                src = xp[j * 128:(j + 1) * 128, :] if j < 16 else xs
                p.dma("sp", lambda e, k=k, src=src: e.dma_start(out=xt[k], in_=src), writes=[Bxt[k]])
                norm_to_hnT(xt[k], Bxt[k], j, sq[k], Bsq[k], hb[k], Bhb[k], k)
        if stop == "N":
            break

        import os
        SKIPAB = os.environ.get('SKIPAB') == '1'
        p.barrier(); ar.reset()
        wA = [ar.alloc([128, 8, 256], BF16) for _ in range(2)]; BwA = bufs("wA", 2)
        pw = [ar.alloc([128, 128], BF16) for _ in range(2)]; Bpw = bufs("pw", 2)
        psc = ar.alloc([128, 4], F32); Bpsc = Buf("psc")
        ub = ar.alloc([128, 15 + TP], F32); Bub = Buf("ub")
        sA = ar.alloc([128, 15 + TP], F32); BsA = Buf("sA")
        sB = ar.alloc([128, 15 + TP], F32); BsB = Buf("sB")
        zs = ar.alloc([128, NTOK], F32); Bzs = Buf("zs")
        dbf = ar.alloc([128, NTOK], BF16); Bdbf = Buf("dbf")
        us = ar.alloc([128, 16, 23], F32); Bus = Buf("us")
        ssA = ar.alloc([128, 16, 23], F32); BssA = Buf("ssA")
        ssB = ar.alloc([128, 16, 23], F32); BssB = Buf("ssB")
        hist = [ar.alloc([120, 512], F32) for _ in range(2)]; Bhist = bufs("hist", 2)
        tmp15 = ar.alloc([128, 16], F32); Btmp15 = Buf("tmp15")
        utm = [ar.alloc([128, 512], F32) for _ in range(2)]; Butm = bufs("utm", 2)
        wU = ar.alloc([128, 8, 512], BF16); BwU = Buf("wU")

        for g in range(4):
            p.dma("sp", lambda e, l=l, g=g: e.dma_start(out=psc[:, g:g + 1], in_=pool_scale[l, g * 128:(g + 1) * 128].rearrange("(c o) -> c o", o=1)), writes=[Bpsc])
        for h2 in range(2):
            p.dma("sp", lambda e, l=l, h2=h2: e.dma_start(out=hist[h2], in_=spool[l, 8 * h2:8 * h2 + 8].rearrange("b j c -> (b j) c")),
                  writes=[Bhist[h2]])
        p.op("pool", lambda e: e.memset(ub[:, 0:15], 0.0), writes=[Bub])
        load_w(wU, w_in[l][:, C_UA:C_UA + 512], BwU)
        for k, j in enumerate((15, 16)):
            proj_tm(wU, BwU, 0, 512, j, PS[6 + k], BPS[6 + k])
            p.op("act", lambda e, k=k: e.copy(out=utm[k], in_=PS[6 + k]), reads=[BPS[6 + k]], writes=[Butm[k]])
        p.dma("sp", lambda e, l=l: e.dma_start(out=poolp[l], in_=utm[0][113:128, :]), reads=[Butm[0]])
        for b in range(NSEQ):
            p.dma("sp", lambda e, l=l, b=b: e.dma_start(out=pools[l, b, 7:15, :], in_=utm[1][8 * b:8 * b + 8, :]), reads=[Butm[1]])
        p.dma("sp", lambda e, l=l: e.dma_start(out=pools[l, :, 0:7, :], in_=spool[l, :, 8:15, :]), owner=Butm[1])

        for g in range(4):
            w = 2 ** (g + 1)
            k = g % 2
            load_w(wA[k][:, :, 0:128], w_in[l][:, C_UA + g * 128:C_UA + (g + 1) * 128], BwA[k])
            load_w(wA[k][:, :, 128:256], w_in[l][:, C_ZA + g * 128:C_ZA + (g + 1) * 128], BwA[k])
            p.dma("pool", lambda e, l=l, g=g, k=k: e.dma_start(out=pw[k], in_=pool_w[l, g]), writes=[Bpw[k]])
            for tg in range(5):
                tok0 = tg * 512
                ntok = 512 if tg < 4 else 128
                pu, pz = (tg * 2) % 6, (tg * 2 + 1) % 6
                proj_fm(wA[k], BwA[k], 0, 128, tok0, ntok, PS[pu][:, 0:ntok], BPS[pu])
                proj_fm(wA[k], BwA[k], 128, 128, tok0, ntok, PS[pz][:, 0:ntok], BPS[pz])
                if tg < 4:
                    p.op("act", lambda e, pu=pu, tok0=tok0: e.copy(out=ub[:, 15 + tok0:15 + tok0 + 512], in_=PS[pu]), reads=[BPS[pu]], writes=[Bub])
                else:
                    p.op("act", lambda e, pu=pu: e.copy(out=us[:, :, 15:23], in_=PS[pu][:, 0:128].rearrange("p (b t) -> p b t", b=16)),
                         reads=[BPS[pu]], writes=[Bus])
                p.op("act", lambda e, pz=pz, tok0=tok0, ntok=ntok: e.activation(out=zs[:, tok0:tok0 + ntok], in_=PS[pz][:, 0:ntok], func=AF.Silu),
                     reads=[BPS[pz]], writes=[Bzs])
            for h2 in range(2):
                p.op("pe", lambda e, g=g, h2=h2: e.transpose(PS[6][:, h2 * 128:h2 * 128 + 120], hist[h2][:, g * 128:(g + 1) * 128], identf[0:120, 0:120]),
                     reads=[Bhist[h2], Bc], writes=[BPS[6]])
            p.op("act", lambda e: e.copy(out=us[:, :, 0:15].rearrange("p (h b) j -> p h b j", h=2),
                                         in_=PS[6][:, 0:256].rearrange("p (h x) -> p h x", h=2)[:, :, 0:120].rearrange("p h (b j) -> p h b j", b=8)),
                 reads=[BPS[6]], writes=[Bus])
            src_p, Bsrc_p, src_s, Bsrc_s = ub, Bub, us, Bus
            lo = 0
            pp = [(sA, BsA, ssA, BssA), (sB, BsB, ssB, BssB)]
            for step in range(g + 1):
                sh = 2 ** step
                dp, Bdp, ds_, Bds = pp[step % 2]
                n = 15 + TP
                p.op("dve", lambda e, dp=dp, sp_=src_p, lo=lo, sh=sh, n=n: e.tensor_tensor(out=dp[:, lo + sh:n], in0=sp_[:, lo + sh:n], in1=sp_[:, lo:n - sh], op=ALU.add),
                     reads=[Bsrc_p], writes=[Bdp])
                p.op("pool", lambda e, ds_=ds_, ss_=src_s, lo=lo, sh=sh: e.tensor_tensor(out=ds_[:, :, lo + sh:23], in0=ss_[:, :, lo + sh:23], in1=ss_[:, :, lo:23 - sh], op=ALU.add),
                     reads=[Bsrc_s], writes=[Bds])
                src_p, Bsrc_p, src_s, Bsrc_s = dp, Bdp, ds_, Bds
                lo += sh
            p.op("dve", lambda e, sp_=src_p, w=w: e.scalar_tensor_tensor(out=dbf[:, 0:TP], in0=sp_[:, 15:15 + TP], scalar=1.0 / w, in1=ub[:, 15:15 + TP], op0=ALU.mult, op1=ALU.subtract),
                 reads=[Bsrc_p, Bub], writes=[Bdbf])
            p.op("dve", lambda e, sp_=src_p, g=g: e.tensor_tensor(out=tmp15[:, 0:15], in0=sp_[:, 15:30], in1=invc[:, g, 0:15], op=ALU.mult),
                 reads=[Bsrc_p, Bc], writes=[Btmp15])
            p.op("dve", lambda e: e.tensor_tensor(out=dbf[:, 0:15], in0=tmp15[:, 0:15], in1=ub[:, 15:30], op=ALU.subtract),
                 reads=[Btmp15, Bub, Bdbf], writes=[Bdbf])
            p.op("dve", lambda e, ss_=src_s, w=w: e.scalar_tensor_tensor(out=dbf[:, TP:NTOK].rearrange("p (b t) -> p b t", b=16), in0=ss_[:, :, 15:23], scalar=1.0 / w, in1=us[:, :, 15:23], op0=ALU.mult, op1=ALU.subtract),
                 reads=[Bsrc_s, Bus, Bdbf], writes=[Bdbf])
            for tg in range(5):
                tok0 = tg * 512
                ntok = 512 if tg < 4 else 128
                py = tg % 6
                p.op("pe", lambda e, k=k, py=py, tok0=tok0, ntok=ntok: e.matmul(PS[py][:, 0:ntok], lhsT=pw[k], rhs=dbf[:, tok0:tok0 + ntok], start=True, stop=True),
                     reads=[Bpw[k], Bdbf], writes=[BPS[py]])
                p.op("dve", lambda e, g=g, py=py, tok0=tok0, ntok=ntok: e.scalar_tensor_tensor(out=yT[:, g, tok0:tok0 + ntok], in0=PS[py][:, 0:ntok], scalar=psc[:, g:g + 1], in1=zs[:, tok0:tok0 + ntok], op0=ALU.mult, op1=ALU.mult),
                     reads=[BPS[py], Bpsc, Bzs], writes=tile_bufs(ByT[0], tok0, ntok))
        if stop == "A":
            break

        p.barrier(); ar.reset()
        wF = ar.alloc([128, 8, 8], BF16); BwF = Buf("wF")
        fbb = ar.alloc([128, 8], F32); Bfbb = Buf("fbb")
        lf = ar.alloc([128, 17, 8], F32); Blf = Buf("lf")
        qTs = ar.alloc([128, 4, 128], BF16); kTs = ar.alloc([128, 4, 128], BF16); Bqks = Buf("qks")
        vnew = ar.alloc([128, 512], BF16); Bvnew = Buf("vnew")
        zsTs = ar.alloc([128, 4, 128], F32); BzsTs = Buf("zsTs")
        markB = ar.mark()
        tot = ar.alloc([128, 16, 8], F32); carry = ar.alloc([128, 17, 8], F32); ccum = ar.alloc([128, 16, 8], F32); Bcc = Buf("cc")
        btab = ar.alloc([128, 16, 16, 8], F32); Bbtab = Buf("btab")
        wB = [ar.alloc([128, 8, 512], BF16) for _ in range(2)]; BwB = bufs("wB", 2)
        qT = ar.alloc([128, NTOK], BF16); BqT = Buf("qT")
        kT = ar.alloc([128, NTOK], BF16); BkT = Buf("kT")
        vaug = ar.alloc([128, 16, 2, 66], BF16); Bvaug = bufs("vaug", 16)
        kvst = [ar.alloc([128, 256], F32) for _ in range(3)]; Bkvst = bufs("kvst", 3)
        pT = [ar.alloc([128, 128], BF16) for _ in range(4)]; BpT = bufs("pT", 4)
        zsb = [ar.alloc([128, 128], F32) for _ in range(2)]; Bzsb = bufs("zsb", 2)
        rinv = [ar.alloc([128, 1], F32) for _ in range(2)]; Brinv = bufs("rinv", 2)
        ybt = [ar.alloc([128, 128], BF16) for _ in range(2)]; Bybt = bufs("ybt", 2)

        load_w(wF, w_in[l][:, C_F:C_F + 8], BwF)
        p.dma("sp", lambda e, l=l: e.dma_start(out=fbb, in_=f_bias[l].partition_broadcast(128)), writes=[Bfbb])
        for j in range(NT):
            proj_tm(wF, BwF, 0, 8, j, PS[0][:, j * 8:(j + 1) * 8], BPS[0])
        lf_flat = lf.rearrange("p j h -> p (j h)")
        p.op("dve", lambda e: e.tensor_tensor(out=lf, in0=PS[0][:, 0:136].rearrange("p (j h) -> p j h", h=8), in1=fbb.unsqueeze(1).to_broadcast([128, 17, 8]), op=ALU.add),
             reads=[BPS[0], Bfbb], writes=[Blf])
        p.op("act", lambda e: e.activation(out=lf_flat, in_=lf_flat, func=AF.Exp, scale=-1.0), reads=[Blf], writes=[Blf])
        p.op("act", lambda e: e.activation(out=lf_flat, in_=lf_flat, func=AF.Ln, bias=1.0), reads=[Blf], writes=[Blf])
        p.op("dve", lambda e: e.tensor_scalar(out=lf_flat, in0=lf_flat, scalar1=-1.0, scalar2=None, op0=ALU.mult), reads=[Blf], writes=[Blf])
        p.dma("sp", lambda e, l=l: e.dma_start(out=lfp[l].rearrange("(j p) h -> p j h", p=128), in_=lf[:, 0:16, :]), reads=[Blf])
        p.dma("sp", lambda e, l=l: e.dma_start(out=lfs[l], in_=lf[:, 16, :]), reads=[Blf])
        if stop == "B0a":
            break
        p.op("pe", lambda e: e.matmul(PS[1][:, 0:128], lhsT=onesf, rhs=lf_flat[:, 0:128], start=True, stop=True), reads=[Blf, Bc], writes=[BPS[1]])
        p.op("pe", lambda e: e.matmul(PS[2][:, 0:128], lhsT=trif, rhs=lf_flat[:, 0:128], start=True, stop=True), reads=[Blf, Bc], writes=[BPS[2]])
        p.op("dve", lambda e: e.tensor_copy(out=tot.rearrange("p j h -> p (j h)"), in_=PS[1][:, 0:128]), reads=[BPS[1]], writes=[Bcc])
        p.op("dve", lambda e: e.memset(carry[:, 0, :], 0.0), reads=[Bcc], writes=[Bcc])
        for j in range(1, 17):
            p.op("dve", lambda e, j=j: e.tensor_tensor(out=carry[:, j, :], in0=carry[:, j - 1, :], in1=tot[:, j - 1, :], op=ALU.add), reads=[Bcc], writes=[Bcc])
        p.op("dve", lambda e: e.tensor_tensor(out=ccum, in0=PS[2][:, 0:128].rearrange("p (j h) -> p j h", h=8), in1=carry[:, 0:16, :], op=ALU.add), reads=[BPS[2], Bcc], writes=[Bcc])
        if stop == "B0c":
            break
        for Q in range(16):
            p.op("dve", lambda e, Q=Q: e.tensor_tensor(out=btab[:, Q, 0:Q + 1, :], in0=carry[:, Q + 1:Q + 2, :].to_broadcast([128, Q + 1, 8]), in1=ccum[:, 0:Q + 1, :], op=ALU.subtract),
                 reads=[Bcc, Bbtab], writes=[Bbtab])
        p.op("dve", lambda e: e.memset(vaug[:, :, :, 64:66], 1.0), writes=Bvaug)
        if stop == "B0":
            break

        import os
        for pr in range(int(os.environ.get('NPR', 4))):
            k = pr % 2
            for i, c0 in enumerate((C_Q, C_K, C_V, C_ZB)):
                load_w(wB[k][:, :, i * 128:(i + 1) * 128], w_in[l][:, c0 + pr * 128:c0 + (pr + 1) * 128], BwB[k])
            for tg in range(5):
                tok0 = tg * 512
                ntok = 512 if tg < 4 else 128
                pa, pb = 5 + (2 * tg) % 3, 5 + (2 * tg + 1) % 3
                proj_fm(wB[k], BwB[k], 0, 128, tok0, ntok, PS[pa][:, 0:ntok], BPS[pa])
                p.op("dve", lambda e, pa=pa, tok0=tok0, ntok=ntok: e.tensor_scalar(out=qT[:, tok0:tok0 + ntok], in0=PS[pa][:, 0:ntok], scalar1=0.125, scalar2=None, op0=ALU.mult),
                     reads=[BPS[pa]], writes=[BqT])
                proj_fm(wB[k], BwB[k], 128, 128, tok0, ntok, PS[pb][:, 0:ntok], BPS[pb])
                p.op("dve", lambda e, pb=pb, tok0=tok0, ntok=ntok: e.tensor_copy(out=kT[:, tok0:tok0 + ntok], in_=PS[pb][:, 0:ntok]),
                     reads=[BPS[pb]], writes=[BkT])
            if stop == "B0b1":
                continue
            p.op("pool", lambda e, pr=pr: e.tensor_copy(out=qTs[:, pr, :], in_=qT[:, TP:NTOK]), reads=[BqT], writes=[Bqks])
            p.op("pool", lambda e, pr=pr: e.tensor_copy(out=kTs[:, pr, :], in_=kT[:, TP:NTOK]), reads=[BkT], writes=[Bqks])
            proj_fm(wB[k], BwB[k], 384, 128, TP, 128, PS[5][:, 0:128], BPS[5])
            p.op("act", lambda e, pr=pr: e.activation(out=zsTs[:, pr, :], in_=PS[5][:, 0:128], func=AF.Silu), reads=[BPS[5]], writes=[BzsTs])
            if stop == "B0b2":
                continue
            import os
            SK = os.environ.get("SK", "")
            for j in range(int(os.environ.get("NTJ", NT))):
                pb = 5 + j % 3
                m = j % 3
                proj_tm(wB[k], BwB[k], 128, 256, j, PS[pb][:, 0:256], BPS[pb])
                p.op("act", lambda e, pb=pb, m=m: e.copy(out=kvst[m], in_=PS[pb][:, 0:256]), reads=[BPS[pb]], writes=[Bkvst[m]])
                if j < 16:
                    if "v" not in SK:
                        for a in range(2):
                            p.op("dve", lambda e, j=j, a=a, m=m: e.tensor_copy(out=vaug[:, j, a, 0:64], in_=kvst[m][:, 128 + 64 * a:192 + 64 * a]),
                                 reads=[Bkvst[m]], writes=[Bvaug[j]])
                    if "d" not in SK:
                        p.dma("sp", lambda e, l=l, j=j, pr=pr, m=m: e.dma_start(out=kp[l, j * 128:(j + 1) * 128, pr * 128:(pr + 1) * 128], in_=kvst[m][:, 0:128]), reads=[Bkvst[m]])
                        p.dma("sp", lambda e, l=l, j=j, pr=pr, m=m: e.dma_start(out=vp[l, j * 128:(j + 1) * 128, pr * 128:(pr + 1) * 128], in_=kvst[m][:, 128:256]), reads=[Bkvst[m]])
                else:
                    if "n" not in SK:
                        p.op("dve", lambda e, m=m, pr=pr: e.tensor_copy(out=vnew[:, pr * 128:(pr + 1) * 128], in_=kvst[m][:, 128:256]), reads=[Bkvst[m]], writes=[Bvnew])
                    if "e" not in SK:
                        p.dma("sp", lambda e, l=l, pr=pr, m=m: e.dma_start(out=kso[l, :, pr * 128:(pr + 1) * 128], in_=kvst[m][:, 0:128]), reads=[Bkvst[m]])
                        p.dma("sp", lambda e, l=l, pr=pr, m=m: e.dma_start(out=vso[l, :, pr * 128:(pr + 1) * 128], in_=kvst[m][:, 128:256]), reads=[Bkvst[m]])
            if stop == "B0b":
                continue
            it = 0
            for Q in range(16):
                zq = Q % 2
                proj_tm(wB[k], BwB[k], 384, 128, Q, PS[5][:, 0:128], BPS[5])
                p.op("act", lambda e, zq=zq: e.activation(out=zsb[zq], in_=PS[5][:, 0:128], func=AF.Silu), reads=[BPS[5]], writes=[Bzsb[zq]])
                for h2 in range(2):
                    hb = 64 * h2
                    h = 2 * pr + h2
                    po = 3 + h2

                    def qk(S, it):
                        sb_ = it % 3
                        p.op("pe", lambda e, S=S, sb_=sb_, hb=hb, Q=Q: e.matmul(PS[sb_][:, 0:128], lhsT=kT[hb:hb + 64, S * 128:(S + 1) * 128], rhs=qT[hb:hb + 64, Q * 128:(Q + 1) * 128], start=True, stop=(S != Q)),
                             reads=[BkT, BqT], writes=[BPS[sb_]])
                        if S == Q:
                            p.op("pe", lambda e, sb_=sb_: e.matmul(PS[sb_][:, 0:128], lhsT=identb, rhs=maskb, start=False, stop=True), reads=[Bc], writes=[BPS[sb_]])
                    qk(0, it)
                    for S in range(Q + 1):
                        if S + 1 <= Q:
                            qk(S + 1, it + 1)
                        sb_ = it % 3
                        pi = it % 4
                        p.op("act", lambda e, sb_=sb_, pi=pi, Q=Q, S=S, h=h: e.activation(out=pT[pi], in_=PS[sb_][:, 0:128], func=AF.Exp, bias=btab[:, Q, S, h:h + 1], scale=1.0),
                             reads=[BPS[sb_], Bbtab], writes=[BpT[pi]])
                        p.op("pe", lambda e, pi=pi, S=S, h2=h2, po=po, Q=Q: e.matmul(PS[po][:, 0:65], lhsT=pT[pi], rhs=vaug[:, S, h2, 0:65], start=(S == 0), stop=(S == Q)),
                             reads=[BpT[pi], Bvaug[S]], writes=[BPS[po]])
                        it += 1
                    p.op("dve", lambda e, po=po, h2=h2: e.reciprocal(out=rinv[h2], in_=PS[po][:, 64:65]), reads=[BPS[po]], writes=[Brinv[h2]])
                    p.op("dve", lambda e, po=po, h2=h2, hb=hb, zq=zq: e.scalar_tensor_tensor(out=ybt[zq][:, hb:hb + 64], in0=PS[po][:, 0:64], scalar=rinv[h2][:, 0:1], in1=zsb[zq][:, hb:hb + 64], op0=ALU.mult, op1=ALU.mult),
                         reads=[BPS[po], Brinv[h2], Bzsb[zq]], writes=[Bybt[zq]])
                pst = PS[6].bitcast(BF16)
                p.op("pe", lambda e, zq=zq, pst=pst: e.transpose(pst[:, 0:128], ybt[zq], identb), reads=[Bybt[zq], Bc], writes=[BPS[6]])
                p.op("dve", lambda e, pst=pst, pr=pr, Q=Q: e.tensor_copy(out=yT[:, 4 + pr, Q * 128:(Q + 1) * 128], in_=pst[:, 0:128]), reads=[BPS[6]], writes=[ByT[1][Q]])
        if stop == "B1":
            break

        p.barrier(); ar.reset(markB)
        sufm = ar.alloc([128, 128], F32); Bsufm = Buf("sufm")
        lfg = ar.alloc([128, 1024], F32); Blfg = Buf("lfg")
        lft = ar.alloc([128, 8, 128], F32); Blft = Buf("lft")
        totS = ar.alloc([128, 8, 8, 16], F32); later = ar.alloc([128, 8, 8, 16], F32); Blat = Buf("later")
        bpast = ar.alloc([128, 8, 2, 128], F32); Bbpast = Buf("bpast")
        bnew = ar.alloc([128, 8], F32); Bbnew = Buf("bnew")
        qbd = ar.alloc([128, 4, 16, 16], BF16); Bqbd = Buf("qbd")
        kbf = [ar.alloc([128, 512], BF16) for _ in range(4)]; Bkbf = bufs("kbf", 4)
        KT = [ar.alloc([128, 4, 128], BF16) for _ in range(2)]; BKT = bufs("KT", 2)
        Vb = ar.alloc([128, 17, 512], BF16); BVb = bufs("Vb", 17)
        sc = ar.alloc([128, 17, 64], F32); Bsc = Buf("sc")
        pTs = ar.alloc([128, 17, 64], BF16); BpTs = Buf("pTs")
        rs = ar.alloc([128, 16, 64], F32); Brs = Buf("rs")
        ot = ar.alloc([128, 4, 16, 16], F32); Bot = Buf("ot")

        p.op("dve", lambda e: e.tensor_tensor(out=sufm, in0=onesf, in1=trif, op=ALU.subtract), reads=[Bc], writes=[Bsufm])
        kf = ar.alloc([128, 256], F32); kidxL = ar.alloc([128, 256], I32); lf2 = ar.alloc([128, 2], F32); lidxL = ar.alloc([128, 2], I32); BidxL = Buf("idxL")
        p.op("dve", lambda e: e.tensor_copy(out=kf, in_=kidx), reads=[Bpt], writes=[BidxL])
        p.op("dve", lambda e, l=l: e.tensor_scalar(out=kf, in0=kf, scalar1=float(l * n_pool * 128), scalar2=None, op0=ALU.add), reads=[BidxL], writes=[BidxL])
        p.op("dve", lambda e: e.tensor_copy(out=kidxL, in_=kf), reads=[BidxL], writes=[BidxL])
        p.op("dve", lambda e: e.tensor_copy(out=lf2, in_=lidx), reads=[Bpt, BidxL], writes=[BidxL])
        p.op("dve", lambda e, l=l: e.tensor_scalar(out=lf2, in0=lf2, scalar1=float(l * n_pool), scalar2=None, op0=ALU.add), reads=[BidxL], writes=[BidxL])
        p.op("dve", lambda e: e.tensor_copy(out=lidxL, in_=lf2), reads=[BidxL], writes=[BidxL])
        ck_t = ck.rearrange("l r c -> (l r) c"); cv_t = cv.rearrange("l r c -> (l r) c"); clf_t = clf.rearrange("l r c -> (l r) c")
        p.op("dve", lambda e: e.memset(qbd, 0.0), writes=[Bqbd])
        for pr in range(4):
            p.op("dve", lambda e, pr=pr: e.tensor_copy(out=qbd[0:64, pr, :, 0:8], in_=qTs[0:64, pr, :].rearrange("p (b t) -> p b t", b=16)), reads=[Bqks, Bqbd], writes=[Bqbd])
            p.op("dve", lambda e, pr=pr: e.tensor_copy(out=qbd[64:128, pr, :, 8:16], in_=qTs[64:128, pr, :].rearrange("p (b t) -> p b t", b=16)), reads=[Bqks, Bqbd], writes=[Bqbd])
        p.op("pe", lambda e: e.matmul(PS[0][:, 0:8], lhsT=btrif, rhs=lf[:, 16, :], start=True, stop=True), reads=[Blf, Bc], writes=[BPS[0]])
        p.op("dve", lambda e: e.tensor_scalar(out=bnew, in0=PS[0][:, 0:8], scalar1=-1.0, scalar2=None, op0=ALU.mult), reads=[BPS[0]], writes=[Bbnew])
        for half in range(2):
            p.dma("pool", lambda e, l=l, half=half: e.indirect_dma_start(out=lfg, out_offset=None, in_=clf_t, in_offset=bass.IndirectOffsetOnAxis(ap=lidxL[:, half:half + 1], axis=0)),
                  reads=[BidxL], writes=[Blfg])
            lfg3 = lfg.rearrange("p (s h) -> p s h", h=8)
            for h in range(8):
                pb_ = 1 + h // 4
                p.op("pe", lambda e, h=h, pb_=pb_: e.transpose(PS[pb_][:, (h % 4) * 128:(h % 4 + 1) * 128], lfg3[:, :, h], identf), reads=[Blfg, Bc], writes=[BPS[pb_]])
            for q4 in range(2):
                p.op("act", lambda e, q4=q4: e.copy(out=lft[:, 4 * q4:4 * q4 + 4, :], in_=PS[1 + q4].rearrange("p (h x) -> p h x", h=4)), reads=[BPS[1 + q4]], writes=[Blft])
            for q4 in range(2):
                p.op("pe", lambda e, q4=q4: e.matmul(PS[3 + q4], lhsT=onesf, rhs=lft[:, 4 * q4:4 * q4 + 4, :].rearrange("p h x -> p (h x)"), start=True, stop=True), reads=[Blft, Bc], writes=[BPS[3 + q4]])
                p.op("act", lambda e, q4=q4: e.copy(out=totS[:, 4 * q4:4 * q4 + 4, :, :].rearrange("p h b j -> p (h b j)"), in_=PS[3 + q4]), reads=[BPS[3 + q4]], writes=[Blat])
            p.op("dve", lambda e: e.memset(later[:, :, :, 15:16], 0.0), reads=[Blat], writes=[Blat])
            for j in range(14, -1, -1):
                p.op("dve", lambda e, j=j: e.tensor_tensor(out=later[:, :, :, j:j + 1], in0=later[:, :, :, j + 1:j + 2], in1=totS[:, :, :, j + 1:j + 2], op=ALU.add), reads=[Blat], writes=[Blat])
            for q4 in range(2):
                p.op("pe", lambda e, q4=q4: e.matmul(PS[5 + q4], lhsT=sufm, rhs=lft[:, 4 * q4:4 * q4 + 4, :].rearrange("p h x -> p (h x)"), start=True, stop=True), reads=[Blft, Bsufm], writes=[BPS[5 + q4]])
                p.op("dve", lambda e, q4=q4, half=half: e.tensor_tensor(out=bpast[:, 4 * q4:4 * q4 + 4, half, :], in0=PS[5 + q4].rearrange("p (h x) -> p h x", h=4),
                                                                    in1=later[:, 4 * q4:4 * q4 + 4, :, :].rearrange("p h b j -> p h (b j)"), op=ALU.add),
                     reads=[BPS[5 + q4], Blat], writes=[Bbpast])
        p.op("dve", lambda e: e.tensor_copy(out=Vb[:, 16, :], in_=vnew), reads=[Bvnew], writes=[BVb[16]])
        for b in range(NSEQ):
            half, b8 = b // 8, b % 8
            for j in range(NPG):
                m = j % 4
                col = b * 16 + j
                p.dma("pool", lambda e, m=m, col=col: e.indirect_dma_start(out=kbf[m], out_offset=None, in_=ck_t, in_offset=bass.IndirectOffsetOnAxis(ap=kidxL[:, col:col + 1], axis=0)),
                      reads=[BidxL], writes=[Bkbf[m]])
                p.dma("pool", lambda e, j=j, col=col: e.indirect_dma_start(out=Vb[:, j, :], out_offset=None, in_=cv_t, in_offset=bass.IndirectOffsetOnAxis(ap=kidxL[:, col:col + 1], axis=0)),
                      reads=[BidxL], writes=[BVb[j]])
            for j in range(NPG):
                m = j % 4
                m2 = j % 2
                pst = PS[3].bitcast(BF16)
                for pr in range(4):
                    p.op("pe", lambda e, m=m, pr=pr, pst=pst: e.transpose(pst[:, pr * 128:(pr + 1) * 128], kbf[m][:, pr * 128:(pr + 1) * 128], identb), reads=[Bkbf[m], Bc], writes=[BPS[3]])
                p.op("dve", lambda e, m2=m2, pst=pst: e.tensor_copy(out=KT[m2], in_=pst[:, 0:512].rearrange("p (r s) -> p r s", r=4)), reads=[BPS[3]], writes=[BKT[m2]])
                bank = j // 8
                for pr in range(4):
                    c0 = (j % 8) * 64 + pr * 16
                    p.op("pe", lambda e, m2=m2, pr=pr, bank=bank, c0=c0, b=b: e.matmul(PS[bank][:, c0:c0 + 16], lhsT=KT[m2][:, pr, :], rhs=qbd[:, pr, b, :], start=True, stop=True),
                         reads=[BKT[m2], Bqbd], writes=[BPS[bank]])
            for pr in range(4):
                p.op("pe", lambda e, pr=pr, b=b: e.matmul(PS[2][:, pr * 16:(pr + 1) * 16], lhsT=kTs[:, pr, :], rhs=qbd[:, pr, b, :], start=True, stop=True),
                     reads=[Bqks, Bqbd], writes=[BPS[2]])
            for bank in range(2):
                bias_v = bpast[:, :, half, b8 * 16 + bank * 8:b8 * 16 + bank * 8 + 8].rearrange("p h j -> p j h").unsqueeze(3).to_broadcast([128, 8, 8, 8])
                p.op("dve", lambda e, bank=bank, bias_v=bias_v: e.tensor_tensor(out=sc[:, bank * 8:(bank + 1) * 8, :].rearrange("p j (h t) -> p j h t", h=8),
                                                                           in0=PS[bank].rearrange("p (j h t) -> p j h t", j=8, h=8), in1=bias_v, op=ALU.add),
                     reads=[BPS[bank], Bbpast], writes=[Bsc])
            p.op("dve", lambda e, b=b: e.tensor_tensor(out=sc[:, 16, :].rearrange("p (h t) -> p h t", h=8), in0=PS[2][:, 0:64].rearrange("p (h t) -> p h t", h=8),
                                                  in1=negm_s[:, b * 8:(b + 1) * 8].unsqueeze(1).to_broadcast([128, 8, 8]), op=ALU.add),
                 reads=[BPS[2], Bc, Bsc], writes=[Bsc])
            p.op("dve", lambda e: e.tensor_tensor(out=sc[:, 16, :].rearrange("p (h t) -> p h t", h=8), in0=sc[:, 16, :].rearrange("p (h t) -> p h t", h=8),
                                             in1=bnew.unsqueeze(2).to_broadcast([128, 8, 8]), op=ALU.add),
                 reads=[Bbnew, Bsc], writes=[Bsc])
            p.op("act", lambda e: e.activation(out=pTs.rearrange("p j x -> p (j x)"), in_=sc.rearrange("p j x -> p (j x)"), func=AF.Exp), reads=[Bsc], writes=[BpTs])
            for pr in range(4):
                ob = 4 + pr // 2
                c0 = (pr % 2) * 256 + b * 16
                for j in range(17):
                    p.op("pe", lambda e, pr=pr, j=j, ob=ob, c0=c0: e.matmul(PS[ob][:, c0:c0 + 16], lhsT=Vb[:, j, pr * 128:(pr + 1) * 128], rhs=pTs[:, j, pr * 16:(pr + 1) * 16], start=(j == 0), stop=(j == 16)),
                         reads=[BVb[j], BpTs], writes=[BPS[ob]])
            sbk = 6 + b // 8
            for j in range(17):
                p.op("pe", lambda e, j=j, sbk=sbk, b8=b8: e.matmul(PS[sbk][:, b8 * 64:(b8 + 1) * 64], lhsT=onesb, rhs=pTs[:, j, :], start=(j == 0), stop=(j == 16)),
                     reads=[Bc, BpTs], writes=[BPS[sbk]])
        for hf in range(2):
            p.op("dve", lambda e, hf=hf: e.reciprocal(out=rs[:, 8 * hf:8 * hf + 8, :].rearrange("p b x -> p (b x)"), in_=PS[6 + hf]), reads=[BPS[6 + hf]], writes=[Brs])
        for q2 in range(2):
            p.op("dve", lambda e, q2=q2: e.tensor_tensor(out=ot[:, 2 * q2:2 * q2 + 2, :, :], in0=PS[4 + q2].rearrange("p (r b x) -> p r b x", r=2, b=16),
                                                    in1=rs.rearrange("p b (r x) -> p r b x", r=4)[:, 2 * q2:2 * q2 + 2, :, :], op=ALU.mult),
                 reads=[BPS[4 + q2], Brs], writes=[Bot])
        for h2 in range(2):
            hb = 64 * h2
            p.op("dve", lambda e, h2=h2, hb=hb: e.tensor_tensor(out=yT[hb:hb + 64, 4:8, TP:NTOK].rearrange("p r (b t) -> p r b t", b=16),
                                                           in0=ot[hb:hb + 64, :, :, h2 * 8:(h2 + 1) * 8],
                                                           in1=zsTs[hb:hb + 64, :, :].rearrange("p r (b t) -> p r b t", b=16), op=ALU.mult),
                 reads=[Bot, BzsTs], writes=[ByT[1][16]])
        if stop == "B2":
            break

        p.barrier(); ar.reset()
        cwb = ar.alloc([128, 12, 5], F32); Bcw = Buf("cw")
        Ebc = ar.alloc([128, 16, 128], BF16); blkrow = ar.alloc([128, 16], F32); BE = Buf("Ebc")
        markC = ar.mark()
        cwT = ar.alloc([5, 1536], F32)
        p.dma("sp", lambda e, l=l: e.dma_start(out=cwT[0:4, :], in_=conv_w[l]), writes=[Bcw])
        p.dma("sp", lambda e, l=l: e.dma_start(out=cwT[4:5, :], in_=conv_b[l].rearrange("(o c) -> o c", o=1)), writes=[Bcw])
        for cc in range(12):
            p.op("pe", lambda e, cc=cc: e.transpose(PS[0][:, cc * 8:cc * 8 + 5], cwT[:, cc * 128:(cc + 1) * 128], identf[0:5, 0:5]), reads=[Bcw, Bc], writes=[BPS[0]])
        p.op("dve", lambda e: e.tensor_copy(out=cwb, in_=PS[0][:, 0:96].rearrange("p (c x) -> p c x", x=8)[:, :, 0:5]), reads=[BPS[0]], writes=[Bcw])
        p.op("pool", lambda e: e.memset(Ebc, 1.0), writes=[BE])
        p.op("pool", lambda e: e.affine_select(out=Ebc, in_=Ebc, pattern=[[-8, 16], [1, 128]], compare_op=ALU.is_ge, fill=0.0, base=0, channel_multiplier=0), reads=[BE], writes=[BE])
        p.op("pool", lambda e: e.affine_select(out=Ebc, in_=Ebc, pattern=[[8, 16], [-1, 128]], compare_op=ALU.is_ge, fill=0.0, base=7, channel_multiplier=0), reads=[BE], writes=[BE])
        p.op("pool", lambda e: e.affine_select(out=blkrow, in_=onesf[:, 0:16], pattern=[[-8, 16]], compare_op=ALU.is_ge, fill=0.0, base=0, channel_multiplier=1), reads=[BE, Bc], writes=[BE])
        p.op("pool", lambda e: e.affine_select(out=blkrow, in_=blkrow, pattern=[[8, 16]], compare_op=ALU.is_ge, fill=0.0, base=7, channel_multiplier=-1), reads=[BE], writes=[BE])
        p.dma("sp", lambda e, l=l: e.dma_start(out=par[:, 0:16], in_=dt_bias[l].partition_broadcast(128)), writes=[Bpar])
        p.dma("sp", lambda e, l=l: e.dma_start(out=par[:, 16:32], in_=a_log[l].partition_broadcast(128)), writes=[Bpar])
        p.dma("sp", lambda e, l=l: e.dma_start(out=par[:, 32:48], in_=d_skip[l].partition_broadcast(128)), writes=[Bpar])
        p.op("act", lambda e: e.activation(out=par[:, 16:32], in_=par[:, 16:32], func=AF.Exp), reads=[Bpar], writes=[Bpar])
        p.op("dve", lambda e: e.tensor_scalar(out=par[:, 16:32], in0=par[:, 16:32], scalar1=-1.0, scalar2=None, op0=ALU.mult), reads=[Bpar], writes=[Bpar])

        if stop == "C0":
            break
        for g in range(int(os.environ.get("NGC", 2))):
            p.barrier(); ar.reset(markC)
            Wc = ar.alloc([128, 8, 1280], BF16); BWc = Buf("Wc")
            Wdt = ar.alloc([128, 8, 8], BF16); BWdt = Buf("Wdt")
            snb = ar.alloc([128, 512], F32); Bsnb = Buf("snb")
            stT = ar.alloc([128, 512], F32); stTb = ar.alloc([128, 512], BF16); BstT = Buf("stT")
            xr = ar.alloc([128, 6 * 176], F32); Bxraw = Buf("xraw"); Bxraw_s = Bxraw
            xraw = xr[:, 0:786].rearrange("p (c t) -> p c t", c=6)
            xraw_s = xr.rearrange("p (c b t) -> p c b t", c=6, b=16)
            acc = ar.alloc([128, 6, 128], F32); Baccs = bufs("acc", 6)
            xc = acc[:, 0:4, :]
            R3 = ar.alloc([48, 768], F32); BR3 = Buf("R3")
            BCt = ar.alloc([128, 2, 128], BF16); BBCt = Buf("BCt")
            x_tm = ar.alloc([128, 512], F32); xdt = ar.alloc([128, 512], BF16); xw = ar.alloc([128, 512], BF16); Bxtm = Buf("xtm")
            Btm = ar.alloc([128, 128], BF16); BBtm = Buf("Btm")
            sm = ar.alloc([128, 8, 8], F32); Bsm = Buf("sm")
            dtab = ar.alloc([128, 8, 128], F32); negcm = ar.alloc([128, 8, 128], F32); Bdtab = Buf("dtab")
            seg0 = ar.alloc([128, 512], F32); seg = [seg0, seg0]; Bseg0 = Buf("seg"); Bseg = [Bseg0, Bseg0]
            MixT = [ar.alloc([128, 4, 128], BF16) for _ in range(2)]; BMix = bufs("Mix", 2)
            cbT = ar.alloc([128, 128], F32); BcbT = Buf("cbT")
            t1 = ar.alloc([128, 512], F32); Bt1 = Buf("t1")
            yc = ar.alloc([128, 512], F32); Byc = Buf("yc")
            zsc = ar.alloc([128, 512], F32); Bzsc = Buf("zsc")
            ycb = ar.alloc([128, 512], BF16); Bycb = Buf("ycb")
            ssq = ar.alloc([128, 4], F32); Bssq = Buf("ssq")
            CzT = ar.alloc([128, 16, 128], BF16); Bz = ar.alloc([128, 16, 128], BF16); BCz = Buf("CzT")
            elb = ar.alloc([128, 16, 8], F32); cumexp = ar.alloc([128, 16, 8], F32); Belb = Buf("elb")
            stin = [ar.alloc([128, 4, 128], F32) for _ in range(2)]; Bstin = bufs("stin", 2)
            sts = [ar.alloc([128, 512], F32) for _ in range(2)]; stsb = [ar.alloc([128, 512], BF16) for _ in range(2)]; Bsts = bufs("sts", 2)
            stout = stin; Bstout = Bstin

            chans = [g * 512 + i * 128 for i in range(4)] + [1024 + g * 128, 1280 + g * 128]
            load_w(Wc[:, :, 0:512], w_in[l][:, C_XBC + g * 512:C_XBC + (g + 1) * 512], BWc)
            load_w(Wc[:, :, 512:640], w_in[l][:, C_XBC + 1024 + g * 128:C_XBC + 1024 + (g + 1) * 128], BWc)
            load_w(Wc[:, :, 640:768], w_in[l][:, C_XBC + 1280 + g * 128:C_XBC + 1280 + (g + 1) * 128], BWc)
            load_w(Wc[:, :, 768:1280], w_in[l][:, C_ZC + g * 512:C_ZC + (g + 1) * 512], BWc)
            load_w(Wdt, w_in[l][:, C_DT + g * 8:C_DT + (g + 1) * 8], BWdt)
            p.dma("sp", lambda e, l=l, g=g: e.dma_start(out=snb, in_=ssm_norm[l, g * 512:(g + 1) * 512].partition_broadcast(128)), writes=[Bsnb])
            p.op("pool", lambda e: e.memset(stT, 0.0), writes=[BstT])
            p.op("dve", lambda e: e.memset(stTb, 0.0), reads=[BstT], writes=[BstT])
            p.op("pool", lambda e: e.memset(xraw[:, :, 0:3], 0.0), writes=[Bxraw])
            dtb = par[:, g * 8:(g + 1) * 8]; a_bc = par[:, 16 + g * 8:16 + (g + 1) * 8]; dsk = par[:, 32 + g * 8:32 + (g + 1) * 8]

            for j in range(int(os.environ.get("NTC", NT))):
                tok0 = j * 128
                smp = (j == 16)
                TRI = btrif if smp else trif
                LAST = L2s if smp else sel127
                NEGM = negm_s if smp else negm_p
                for cc in range(6):
                    pb_, c0 = (0, cc * 128) if cc < 4 else (1, (cc - 4) * 128)
                    proj_fm(Wc, BWc, cc * 128, 128, tok0, 128, PS[pb_][:, c0:c0 + 128], BPS[pb_])
                proj_tm(Wdt, BWdt, 0, 8, j, PS[1][:, 256:264], BPS[1])
                if not smp:
                    p.op("act", lambda e: e.copy(out=xraw[:, 0:4, 3:131], in_=PS[0].rearrange("p (c t) -> p c t", c=4)), reads=[BPS[0]], writes=[Bxraw])
                    p.op("act", lambda e: e.copy(out=xraw[:, 4:6, 3:131], in_=PS[1][:, 0:256].rearrange("p (c t) -> p c t", c=2)), reads=[BPS[1]], writes=[Bxraw])
                    src = lambda cc, tap: xraw[:, cc, tap:tap + 128]
                    accv = lambda cc: acc[:, cc, :]
                    Bsrc = Bxraw
                else:
                    for cc in range(6):
                        p.dma("sp", lambda e, l=l, cc=cc, c0=chans[cc]: e.dma_start(out=R3[:, cc * 128:(cc + 1) * 128], in_=sconv[l][:, :, c0:c0 + 128].rearrange("b j c -> (b j) c")), writes=[BR3])
                    for cc in range(6):
                        p.op("pe", lambda e, cc=cc: e.transpose(PS[3][:, cc * 48:(cc + 1) * 48], R3[:, cc * 128:(cc + 1) * 128], identf[0:48, 0:48]), reads=[BR3, Bc], writes=[BPS[3]])
                    p.op("act", lambda e: e.copy(out=xraw_s[:, :, :, 0:3], in_=PS[3][:, 0:288].rearrange("p (c b j) -> p c b j", c=6, b=16)), reads=[BPS[3]], writes=[Bxraw_s])
                    p.op("act", lambda e: e.copy(out=xraw_s[:, 0:4, :, 3:11], in_=PS[0].rearrange("p (c b t) -> p c b t", c=4, b=16)), reads=[BPS[0]], writes=[Bxraw_s])
                    p.op("act", lambda e: e.copy(out=xraw_s[:, 4:6, :, 3:11], in_=PS[1][:, 0:256].rearrange("p (c b t) -> p c b t", c=2, b=16)), reads=[BPS[1]], writes=[Bxraw_s])
                    src = lambda cc, tap: xraw_s[:, cc, :, tap:tap + 8]
                    accv = lambda cc: acc[:, cc, :].rearrange("p (b t) -> p b t", b=16)
                    Bsrc = Bxraw_s
                for cc in range(6):
                    ch = chans[cc] // 128
                    p.op("dve", lambda e, cc=cc, ch=ch, src=src, accv=accv: e.tensor_scalar(out=accv(cc), in0=src(cc, 3), scalar1=cwb[:, ch, 3:4], scalar2=cwb[:, ch, 4:5], op0=ALU.mult, op1=ALU.add),
                         reads=[Bsrc, Bcw], writes=[Baccs[cc]])
                for tap in range(3):
                    for cc in range(6):
                        ch = chans[cc] // 128
                        p.op("dve", lambda e, cc=cc, ch=ch, tap=tap, src=src, accv=accv: e.scalar_tensor_tensor(out=accv(cc), in0=src(cc, tap), scalar=cwb[:, ch, tap:tap + 1], in1=accv(cc), op0=ALU.mult, op1=ALU.add),
                             reads=[Bsrc, Bcw, Baccs[cc]], writes=[Baccs[cc]])
                p.op("act", lambda e: e.activation(out=xc, in_=acc[:, 0:4, :], func=AF.Silu), reads=Baccs[0:4], writes=Baccs[0:4])
                p.op("act", lambda e: e.activation(out=BCt, in_=acc[:, 4:6, :], func=AF.Silu), reads=Baccs[4:6], writes=[BBCt])
                if not smp:
                    if j < 15:
                        p.op("pool", lambda e: e.tensor_copy(out=xraw[:, :, 0:3], in_=xraw[:, :, 128:131]), reads=[Bxraw], writes=[Bxraw])
                    else:
                        for cc in range(6):
                            p.op("pe", lambda e, cc=cc: e.transpose(PS[3][0:3, cc * 128:(cc + 1) * 128] if cc < 4 else PS[2][0:3, (cc - 4) * 128:(cc - 3) * 128], xraw[:, cc, 128:131], identf), reads=[Bxraw, Bc], writes=[BPS[3] if cc < 4 else BPS[2]])
                        p.op("dve", lambda e: e.tensor_copy(out=R3[0:3, 0:512], in_=PS[3][0:3, 0:512]), reads=[BPS[3]], writes=[BR3])
                        p.op("dve", lambda e: e.tensor_copy(out=R3[0:3, 512:768], in_=PS[2][0:3, 0:256]), reads=[BPS[2]], writes=[BR3])
                        for cc in range(6):
                            p.dma("sp", lambda e, l=l, cc=cc, c0=chans[cc]: e.dma_start(out=convp[l][:, c0:c0 + 128], in_=R3[0:3, cc * 128:(cc + 1) * 128]), reads=[BR3])
                else:
                    p.op("pool", lambda e: e.tensor_copy(out=t1[:, 0:288].rearrange("p (c b t) -> p c b t", c=6, b=16), in_=xraw_s[:, :, :, 8:11]), reads=[Bxraw_s], writes=[Bt1])
                    for cc in range(6):
                        p.op("pe", lambda e, cc=cc: e.transpose(PS[3][0:48, (cc % 4) * 128:(cc % 4 + 1) * 128] if cc < 4 else PS[2][0:48, (cc - 4) * 128:(cc - 3) * 128],
                                                                 t1[:, cc * 48:(cc + 1) * 48], identf), reads=[Bt1, Bc], writes=[BPS[3] if cc < 4 else BPS[2]])
                    p.op("dve", lambda e: e.tensor_copy(out=R3[:, 0:512], in_=PS[3][0:48, 0:512]), reads=[BPS[3]], writes=[BR3])
                    p.op("dve", lambda e: e.tensor_copy(out=R3[:, 512:768], in_=PS[2][0:48, 0:256]), reads=[BPS[2]], writes=[BR3])
                    for cc in range(6):
                        p.dma("sp", lambda e, l=l, cc=cc, c0=chans[cc]: e.dma_start(out=convs[l][:, :, c0:c0 + 128].rearrange("b t c -> (b t) c"), in_=R3[:, cc * 128:(cc + 1) * 128]), reads=[BR3])
                p.op("dve", lambda e, dtb=dtb: e.tensor_tensor(out=sm[:, 0, :], in0=PS[1][:, 256:264], in1=dtb, op=ALU.add), reads=[BPS[1], Bpar], writes=[Bsm])
                p.op("act", lambda e: e.activation(out=sm[:, 0, :], in_=sm[:, 0, :], func=AF.Exp), reads=[Bsm], writes=[Bsm])
                p.op("act", lambda e: e.activation(out=sm[:, 0, :], in_=sm[:, 0, :], func=AF.Ln, bias=1.0), reads=[Bsm], writes=[Bsm])
                p.op("dve", lambda e, a_bc=a_bc: e.tensor_tensor(out=sm[:, 1, :], in0=sm[:, 0, :], in1=a_bc, op=ALU.mult), reads=[Bsm, Bpar], writes=[Bsm])
                p.op("pe", lambda e, TRI=TRI: e.matmul(PS[3][:, 0:8], lhsT=TRI, rhs=sm[:, 1, :], start=True, stop=True), reads=[Bsm, Bc], writes=[BPS[3]])
                p.op("dve", lambda e: e.tensor_copy(out=sm[:, 2, :], in_=PS[3][:, 0:8]), reads=[BPS[3]], writes=[Bsm])
                p.op("pe", lambda e, LAST=LAST: e.matmul(PS[3][:, 8:16], lhsT=LAST, rhs=sm[:, 2, :], start=True, stop=True), reads=[Bsm, Bc], writes=[BPS[3]])
                p.op("act", lambda e: e.activation(out=sm[:, 3, :], in_=sm[:, 2, :], func=AF.Exp), reads=[Bsm], writes=[Bsm])
                p.op("dve", lambda e: e.tensor_tensor(out=sm[:, 4, :], in0=PS[3][:, 8:16], in1=sm[:, 2, :], op=ALU.subtract), reads=[BPS[3], Bsm], writes=[Bsm])
                p.op("act", lambda e: e.activation(out=sm[:, 4, :], in_=sm[:, 4, :], func=AF.Exp), reads=[Bsm], writes=[Bsm])
                p.op("dve", lambda e: e.tensor_copy(out=sm[:, 6, :], in_=PS[3][:, 8:16]), reads=[BPS[3]], writes=[Bsm])
                p.op("act", lambda e: e.activation(out=sm[:, 5, :], in_=sm[:, 6, :], func=AF.Exp), reads=[Bsm], writes=[Bsm])
                for cc in range(4):
                    p.op("pe", lambda e, cc=cc: e.transpose(PS[2][:, cc * 128:(cc + 1) * 128], xc[:, cc, :], identf), reads=[Baccs[cc], Bc], writes=[BPS[2]])
                p.op("act", lambda e: e.copy(out=x_tm, in_=PS[2]), reads=[BPS[2]], writes=[Bxtm])
                p.op("dve", lambda e: e.tensor_tensor(out=xdt.rearrange("p (h d) -> p h d", h=8), in0=x_tm.rearrange("p (h d) -> p h d", h=8), in1=sm[:, 0, :].unsqueeze(2).to_broadcast([128, 8, 64]), op=ALU.mult),
                     reads=[Bxtm, Bsm], writes=[Bxtm])
                p.op("dve", lambda e: e.tensor_tensor(out=xw.rearrange("p (h d) -> p h d", h=8), in0=xdt.rearrange("p (h d) -> p h d", h=8), in1=sm[:, 4, :].unsqueeze(2).to_broadcast([128, 8, 64]), op=ALU.mult),
                     reads=[Bxtm, Bsm], writes=[Bxtm])
                pst3 = PS[3].bitcast(BF16)
                p.op("pe", lambda e, pst3=pst3: e.transpose(pst3[:, 512:640], BCt[:, 0, :], identb), reads=[BBCt, Bc], writes=[BPS[3]])
                p.op("dve", lambda e, pst3=pst3: e.tensor_copy(out=Btm, in_=pst3[:, 512:640]), reads=[BPS[3]], writes=[BBtm])
                p.op("pe", lambda e: e.matmul(PS[3][:, 128:256], lhsT=BCt[:, 0, :], rhs=BCt[:, 1, :], start=True, stop=True), reads=[BBCt], writes=[BPS[3]])
                p.op("dve", lambda e: e.tensor_copy(out=cbT, in_=PS[3][:, 128:256]), reads=[BPS[3]], writes=[BcbT])
                p.op("pool", lambda e: e.tensor_copy(out=dtab, in_=sm[:, 1, :].unsqueeze(2).to_broadcast([128, 8, 128])), reads=[Bsm], writes=[Bdtab])
                p.op("pool", lambda e, NEGM=NEGM: e.tensor_tensor(out=negcm, in0=NEGM.unsqueeze(1).to_broadcast([128, 8, 128]), in1=sm[:, 2, :].unsqueeze(2).to_broadcast([128, 8, 128]), op=ALU.subtract),
                     reads=[Bsm, Bc, Bdtab], writes=[Bdtab])
                for hh in range(2):
                    for i in range(4):
                        h = hh * 4 + i
                        p.op("pe", lambda e, i=i, h=h, TRI=TRI: e.matmul(PS[4][:, i * 128:(i + 1) * 128], lhsT=dtab[:, h, :], rhs=TRI, start=True, stop=True), reads=[Bdtab, Bc], writes=[BPS[4]])
                    p.op("dve", lambda e, hh=hh: e.tensor_tensor(out=seg[hh], in0=PS[4], in1=negcm[:, hh * 4:(hh + 1) * 4, :].rearrange("p h t -> p (h t)"), op=ALU.add),
                         reads=[BPS[4], Bdtab], writes=[Bseg[hh]])
                    p.op("act", lambda e, hh=hh: e.activation(out=seg[hh], in_=seg[hh], func=AF.Exp), reads=[Bseg[hh]], writes=[Bseg[hh]])
                    p.op("dve", lambda e, hh=hh: e.tensor_tensor(out=MixT[hh], in0=seg[hh].rearrange("p (h t) -> p h t", h=4), in1=cbT.unsqueeze(1).to_broadcast([128, 4, 128]), op=ALU.mult),
                         reads=[Bseg[hh], BcbT], writes=[BMix[hh]])
                    for i in range(4):
                        h = hh * 4 + i
                        p.op("pe", lambda e, i=i, h=h, hh=hh: e.matmul(PS[5][:, h * 64:(h + 1) * 64], lhsT=MixT[hh][:, i, :], rhs=xdt[:, h * 64:(h + 1) * 64], start=True, stop=True),
                             reads=[BMix[hh], Bxtm], writes=[BPS[5]])
                if not smp:
                    p.op("pe", lambda e: e.matmul(PS[6], lhsT=BCt[:, 1, :], rhs=stTb, start=True, stop=True), reads=[BBCt, BstT], writes=[BPS[6]])
                    p.op("pe", lambda e: e.matmul(PS[7], lhsT=Btm, rhs=xw, start=True, stop=True), reads=[BBtm, Bxtm], writes=[BPS[7]])
                    p.op("dve", lambda e: e.tensor_tensor(out=stT.rearrange("p (h d) -> p h d", h=8), in0=stT.rearrange("p (h d) -> p h d", h=8), in1=sm[:, 5, :].unsqueeze(2).to_broadcast([128, 8, 64]), op=ALU.mult),
                         reads=[BstT, Bsm, BPS[6]], writes=[BstT])
                    p.op("dve", lambda e: e.tensor_tensor(out=stT, in0=stT, in1=PS[7], op=ALU.add), reads=[BstT, BPS[7]], writes=[BstT])
                    p.op("act", lambda e: e.copy(out=stTb, in_=stT), reads=[BstT], writes=[BstT])
                    if j == 15:
                        for cc in range(4):
                            p.op("pe", lambda e, cc=cc: e.transpose(PS[7][:, cc * 128:(cc + 1) * 128], stT[:, cc * 128:(cc + 1) * 128], identf), reads=[BstT, Bc], writes=[BPS[7]])
                        p.op("act", lambda e: e.copy(out=stout[0], in_=PS[7].rearrange("p (c n) -> p c n", c=4)), reads=[BPS[7]], writes=[Bstout[0]])
                        p.dma("sp", lambda e, l=l, g=g: e.dma_start(out=ssmp[l, g * 512:(g + 1) * 512, :].rearrange("(c p) n -> p c n", p=128), in_=stout[0]), reads=[Bstout[0]])
                else:
                    p.op("dve", lambda e: e.tensor_tensor(out=CzT, in0=BCt[:, 1, :].unsqueeze(1).to_broadcast([128, 16, 128]), in1=Ebc, op=ALU.mult), reads=[BBCt, BE], writes=[BCz])
                    p.op("dve", lambda e: e.tensor_tensor(out=Bz, in0=Btm.unsqueeze(1).to_broadcast([128, 16, 128]), in1=blkrow.unsqueeze(2).to_broadcast([128, 16, 128]), op=ALU.mult), reads=[BBtm, BE], writes=[BCz])
                    p.op("dve", lambda e: e.tensor_tensor(out=cumexp, in0=sm[:, 2, :].unsqueeze(1).to_broadcast([128, 16, 8]), in1=lastmask.unsqueeze(2).to_broadcast([128, 16, 8]), op=ALU.mult), reads=[Bsm, Bc], writes=[Belb])
                    p.op("pe", lambda e: e.matmul(PS[3][:, 256:384], lhsT=onesf, rhs=cumexp.rearrange("p b h -> p (b h)"), start=True, stop=True), reads=[Belb, Bc], writes=[BPS[3]])
                    p.op("act", lambda e: e.activation(out=elb.rearrange("p b h -> p (b h)"), in_=PS[3][:, 256:384], func=AF.Exp), reads=[BPS[3]], writes=[Belb])
                    for b in range(NSEQ):
                        m = b % 2
                        p.dma("sp", lambda e, l=l, g=g, b=b, m=m: e.dma_start(out=stin[m], in_=sssm[l, b, g * 512:(g + 1) * 512, :].rearrange("(c p) n -> p c n", p=128)), writes=[Bstin[m]])
                        for cc in range(4):
                            p.op("pe", lambda e, cc=cc, m=m: e.transpose(PS[7][:, cc * 128:(cc + 1) * 128], stin[m][:, cc, :], identf), reads=[Bstin[m], Bc], writes=[BPS[7]])
                        p.op("act", lambda e, m=m: e.copy(out=sts[m], in_=PS[7]), reads=[BPS[7]], writes=[Bsts[m]])
                        p.op("dve", lambda e, m=m: e.tensor_copy(out=stsb[m], in_=sts[m]), reads=[Bsts[m]], writes=[Bsts[m]])
                        p.op("pe", lambda e, b=b, m=m: e.matmul(PS[6], lhsT=CzT[:, b, :], rhs=stsb[m], start=(b == 0), stop=(b == NSEQ - 1)), reads=[BCz, Bsts[m]], writes=[BPS[6]])
                        p.op("pe", lambda e, b=b: e.matmul(PS[0], lhsT=Bz[:, b, :], rhs=xw, start=True, stop=True), reads=[BCz, Bxtm], writes=[BPS[0]])
                        p.op("dve", lambda e, b=b, m=m: e.tensor_tensor(out=sts[m].rearrange("p (h d) -> p h d", h=8), in0=sts[m].rearrange("p (h d) -> p h d", h=8), in1=elb[:, b, :].unsqueeze(2).to_broadcast([128, 8, 64]), op=ALU.mult),
                             reads=[Bsts[m], Belb], writes=[Bsts[m]])
                        p.op("dve", lambda e, m=m: e.tensor_tensor(out=sts[m], in0=sts[m], in1=PS[0], op=ALU.add), reads=[Bsts[m], BPS[0]], writes=[Bsts[m]])
                        for cc in range(4):
                            p.op("pe", lambda e, cc=cc, m=m: e.transpose(PS[1][:, cc * 128:(cc + 1) * 128], sts[m][:, cc * 128:(cc + 1) * 128], identf), reads=[Bsts[m], Bc], writes=[BPS[1]])
                        p.op("act", lambda e, m=m: e.copy(out=stout[m], in_=PS[1].rearrange("p (c n) -> p c n", c=4)), reads=[BPS[1]], writes=[Bstout[m]])
                        p.dma("sp", lambda e, l=l, g=g, b=b, m=m: e.dma_start(out=ssms[l, b, g * 512:(g + 1) * 512, :].rearrange("(c p) n -> p c n", p=128), in_=stout[m]), reads=[Bstout[m]])
                p.op("dve", lambda e: e.tensor_tensor(out=t1.rearrange("p (h d) -> p h d", h=8), in0=PS[6].rearrange("p (h d) -> p h d", h=8), in1=sm[:, 3, :].unsqueeze(2).to_broadcast([128, 8, 64]), op=ALU.mult),
                     reads=[BPS[6], Bsm], writes=[Bt1])
                p.op("dve", lambda e: e.tensor_tensor(out=yc, in0=PS[5], in1=t1, op=ALU.add), reads=[BPS[5], Bt1], writes=[Byc])
                p.op("pool", lambda e, dsk=dsk: e.tensor_tensor(out=t1.rearrange("p (h d) -> p h d", h=8), in0=x_tm.rearrange("p (h d) -> p h d", h=8), in1=dsk.unsqueeze(2).to_broadcast([128, 8, 64]), op=ALU.mult),
                     reads=[Bxtm, Bpar, Byc], writes=[Bt1])
                p.op("pool", lambda e: e.tensor_tensor(out=yc, in0=yc, in1=t1, op=ALU.add), reads=[Bt1, Byc], writes=[Byc])
                for hf in range(1):
                    proj_tm(Wc, BWc, 768, 512, j, PS[7], BPS[7])
                p.op("act", lambda e: e.activation(out=zsc, in_=PS[7], func=AF.Silu), reads=[BPS[7]], writes=[Bzsc])
                p.op("pool", lambda e: e.tensor_tensor(out=yc, in0=yc, in1=zsc, op=ALU.mult), reads=[Bzsc, Byc], writes=[Byc])
                p.op("pool", lambda e: e.memset(ssq, 0.0), writes=[Bssq])
                p.op("act", lambda e: e.activation(out=zsc, in_=yc, func=AF.Square, accum_out=ssq[:, 0:1]), reads=[Byc, Bssq], writes=[Bzsc, Bssq])
                p.op("act", lambda e: e.activation(out=ssq[:, 1:2], in_=ssq[:, 0:1], func=AF.Ln, bias=EPS, scale=1.0 / 512), reads=[Bssq], writes=[Bssq])
                p.op("act", lambda e: e.activation(out=ssq[:, 2:3], in_=ssq[:, 1:2], func=AF.Exp, scale=-0.5), reads=[Bssq], writes=[Bssq])
                p.op("dve", lambda e: e.scalar_tensor_tensor(out=ycb, in0=yc, scalar=ssq[:, 2:3], in1=snb, op0=ALU.mult, op1=ALU.mult), reads=[Byc, Bssq, Bsnb], writes=[Bycb])
                pst7 = PS[7].bitcast(BF16)
                for cc in range(4):
                    p.op("pe", lambda e, cc=cc, pst7=pst7: e.transpose(pst7[:, cc * 128:(cc + 1) * 128], ycb[:, cc * 128:(cc + 1) * 128], identb), reads=[Bycb, Bc], writes=[BPS[7]])
                p.op("dve", lambda e, pst7=pst7, g=g, tok0=tok0: e.tensor_copy(out=yT[:, 8 + g * 4:12 + g * 4, tok0:tok0 + 128], in_=pst7[:, 0:512].rearrange("p (c t) -> p c t", c=4)),
                     reads=[BPS[7]], writes=[ByT[2][j]])
        if stop == "C":
            break

        p.barrier(); ar.reset()
        mT = ar.alloc([128, 8, NTOK], BF16); BmT = bufs("mT", NT)
        markM = ar.mark()
        wM = [ar.alloc([128, 40, 128], BF16) for _ in range(2)]; BwM = bufs("wM", 2)
        sg = [ar.alloc([128, 512], F32) for _ in range(2)]; Bsg = bufs("sg", 2)
        accm = [ar.alloc([128, 512], F32) for _ in range(2)]; Baccm = bufs("accm", 2)
        tmpm = [ar.alloc([128, 512], F32) for _ in range(2)]; Btmpm = bufs("tmpm", 2)
        KOFF = (0, 4, 8); KCN = (4, 4, 8)
        WSRC = (w_a, w_b, w_c)
        it = 0
        for nj in range(8):
            k = nj % 2
            for br in range(3):
                load_w(wM[k][:, KOFF[br]:KOFF[br] + KCN[br], :], WSRC[br][l][:, nj * 128:(nj + 1) * 128], BwM[k])
                load_w(wM[k][:, 16 + 8 * br:24 + 8 * br, :], w_in[l][:, C_G + br * 1024 + nj * 128:C_G + br * 1024 + (nj + 1) * 128], BwM[k])
            for tg in range(5):
                tok0 = tg * 512
                ntok = 512 if tg < 4 else 128
                a2 = it % 2
                for br in range(3):
                    pp_, pg_ = it % 3, 3 + it % 3
                    s2 = it % 2
                    for kc in range(KCN[br]):
                        p.op("pe", lambda e, k=k, br=br, kc=kc, pp_=pp_, tok0=tok0, ntok=ntok: e.matmul(PS[pp_][:, 0:ntok], lhsT=wM[k][:, KOFF[br] + kc, :], rhs=yT[:, YOFF[br] + kc, tok0:tok0 + ntok],
                                                                                                       start=(kc == 0), stop=(kc == KCN[br] - 1)),
                             reads=[BwM[k]] + tile_bufs(ByT[br], tok0, ntok), writes=[BPS[pp_]])
                    for kc in range(8):
                        p.op("pe", lambda e, k=k, br=br, kc=kc, pg_=pg_, tok0=tok0, ntok=ntok: e.matmul(PS[pg_][:, 0:ntok], lhsT=wM[k][:, 16 + 8 * br + kc, :], rhs=hnT[:, kc, tok0:tok0 + ntok],
                                                                                                       start=(kc == 0), stop=(kc == 7)),
                             reads=[BwM[k]] + tile_bufs(BhnT, tok0, ntok), writes=[BPS[pg_]])
                    p.op("act", lambda e, pg_=pg_, s2=s2, ntok=ntok: e.activation(out=sg[s2][:, 0:ntok], in_=PS[pg_][:, 0:ntok], func=AF.Sigmoid), reads=[BPS[pg_]], writes=[Bsg[s2]])
                    if br == 0:
                        p.op("dve", lambda e, pp_=pp_, s2=s2, a2=a2, ntok=ntok: e.tensor_tensor(out=accm[a2][:, 0:ntok], in0=PS[pp_][:, 0:ntok], in1=sg[s2][:, 0:ntok], op=ALU.mult),
                             reads=[BPS[pp_], Bsg[s2]], writes=[Baccm[a2]])
                    else:
                        p.op("dve", lambda e, pp_=pp_, s2=s2, a2=a2, ntok=ntok: e.tensor_tensor(out=tmpm[a2][:, 0:ntok], in0=PS[pp_][:, 0:ntok], in1=sg[s2][:, 0:ntok], op=ALU.mult),
                             reads=[BPS[pp_], Bsg[s2]], writes=[Btmpm[a2]])
                        if br == 1:
                            p.op("pool", lambda e, a2=a2, ntok=ntok: e.tensor_tensor(out=accm[a2][:, 0:ntok], in0=accm[a2][:, 0:ntok], in1=tmpm[a2][:, 0:ntok], op=ALU.add),
                                 reads=[Btmpm[a2], Baccm[a2]], writes=[Baccm[a2]])
                        else:
                            p.op("pool", lambda e, a2=a2, nj=nj, tok0=tok0, ntok=ntok: e.tensor_tensor(out=mT[:, nj, tok0:tok0 + ntok], in0=accm[a2][:, 0:ntok], in1=tmpm[a2][:, 0:ntok], op=ALU.add),
                                 reads=[Btmpm[a2], Baccm[a2]], writes=tile_bufs(BmT, tok0, ntok))
                    it += 1
        p.barrier(); ar.reset(markM)
        Wo = ar.alloc([128, 8, 1024], BF16); BWo = Buf("Wo")
        npost = ar.alloc([128, D], F32); Bnpost = Buf("npost")
        xt2 = [ar.alloc([128, D], F32) for _ in range(2)]; Bxt2 = bufs("xt2", 2)
        rr = [ar.alloc([128, D], F32) for _ in range(2)]; Brr = bufs("rr", 2)
        hb2 = [ar.alloc([128, D], BF16) for _ in range(2)]; Bhb2 = bufs("hb2", 2)
        sq2 = [ar.alloc([128, 4], F32) for _ in range(2)]; Bsq2 = bufs("sq2", 2)
        ss2 = [ar.alloc([128, 8], F32) for _ in range(2)]; Bss2 = bufs("ss2", 2)
        junk = ar.alloc([128, 512], F32); Bjunk = Buf("junk")
        load_w(Wo[:, :, 0:512], w_out[l][:, 0:512], BWo)
        load_w(Wo[:, :, 512:1024], w_out[l][:, 512:1024], BWo)
        p.dma("sp", lambda e, l=l: e.dma_start(out=npost, in_=norm_post[l].partition_broadcast(128)), writes=[Bnpost])
        if l == 0:
            p.dma("sp", lambda e: e.dma_start(out=npre_bc, in_=norm_pre[1].partition_broadcast(128)), writes=[Bnpre])
        for j in range(NT):
            k = j % 2
            if l == 0:
                srcx = xp[j * 128:(j + 1) * 128, :] if j < 16 else xs
                p.dma("sp", lambda e, k=k, srcx=srcx: e.dma_start(out=xt2[k], in_=srcx), writes=[Bxt2[k]])
            else:
                p.dma("sp", lambda e, k=k, j=j: e.dma_start(out=xt2[k], in_=x1[j * 128:(j + 1) * 128, :]), reads=[Bx1[j]], writes=[Bxt2[k]])
            for hf in range(2):
                for kc in range(8):
                    p.op("pe", lambda e, j=j, hf=hf, kc=kc: e.matmul(PS[6 + hf], lhsT=mT[:, kc, j * 128:(j + 1) * 128], rhs=Wo[:, kc, hf * 512:(hf + 1) * 512], start=(kc == 0), stop=(kc == 7)),
                         reads=[BmT[j], BWo], writes=[BPS[6 + hf]])
            p.op("pool", lambda e, k=k: e.memset(ss2[k], 0.0), writes=[Bss2[k]])
            for hf in range(2):
                p.op("act", lambda e, k=k, hf=hf: e.activation(out=junk, in_=PS[6 + hf], func=AF.Square, accum_out=ss2[k][:, hf:hf + 1]), reads=[BPS[6 + hf], Bss2[k]], writes=[Bjunk, Bss2[k]])
            p.op("dve", lambda e, k=k: e.tensor_tensor(out=ss2[k][:, 2:3], in0=ss2[k][:, 0:1], in1=ss2[k][:, 1:2], op=ALU.add), reads=[Bss2[k]], writes=[Bss2[k]])
            p.op("act", lambda e, k=k: e.activation(out=ss2[k][:, 3:4], in_=ss2[k][:, 2:3], func=AF.Ln, bias=EPS, scale=1.0 / D), reads=[Bss2[k]], writes=[Bss2[k]])
            p.op("act", lambda e, k=k: e.activation(out=ss2[k][:, 4:5], in_=ss2[k][:, 3:4], func=AF.Exp, scale=-0.5), reads=[Bss2[k]], writes=[Bss2[k]])
            for hf in range(2):
                p.op("dve", lambda e, k=k, hf=hf: e.scalar_tensor_tensor(out=rr[k][:, hf * 512:(hf + 1) * 512], in0=PS[6 + hf], scalar=ss2[k][:, 4:5], in1=npost[:, hf * 512:(hf + 1) * 512], op0=ALU.mult, op1=ALU.mult),
                     reads=[BPS[6 + hf], Bss2[k], Bnpost], writes=[Brr[k]])
            p.op("pool", lambda e, k=k: e.tensor_tensor(out=rr[k], in0=rr[k], in1=xt2[k], op=ALU.add), reads=[Bxt2[k], Brr[k]], writes=[Brr[k]])
            if l == 0:
                p.dma("sp", lambda e, k=k, j=j: e.dma_start(out=x1[j * 128:(j + 1) * 128, :], in_=rr[k]), reads=[Brr[k]], writes=[Bx1[j]], owner=Brr[k])
                norm_to_hnT(rr[k], Brr[k], j, sq2[k], Bsq2[k], hb2[k], Bhb2[k], 4 + k)
            else:
                dst = yp[j * 128:(j + 1) * 128, :] if j < 16 else ys
                p.dma("sp", lambda e, k=k, dst=dst: e.dma_start(out=dst, in_=rr[k]), reads=[Brr[k]])

    dbg_dump("hnT", hnT[:, :, :], BhnT)
    dbg_dump("yT", yT[:, :, :], ByT[0] + ByT[1] + ByT[2])
    p.emit()
    return nc


def make_in_maps(inputs, n_cores=8):
    g = {k: np.asarray(v) for k, v in inputs.items()}
    n_pool = g["cache_k"].shape[1]
    ckf = np.ascontiguousarray(g["cache_k"]).reshape(DEPTH, n_pool * 128, 512)
    cvf = np.ascontiguousarray(g["cache_v"]).reshape(DEPTH, n_pool * 128, 512)
    clff = np.ascontiguousarray(g["cache_logf"]).reshape(DEPTH, n_pool, 1024)
    maps = []
    for c in range(n_cores):
        sl = slice(NSEQ * c, NSEQ * (c + 1))
        m = dict(
            xp=np.ascontiguousarray(g["x_prompt"][c]), xs=np.ascontiguousarray(g["x_sample"][sl]).reshape(128, D),
            ck=ckf, cv=cvf, clf=clff,
            spool=np.ascontiguousarray(g["state_pool"][:, sl]), sconv=np.ascontiguousarray(g["state_conv"][:, sl]),
            sssm=np.ascontiguousarray(g["state_ssm"][:, sl]).reshape(DEPTH, NSEQ, 1024, 128),
            pt=np.ascontiguousarray(g["page_table"][sl]).astype(np.int32),
            norm_pre=g["norm_pre"], w_in=g["w_in"], pool_w=g["pool_w"], pool_scale=g["pool_scale"], f_bias=g["f_bias"],
            conv_w=g["conv_w"], conv_b=g["conv_b"], dt_bias=g["dt_bias"], a_log=g["a_log"], d_skip=g["d_skip"],
            ssm_norm=g["ssm_norm"], w_a=g["w_branch_a"], w_b=g["w_branch_b"], w_c=g["w_branch_c"], w_out=g["w_out"],
            norm_post=g["norm_post"])
        maps.append(m)
    return maps, n_pool


def assemble(res, n_cores=8):
    R = res
    cat = lambda k: np.stack([R[c][k] for c in range(n_cores)])
    y_prompt = cat("yp")
    y_sample = cat("ys").reshape(n_cores * NSEQ, 8, D)
    def pl(k, shp):
        return np.stack([R[c][k] for c in range(n_cores)], axis=1).reshape(shp)
    nb = n_cores
    k_p = pl("kp", (DEPTH, nb, TP, 8, 64)); v_p = pl("vp", (DEPTH, nb, TP, 8, 64)); lf_p = pl("lfp", (DEPTH, nb, TP, 8))
    pool_p = pl("poolp", (DEPTH, nb, 15, 512)); conv_p = pl("convp", (DEPTH, nb, 3, 1536)); ssm_p = pl("ssmp", (DEPTH, nb, 16, 64, 128))
    k_s = pl("kso", (DEPTH, nb * NSEQ, 8, 8, 64)); v_s = pl("vso", (DEPTH, nb * NSEQ, 8, 8, 64)); lf_s = pl("lfs", (DEPTH, nb * NSEQ, 8, 8))
    pool_s = pl("pools", (DEPTH, nb * NSEQ, 15, 512)); conv_s = pl("convs", (DEPTH, nb * NSEQ, 3, 1536))
    ssm_s = pl("ssms", (DEPTH, nb * NSEQ, 16, 64, 128))
    return (y_prompt, y_sample, k_p, v_p, lf_p, pool_p, conv_p, ssm_p, k_s, v_s, lf_s, pool_s, conv_s, ssm_s)


def kernel(**inputs):
    maps, n_pool = make_in_maps(inputs)
    nc = build(n_pool=n_pool)
    res = run_bass_kernel_spmd(nc, maps, core_ids=list(range(8)))
    return tuple(np.ascontiguousarray(a, dtype=np.float32) for a in assemble(res.results))
```

```python
import numpy as np
import concourse.bass as bass
import concourse.mybir as mybir
from concourse.bass_utils import run_bass_kernel_spmd
from contextlib import ExitStack

F32 = mybir.dt.float32
BF16 = mybir.dt.bfloat16
I32 = mybir.dt.int32
ALU = mybir.AluOpType
AF = mybir.ActivationFunctionType

DEPTH = 2
D = 1024
DIN = 8728
TP = 2048
NTOK = 2176
NT = 17
NSEQ = 16
NPG = 16
C_UA, C_ZA, C_Q, C_K, C_V, C_F, C_ZB, C_ZC, C_XBC, C_DT, C_G = (
    0, 512, 1024, 1536, 2048, 2560, 2568, 3080, 4104, 5640, 5656)
EPS = 1e-6
NEG = -30000.0


class Buf:
    __slots__ = ("name", "last_w", "reads", "dsem", "dcnt", "excl")

    def __init__(self, name, excl=False):
        self.name = name
        self.excl = excl
        self.last_w = None
        self.reads = {}
        self.dsem = None
        self.dcnt = 0


def bufs(name, n):
    return [Buf("%s%d" % (name, i)) for i in range(n)]


class Prog:
    ENGS = ("pe", "act", "dve", "pool", "sp")

    def __init__(self, nc, strict=True):
        self.nc = nc
        self.ops = {e: [] for e in self.ENGS}
        self.cnt = {e: 0 for e in self.ENGS}
        self.seen = {e: {} for e in self.ENGS}
        self.strict = strict
        self.ndsem = 0
        self.dsem_final = {}
        self.qhist = {e: [] for e in self.ENGS}
        self.dreg = {}
        self.st = ExitStack()

    def sb(self, name, shape, dtype):
        return self.st.enter_context(self.nc.sbuf_tensor(name, list(shape), dtype))

    def ps(self, name, shape, dtype):
        return self.st.enter_context(self.nc.psum_tensor(name, list(shape), dtype))

    def _deps(self, eng, reads, writes):
        need = {}
        for b in reads:
            t = b.last_w
            if t is not None and need.get(t[0], 0) < t[1]:
                need[t[0]] = t[1]
            if b.excl:
                for k, v in b.reads.items():
                    if k != eng and need.get(k, 0) < v:
                        need[k] = v
        for b in writes:
            t = b.last_w
            if t is not None and need.get(t[0], 0) < t[1]:
                need[t[0]] = t[1]
            for k, v in b.reads.items():
                if need.get(k, 0) < v:
                    need[k] = v
        waits = []
        seen = self.seen[eng]
        for k, v in need.items():
            if k == eng and (eng == "pe" or not self.strict):
                continue
            if seen.get(k, 0) >= v:
                continue
            seen[k] = v
            waits.append((k, v))
        return waits

    def _mark(self, tok, reads, writes):
        for b in reads:
            if b.reads.get(tok[0], 0) < tok[1]:
                b.reads[tok[0]] = tok[1]
        for b in writes:
            b.last_w = tok
            b.reads = {}

    def op(self, eng, fn, reads=(), writes=()):
        waits = self._deps(eng, reads, writes)
        self.cnt[eng] += 1
        tok = (eng, self.cnt[eng])
        self.ops[eng].append((waits, fn, (eng, 1)))
        self._mark(tok, reads, writes)

    def dma(self, q, fn, reads=(), writes=(), owner=None):
        if owner is None:
            owner = writes[0] if writes else reads[0]
        reg = self.dreg.get(owner.name)
        if reg is None:
            reg = ["d%d" % self.ndsem, 0]
            self.ndsem += 1
            self.dreg[owner.name] = reg
        owner.dsem = reg[0]
        waits = self._deps(q, reads, writes)
        qh = self.qhist[q]
        if len(qh) >= 24:
            k0, v0 = qh[-24]
            if self.seen[q].get(k0, 0) < v0:
                self.seen[q][k0] = v0
                waits.append((k0, v0))
        reg[1] += 1
        owner.dcnt = reg[1]
        tok = (owner.dsem, 16 * owner.dcnt)
        self.dsem_final[owner.dsem] = 16 * owner.dcnt
        self.ops[q].append((waits, fn, (owner.dsem, 16)))
        qh.append(tok)
        self._mark(tok, reads, writes)

    def barrier(self):
        state = dict(self.cnt)
        for e in self.ENGS:
            waits = []
            for k, v in list(state.items()) + list(self.dsem_final.items()):
                if k == e or v == 0:
                    continue
                if self.seen[e].get(k, 0) >= v:
                    continue
                self.seen[e][k] = v
                waits.append((k, v))
            if waits:
                self.ops[e].append((waits, None, None))

    def emit(self):
        nc = self.nc
        with self.st as st:
            sems = {}
            for e in self.ENGS:
                sems[e] = st.enter_context(nc.semaphore("s_" + e))
            for i in range(self.ndsem):
                sems["d%d" % i] = st.enter_context(nc.semaphore("sd%d" % i))
            block = st.enter_context(nc.Block())

            def run(ename, eng):
                for waits, fn, inc in self.ops[ename]:
                    for k, v in waits:
                        eng.wait_ge(sems[k], v)
                    if fn is None:
                        continue
                    ins = fn(eng)
                    ins.then_inc(sems[inc[0]], inc[1])
                if ename == "sp":
                    for k, v in self.dsem_final.items():
                        eng.wait_ge(sems[k], v)

            @block.tensor
            def _(e):
                run("pe", e)

            @block.scalar
            def _(e):
                run("act", e)

            @block.vector
            def _(e):
                run("dve", e)

            @block.gpsimd
            def _(e):
                run("pool", e)

            @block.sync
            def _(e):
                run("sp", e)


class Arena:
    def __init__(self, p, name, nbytes):
        self.t = p.sb(name, [128, nbytes // 4], F32)
        self.cap = nbytes // 4
        self.off = 0

    def reset(self, off=0):
        self.off = off

    def mark(self):
        return self.off

    def alloc(self, shape, dtype):
        P = shape[0]
        free = list(shape[1:])
        n = 1
        for s in free:
            n *= s
        esz = 4 if dtype in (F32, I32) else 2
        w = (n * esz + 3) // 4
        w = (w + 3) // 4 * 4
        assert self.off + w <= self.cap, ("arena overflow", self.off, w, self.cap)
        v = self.t[0:P, self.off:self.off + w]
        self.off += w
        if dtype != F32:
            v = v.bitcast(dtype)
        v = v[:, 0:n]
        if len(free) == 1:
            return v
        names = " ".join("abcd"[:len(free)])
        kw = {k: s for k, s in zip(names.split(), free)}
        return v.rearrange("p (%s) -> p %s" % (names, names), **kw)


def build(n_pool=2560, dbg=None, stop=None, strict=True):
    nc = bass.Bass("TRN2", target_bir_lowering=False)
    p = Prog(nc, strict=strict)

    def din(name, shape, dt=F32):
        return nc.dram_tensor(name, list(shape), dt, kind="ExternalInput").ap()

    def dout(name, shape, dt=F32):
        return nc.dram_tensor(name, list(shape), dt, kind="ExternalOutput").ap()

    xp = din("xp", [TP, D]); xs = din("xs", [128, D])
    ck = din("ck", [DEPTH, n_pool * 128, 512]); cv = din("cv", [DEPTH, n_pool * 128, 512])
    clf = din("clf", [DEPTH, n_pool, 1024])
    spool = din("spool", [DEPTH, NSEQ, 15, 512]); sconv = din("sconv", [DEPTH, NSEQ, 3, 1536])
    sssm = din("sssm", [DEPTH, NSEQ, 1024, 128])
    pt = din("pt", [NSEQ, NPG], I32)
    norm_pre = din("norm_pre", [DEPTH, D]); w_in = din("w_in", [DEPTH, D, DIN])
    pool_w = din("pool_w", [DEPTH, 4, 128, 128]); pool_scale = din("pool_scale", [DEPTH, 512])
    f_bias = din("f_bias", [DEPTH, 8]); conv_w = din("conv_w", [DEPTH, 4, 1536]); conv_b = din("conv_b", [DEPTH, 1536])
    dt_bias = din("dt_bias", [DEPTH, 16]); a_log = din("a_log", [DEPTH, 16]); d_skip = din("d_skip", [DEPTH, 16])
    ssm_norm = din("ssm_norm", [DEPTH, D])
    w_a = din("w_a", [DEPTH, 512, D]); w_b = din("w_b", [DEPTH, 512, D]); w_c = din("w_c", [DEPTH, D, D])
    w_out = din("w_out", [DEPTH, D, D]); norm_post = din("norm_post", [DEPTH, D])

    yp = dout("yp", [TP, D]); ys = dout("ys", [128, D])
    kp = dout("kp", [DEPTH, TP, 512]); vp = dout("vp", [DEPTH, TP, 512]); lfp = dout("lfp", [DEPTH, TP, 8])
    poolp = dout("poolp", [DEPTH, 15, 512]); convp = dout("convp", [DEPTH, 3, 1536]); ssmp = dout("ssmp", [DEPTH, 1024, 128])
    kso = dout("kso", [DEPTH, 128, 512]); vso = dout("vso", [DEPTH, 128, 512]); lfs = dout("lfs", [DEPTH, 128, 8])
    pools = dout("pools", [DEPTH, NSEQ, 15, 512]); convs = dout("convs", [DEPTH, NSEQ, 3, 1536])
    ssms = dout("ssms", [DEPTH, NSEQ, 1024, 128])
    x1 = nc.dram_tensor("x1", [NTOK, D], F32, kind="Internal").ap()
    Bx1 = bufs("x1_", NT)
    dbg_out = {}
    if dbg:
        for k, shp in dbg.items():
            dbg_out[k] = dout("dbg_" + k, shp, BF16 if k in ("hnT", "yT", "mT") else F32)

    hnT = p.sb("hnT", [128, 8, NTOK], BF16); BhnT = bufs("hnT", NT)
    yT = p.sb("yT", [128, 16, NTOK], BF16)
    ByT = [bufs("yTa", NT), bufs("yTb", NT), bufs("yTc", NT)]
    YOFF = (0, 4, 8)
    PS = [p.ps("ps%d" % i, [128, 512], F32)[:, :] for i in range(8)]
    BPS = [Buf("ps%d" % i, excl=True) for i in range(8)]
    cst = Arena(p, "cst", 15 * 1024)
    ar = Arena(p, "arena", 90 * 1024)

    Bc = Buf("consts")
    onesf = cst.alloc([128, 128], F32); identf = cst.alloc([128, 128], F32); trif = cst.alloc([128, 128], F32)
    zerof = cst.alloc([128, 128], F32)
    identb = cst.alloc([128, 128], BF16); onesb = cst.alloc([128, 128], BF16); maskb = cst.alloc([128, 128], BF16)
    sel127 = cst.alloc([128, 128], F32)
    blk = cst.alloc([128, 128], F32); btrif = cst.alloc([128, 128], F32)
    negm_p = cst.alloc([128, 128], F32); negm_s = cst.alloc([128, 128], F32)
    lastmask = cst.alloc([128, 16], F32); lastrow = cst.alloc([128, 1], F32); L2s = cst.alloc([128, 128], F32)
    Emat = cst.alloc([128, 128], F32)
    invc = cst.alloc([128, 4, 16], F32)
    iot = cst.alloc([128, 16], I32)
    iop = cst.alloc([128, 1], I32); iopf = cst.alloc([128, 1], F32)

    def cop(eng, fn):
        p.op(eng, fn, reads=[Bc], writes=[Bc])

    cop("pool", lambda e: e.memset(onesf, 1.0))
    cop("pool", lambda e: e.memset(zerof, 0.0))
    cop("pool", lambda e: e.affine_select(out=identf, in_=onesf, pattern=[[-1, 128]], compare_op=ALU.is_equal, fill=0.0, base=0, channel_multiplier=1))
    cop("pool", lambda e: e.affine_select(out=trif, in_=onesf, pattern=[[1, 128]], compare_op=ALU.is_ge, fill=0.0, base=0, channel_multiplier=-1))
    cop("pool", lambda e: e.affine_select(out=sel127, in_=onesf, pattern=[[0, 128]], compare_op=ALU.is_equal, fill=0.0, base=-127, channel_multiplier=1))
    cop("pool", lambda e: e.tensor_copy(out=identb, in_=identf))
    cop("pool", lambda e: e.tensor_copy(out=onesb, in_=onesf))
    cop("pool", lambda e: e.affine_select(out=negm_p, in_=zerof, pattern=[[1, 128]], compare_op=ALU.is_ge, fill=NEG, base=0, channel_multiplier=-1))
    cop("pool", lambda e: e.tensor_copy(out=maskb, in_=negm_p))
    cop("pool", lambda e: e.affine_select(out=Emat, in_=onesf, pattern=[[1, 128]], compare_op=ALU.is_ge, fill=0.0, base=0, channel_multiplier=-8))
    cop("pool", lambda e: e.affine_select(out=Emat, in_=Emat, pattern=[[-1, 128]], compare_op=ALU.is_ge, fill=0.0, base=7, channel_multiplier=8))
    cop("pe", lambda e: e.matmul(PS[0][:, 0:128], lhsT=Emat[0:16, :], rhs=Emat[0:16, :], start=True, stop=True))
    cop("dve", lambda e: e.tensor_copy(out=blk, in_=PS[0][:, 0:128]))
    cop("dve", lambda e: e.tensor_tensor(out=btrif, in0=blk, in1=trif, op=ALU.mult))
    cop("dve", lambda e: e.tensor_scalar(out=negm_s, in0=btrif, scalar1=-1.0, scalar2=-NEG, op0=ALU.add, op1=ALU.mult))
    cop("pool", lambda e: e.affine_select(out=lastmask, in_=onesf[:, 0:16], pattern=[[-8, 16]], compare_op=ALU.is_equal, fill=0.0, base=-7, channel_multiplier=1))
    cop("dve", lambda e: e.tensor_reduce(out=lastrow, in_=lastmask, axis=mybir.AxisListType.X, op=ALU.add))
    cop("dve", lambda e: e.tensor_scalar(out=L2s, in0=blk, scalar1=lastrow[:, 0:1], scalar2=None, op0=ALU.mult))
    cop("pool", lambda e: e.iota(iot, pattern=[[1, 16]], base=1, channel_multiplier=0))
    cop("pool", lambda e: e.iota(iop, pattern=[[0, 1]], base=0, channel_multiplier=1))
    cop("dve", lambda e: e.tensor_copy(out=iopf, in_=iop))
    for g in range(4):
        cop("dve", lambda e, g=g: e.tensor_copy(out=invc[:, g, :], in_=iot))
        cop("dve", lambda e, g=g: e.tensor_scalar(out=invc[:, g, :], in0=invc[:, g, :], scalar1=float(2 ** (g + 1)), scalar2=None, op0=ALU.min))
        cop("dve", lambda e, g=g: e.reciprocal(out=invc[:, g, :], in_=invc[:, g, :]))

    ptb = cst.alloc([128, 256], I32); ptf = cst.alloc([128, 256], F32); kidx = cst.alloc([128, 256], I32)
    lidx = cst.alloc([128, 2], I32)
    Bpt = Buf("pt")
    p.dma("sp", lambda e: e.dma_start(out=ptb, in_=pt.rearrange("a b -> (a b)").partition_broadcast(128)), writes=[Bpt])
    for h in range(2):
        p.dma("sp", lambda e, h=h: e.dma_start(out=lidx[:, h:h + 1], in_=pt[8 * h:8 * h + 8, :].rearrange("a (b o) -> (a b) o", o=1)), writes=[Bpt])
    p.op("dve", lambda e: e.tensor_copy(out=ptf, in_=ptb), reads=[Bpt, Bc], writes=[Bpt])
    p.op("dve", lambda e: e.tensor_scalar(out=ptf, in0=ptf, scalar1=128.0, scalar2=iopf[:, 0:1], op0=ALU.mult, op1=ALU.add), reads=[Bpt], writes=[Bpt])
    p.op("dve", lambda e: e.tensor_copy(out=kidx, in_=ptf), reads=[Bpt], writes=[Bpt])

    npre_bc = cst.alloc([128, D], F32); Bnpre = Buf("npre")
    par = cst.alloc([128, 64], F32); Bpar = Buf("par")

    def dbg_dump(key, src_ap, rbufs, eng="sp"):
        if key in dbg_out:
            p.dma(eng, lambda e: e.dma_start(out=dbg_out[key], in_=src_ap), reads=rbufs)

    def tile_bufs(blist, tok0, ntok):
        return blist[tok0 // 128:(tok0 + ntok + 127) // 128]

    def load_w(dst, src, wbuf, q="pool"):
        p.dma(q, lambda e: e.dma_start(out=dst, in_=src.rearrange("(kc p) n -> p kc n", p=128)), writes=[wbuf])

    def proj_fm(W, wbuf, c0, ncols, tok0, ntok, ps_ap, psbuf):
        for kc in range(8):
            p.op("pe", lambda e, kc=kc: e.matmul(ps_ap, lhsT=W[:, kc, c0:c0 + ncols], rhs=hnT[:, kc, tok0:tok0 + ntok],
                                                  start=(kc == 0), stop=(kc == 7)),
                 reads=[wbuf] + tile_bufs(BhnT, tok0, ntok), writes=[psbuf])

    def proj_tm(W, wbuf, c0, ncols, j, ps_ap, psbuf):
        for kc in range(8):
            p.op("pe", lambda e, kc=kc: e.matmul(ps_ap, lhsT=hnT[:, kc, j * 128:(j + 1) * 128], rhs=W[:, kc, c0:c0 + ncols],
                                                  start=(kc == 0), stop=(kc == 7)),
                 reads=[wbuf, BhnT[j]], writes=[psbuf])

    def norm_to_hnT(xt, Bxt, j, sq, Bsq, hb, Bhb, psi):
        p.op("pool", lambda e: e.memset(sq, 0.0), writes=[Bsq])
        p.op("act", lambda e: e.activation(out=hb, in_=xt, func=AF.Square, accum_out=sq[:, 0:1]), reads=[Bxt], writes=[Bsq, Bhb])
        p.op("act", lambda e: e.activation(out=sq[:, 1:2], in_=sq[:, 0:1], func=AF.Ln, bias=EPS, scale=1.0 / D), reads=[Bsq], writes=[Bsq])
        p.op("act", lambda e: e.activation(out=sq[:, 2:3], in_=sq[:, 1:2], func=AF.Exp, scale=-0.5), reads=[Bsq], writes=[Bsq])
        p.op("dve", lambda e: e.scalar_tensor_tensor(out=hb, in0=xt, scalar=sq[:, 2:3], in1=npre_bc, op0=ALU.mult, op1=ALU.mult),
             reads=[Bxt, Bsq, Bnpre], writes=[Bhb])
        pst = PS[psi].bitcast(BF16)
        for kc in range(8):
            p.op("pe", lambda e, kc=kc: e.transpose(pst[:, kc * 128:(kc + 1) * 128], hb[:, kc * 128:(kc + 1) * 128], identb),
                 reads=[Bhb, Bc], writes=[BPS[psi]])
        p.op("act", lambda e: e.copy(out=hnT[:, :, j * 128:(j + 1) * 128], in_=pst.rearrange("p (k t) -> p k t", k=8)),
             reads=[BPS[psi]], writes=[BhnT[j]])

    for l in range(DEPTH):
        if l == 0:
            p.dma("sp", lambda e: e.dma_start(out=npre_bc, in_=norm_pre[0].partition_broadcast(128)), writes=[Bnpre])
            p.barrier(); ar.reset()
            xt = [ar.alloc([128, D], F32) for _ in range(2)]; Bxt = bufs("xt", 2)
            hb = [ar.alloc([128, D], BF16) for _ in range(2)]; Bhb = bufs("hb", 2)
            sq = [ar.alloc([128, 4], F32) for _ in range(2)]; Bsq = bufs("sq", 2)
            for j in range(NT):
                k = j % 2
                src = xp[j * 128:(j + 1) * 128, :] if j < 16 else xs
                p.dma("sp", lambda e, k=k, src=src: e.dma_start(out=xt[k], in_=src), writes=[Bxt[k]])
                norm_to_hnT(xt[k], Bxt[k], j, sq[k], Bsq[k], hb[k], Bhb[k], k)
        if stop == "N":
            break

        import os
        SKIPAB = os.environ.get('SKIPAB') == '1'
        p.barrier(); ar.reset()
        wA = [ar.alloc([128, 8, 256], BF16) for _ in range(2)]; BwA = bufs("wA", 2)
        pw = [ar.alloc([128, 128], BF16) for _ in range(2)]; Bpw = bufs("pw", 2)
        psc = ar.alloc([128, 4], F32); Bpsc = Buf("psc")
        ub = ar.alloc([128, 15 + TP], F32); Bub = Buf("ub")
        sA = ar.alloc([128, 15 + TP], F32); BsA = Buf("sA")
        sB = ar.alloc([128, 15 + TP], F32); BsB = Buf("sB")
        zs = ar.alloc([128, NTOK], F32); Bzs = Buf("zs")
        dbf = ar.alloc([128, NTOK], BF16); Bdbf = Buf("dbf")
        us = ar.alloc([128, 16, 23], F32); Bus = Buf("us")
        ssA = ar.alloc([128, 16, 23], F32); BssA = Buf("ssA")
        ssB = ar.alloc([128, 16, 23], F32); BssB = Buf("ssB")
        hist = [ar.alloc([120, 512], F32) for _ in range(2)]; Bhist = bufs("hist", 2)
        tmp15 = ar.alloc([128, 16], F32); Btmp15 = Buf("tmp15")
        utm = [ar.alloc([128, 512], F32) for _ in range(2)]; Butm = bufs("utm", 2)
        wU = ar.alloc([128, 8, 512], BF16); BwU = Buf("wU")

        for g in range(4):
            p.dma("sp", lambda e, l=l, g=g: e.dma_start(out=psc[:, g:g + 1], in_=pool_scale[l, g * 128:(g + 1) * 128].rearrange("(c o) -> c o", o=1)), writes=[Bpsc])
        for h2 in range(2):
            p.dma("sp", lambda e, l=l, h2=h2: e.dma_start(out=hist[h2], in_=spool[l, 8 * h2:8 * h2 + 8].rearrange("b j c -> (b j) c")),
                  writes=[Bhist[h2]])
        p.op("pool", lambda e: e.memset(ub[:, 0:15], 0.0), writes=[Bub])
        load_w(wU, w_in[l][:, C_UA:C_UA + 512], BwU)
        for k, j in enumerate((15, 16)):
            proj_tm(wU, BwU, 0, 512, j, PS[6 + k], BPS[6 + k])
            p.op("act", lambda e, k=k: e.copy(out=utm[k], in_=PS[6 + k]), reads=[BPS[6 + k]], writes=[Butm[k]])
        p.dma("sp", lambda e, l=l: e.dma_start(out=poolp[l], in_=utm[0][113:128, :]), reads=[Butm[0]])
        for b in range(NSEQ):
            p.dma("sp", lambda e, l=l, b=b: e.dma_start(out=pools[l, b, 7:15, :], in_=utm[1][8 * b:8 * b + 8, :]), reads=[Butm[1]])
        p.dma("sp", lambda e, l=l: e.dma_start(out=pools[l, :, 0:7, :], in_=spool[l, :, 8:15, :]), owner=Butm[1])

        for g in range(4):
            w = 2 ** (g + 1)
            k = g % 2
            load_w(wA[k][:, :, 0:128], w_in[l][:, C_UA + g * 128:C_UA + (g + 1) * 128], BwA[k])
            load_w(wA[k][:, :, 128:256], w_in[l][:, C_ZA + g * 128:C_ZA + (g + 1) * 128], BwA[k])
            p.dma("pool", lambda e, l=l, g=g, k=k: e.dma_start(out=pw[k], in_=pool_w[l, g]), writes=[Bpw[k]])
            for tg in range(5):
                tok0 = tg * 512
                ntok = 512 if tg < 4 else 128
                pu, pz = (tg * 2) % 6, (tg * 2 + 1) % 6
                proj_fm(wA[k], BwA[k], 0, 128, tok0, ntok, PS[pu][:, 0:ntok], BPS[pu])
                proj_fm(wA[k], BwA[k], 128, 128, tok0, ntok, PS[pz][:, 0:ntok], BPS[pz])
                if tg < 4:
                    p.op("act", lambda e, pu=pu, tok0=tok0: e.copy(out=ub[:, 15 + tok0:15 + tok0 + 512], in_=PS[pu]), reads=[BPS[pu]], writes=[Bub])
                else:
                    p.op("act", lambda e, pu=pu: e.copy(out=us[:, :, 15:23], in_=PS[pu][:, 0:128].rearrange("p (b t) -> p b t", b=16)),
                         reads=[BPS[pu]], writes=[Bus])
                p.op("act", lambda e, pz=pz, tok0=tok0, ntok=ntok: e.activation(out=zs[:, tok0:tok0 + ntok], in_=PS[pz][:, 0:ntok], func=AF.Silu),
                     reads=[BPS[pz]], writes=[Bzs])
            for h2 in range(2):
                p.op("pe", lambda e, g=g, h2=h2: e.transpose(PS[6][:, h2 * 128:h2 * 128 + 120], hist[h2][:, g * 128:(g + 1) * 128], identf[0:120, 0:120]),
                     reads=[Bhist[h2], Bc], writes=[BPS[6]])
            p.op("act", lambda e: e.copy(out=us[:, :, 0:15].rearrange("p (h b) j -> p h b j", h=2),
                                         in_=PS[6][:, 0:256].rearrange("p (h x) -> p h x", h=2)[:, :, 0:120].rearrange("p h (b j) -> p h b j", b=8)),
                 reads=[BPS[6]], writes=[Bus])
            src_p, Bsrc_p, src_s, Bsrc_s = ub, Bub, us, Bus
            lo = 0
            pp = [(sA, BsA, ssA, BssA), (sB, BsB, ssB, BssB)]
            for step in range(g + 1):
                sh = 2 ** step
                dp, Bdp, ds_, Bds = pp[step % 2]
                n = 15 + TP
                p.op("dve", lambda e, dp=dp, sp_=src_p, lo=lo, sh=sh, n=n: e.tensor_tensor(out=dp[:, lo + sh:n], in0=sp_[:, lo + sh:n], in1=sp_[:, lo:n - sh], op=ALU.add),
                     reads=[Bsrc_p], writes=[Bdp])
                p.op("pool", lambda e, ds_=ds_, ss_=src_s, lo=lo, sh=sh: e.tensor_tensor(out=ds_[:, :, lo + sh:23], in0=ss_[:, :, lo + sh:23], in1=ss_[:, :, lo:23 - sh], op=ALU.add),
                     reads=[Bsrc_s], writes=[Bds])
                src_p, Bsrc_p, src_s, Bsrc_s = dp, Bdp, ds_, Bds
                lo += sh
            p.op("dve", lambda e, sp_=src_p, w=w: e.scalar_tensor_tensor(out=dbf[:, 0:TP], in0=sp_[:, 15:15 + TP], scalar=1.0 / w, in1=ub[:, 15:15 + TP], op0=ALU.mult, op1=ALU.subtract),
                 reads=[Bsrc_p, Bub], writes=[Bdbf])
            p.op("dve", lambda e, sp_=src_p, g=g: e.tensor_tensor(out=tmp15[:, 0:15], in0=sp_[:, 15:30], in1=invc[:, g, 0:15], op=ALU.mult),
                 reads=[Bsrc_p, Bc], writes=[Btmp15])
            p.op("dve", lambda e: e.tensor_tensor(out=dbf[:, 0:15], in0=tmp15[:, 0:15], in1=ub[:, 15:30], op=ALU.subtract),
                 reads=[Btmp15, Bub, Bdbf], writes=[Bdbf])
            p.op("dve", lambda e, ss_=src_s, w=w: e.scalar_tensor_tensor(out=dbf[:, TP:NTOK].rearrange("p (b t) -> p b t", b=16), in0=ss_[:, :, 15:23], scalar=1.0 / w, in1=us[:, :, 15:23], op0=ALU.mult, op1=ALU.subtract),
                 reads=[Bsrc_s, Bus, Bdbf], writes=[Bdbf])
            for tg in range(5):
                tok0 = tg * 512
                ntok = 512 if tg < 4 else 128
                py = tg % 6
                p.op("pe", lambda e, k=k, py=py, tok0=tok0, ntok=ntok: e.matmul(PS[py][:, 0:ntok], lhsT=pw[k], rhs=dbf[:, tok0:tok0 + ntok], start=True, stop=True),
                     reads=[Bpw[k], Bdbf], writes=[BPS[py]])
                p.op("dve", lambda e, g=g, py=py, tok0=tok0, ntok=ntok: e.scalar_tensor_tensor(out=yT[:, g, tok0:tok0 + ntok], in0=PS[py][:, 0:ntok], scalar=psc[:, g:g + 1], in1=zs[:, tok0:tok0 + ntok], op0=ALU.mult, op1=ALU.mult),
                     reads=[BPS[py], Bpsc, Bzs], writes=tile_bufs(ByT[0], tok0, ntok))
        if stop == "A":
            break

        p.barrier(); ar.reset()
        wF = ar.alloc([128, 8, 8], BF16); BwF = Buf("wF")
        fbb = ar.alloc([128, 8], F32); Bfbb = Buf("fbb")
        lf = ar.alloc([128, 17, 8], F32); Blf = Buf("lf")
        qTs = ar.alloc([128, 4, 128], BF16); kTs = ar.alloc([128, 4, 128], BF16); Bqks = Buf("qks")
        vnew = ar.alloc([128, 512], BF16); Bvnew = Buf("vnew")
        zsTs = ar.alloc([128, 4, 128], F32); BzsTs = Buf("zsTs")
        markB = ar.mark()
        tot = ar.alloc([128, 16, 8], F32); carry = ar.alloc([128, 17, 8], F32); ccum = ar.alloc([128, 16, 8], F32); Bcc = Buf("cc")
        btab = ar.alloc([128, 16, 16, 8], F32); Bbtab = Buf("btab")
        wB = [ar.alloc([128, 8, 512], BF16) for _ in range(2)]; BwB = bufs("wB", 2)
        qT = ar.alloc([128, NTOK], BF16); BqT = Buf("qT")
        kT = ar.alloc([128, NTOK], BF16); BkT = Buf("kT")
        vaug = ar.alloc([128, 16, 2, 66], BF16); Bvaug = bufs("vaug", 16)
        kvst = [ar.alloc([128, 256], F32) for _ in range(3)]; Bkvst = bufs("kvst", 3)
        pT = [ar.alloc([128, 128], BF16) for _ in range(4)]; BpT = bufs("pT", 4)
        zsb = [ar.alloc([128, 128], F32) for _ in range(16)]; Bzsb = bufs("zsb", 16)
        rinv = [ar.alloc([128, 1], F32) for _ in range(2)]; Brinv = bufs("rinv", 2)
        ybt = [ar.alloc([128, 128], BF16) for _ in range(2)]; Bybt = bufs("ybt", 2)

        load_w(wF, w_in[l][:, C_F:C_F + 8], BwF)
        p.dma("sp", lambda e, l=l: e.dma_start(out=fbb, in_=f_bias[l].partition_broadcast(128)), writes=[Bfbb])
        for j in range(NT):
            proj_tm(wF, BwF, 0, 8, j, PS[0][:, j * 8:(j + 1) * 8], BPS[0])
        lf_flat = lf.rearrange("p j h -> p (j h)")
        p.op("dve", lambda e: e.tensor_tensor(out=lf, in0=PS[0][:, 0:136].rearrange("p (j h) -> p j h", h=8), in1=fbb.unsqueeze(1).to_broadcast([128, 17, 8]), op=ALU.add),
             reads=[BPS[0], Bfbb], writes=[Blf])
        p.op("act", lambda e: e.activation(out=lf_flat, in_=lf_flat, func=AF.Exp, scale=-1.0), reads=[Blf], writes=[Blf])
        p.op("act", lambda e: e.activation(out=lf_flat, in_=lf_flat, func=AF.Ln, bias=1.0), reads=[Blf], writes=[Blf])
        p.op("dve", lambda e: e.tensor_scalar(out=lf_flat, in0=lf_flat, scalar1=-1.0, scalar2=None, op0=ALU.mult), reads=[Blf], writes=[Blf])
        p.dma("sp", lambda e, l=l: e.dma_start(out=lfp[l].rearrange("(j p) h -> p j h", p=128), in_=lf[:, 0:16, :]), reads=[Blf])
        p.dma("sp", lambda e, l=l: e.dma_start(out=lfs[l], in_=lf[:, 16, :]), reads=[Blf])
        if stop == "B0a":
            break
        p.op("pe", lambda e: e.matmul(PS[1][:, 0:128], lhsT=onesf, rhs=lf_flat[:, 0:128], start=True, stop=True), reads=[Blf, Bc], writes=[BPS[1]])
        p.op("pe", lambda e: e.matmul(PS[2][:, 0:128], lhsT=trif, rhs=lf_flat[:, 0:128], start=True, stop=True), reads=[Blf, Bc], writes=[BPS[2]])
        p.op("dve", lambda e: e.tensor_copy(out=tot.rearrange("p j h -> p (j h)"), in_=PS[1][:, 0:128]), reads=[BPS[1]], writes=[Bcc])
        p.op("dve", lambda e: e.memset(carry[:, 0, :], 0.0), reads=[Bcc], writes=[Bcc])
        for j in range(1, 17):
            p.op("dve", lambda e, j=j: e.tensor_tensor(out=carry[:, j, :], in0=carry[:, j - 1, :], in1=tot[:, j - 1, :], op=ALU.add), reads=[Bcc], writes=[Bcc])
        p.op("dve", lambda e: e.tensor_tensor(out=ccum, in0=PS[2][:, 0:128].rearrange("p (j h) -> p j h", h=8), in1=carry[:, 0:16, :], op=ALU.add), reads=[BPS[2], Bcc], writes=[Bcc])
        if stop == "B0c":
            break
        for Q in range(16):
            p.op("dve", lambda e, Q=Q: e.tensor_tensor(out=btab[:, Q, 0:Q + 1, :], in0=carry[:, Q + 1:Q + 2, :].to_broadcast([128, Q + 1, 8]), in1=ccum[:, 0:Q + 1, :], op=ALU.subtract),
                 reads=[Bcc, Bbtab], writes=[Bbtab])
        p.op("dve", lambda e: e.memset(vaug[:, :, :, 64:66], 1.0), writes=Bvaug)
        if stop == "B0":
            break

        import os
        for pr in range(int(os.environ.get('NPR', 4))):
            k = pr % 2
            for i, c0 in enumerate((C_Q, C_K, C_V, C_ZB)):
                load_w(wB[k][:, :, i * 128:(i + 1) * 128], w_in[l][:, c0 + pr * 128:c0 + (pr + 1) * 128], BwB[k])
            for tg in range(5):
                tok0 = tg * 512
                ntok = 512 if tg < 4 else 128
                pa, pb = 5 + (2 * tg) % 3, 5 + (2 * tg + 1) % 3
                proj_fm(wB[k], BwB[k], 0, 128, tok0, ntok, PS[pa][:, 0:ntok], BPS[pa])
                p.op("dve", lambda e, pa=pa, tok0=tok0, ntok=ntok: e.tensor_scalar(out=qT[:, tok0:tok0 + ntok], in0=PS[pa][:, 0:ntok], scalar1=0.125, scalar2=None, op0=ALU.mult),
                     reads=[BPS[pa]], writes=[BqT])
                proj_fm(wB[k], BwB[k], 128, 128, tok0, ntok, PS[pb][:, 0:ntok], BPS[pb])
                p.op("dve", lambda e, pb=pb, tok0=tok0, ntok=ntok: e.tensor_copy(out=kT[:, tok0:tok0 + ntok], in_=PS[pb][:, 0:ntok]),
                     reads=[BPS[pb]], writes=[BkT])
            if stop == "B0b1":
                continue
            p.op("pool", lambda e, pr=pr: e.tensor_copy(out=qTs[:, pr, :], in_=qT[:, TP:NTOK]), reads=[BqT], writes=[Bqks])
            p.op("pool", lambda e, pr=pr: e.tensor_copy(out=kTs[:, pr, :], in_=kT[:, TP:NTOK]), reads=[BkT], writes=[Bqks])
            proj_fm(wB[k], BwB[k], 384, 128, TP, 128, PS[5][:, 0:128], BPS[5])
            p.op("act", lambda e, pr=pr: e.activation(out=zsTs[:, pr, :], in_=PS[5][:, 0:128], func=AF.Silu), reads=[BPS[5]], writes=[BzsTs])
            if stop == "B0b2":
                continue
            import os
            SK = os.environ.get("SK", "")
            for j in range(int(os.environ.get("NTJ", NT))):
                pb = 5 + j % 3
                m = j % 3
                proj_tm(wB[k], BwB[k], 128, 256, j, PS[pb][:, 0:256], BPS[pb])
                p.op("act", lambda e, pb=pb, m=m: e.copy(out=kvst[m], in_=PS[pb][:, 0:256]), reads=[BPS[pb]], writes=[Bkvst[m]])
                if j < 16:
                    if "v" not in SK:
                        for a in range(2):
                            p.op("dve", lambda e, j=j, a=a, m=m: e.tensor_copy(out=vaug[:, j, a, 0:64], in_=kvst[m][:, 128 + 64 * a:192 + 64 * a]),
                                 reads=[Bkvst[m]], writes=[Bvaug[j]])
                    if "d" not in SK:
                        p.dma("sp", lambda e, l=l, j=j, pr=pr, m=m: e.dma_start(out=kp[l, j * 128:(j + 1) * 128, pr * 128:(pr + 1) * 128], in_=kvst[m][:, 0:128]), reads=[Bkvst[m]])
                        p.dma("sp", lambda e, l=l, j=j, pr=pr, m=m: e.dma_start(out=vp[l, j * 128:(j + 1) * 128, pr * 128:(pr + 1) * 128], in_=kvst[m][:, 128:256]), reads=[Bkvst[m]])
                else:
                    if "n" not in SK:
                        p.op("dve", lambda e, m=m, pr=pr: e.tensor_copy(out=vnew[:, pr * 128:(pr + 1) * 128], in_=kvst[m][:, 128:256]), reads=[Bkvst[m]], writes=[Bvnew])
                    if "e" not in SK:
                        p.dma("sp", lambda e, l=l, pr=pr, m=m: e.dma_start(out=kso[l, :, pr * 128:(pr + 1) * 128], in_=kvst[m][:, 0:128]), reads=[Bkvst[m]])
                        p.dma("sp", lambda e, l=l, pr=pr, m=m: e.dma_start(out=vso[l, :, pr * 128:(pr + 1) * 128], in_=kvst[m][:, 128:256]), reads=[Bkvst[m]])
            if stop == "B0b":
                continue
            it = 0
            for Q in range(16):
                pz_ = 5 + Q % 3
                proj_tm(wB[k], BwB[k], 384, 128, Q, PS[pz_][:, 0:128], BPS[pz_])
                p.op("act", lambda e, Q=Q, pz_=pz_: e.activation(out=zsb[Q], in_=PS[pz_][:, 0:128], func=AF.Silu), reads=[BPS[pz_]], writes=[Bzsb[Q]])
            for Q in range(16):
                zq = Q % 2
                for h2 in range(2):
                    hb = 64 * h2
                    h = 2 * pr + h2
                    po = 3 + h2

                    def qk(S, it):
                        sb_ = it % 3
                        p.op("pe", lambda e, S=S, sb_=sb_, hb=hb, Q=Q: e.matmul(PS[sb_][:, 0:128], lhsT=kT[hb:hb + 64, S * 128:(S + 1) * 128], rhs=qT[hb:hb + 64, Q * 128:(Q + 1) * 128], start=True, stop=(S != Q)),
                             reads=[BkT, BqT], writes=[BPS[sb_]])
                        if S == Q:
                            p.op("pe", lambda e, sb_=sb_: e.matmul(PS[sb_][:, 0:128], lhsT=identb, rhs=maskb, start=False, stop=True), reads=[Bc], writes=[BPS[sb_]])
                    qk(0, it)
                    for S in range(Q + 1):
                        if S + 1 <= Q:
                            qk(S + 1, it + 1)
                        sb_ = it % 3
                        pi = it % 4
                        p.op("act", lambda e, sb_=sb_, pi=pi, Q=Q, S=S, h=h: e.activation(out=pT[pi], in_=PS[sb_][:, 0:128], func=AF.Exp, bias=btab[:, Q, S, h:h + 1], scale=1.0),
                             reads=[BPS[sb_], Bbtab], writes=[BpT[pi]])
                        p.op("pe", lambda e, pi=pi, S=S, h2=h2, po=po, Q=Q: e.matmul(PS[po][:, 0:65], lhsT=pT[pi], rhs=vaug[:, S, h2, 0:65], start=(S == 0), stop=(S == Q)),
                             reads=[BpT[pi], Bvaug[S]], writes=[BPS[po]])
                        it += 1
                    p.op("dve", lambda e, po=po, h2=h2: e.reciprocal(out=rinv[h2], in_=PS[po][:, 64:65]), reads=[BPS[po]], writes=[Brinv[h2]])
                    p.op("dve", lambda e, po=po, h2=h2, hb=hb, zq=zq, Q=Q: e.scalar_tensor_tensor(out=ybt[zq][:, hb:hb + 64], in0=PS[po][:, 0:64], scalar=rinv[h2][:, 0:1], in1=zsb[Q][:, hb:hb + 64], op0=ALU.mult, op1=ALU.mult),
                         reads=[BPS[po], Brinv[h2], Bzsb[Q]], writes=[Bybt[zq]])
                pst = PS[6].bitcast(BF16)
                p.op("pe", lambda e, zq=zq, pst=pst: e.transpose(pst[:, 0:128], ybt[zq], identb), reads=[Bybt[zq], Bc], writes=[BPS[6]])
                p.op("dve", lambda e, pst=pst, pr=pr, Q=Q: e.tensor_copy(out=yT[:, 4 + pr, Q * 128:(Q + 1) * 128], in_=pst[:, 0:128]), reads=[BPS[6]], writes=[ByT[1][Q]])
        if stop == "B1":
            break

        p.barrier(); ar.reset(markB)
        sufm = ar.alloc([128, 128], F32); Bsufm = Buf("sufm")
        lfg = ar.alloc([128, 1024], F32); Blfg = Buf("lfg")
        lft = ar.alloc([128, 8, 128], F32); Blft = Buf("lft")
        totS = ar.alloc([128, 8, 8, 16], F32); later = ar.alloc([128, 8, 8, 16], F32); Blat = Buf("later")
        bpast = ar.alloc([128, 8, 2, 128], F32); Bbpast = Buf("bpast")
        bnew = ar.alloc([128, 8], F32); Bbnew = Buf("bnew")
        qbd = ar.alloc([128, 4, 16, 16], BF16); Bqbd = Buf("qbd")
        kbf = [ar.alloc([128, 512], BF16) for _ in range(4)]; Bkbf = bufs("kbf", 4)
        KT = [ar.alloc([128, 4, 128], BF16) for _ in range(2)]; BKT = bufs("KT", 2)
        Vb = ar.alloc([128, 17, 512], BF16); BVb = bufs("Vb", 17)
        sc = ar.alloc([128, 17, 64], F32); Bsc = Buf("sc")
        pTs = ar.alloc([128, 17, 64], BF16); BpTs = Buf("pTs")
        rs = ar.alloc([128, 16, 64], F32); Brs = Buf("rs")
        ot = ar.alloc([128, 4, 16, 16], F32); Bot = Buf("ot")

        p.op("dve", lambda e: e.tensor_tensor(out=sufm, in0=onesf, in1=trif, op=ALU.subtract), reads=[Bc], writes=[Bsufm])
        kf = ar.alloc([128, 256], F32); kidxL = ar.alloc([128, 256], I32); lf2 = ar.alloc([128, 2], F32); lidxL = ar.alloc([128, 2], I32); BidxL = Buf("idxL")
        p.op("dve", lambda e: e.tensor_copy(out=kf, in_=kidx), reads=[Bpt], writes=[BidxL])
        p.op("dve", lambda e, l=l: e.tensor_scalar(out=kf, in0=kf, scalar1=float(l * n_pool * 128), scalar2=None, op0=ALU.add), reads=[BidxL], writes=[BidxL])
        p.op("dve", lambda e: e.tensor_copy(out=kidxL, in_=kf), reads=[BidxL], writes=[BidxL])
        p.op("dve", lambda e: e.tensor_copy(out=lf2, in_=lidx), reads=[Bpt, BidxL], writes=[BidxL])
        p.op("dve", lambda e, l=l: e.tensor_scalar(out=lf2, in0=lf2, scalar1=float(l * n_pool), scalar2=None, op0=ALU.add), reads=[BidxL], writes=[BidxL])
        p.op("dve", lambda e: e.tensor_copy(out=lidxL, in_=lf2), reads=[BidxL], writes=[BidxL])
        ck_t = ck.rearrange("l r c -> (l r) c"); cv_t = cv.rearrange("l r c -> (l r) c"); clf_t = clf.rearrange("l r c -> (l r) c")
        p.op("dve", lambda e: e.memset(qbd, 0.0), writes=[Bqbd])
        for pr in range(4):
            p.op("dve", lambda e, pr=pr: e.tensor_copy(out=qbd[0:64, pr, :, 0:8], in_=qTs[0:64, pr, :].rearrange("p (b t) -> p b t", b=16)), reads=[Bqks, Bqbd], writes=[Bqbd])
            p.op("dve", lambda e, pr=pr: e.tensor_copy(out=qbd[64:128, pr, :, 8:16], in_=qTs[64:128, pr, :].rearrange("p (b t) -> p b t", b=16)), reads=[Bqks, Bqbd], writes=[Bqbd])
        p.op("pe", lambda e: e.matmul(PS[0][:, 0:8], lhsT=btrif, rhs=lf[:, 16, :], start=True, stop=True), reads=[Blf, Bc], writes=[BPS[0]])
        p.op("dve", lambda e: e.tensor_scalar(out=bnew, in0=PS[0][:, 0:8], scalar1=-1.0, scalar2=None, op0=ALU.mult), reads=[BPS[0]], writes=[Bbnew])
        for half in range(2):
            p.dma("pool", lambda e, l=l, half=half: e.indirect_dma_start(out=lfg, out_offset=None, in_=clf_t, in_offset=bass.IndirectOffsetOnAxis(ap=lidxL[:, half:half + 1], axis=0)),
                  reads=[BidxL], writes=[Blfg])
            lfg3 = lfg.rearrange("p (s h) -> p s h", h=8)
            for h in range(8):
                pb_ = 1 + h // 4
                p.op("pe", lambda e, h=h, pb_=pb_: e.transpose(PS[pb_][:, (h % 4) * 128:(h % 4 + 1) * 128], lfg3[:, :, h], identf), reads=[Blfg, Bc], writes=[BPS[pb_]])
            for q4 in range(2):
                p.op("act", lambda e, q4=q4: e.copy(out=lft[:, 4 * q4:4 * q4 + 4, :], in_=PS[1 + q4].rearrange("p (h x) -> p h x", h=4)), reads=[BPS[1 + q4]], writes=[Blft])
            for q4 in range(2):
                p.op("pe", lambda e, q4=q4: e.matmul(PS[3 + q4], lhsT=onesf, rhs=lft[:, 4 * q4:4 * q4 + 4, :].rearrange("p h x -> p (h x)"), start=True, stop=True), reads=[Blft, Bc], writes=[BPS[3 + q4]])
                p.op("act", lambda e, q4=q4: e.copy(out=totS[:, 4 * q4:4 * q4 + 4, :, :].rearrange("p h b j -> p (h b j)"), in_=PS[3 + q4]), reads=[BPS[3 + q4]], writes=[Blat])
            p.op("dve", lambda e: e.memset(later[:, :, :, 15:16], 0.0), reads=[Blat], writes=[Blat])
            for j in range(14, -1, -1):
                p.op("dve", lambda e, j=j: e.tensor_tensor(out=later[:, :, :, j:j + 1], in0=later[:, :, :, j + 1:j + 2], in1=totS[:, :, :, j + 1:j + 2], op=ALU.add), reads=[Blat], writes=[Blat])
            for q4 in range(2):
                p.op("pe", lambda e, q4=q4: e.matmul(PS[5 + q4], lhsT=sufm, rhs=lft[:, 4 * q4:4 * q4 + 4, :].rearrange("p h x -> p (h x)"), start=True, stop=True), reads=[Blft, Bsufm], writes=[BPS[5 + q4]])
                p.op("dve", lambda e, q4=q4, half=half: e.tensor_tensor(out=bpast[:, 4 * q4:4 * q4 + 4, half, :], in0=PS[5 + q4].rearrange("p (h x) -> p h x", h=4),
                                                                    in1=later[:, 4 * q4:4 * q4 + 4, :, :].rearrange("p h b j -> p h (b j)"), op=ALU.add),
                     reads=[BPS[5 + q4], Blat], writes=[Bbpast])
        p.op("dve", lambda e: e.tensor_copy(out=Vb[:, 16, :], in_=vnew), reads=[Bvnew], writes=[BVb[16]])
        for b in range(NSEQ):
            half, b8 = b // 8, b % 8
            for j in range(NPG):
                m = j % 4
                col = b * 16 + j
                p.dma("pool", lambda e, m=m, col=col: e.indirect_dma_start(out=kbf[m], out_offset=None, in_=ck_t, in_offset=bass.IndirectOffsetOnAxis(ap=kidxL[:, col:col + 1], axis=0)),
                      reads=[BidxL], writes=[Bkbf[m]])
                p.dma("pool", lambda e, j=j, col=col: e.indirect_dma_start(out=Vb[:, j, :], out_offset=None, in_=cv_t, in_offset=bass.IndirectOffsetOnAxis(ap=kidxL[:, col:col + 1], axis=0)),
                      reads=[BidxL], writes=[BVb[j]])
            for j in range(NPG):
                m = j % 4
                m2 = j % 2
                pst = PS[3].bitcast(BF16)
                for pr in range(4):
                    p.op("pe", lambda e, m=m, pr=pr, pst=pst: e.transpose(pst[:, pr * 128:(pr + 1) * 128], kbf[m][:, pr * 128:(pr + 1) * 128], identb), reads=[Bkbf[m], Bc], writes=[BPS[3]])
                p.op("dve", lambda e, m2=m2, pst=pst: e.tensor_copy(out=KT[m2], in_=pst[:, 0:512].rearrange("p (r s) -> p r s", r=4)), reads=[BPS[3]], writes=[BKT[m2]])
                bank = j // 8
                for pr in range(4):
                    c0 = (j % 8) * 64 + pr * 16
                    p.op("pe", lambda e, m2=m2, pr=pr, bank=bank, c0=c0, b=b: e.matmul(PS[bank][:, c0:c0 + 16], lhsT=KT[m2][:, pr, :], rhs=qbd[:, pr, b, :], start=True, stop=True),
                         reads=[BKT[m2], Bqbd], writes=[BPS[bank]])
            for pr in range(4):
                p.op("pe", lambda e, pr=pr, b=b: e.matmul(PS[2][:, pr * 16:(pr + 1) * 16], lhsT=kTs[:, pr, :], rhs=qbd[:, pr, b, :], start=True, stop=True),
                     reads=[Bqks, Bqbd], writes=[BPS[2]])
            for bank in range(2):
                bias_v = bpast[:, :, half, b8 * 16 + bank * 8:b8 * 16 + bank * 8 + 8].rearrange("p h j -> p j h").unsqueeze(3).to_broadcast([128, 8, 8, 8])
                p.op("dve", lambda e, bank=bank, bias_v=bias_v: e.tensor_tensor(out=sc[:, bank * 8:(bank + 1) * 8, :].rearrange("p j (h t) -> p j h t", h=8),
                                                                           in0=PS[bank].rearrange("p (j h t) -> p j h t", j=8, h=8), in1=bias_v, op=ALU.add),
                     reads=[BPS[bank], Bbpast], writes=[Bsc])
            p.op("dve", lambda e, b=b: e.tensor_tensor(out=sc[:, 16, :].rearrange("p (h t) -> p h t", h=8), in0=PS[2][:, 0:64].rearrange("p (h t) -> p h t", h=8),
                                                  in1=negm_s[:, b * 8:(b + 1) * 8].unsqueeze(1).to_broadcast([128, 8, 8]), op=ALU.add),
                 reads=[BPS[2], Bc, Bsc], writes=[Bsc])
            p.op("dve", lambda e: e.tensor_tensor(out=sc[:, 16, :].rearrange("p (h t) -> p h t", h=8), in0=sc[:, 16, :].rearrange("p (h t) -> p h t", h=8),
                                             in1=bnew.unsqueeze(2).to_broadcast([128, 8, 8]), op=ALU.add),
                 reads=[Bbnew, Bsc], writes=[Bsc])
            p.op("act", lambda e: e.activation(out=pTs.rearrange("p j x -> p (j x)"), in_=sc.rearrange("p j x -> p (j x)"), func=AF.Exp), reads=[Bsc], writes=[BpTs])
            for pr in range(4):
                ob = 4 + pr // 2
                c0 = (pr % 2) * 256 + b * 16
                for j in range(17):
                    p.op("pe", lambda e, pr=pr, j=j, ob=ob, c0=c0: e.matmul(PS[ob][:, c0:c0 + 16], lhsT=Vb[:, j, pr * 128:(pr + 1) * 128], rhs=pTs[:, j, pr * 16:(pr + 1) * 16], start=(j == 0), stop=(j == 16)),
                         reads=[BVb[j], BpTs], writes=[BPS[ob]])
            sbk = 6 + b // 8
            for j in range(17):
                p.op("pe", lambda e, j=j, sbk=sbk, b8=b8: e.matmul(PS[sbk][:, b8 * 64:(b8 + 1) * 64], lhsT=onesb, rhs=pTs[:, j, :], start=(j == 0), stop=(j == 16)),
                     reads=[Bc, BpTs], writes=[BPS[sbk]])
        for hf in range(2):
            p.op("dve", lambda e, hf=hf: e.reciprocal(out=rs[:, 8 * hf:8 * hf + 8, :].rearrange("p b x -> p (b x)"), in_=PS[6 + hf]), reads=[BPS[6 + hf]], writes=[Brs])
        for q2 in range(2):
            p.op("dve", lambda e, q2=q2: e.tensor_tensor(out=ot[:, 2 * q2:2 * q2 + 2, :, :], in0=PS[4 + q2].rearrange("p (r b x) -> p r b x", r=2, b=16),
                                                    in1=rs.rearrange("p b (r x) -> p r b x", r=4)[:, 2 * q2:2 * q2 + 2, :, :], op=ALU.mult),
                 reads=[BPS[4 + q2], Brs], writes=[Bot])
        for h2 in range(2):
            hb = 64 * h2
            p.op("dve", lambda e, h2=h2, hb=hb: e.tensor_tensor(out=yT[hb:hb + 64, 4:8, TP:NTOK].rearrange("p r (b t) -> p r b t", b=16),
                                                           in0=ot[hb:hb + 64, :, :, h2 * 8:(h2 + 1) * 8],
                                                           in1=zsTs[hb:hb + 64, :, :].rearrange("p r (b t) -> p r b t", b=16), op=ALU.mult),
                 reads=[Bot, BzsTs], writes=[ByT[1][16]])
        if stop == "B2":
            break

        p.barrier(); ar.reset()
        cwb = ar.alloc([128, 12, 5], F32); Bcw = Buf("cw")
        Ebc = ar.alloc([128, 16, 128], BF16); blkrow = ar.alloc([128, 16], F32); BE = Buf("Ebc")
        markC = ar.mark()
        cwT = ar.alloc([5, 1536], F32)
        p.dma("sp", lambda e, l=l: e.dma_start(out=cwT[0:4, :], in_=conv_w[l]), writes=[Bcw])
        p.dma("sp", lambda e, l=l: e.dma_start(out=cwT[4:5, :], in_=conv_b[l].rearrange("(o c) -> o c", o=1)), writes=[Bcw])
        for cc in range(12):
            p.op("pe", lambda e, cc=cc: e.transpose(PS[0][:, cc * 8:cc * 8 + 5], cwT[:, cc * 128:(cc + 1) * 128], identf[0:5, 0:5]), reads=[Bcw, Bc], writes=[BPS[0]])
        p.op("dve", lambda e: e.tensor_copy(out=cwb, in_=PS[0][:, 0:96].rearrange("p (c x) -> p c x", x=8)[:, :, 0:5]), reads=[BPS[0]], writes=[Bcw])
        p.op("pool", lambda e: e.memset(Ebc, 1.0), writes=[BE])
        p.op("pool", lambda e: e.affine_select(out=Ebc, in_=Ebc, pattern=[[-8, 16], [1, 128]], compare_op=ALU.is_ge, fill=0.0, base=0, channel_multiplier=0), reads=[BE], writes=[BE])
        p.op("pool", lambda e: e.affine_select(out=Ebc, in_=Ebc, pattern=[[8, 16], [-1, 128]], compare_op=ALU.is_ge, fill=0.0, base=7, channel_multiplier=0), reads=[BE], writes=[BE])
        p.op("pool", lambda e: e.affine_select(out=blkrow, in_=onesf[:, 0:16], pattern=[[-8, 16]], compare_op=ALU.is_ge, fill=0.0, base=0, channel_multiplier=1), reads=[BE, Bc], writes=[BE])
        p.op("pool", lambda e: e.affine_select(out=blkrow, in_=blkrow, pattern=[[8, 16]], compare_op=ALU.is_ge, fill=0.0, base=7, channel_multiplier=-1), reads=[BE], writes=[BE])
        p.dma("sp", lambda e, l=l: e.dma_start(out=par[:, 0:16], in_=dt_bias[l].partition_broadcast(128)), writes=[Bpar])
        p.dma("sp", lambda e, l=l: e.dma_start(out=par[:, 16:32], in_=a_log[l].partition_broadcast(128)), writes=[Bpar])
        p.dma("sp", lambda e, l=l: e.dma_start(out=par[:, 32:48], in_=d_skip[l].partition_broadcast(128)), writes=[Bpar])
        p.op("act", lambda e: e.activation(out=par[:, 16:32], in_=par[:, 16:32], func=AF.Exp), reads=[Bpar], writes=[Bpar])
        p.op("dve", lambda e: e.tensor_scalar(out=par[:, 16:32], in0=par[:, 16:32], scalar1=-1.0, scalar2=None, op0=ALU.mult), reads=[Bpar], writes=[Bpar])

        if stop == "C0":
            break
        for g in range(int(os.environ.get("NGC", 2))):
            p.barrier(); ar.reset(markC)
            Wc = ar.alloc([128, 8, 1280], BF16); BWc = Buf("Wc")
            Wdt = ar.alloc([128, 8, 8], BF16); BWdt = Buf("Wdt")
            snb = ar.alloc([128, 512], F32); Bsnb = Buf("snb")
            stT = ar.alloc([128, 512], F32); stTb = ar.alloc([128, 512], BF16); BstT = Buf("stT")
            xr = ar.alloc([128, 6 * 176], F32); Bxraw = Buf("xraw"); Bxraw_s = Bxraw
            xraw = xr[:, 0:786].rearrange("p (c t) -> p c t", c=6)
            xraw_s = xr.rearrange("p (c b t) -> p c b t", c=6, b=16)
            acc = ar.alloc([128, 6, 128], F32); Baccs = bufs("acc", 6)
            xc = acc[:, 0:4, :]
            R3 = ar.alloc([48, 768], F32); BR3 = Buf("R3")
            BCt = ar.alloc([128, 2, 128], BF16); BBCt = Buf("BCt")
            x_tm = ar.alloc([128, 512], F32); xdt = ar.alloc([128, 512], BF16); xw = ar.alloc([128, 512], BF16); Bxtm = Buf("xtm")
            Btm = ar.alloc([128, 128], BF16); BBtm = Buf("Btm")
            sm = ar.alloc([128, 8, 8], F32); Bsm = Buf("sm")
            dtab = ar.alloc([128, 8, 128], F32); negcm = ar.alloc([128, 8, 128], F32); Bdtab = Buf("dtab")
            seg0 = ar.alloc([128, 512], F32); seg = [seg0, seg0]; Bseg0 = Buf("seg"); Bseg = [Bseg0, Bseg0]
            MixT = [ar.alloc([128, 4, 128], BF16) for _ in range(2)]; BMix = bufs("Mix", 2)
            cbT = ar.alloc([128, 128], F32); BcbT = Buf("cbT")
            t1 = ar.alloc([128, 512], F32); Bt1 = Buf("t1")
            yc = ar.alloc([128, 512], F32); Byc = Buf("yc")
            zsc = ar.alloc([128, 512], F32); Bzsc = Buf("zsc")
            ycb = ar.alloc([128, 512], BF16); Bycb = Buf("ycb")
            ssq = ar.alloc([128, 4], F32); Bssq = Buf("ssq")
            CzT = ar.alloc([128, 16, 128], BF16); Bz = ar.alloc([128, 16, 128], BF16); BCz = Buf("CzT")
            elb = ar.alloc([128, 16, 8], F32); cumexp = ar.alloc([128, 16, 8], F32); Belb = Buf("elb")
            stin = [ar.alloc([128, 4, 128], F32) for _ in range(2)]; Bstin = bufs("stin", 2)
            sts = [ar.alloc([128, 512], F32) for _ in range(2)]; stsb = [ar.alloc([128, 512], BF16) for _ in range(2)]; Bsts = bufs("sts", 2)
            stout = stin; Bstout = Bstin

            chans = [g * 512 + i * 128 for i in range(4)] + [1024 + g * 128, 1280 + g * 128]
            load_w(Wc[:, :, 0:512], w_in[l][:, C_XBC + g * 512:C_XBC + (g + 1) * 512], BWc)
            load_w(Wc[:, :, 512:640], w_in[l][:, C_XBC + 1024 + g * 128:C_XBC + 1024 + (g + 1) * 128], BWc)
            load_w(Wc[:, :, 640:768], w_in[l][:, C_XBC + 1280 + g * 128:C_XBC + 1280 + (g + 1) * 128], BWc)
            load_w(Wc[:, :, 768:1280], w_in[l][:, C_ZC + g * 512:C_ZC + (g + 1) * 512], BWc)
            load_w(Wdt, w_in[l][:, C_DT + g * 8:C_DT + (g + 1) * 8], BWdt)
            p.dma("sp", lambda e, l=l, g=g: e.dma_start(out=snb, in_=ssm_norm[l, g * 512:(g + 1) * 512].partition_broadcast(128)), writes=[Bsnb])
            p.op("pool", lambda e: e.memset(stT, 0.0), writes=[BstT])
            p.op("dve", lambda e: e.memset(stTb, 0.0), reads=[BstT], writes=[BstT])
            p.op("pool", lambda e: e.memset(xraw[:, :, 0:3], 0.0), writes=[Bxraw])
            dtb = par[:, g * 8:(g + 1) * 8]; a_bc = par[:, 16 + g * 8:16 + (g + 1) * 8]; dsk = par[:, 32 + g * 8:32 + (g + 1) * 8]

            for j in range(int(os.environ.get("NTC", NT))):
                tok0 = j * 128
                smp = (j == 16)
                TRI = btrif if smp else trif
                LAST = L2s if smp else sel127
                NEGM = negm_s if smp else negm_p
                for cc in range(6):
                    pb_, c0 = (0, cc * 128) if cc < 4 else (1, (cc - 4) * 128)
                    proj_fm(Wc, BWc, cc * 128, 128, tok0, 128, PS[pb_][:, c0:c0 + 128], BPS[pb_])
                proj_tm(Wdt, BWdt, 0, 8, j, PS[1][:, 256:264], BPS[1])
                for hf in range(1):
                    proj_tm(Wc, BWc, 768, 512, j, PS[7], BPS[7])
                p.op("act", lambda e: e.activation(out=zsc, in_=PS[7], func=AF.Silu), reads=[BPS[7]], writes=[Bzsc])
                p.op("dve", lambda e, dtb=dtb: e.tensor_tensor(out=sm[:, 0, :], in0=PS[1][:, 256:264], in1=dtb, op=ALU.add), reads=[BPS[1], Bpar], writes=[Bsm])
                p.op("act", lambda e: e.activation(out=sm[:, 0, :], in_=sm[:, 0, :], func=AF.Exp), reads=[Bsm], writes=[Bsm])
                p.op("act", lambda e: e.activation(out=sm[:, 0, :], in_=sm[:, 0, :], func=AF.Ln, bias=1.0), reads=[Bsm], writes=[Bsm])
                p.op("dve", lambda e, a_bc=a_bc: e.tensor_tensor(out=sm[:, 1, :], in0=sm[:, 0, :], in1=a_bc, op=ALU.mult), reads=[Bsm, Bpar], writes=[Bsm])
                p.op("pe", lambda e, TRI=TRI: e.matmul(PS[3][:, 0:8], lhsT=TRI, rhs=sm[:, 1, :], start=True, stop=True), reads=[Bsm, Bc], writes=[BPS[3]])
                p.op("dve", lambda e: e.tensor_copy(out=sm[:, 2, :], in_=PS[3][:, 0:8]), reads=[BPS[3]], writes=[Bsm])
                p.op("pe", lambda e, LAST=LAST: e.matmul(PS[3][:, 8:16], lhsT=LAST, rhs=sm[:, 2, :], start=True, stop=True), reads=[Bsm, Bc], writes=[BPS[3]])
                p.op("act", lambda e: e.activation(out=sm[:, 3, :], in_=sm[:, 2, :], func=AF.Exp), reads=[Bsm], writes=[Bsm])
                p.op("dve", lambda e: e.tensor_tensor(out=sm[:, 4, :], in0=PS[3][:, 8:16], in1=sm[:, 2, :], op=ALU.subtract), reads=[BPS[3], Bsm], writes=[Bsm])
                p.op("act", lambda e: e.activation(out=sm[:, 4, :], in_=sm[:, 4, :], func=AF.Exp), reads=[Bsm], writes=[Bsm])
                p.op("dve", lambda e: e.tensor_copy(out=sm[:, 6, :], in_=PS[3][:, 8:16]), reads=[BPS[3]], writes=[Bsm])
                p.op("act", lambda e: e.activation(out=sm[:, 5, :], in_=sm[:, 6, :], func=AF.Exp), reads=[Bsm], writes=[Bsm])
                if not smp:
                    p.op("act", lambda e: e.copy(out=xraw[:, 0:4, 3:131], in_=PS[0].rearrange("p (c t) -> p c t", c=4)), reads=[BPS[0]], writes=[Bxraw])
                    p.op("act", lambda e: e.copy(out=xraw[:, 4:6, 3:131], in_=PS[1][:, 0:256].rearrange("p (c t) -> p c t", c=2)), reads=[BPS[1]], writes=[Bxraw])
                    src = lambda cc, tap: xraw[:, cc, tap:tap + 128]
                    accv = lambda cc: acc[:, cc, :]
                    Bsrc = Bxraw
                else:
                    for cc in range(6):
                        p.dma("sp", lambda e, l=l, cc=cc, c0=chans[cc]: e.dma_start(out=R3[:, cc * 128:(cc + 1) * 128], in_=sconv[l][:, :, c0:c0 + 128].rearrange("b j c -> (b j) c")), writes=[BR3])
                    for cc in range(6):
                        p.op("pe", lambda e, cc=cc: e.transpose(PS[3][:, cc * 48:(cc + 1) * 48], R3[:, cc * 128:(cc + 1) * 128], identf[0:48, 0:48]), reads=[BR3, Bc], writes=[BPS[3]])
                    p.op("act", lambda e: e.copy(out=xraw_s[:, :, :, 0:3], in_=PS[3][:, 0:288].rearrange("p (c b j) -> p c b j", c=6, b=16)), reads=[BPS[3]], writes=[Bxraw_s])
                    p.op("act", lambda e: e.copy(out=xraw_s[:, 0:4, :, 3:11], in_=PS[0].rearrange("p (c b t) -> p c b t", c=4, b=16)), reads=[BPS[0]], writes=[Bxraw_s])
                    p.op("act", lambda e: e.copy(out=xraw_s[:, 4:6, :, 3:11], in_=PS[1][:, 0:256].rearrange("p (c b t) -> p c b t", c=2, b=16)), reads=[BPS[1]], writes=[Bxraw_s])
                    src = lambda cc, tap: xraw_s[:, cc, :, tap:tap + 8]
                    accv = lambda cc: acc[:, cc, :].rearrange("p (b t) -> p b t", b=16)
                    Bsrc = Bxraw_s
                for cc in range(6):
                    ch = chans[cc] // 128
                    p.op("dve", lambda e, cc=cc, ch=ch, src=src, accv=accv: e.tensor_scalar(out=accv(cc), in0=src(cc, 3), scalar1=cwb[:, ch, 3:4], scalar2=cwb[:, ch, 4:5], op0=ALU.mult, op1=ALU.add),
                         reads=[Bsrc, Bcw], writes=[Baccs[cc]])
                for tap in range(3):
                    for cc in range(6):
                        ch = chans[cc] // 128
                        p.op("dve", lambda e, cc=cc, ch=ch, tap=tap, src=src, accv=accv: e.scalar_tensor_tensor(out=accv(cc), in0=src(cc, tap), scalar=cwb[:, ch, tap:tap + 1], in1=accv(cc), op0=ALU.mult, op1=ALU.add),
                             reads=[Bsrc, Bcw, Baccs[cc]], writes=[Baccs[cc]])
                p.op("act", lambda e: e.activation(out=xc, in_=acc[:, 0:4, :], func=AF.Silu), reads=Baccs[0:4], writes=Baccs[0:4])
                p.op("act", lambda e: e.activation(out=BCt, in_=acc[:, 4:6, :], func=AF.Silu), reads=Baccs[4:6], writes=[BBCt])
                if not smp:
                    if j < 15:
                        p.op("pool", lambda e: e.tensor_copy(out=xraw[:, :, 0:3], in_=xraw[:, :, 128:131]), reads=[Bxraw], writes=[Bxraw])
                    else:
                        for cc in range(6):
                            p.op("pe", lambda e, cc=cc: e.transpose(PS[3][0:3, cc * 128:(cc + 1) * 128] if cc < 4 else PS[2][0:3, (cc - 4) * 128:(cc - 3) * 128], xraw[:, cc, 128:131], identf), reads=[Bxraw, Bc], writes=[BPS[3] if cc < 4 else BPS[2]])
                        p.op("dve", lambda e: e.tensor_copy(out=R3[0:3, 0:512], in_=PS[3][0:3, 0:512]), reads=[BPS[3]], writes=[BR3])
                        p.op("dve", lambda e: e.tensor_copy(out=R3[0:3, 512:768], in_=PS[2][0:3, 0:256]), reads=[BPS[2]], writes=[BR3])
                        for cc in range(6):
                            p.dma("sp", lambda e, l=l, cc=cc, c0=chans[cc]: e.dma_start(out=convp[l][:, c0:c0 + 128], in_=R3[0:3, cc * 128:(cc + 1) * 128]), reads=[BR3])
                else:
                    p.op("pool", lambda e: e.tensor_copy(out=t1[:, 0:288].rearrange("p (c b t) -> p c b t", c=6, b=16), in_=xraw_s[:, :, :, 8:11]), reads=[Bxraw_s], writes=[Bt1])
                    for cc in range(6):
                        p.op("pe", lambda e, cc=cc: e.transpose(PS[3][0:48, (cc % 4) * 128:(cc % 4 + 1) * 128] if cc < 4 else PS[2][0:48, (cc - 4) * 128:(cc - 3) * 128],
                                                                 t1[:, cc * 48:(cc + 1) * 48], identf), reads=[Bt1, Bc], writes=[BPS[3] if cc < 4 else BPS[2]])
                    p.op("dve", lambda e: e.tensor_copy(out=R3[:, 0:512], in_=PS[3][0:48, 0:512]), reads=[BPS[3]], writes=[BR3])
                    p.op("dve", lambda e: e.tensor_copy(out=R3[:, 512:768], in_=PS[2][0:48, 0:256]), reads=[BPS[2]], writes=[BR3])
                    for cc in range(6):
                        p.dma("sp", lambda e, l=l, cc=cc, c0=chans[cc]: e.dma_start(out=convs[l][:, :, c0:c0 + 128].rearrange("b t c -> (b t) c"), in_=R3[:, cc * 128:(cc + 1) * 128]), reads=[BR3])
                for cc in range(4):
                    p.op("pe", lambda e, cc=cc: e.transpose(PS[2][:, cc * 128:(cc + 1) * 128], xc[:, cc, :], identf), reads=[Baccs[cc], Bc], writes=[BPS[2]])
                p.op("act", lambda e: e.copy(out=x_tm, in_=PS[2]), reads=[BPS[2]], writes=[Bxtm])
                p.op("dve", lambda e: e.tensor_tensor(out=xdt.rearrange("p (h d) -> p h d", h=8), in0=x_tm.rearrange("p (h d) -> p h d", h=8), in1=sm[:, 0, :].unsqueeze(2).to_broadcast([128, 8, 64]), op=ALU.mult),
                     reads=[Bxtm, Bsm], writes=[Bxtm])
                p.op("dve", lambda e: e.tensor_tensor(out=xw.rearrange("p (h d) -> p h d", h=8), in0=xdt.rearrange("p (h d) -> p h d", h=8), in1=sm[:, 4, :].unsqueeze(2).to_broadcast([128, 8, 64]), op=ALU.mult),
                     reads=[Bxtm, Bsm], writes=[Bxtm])
                pst3 = PS[3].bitcast(BF16)
                p.op("pe", lambda e, pst3=pst3: e.transpose(pst3[:, 512:640], BCt[:, 0, :], identb), reads=[BBCt, Bc], writes=[BPS[3]])
                p.op("dve", lambda e, pst3=pst3: e.tensor_copy(out=Btm, in_=pst3[:, 512:640]), reads=[BPS[3]], writes=[BBtm])
                p.op("pe", lambda e: e.matmul(PS[3][:, 128:256], lhsT=BCt[:, 0, :], rhs=BCt[:, 1, :], start=True, stop=True), reads=[BBCt], writes=[BPS[3]])
                p.op("dve", lambda e: e.tensor_copy(out=cbT, in_=PS[3][:, 128:256]), reads=[BPS[3]], writes=[BcbT])
                p.op("pool", lambda e: e.tensor_copy(out=dtab, in_=sm[:, 1, :].unsqueeze(2).to_broadcast([128, 8, 128])), reads=[Bsm], writes=[Bdtab])
                p.op("pool", lambda e, NEGM=NEGM: e.tensor_tensor(out=negcm, in0=NEGM.unsqueeze(1).to_broadcast([128, 8, 128]), in1=sm[:, 2, :].unsqueeze(2).to_broadcast([128, 8, 128]), op=ALU.subtract),
                     reads=[Bsm, Bc, Bdtab], writes=[Bdtab])
                for hh in range(2):
                    for i in range(4):
                        h = hh * 4 + i
                        p.op("pe", lambda e, i=i, h=h, TRI=TRI: e.matmul(PS[4][:, i * 128:(i + 1) * 128], lhsT=dtab[:, h, :], rhs=TRI, start=True, stop=True), reads=[Bdtab, Bc], writes=[BPS[4]])
                    p.op("dve", lambda e, hh=hh: e.tensor_tensor(out=seg[hh], in0=PS[4], in1=negcm[:, hh * 4:(hh + 1) * 4, :].rearrange("p h t -> p (h t)"), op=ALU.add),
                         reads=[BPS[4], Bdtab], writes=[Bseg[hh]])
                    p.op("act", lambda e, hh=hh: e.activation(out=seg[hh], in_=seg[hh], func=AF.Exp), reads=[Bseg[hh]], writes=[Bseg[hh]])
                    p.op("dve", lambda e, hh=hh: e.tensor_tensor(out=MixT[hh], in0=seg[hh].rearrange("p (h t) -> p h t", h=4), in1=cbT.unsqueeze(1).to_broadcast([128, 4, 128]), op=ALU.mult),
                         reads=[Bseg[hh], BcbT], writes=[BMix[hh]])
                    for i in range(4):
                        h = hh * 4 + i
                        p.op("pe", lambda e, i=i, h=h, hh=hh: e.matmul(PS[5][:, h * 64:(h + 1) * 64], lhsT=MixT[hh][:, i, :], rhs=xdt[:, h * 64:(h + 1) * 64], start=True, stop=True),
                             reads=[BMix[hh], Bxtm], writes=[BPS[5]])
                if not smp:
                    p.op("pe", lambda e: e.matmul(PS[6], lhsT=BCt[:, 1, :], rhs=stTb, start=True, stop=True), reads=[BBCt, BstT], writes=[BPS[6]])
                    p.op("pe", lambda e: e.matmul(PS[7], lhsT=Btm, rhs=xw, start=True, stop=True), reads=[BBtm, Bxtm], writes=[BPS[7]])
                    p.op("dve", lambda e: e.tensor_tensor(out=stT.rearrange("p (h d) -> p h d", h=8), in0=stT.rearrange("p (h d) -> p h d", h=8), in1=sm[:, 5, :].unsqueeze(2).to_broadcast([128, 8, 64]), op=ALU.mult),
                         reads=[BstT, Bsm, BPS[6]], writes=[BstT])
                    p.op("dve", lambda e: e.tensor_tensor(out=stT, in0=stT, in1=PS[7], op=ALU.add), reads=[BstT, BPS[7]], writes=[BstT])
                    p.op("act", lambda e: e.copy(out=stTb, in_=stT), reads=[BstT], writes=[BstT])
                    if j == 15:
                        for cc in range(4):
                            p.op("pe", lambda e, cc=cc: e.transpose(PS[7][:, cc * 128:(cc + 1) * 128], stT[:, cc * 128:(cc + 1) * 128], identf), reads=[BstT, Bc], writes=[BPS[7]])
                        p.op("act", lambda e: e.copy(out=stout[0], in_=PS[7].rearrange("p (c n) -> p c n", c=4)), reads=[BPS[7]], writes=[Bstout[0]])
                        p.dma("sp", lambda e, l=l, g=g: e.dma_start(out=ssmp[l, g * 512:(g + 1) * 512, :].rearrange("(c p) n -> p c n", p=128), in_=stout[0]), reads=[Bstout[0]])
                else:
                    p.op("dve", lambda e: e.tensor_tensor(out=CzT, in0=BCt[:, 1, :].unsqueeze(1).to_broadcast([128, 16, 128]), in1=Ebc, op=ALU.mult), reads=[BBCt, BE], writes=[BCz])
                    p.op("dve", lambda e: e.tensor_tensor(out=Bz, in0=Btm.unsqueeze(1).to_broadcast([128, 16, 128]), in1=blkrow.unsqueeze(2).to_broadcast([128, 16, 128]), op=ALU.mult), reads=[BBtm, BE], writes=[BCz])
                    p.op("dve", lambda e: e.tensor_tensor(out=cumexp, in0=sm[:, 2, :].unsqueeze(1).to_broadcast([128, 16, 8]), in1=lastmask.unsqueeze(2).to_broadcast([128, 16, 8]), op=ALU.mult), reads=[Bsm, Bc], writes=[Belb])
                    p.op("pe", lambda e: e.matmul(PS[3][:, 256:384], lhsT=onesf, rhs=cumexp.rearrange("p b h -> p (b h)"), start=True, stop=True), reads=[Belb, Bc], writes=[BPS[3]])
                    p.op("act", lambda e: e.activation(out=elb.rearrange("p b h -> p (b h)"), in_=PS[3][:, 256:384], func=AF.Exp), reads=[BPS[3]], writes=[Belb])
                    for b in range(NSEQ):
                        m = b % 2
                        p.dma("sp", lambda e, l=l, g=g, b=b, m=m: e.dma_start(out=stin[m], in_=sssm[l, b, g * 512:(g + 1) * 512, :].rearrange("(c p) n -> p c n", p=128)), writes=[Bstin[m]])
                        for cc in range(4):
                            p.op("pe", lambda e, cc=cc, m=m: e.transpose(PS[7][:, cc * 128:(cc + 1) * 128], stin[m][:, cc, :], identf), reads=[Bstin[m], Bc], writes=[BPS[7]])
                        p.op("act", lambda e, m=m: e.copy(out=sts[m], in_=PS[7]), reads=[BPS[7]], writes=[Bsts[m]])
                        p.op("dve", lambda e, m=m: e.tensor_copy(out=stsb[m], in_=sts[m]), reads=[Bsts[m]], writes=[Bsts[m]])
                        p.op("pe", lambda e, b=b, m=m: e.matmul(PS[6], lhsT=CzT[:, b, :], rhs=stsb[m], start=(b == 0), stop=(b == NSEQ - 1)), reads=[BCz, Bsts[m]], writes=[BPS[6]])
                        p.op("pe", lambda e, b=b: e.matmul(PS[0], lhsT=Bz[:, b, :], rhs=xw, start=True, stop=True), reads=[BCz, Bxtm], writes=[BPS[0]])
                        p.op("dve", lambda e, b=b, m=m: e.tensor_tensor(out=sts[m].rearrange("p (h d) -> p h d", h=8), in0=sts[m].rearrange("p (h d) -> p h d", h=8), in1=elb[:, b, :].unsqueeze(2).to_broadcast([128, 8, 64]), op=ALU.mult),
                             reads=[Bsts[m], Belb], writes=[Bsts[m]])
                        p.op("dve", lambda e, m=m: e.tensor_tensor(out=sts[m], in0=sts[m], in1=PS[0], op=ALU.add), reads=[Bsts[m], BPS[0]], writes=[Bsts[m]])
                        for cc in range(4):
                            p.op("pe", lambda e, cc=cc, m=m: e.transpose(PS[1][:, cc * 128:(cc + 1) * 128], sts[m][:, cc * 128:(cc + 1) * 128], identf), reads=[Bsts[m], Bc], writes=[BPS[1]])
                        p.op("act", lambda e, m=m: e.copy(out=stout[m], in_=PS[1].rearrange("p (c n) -> p c n", c=4)), reads=[BPS[1]], writes=[Bstout[m]])
                        p.dma("sp", lambda e, l=l, g=g, b=b, m=m: e.dma_start(out=ssms[l, b, g * 512:(g + 1) * 512, :].rearrange("(c p) n -> p c n", p=128), in_=stout[m]), reads=[Bstout[m]])
                p.op("dve", lambda e: e.tensor_tensor(out=t1.rearrange("p (h d) -> p h d", h=8), in0=PS[6].rearrange("p (h d) -> p h d", h=8), in1=sm[:, 3, :].unsqueeze(2).to_broadcast([128, 8, 64]), op=ALU.mult),
                     reads=[BPS[6], Bsm], writes=[Bt1])
                p.op("dve", lambda e: e.tensor_tensor(out=yc, in0=PS[5], in1=t1, op=ALU.add), reads=[BPS[5], Bt1], writes=[Byc])
                p.op("pool", lambda e, dsk=dsk: e.tensor_tensor(out=t1.rearrange("p (h d) -> p h d", h=8), in0=x_tm.rearrange("p (h d) -> p h d", h=8), in1=dsk.unsqueeze(2).to_broadcast([128, 8, 64]), op=ALU.mult),
                     reads=[Bxtm, Bpar, Byc], writes=[Bt1])
                p.op("pool", lambda e: e.tensor_tensor(out=yc, in0=yc, in1=t1, op=ALU.add), reads=[Bt1, Byc], writes=[Byc])
                p.op("pool", lambda e: e.tensor_tensor(out=yc, in0=yc, in1=zsc, op=ALU.mult), reads=[Bzsc, Byc], writes=[Byc])
                p.op("pool", lambda e: e.memset(ssq, 0.0), writes=[Bssq])
                p.op("act", lambda e: e.activation(out=zsc, in_=yc, func=AF.Square, accum_out=ssq[:, 0:1]), reads=[Byc, Bssq], writes=[Bzsc, Bssq])
                p.op("act", lambda e: e.activation(out=ssq[:, 1:2], in_=ssq[:, 0:1], func=AF.Ln, bias=EPS, scale=1.0 / 512), reads=[Bssq], writes=[Bssq])
                p.op("act", lambda e: e.activation(out=ssq[:, 2:3], in_=ssq[:, 1:2], func=AF.Exp, scale=-0.5), reads=[Bssq], writes=[Bssq])
                p.op("dve", lambda e: e.scalar_tensor_tensor(out=ycb, in0=yc, scalar=ssq[:, 2:3], in1=snb, op0=ALU.mult, op1=ALU.mult), reads=[Byc, Bssq, Bsnb], writes=[Bycb])
                pst7 = PS[7].bitcast(BF16)
                for cc in range(4):
                    p.op("pe", lambda e, cc=cc, pst7=pst7: e.transpose(pst7[:, cc * 128:(cc + 1) * 128], ycb[:, cc * 128:(cc + 1) * 128], identb), reads=[Bycb, Bc], writes=[BPS[7]])
                p.op("dve", lambda e, pst7=pst7, g=g, tok0=tok0: e.tensor_copy(out=yT[:, 8 + g * 4:12 + g * 4, tok0:tok0 + 128], in_=pst7[:, 0:512].rearrange("p (c t) -> p c t", c=4)),
                     reads=[BPS[7]], writes=[ByT[2][j]])
        if stop == "C":
            break

        p.barrier(); ar.reset()
        mT = ar.alloc([128, 8, NTOK], BF16); BmT = bufs("mT", NT)
        markM = ar.mark()
        wM = [ar.alloc([128, 40, 128], BF16) for _ in range(2)]; BwM = bufs("wM", 2)
        sg = [ar.alloc([128, 512], F32) for _ in range(2)]; Bsg = bufs("sg", 2)
        accm = [ar.alloc([128, 512], F32) for _ in range(2)]; Baccm = bufs("accm", 2)
        tmpm = [ar.alloc([128, 512], F32) for _ in range(2)]; Btmpm = bufs("tmpm", 2)
        KOFF = (0, 4, 8); KCN = (4, 4, 8)
        WSRC = (w_a, w_b, w_c)
        it = 0
        for nj in range(8):
            k = nj % 2
            for br in range(3):
                load_w(wM[k][:, KOFF[br]:KOFF[br] + KCN[br], :], WSRC[br][l][:, nj * 128:(nj + 1) * 128], BwM[k])
                load_w(wM[k][:, 16 + 8 * br:24 + 8 * br, :], w_in[l][:, C_G + br * 1024 + nj * 128:C_G + br * 1024 + (nj + 1) * 128], BwM[k])
            for tg in range(5):
                tok0 = tg * 512
                ntok = 512 if tg < 4 else 128
                a2 = it % 2
                for br in range(3):
                    pp_, pg_ = it % 3, 3 + it % 3
                    s2 = it % 2
                    for kc in range(KCN[br]):
                        p.op("pe", lambda e, k=k, br=br, kc=kc, pp_=pp_, tok0=tok0, ntok=ntok: e.matmul(PS[pp_][:, 0:ntok], lhsT=wM[k][:, KOFF[br] + kc, :], rhs=yT[:, YOFF[br] + kc, tok0:tok0 + ntok],
                                                                                                       start=(kc == 0), stop=(kc == KCN[br] - 1)),
                             reads=[BwM[k]] + tile_bufs(ByT[br], tok0, ntok), writes=[BPS[pp_]])
                    for kc in range(8):
                        p.op("pe", lambda e, k=k, br=br, kc=kc, pg_=pg_, tok0=tok0, ntok=ntok: e.matmul(PS[pg_][:, 0:ntok], lhsT=wM[k][:, 16 + 8 * br + kc, :], rhs=hnT[:, kc, tok0:tok0 + ntok],
                                                                                                       start=(kc == 0), stop=(kc == 7)),
                             reads=[BwM[k]] + tile_bufs(BhnT, tok0, ntok), writes=[BPS[pg_]])
                    p.op("act", lambda e, pg_=pg_, s2=s2, ntok=ntok: e.activation(out=sg[s2][:, 0:ntok], in_=PS[pg_][:, 0:ntok], func=AF.Sigmoid), reads=[BPS[pg_]], writes=[Bsg[s2]])
                    if br == 0:
                        p.op("dve", lambda e, pp_=pp_, s2=s2, a2=a2, ntok=ntok: e.tensor_tensor(out=accm[a2][:, 0:ntok], in0=PS[pp_][:, 0:ntok], in1=sg[s2][:, 0:ntok], op=ALU.mult),
                             reads=[BPS[pp_], Bsg[s2]], writes=[Baccm[a2]])
                    else:
                        p.op("dve", lambda e, pp_=pp_, s2=s2, a2=a2, ntok=ntok: e.tensor_tensor(out=tmpm[a2][:, 0:ntok], in0=PS[pp_][:, 0:ntok], in1=sg[s2][:, 0:ntok], op=ALU.mult),
                             reads=[BPS[pp_], Bsg[s2]], writes=[Btmpm[a2]])
                        if br == 1:
                            p.op("pool", lambda e, a2=a2, ntok=ntok: e.tensor_tensor(out=accm[a2][:, 0:ntok], in0=accm[a2][:, 0:ntok], in1=tmpm[a2][:, 0:ntok], op=ALU.add),
                                 reads=[Btmpm[a2], Baccm[a2]], writes=[Baccm[a2]])
                        else:
                            p.op("pool", lambda e, a2=a2, nj=nj, tok0=tok0, ntok=ntok: e.tensor_tensor(out=mT[:, nj, tok0:tok0 + ntok], in0=accm[a2][:, 0:ntok], in1=tmpm[a2][:, 0:ntok], op=ALU.add),
                                 reads=[Btmpm[a2], Baccm[a2]], writes=tile_bufs(BmT, tok0, ntok))
                    it += 1
        p.barrier(); ar.reset(markM)
        Wo = ar.alloc([128, 8, 1024], BF16); BWo = Buf("Wo")
        npost = ar.alloc([128, D], F32); Bnpost = Buf("npost")
        xt2 = [ar.alloc([128, D], F32) for _ in range(2)]; Bxt2 = bufs("xt2", 2)
        rr = [ar.alloc([128, D], F32) for _ in range(2)]; Brr = bufs("rr", 2)
        hb2 = [ar.alloc([128, D], BF16) for _ in range(2)]; Bhb2 = bufs("hb2", 2)
        sq2 = [ar.alloc([128, 4], F32) for _ in range(2)]; Bsq2 = bufs("sq2", 2)
        ss2 = [ar.alloc([128, 8], F32) for _ in range(2)]; Bss2 = bufs("ss2", 2)
        junk = ar.alloc([128, 512], F32); Bjunk = Buf("junk")
        load_w(Wo[:, :, 0:512], w_out[l][:, 0:512], BWo)
        load_w(Wo[:, :, 512:1024], w_out[l][:, 512:1024], BWo)
        p.dma("sp", lambda e, l=l: e.dma_start(out=npost, in_=norm_post[l].partition_broadcast(128)), writes=[Bnpost])
        if l == 0:
            p.dma("sp", lambda e: e.dma_start(out=npre_bc, in_=norm_pre[1].partition_broadcast(128)), writes=[Bnpre])
        for j in range(NT):
            k = j % 2
            if l == 0:
                srcx = xp[j * 128:(j + 1) * 128, :] if j < 16 else xs
                p.dma("sp", lambda e, k=k, srcx=srcx: e.dma_start(out=xt2[k], in_=srcx), writes=[Bxt2[k]])
            else:
                p.dma("sp", lambda e, k=k, j=j: e.dma_start(out=xt2[k], in_=x1[j * 128:(j + 1) * 128, :]), reads=[Bx1[j]], writes=[Bxt2[k]])
            for hf in range(2):
                for kc in range(8):
                    p.op("pe", lambda e, j=j, hf=hf, kc=kc: e.matmul(PS[6 + hf], lhsT=mT[:, kc, j * 128:(j + 1) * 128], rhs=Wo[:, kc, hf * 512:(hf + 1) * 512], start=(kc == 0), stop=(kc == 7)),
                         reads=[BmT[j], BWo], writes=[BPS[6 + hf]])
            p.op("pool", lambda e, k=k: e.memset(ss2[k], 0.0), writes=[Bss2[k]])
            for hf in range(2):
                p.op("act", lambda e, k=k, hf=hf: e.activation(out=junk, in_=PS[6 + hf], func=AF.Square, accum_out=ss2[k][:, hf:hf + 1]), reads=[BPS[6 + hf], Bss2[k]], writes=[Bjunk, Bss2[k]])
            p.op("dve", lambda e, k=k: e.tensor_tensor(out=ss2[k][:, 2:3], in0=ss2[k][:, 0:1], in1=ss2[k][:, 1:2], op=ALU.add), reads=[Bss2[k]], writes=[Bss2[k]])
            p.op("act", lambda e, k=k: e.activation(out=ss2[k][:, 3:4], in_=ss2[k][:, 2:3], func=AF.Ln, bias=EPS, scale=1.0 / D), reads=[Bss2[k]], writes=[Bss2[k]])
            p.op("act", lambda e, k=k: e.activation(out=ss2[k][:, 4:5], in_=ss2[k][:, 3:4], func=AF.Exp, scale=-0.5), reads=[Bss2[k]], writes=[Bss2[k]])
            for hf in range(2):
                p.op("dve", lambda e, k=k, hf=hf: e.scalar_tensor_tensor(out=rr[k][:, hf * 512:(hf + 1) * 512], in0=PS[6 + hf], scalar=ss2[k][:, 4:5], in1=npost[:, hf * 512:(hf + 1) * 512], op0=ALU.mult, op1=ALU.mult),
                     reads=[BPS[6 + hf], Bss2[k], Bnpost], writes=[Brr[k]])
            p.op("pool", lambda e, k=k: e.tensor_tensor(out=rr[k], in0=rr[k], in1=xt2[k], op=ALU.add), reads=[Bxt2[k], Brr[k]], writes=[Brr[k]])
            if l == 0:
                p.dma("sp", lambda e, k=k, j=j: e.dma_start(out=x1[j * 128:(j + 1) * 128, :], in_=rr[k]), reads=[Brr[k]], writes=[Bx1[j]], owner=Brr[k])
                norm_to_hnT(rr[k], Brr[k], j, sq2[k], Bsq2[k], hb2[k], Bhb2[k], 4 + k)
            else:
                dst = yp[j * 128:(j + 1) * 128, :] if j < 16 else ys
                p.dma("sp", lambda e, k=k, dst=dst: e.dma_start(out=dst, in_=rr[k]), reads=[Brr[k]])

    dbg_dump("hnT", hnT[:, :, :], BhnT)
    dbg_dump("yT", yT[:, :, :], ByT[0] + ByT[1] + ByT[2])
    p.emit()
    return nc


def make_in_maps(inputs, n_cores=8):
    g = {k: np.asarray(v) for k, v in inputs.items()}
    n_pool = g["cache_k"].shape[1]
    ckf = np.ascontiguousarray(g["cache_k"]).reshape(DEPTH, n_pool * 128, 512)
    cvf = np.ascontiguousarray(g["cache_v"]).reshape(DEPTH, n_pool * 128, 512)
    clff = np.ascontiguousarray(g["cache_logf"]).reshape(DEPTH, n_pool, 1024)
    maps = []
    for c in range(n_cores):
        sl = slice(NSEQ * c, NSEQ * (c + 1))
        m = dict(
            xp=np.ascontiguousarray(g["x_prompt"][c]), xs=np.ascontiguousarray(g["x_sample"][sl]).reshape(128, D),
            ck=ckf, cv=cvf, clf=clff,
            spool=np.ascontiguousarray(g["state_pool"][:, sl]), sconv=np.ascontiguousarray(g["state_conv"][:, sl]),
            sssm=np.ascontiguousarray(g["state_ssm"][:, sl]).reshape(DEPTH, NSEQ, 1024, 128),
            pt=np.ascontiguousarray(g["page_table"][sl]).astype(np.int32),
            norm_pre=g["norm_pre"], w_in=g["w_in"], pool_w=g["pool_w"], pool_scale=g["pool_scale"], f_bias=g["f_bias"],
            conv_w=g["conv_w"], conv_b=g["conv_b"], dt_bias=g["dt_bias"], a_log=g["a_log"], d_skip=g["d_skip"],
            ssm_norm=g["ssm_norm"], w_a=g["w_branch_a"], w_b=g["w_branch_b"], w_c=g["w_branch_c"], w_out=g["w_out"],
            norm_post=g["norm_post"])
        maps.append(m)
    return maps, n_pool


def assemble(res, n_cores=8):
    R = res
    cat = lambda k: np.stack([R[c][k] for c in range(n_cores)])
    y_prompt = cat("yp")
    y_sample = cat("ys").reshape(n_cores * NSEQ, 8, D)
    def pl(k, shp):
        return np.stack([R[c][k] for c in range(n_cores)], axis=1).reshape(shp)
    nb = n_cores
    k_p = pl("kp", (DEPTH, nb, TP, 8, 64)); v_p = pl("vp", (DEPTH, nb, TP, 8, 64)); lf_p = pl("lfp", (DEPTH, nb, TP, 8))
    pool_p = pl("poolp", (DEPTH, nb, 15, 512)); conv_p = pl("convp", (DEPTH, nb, 3, 1536)); ssm_p = pl("ssmp", (DEPTH, nb, 16, 64, 128))
    k_s = pl("kso", (DEPTH, nb * NSEQ, 8, 8, 64)); v_s = pl("vso", (DEPTH, nb * NSEQ, 8, 8, 64)); lf_s = pl("lfs", (DEPTH, nb * NSEQ, 8, 8))
    pool_s = pl("pools", (DEPTH, nb * NSEQ, 15, 512)); conv_s = pl("convs", (DEPTH, nb * NSEQ, 3, 1536))
    ssm_s = pl("ssms", (DEPTH, nb * NSEQ, 16, 64, 128))
    return (y_prompt, y_sample, k_p, v_p, lf_p, pool_p, conv_p, ssm_p, k_s, v_s, lf_s, pool_s, conv_s, ssm_s)


def kernel(**inputs):
    maps, n_pool = make_in_maps(inputs)
    nc = build(n_pool=n_pool)
    res = run_bass_kernel_spmd(nc, maps, core_ids=list(range(8)))
    return tuple(np.ascontiguousarray(a, dtype=np.float32) for a in assemble(res.results))
```

```python
import numpy as np
import concourse.bass as bass
import concourse.mybir as mybir
from concourse.bass_utils import run_bass_kernel_spmd
from contextlib import ExitStack

F32 = mybir.dt.float32
BF16 = mybir.dt.bfloat16
I32 = mybir.dt.int32
ALU = mybir.AluOpType
AF = mybir.ActivationFunctionType

DEPTH = 2
D = 1024
DIN = 8728
TP = 2048
NTOK = 2176
NT = 17
NSEQ = 16
NPG = 16
C_UA, C_ZA, C_Q, C_K, C_V, C_F, C_ZB, C_ZC, C_XBC, C_DT, C_G = (
    0, 512, 1024, 1536, 2048, 2560, 2568, 3080, 4104, 5640, 5656)
EPS = 1e-6
NEG = -30000.0


class Buf:
    __slots__ = ("name", "last_w", "reads", "dsem", "dcnt", "excl")

    def __init__(self, name, excl=False):
        self.name = name
        self.excl = excl
        self.last_w = None
        self.reads = {}
        self.dsem = None
        self.dcnt = 0


def bufs(name, n):
    return [Buf("%s%d" % (name, i)) for i in range(n)]


class Prog:
    ENGS = ("pe", "act", "dve", "pool", "sp")

    def __init__(self, nc, strict=True):
        self.nc = nc
        self.ops = {e: [] for e in self.ENGS}
        self.cnt = {e: 0 for e in self.ENGS}
        self.seen = {e: {} for e in self.ENGS}
        self.strict = strict
        self.ndsem = 0
        self.dsem_final = {}
        self.qhist = {e: [] for e in self.ENGS}
        self.dreg = {}
        self.st = ExitStack()

    def sb(self, name, shape, dtype):
        return self.st.enter_context(self.nc.sbuf_tensor(name, list(shape), dtype))

    def ps(self, name, shape, dtype):
        return self.st.enter_context(self.nc.psum_tensor(name, list(shape), dtype))

    def _deps(self, eng, reads, writes):
        need = {}
        for b in reads:
            t = b.last_w
            if t is not None and need.get(t[0], 0) < t[1]:
                need[t[0]] = t[1]
            if b.excl:
                for k, v in b.reads.items():
                    if k != eng and need.get(k, 0) < v:
                        need[k] = v
        for b in writes:
            t = b.last_w
            if t is not None and need.get(t[0], 0) < t[1]:
                need[t[0]] = t[1]
            for k, v in b.reads.items():
                if need.get(k, 0) < v:
                    need[k] = v
        waits = []
        seen = self.seen[eng]
        for k, v in need.items():
            if k == eng and (eng == "pe" or not self.strict):
                continue
            if seen.get(k, 0) >= v:
                continue
            seen[k] = v
            waits.append((k, v))
        return waits

    def _mark(self, tok, reads, writes):
        for b in reads:
            if b.reads.get(tok[0], 0) < tok[1]:
                b.reads[tok[0]] = tok[1]
        for b in writes:
            b.last_w = tok
            b.reads = {}

    def op(self, eng, fn, reads=(), writes=()):
        waits = self._deps(eng, reads, writes)
        self.cnt[eng] += 1
        tok = (eng, self.cnt[eng])
        self.ops[eng].append((waits, fn, (eng, 1)))
        self._mark(tok, reads, writes)

    def dma(self, q, fn, reads=(), writes=(), owner=None):
        if owner is None:
            owner = writes[0] if writes else reads[0]
        reg = self.dreg.get(owner.name)
        if reg is None:
            reg = ["d%d" % self.ndsem, 0]
            self.ndsem += 1
            self.dreg[owner.name] = reg
        owner.dsem = reg[0]
        waits = self._deps(q, reads, writes)
        qh = self.qhist[q]
        if len(qh) >= 24:
            k0, v0 = qh[-24]
            if self.seen[q].get(k0, 0) < v0:
                self.seen[q][k0] = v0
                waits.append((k0, v0))
        reg[1] += 1
        owner.dcnt = reg[1]
        tok = (owner.dsem, 16 * owner.dcnt)
        self.dsem_final[owner.dsem] = 16 * owner.dcnt
        self.ops[q].append((waits, fn, (owner.dsem, 16)))
        qh.append(tok)
        self._mark(tok, reads, writes)

    def barrier(self):
        state = dict(self.cnt)
        for e in self.ENGS:
            waits = []
            for k, v in list(state.items()) + list(self.dsem_final.items()):
                if k == e or v == 0:
                    continue
                if self.seen[e].get(k, 0) >= v:
                    continue
                self.seen[e][k] = v
                waits.append((k, v))
            if waits:
                self.ops[e].append((waits, None, None))

    def emit(self):
        nc = self.nc
        with self.st as st:
            sems = {}
            for e in self.ENGS:
                sems[e] = st.enter_context(nc.semaphore("s_" + e))
            for i in range(self.ndsem):
                sems["d%d" % i] = st.enter_context(nc.semaphore("sd%d" % i))
            block = st.enter_context(nc.Block())

            def run(ename, eng):
                for waits, fn, inc in self.ops[ename]:
                    for k, v in waits:
                        eng.wait_ge(sems[k], v)
                    if fn is None:
                        continue
                    ins = fn(eng)
                    ins.then_inc(sems[inc[0]], inc[1])
                if ename == "sp":
                    for k, v in self.dsem_final.items():
                        eng.wait_ge(sems[k], v)

            @block.tensor
            def _(e):
                run("pe", e)

            @block.scalar
            def _(e):
                run("act", e)

            @block.vector
            def _(e):
                run("dve", e)

            @block.gpsimd
            def _(e):
                run("pool", e)

            @block.sync
            def _(e):
                run("sp", e)


class Arena:
    def __init__(self, p, name, nbytes):
        self.t = p.sb(name, [128, nbytes // 4], F32)
        self.cap = nbytes // 4
        self.off = 0

    def reset(self, off=0):
        self.off = off

    def mark(self):
        return self.off

    def alloc(self, shape, dtype):
        P = shape[0]
        free = list(shape[1:])
        n = 1
        for s in free:
            n *= s
        esz = 4 if dtype in (F32, I32) else 2
        w = (n * esz + 3) // 4
        w = (w + 3) // 4 * 4
        assert self.off + w <= self.cap, ("arena overflow", self.off, w, self.cap)
        v = self.t[0:P, self.off:self.off + w]
        self.off += w
        if dtype != F32:
            v = v.bitcast(dtype)
        v = v[:, 0:n]
        if len(free) == 1:
            return v
        names = " ".join("abcd"[:len(free)])
        kw = {k: s for k, s in zip(names.split(), free)}
        return v.rearrange("p (%s) -> p %s" % (names, names), **kw)


def build(n_pool=2560, dbg=None, stop=None, strict=True):
    nc = bass.Bass("TRN2", target_bir_lowering=False)
    p = Prog(nc, strict=strict)

    def din(name, shape, dt=F32):
        return nc.dram_tensor(name, list(shape), dt, kind="ExternalInput").ap()

    def dout(name, shape, dt=F32):
        return nc.dram_tensor(name, list(shape), dt, kind="ExternalOutput").ap()

    xp = din("xp", [TP, D]); xs = din("xs", [128, D])
    ck = din("ck", [DEPTH, n_pool * 128, 512]); cv = din("cv", [DEPTH, n_pool * 128, 512])
    clf = din("clf", [DEPTH, n_pool, 1024])
    spool = din("spool", [DEPTH, NSEQ, 15, 512]); sconv = din("sconv", [DEPTH, NSEQ, 3, 1536])
    sssm = din("sssm", [DEPTH, NSEQ, 1024, 128])
    pt = din("pt", [NSEQ, NPG], I32)
    norm_pre = din("norm_pre", [DEPTH, D]); w_in = din("w_in", [DEPTH, D, DIN])
    pool_w = din("pool_w", [DEPTH, 4, 128, 128]); pool_scale = din("pool_scale", [DEPTH, 512])
    f_bias = din("f_bias", [DEPTH, 8]); conv_w = din("conv_w", [DEPTH, 4, 1536]); conv_b = din("conv_b", [DEPTH, 1536])
    dt_bias = din("dt_bias", [DEPTH, 16]); a_log = din("a_log", [DEPTH, 16]); d_skip = din("d_skip", [DEPTH, 16])
    ssm_norm = din("ssm_norm", [DEPTH, D])
    w_a = din("w_a", [DEPTH, 512, D]); w_b = din("w_b", [DEPTH, 512, D]); w_c = din("w_c", [DEPTH, D, D])
    w_out = din("w_out", [DEPTH, D, D]); norm_post = din("norm_post", [DEPTH, D])

    yp = dout("yp", [TP, D]); ys = dout("ys", [128, D])
    kp = dout("kp", [DEPTH, TP, 512]); vp = dout("vp", [DEPTH, TP, 512]); lfp = dout("lfp", [DEPTH, TP, 8])
    poolp = dout("poolp", [DEPTH, 15, 512]); convp = dout("convp", [DEPTH, 3, 1536]); ssmp = dout("ssmp", [DEPTH, 1024, 128])
    kso = dout("kso", [DEPTH, 128, 512]); vso = dout("vso", [DEPTH, 128, 512]); lfs = dout("lfs", [DEPTH, 128, 8])
    pools = dout("pools", [DEPTH, NSEQ, 15, 512]); convs = dout("convs", [DEPTH, NSEQ, 3, 1536])
    ssms = dout("ssms", [DEPTH, NSEQ, 1024, 128])
    x1 = nc.dram_tensor("x1", [NTOK, D], F32, kind="Internal").ap()
    Bx1 = bufs("x1_", NT)
    dbg_out = {}
    if dbg:
        for k, shp in dbg.items():
            dbg_out[k] = dout("dbg_" + k, shp, BF16 if k in ("hnT", "yT", "mT") else F32)

    hnT = p.sb("hnT", [128, 8, NTOK], BF16); BhnT = bufs("hnT", NT)
    yT = p.sb("yT", [128, 16, NTOK], BF16)
    ByT = [bufs("yTa", NT), bufs("yTb", NT), bufs("yTc", NT)]
    YOFF = (0, 4, 8)
    PS = [p.ps("ps%d" % i, [128, 512], F32)[:, :] for i in range(8)]
    BPS = [Buf("ps%d" % i, excl=True) for i in range(8)]
    cst = Arena(p, "cst", 15 * 1024)
    ar = Arena(p, "arena", 90 * 1024)

    Bc = Buf("consts")
    onesf = cst.alloc([128, 128], F32); identf = cst.alloc([128, 128], F32); trif = cst.alloc([128, 128], F32)
    zerof = cst.alloc([128, 128], F32)
    identb = cst.alloc([128, 128], BF16); onesb = cst.alloc([128, 128], BF16); maskb = cst.alloc([128, 128], BF16)
    sel127 = cst.alloc([128, 128], F32)
    blk = cst.alloc([128, 128], F32); btrif = cst.alloc([128, 128], F32)
    negm_p = cst.alloc([128, 128], F32); negm_s = cst.alloc([128, 128], F32)
    lastmask = cst.alloc([128, 16], F32); lastrow = cst.alloc([128, 1], F32); L2s = cst.alloc([128, 128], F32)
    Emat = cst.alloc([128, 128], F32)
    invc = cst.alloc([128, 4, 16], F32)
    iot = cst.alloc([128, 16], I32)
    iop = cst.alloc([128, 1], I32); iopf = cst.alloc([128, 1], F32)

    def cop(eng, fn):
        p.op(eng, fn, reads=[Bc], writes=[Bc])

    cop("pool", lambda e: e.memset(onesf, 1.0))
    cop("pool", lambda e: e.memset(zerof, 0.0))
    cop("pool", lambda e: e.affine_select(out=identf, in_=onesf, pattern=[[-1, 128]], compare_op=ALU.is_equal, fill=0.0, base=0, channel_multiplier=1))
    cop("pool", lambda e: e.affine_select(out=trif, in_=onesf, pattern=[[1, 128]], compare_op=ALU.is_ge, fill=0.0, base=0, channel_multiplier=-1))
    cop("pool", lambda e: e.affine_select(out=sel127, in_=onesf, pattern=[[0, 128]], compare_op=ALU.is_equal, fill=0.0, base=-127, channel_multiplier=1))
    cop("pool", lambda e: e.tensor_copy(out=identb, in_=identf))
    cop("pool", lambda e: e.tensor_copy(out=onesb, in_=onesf))
    cop("pool", lambda e: e.affine_select(out=negm_p, in_=zerof, pattern=[[1, 128]], compare_op=ALU.is_ge, fill=NEG, base=0, channel_multiplier=-1))
    cop("pool", lambda e: e.tensor_copy(out=maskb, in_=negm_p))
    cop("pool", lambda e: e.affine_select(out=Emat, in_=onesf, pattern=[[1, 128]], compare_op=ALU.is_ge, fill=0.0, base=0, channel_multiplier=-8))
    cop("pool", lambda e: e.affine_select(out=Emat, in_=Emat, pattern=[[-1, 128]], compare_op=ALU.is_ge, fill=0.0, base=7, channel_multiplier=8))
    cop("pe", lambda e: e.matmul(PS[0][:, 0:128], lhsT=Emat[0:16, :], rhs=Emat[0:16, :], start=True, stop=True))
    cop("dve", lambda e: e.tensor_copy(out=blk, in_=PS[0][:, 0:128]))
    cop("dve", lambda e: e.tensor_tensor(out=btrif, in0=blk, in1=trif, op=ALU.mult))
    cop("dve", lambda e: e.tensor_scalar(out=negm_s, in0=btrif, scalar1=-1.0, scalar2=-NEG, op0=ALU.add, op1=ALU.mult))
    cop("pool", lambda e: e.affine_select(out=lastmask, in_=onesf[:, 0:16], pattern=[[-8, 16]], compare_op=ALU.is_equal, fill=0.0, base=-7, channel_multiplier=1))
    cop("dve", lambda e: e.tensor_reduce(out=lastrow, in_=lastmask, axis=mybir.AxisListType.X, op=ALU.add))
    cop("dve", lambda e: e.tensor_scalar(out=L2s, in0=blk, scalar1=lastrow[:, 0:1], scalar2=None, op0=ALU.mult))
    cop("pool", lambda e: e.iota(iot, pattern=[[1, 16]], base=1, channel_multiplier=0))
    cop("pool", lambda e: e.iota(iop, pattern=[[0, 1]], base=0, channel_multiplier=1))
    cop("dve", lambda e: e.tensor_copy(out=iopf, in_=iop))
    for g in range(4):
        cop("dve", lambda e, g=g: e.tensor_copy(out=invc[:, g, :], in_=iot))
        cop("dve", lambda e, g=g: e.tensor_scalar(out=invc[:, g, :], in0=invc[:, g, :], scalar1=float(2 ** (g + 1)), scalar2=None, op0=ALU.min))
        cop("dve", lambda e, g=g: e.reciprocal(out=invc[:, g, :], in_=invc[:, g, :]))

    ptb = cst.alloc([128, 256], I32); ptf = cst.alloc([128, 256], F32); kidx = cst.alloc([128, 256], I32)
    lidx = cst.alloc([128, 2], I32)
    Bpt = Buf("pt")
    p.dma("sp", lambda e: e.dma_start(out=ptb, in_=pt.rearrange("a b -> (a b)").partition_broadcast(128)), writes=[Bpt])
    for h in range(2):
        p.dma("sp", lambda e, h=h: e.dma_start(out=lidx[:, h:h + 1], in_=pt[8 * h:8 * h + 8, :].rearrange("a (b o) -> (a b) o", o=1)), writes=[Bpt])
    p.op("dve", lambda e: e.tensor_copy(out=ptf, in_=ptb), reads=[Bpt, Bc], writes=[Bpt])
    p.op("dve", lambda e: e.tensor_scalar(out=ptf, in0=ptf, scalar1=128.0, scalar2=iopf[:, 0:1], op0=ALU.mult, op1=ALU.add), reads=[Bpt], writes=[Bpt])
    p.op("dve", lambda e: e.tensor_copy(out=kidx, in_=ptf), reads=[Bpt], writes=[Bpt])
    kidx1 = ptb
    lidx1 = cst.alloc([128, 2], I32); lf2c = cst.alloc([128, 2], F32)
    p.op("dve", lambda e: e.tensor_scalar(out=ptf, in0=ptf, scalar1=float(n_pool * 128), scalar2=None, op0=ALU.add), reads=[Bpt], writes=[Bpt])
    p.op("dve", lambda e: e.tensor_copy(out=kidx1, in_=ptf), reads=[Bpt], writes=[Bpt])
    p.op("dve", lambda e: e.tensor_copy(out=lf2c, in_=lidx), reads=[Bpt], writes=[Bpt])
    p.op("dve", lambda e: e.tensor_scalar(out=lf2c, in0=lf2c, scalar1=float(n_pool), scalar2=None, op0=ALU.add), reads=[Bpt], writes=[Bpt])
    p.op("dve", lambda e: e.tensor_copy(out=lidx1, in_=lf2c), reads=[Bpt], writes=[Bpt])

    npre_bc = cst.alloc([128, D], F32); Bnpre = Buf("npre")
    par = cst.alloc([128, 64], F32); Bpar = Buf("par")

    def dbg_dump(key, src_ap, rbufs, eng="sp"):
        if key in dbg_out:
            p.dma(eng, lambda e: e.dma_start(out=dbg_out[key], in_=src_ap), reads=rbufs)

    def tile_bufs(blist, tok0, ntok):
        return blist[tok0 // 128:(tok0 + ntok + 127) // 128]

    def load_w(dst, src, wbuf, q="pool"):
        p.dma(q, lambda e: e.dma_start(out=dst, in_=src.rearrange("(kc p) n -> p kc n", p=128)), writes=[wbuf])

    def proj_fm(W, wbuf, c0, ncols, tok0, ntok, ps_ap, psbuf):
        for kc in range(8):
            p.op("pe", lambda e, kc=kc: e.matmul(ps_ap, lhsT=W[:, kc, c0:c0 + ncols], rhs=hnT[:, kc, tok0:tok0 + ntok],
                                                  start=(kc == 0), stop=(kc == 7)),
                 reads=[wbuf] + tile_bufs(BhnT, tok0, ntok), writes=[psbuf])

    def proj_tm(W, wbuf, c0, ncols, j, ps_ap, psbuf):
        for kc in range(8):
            p.op("pe", lambda e, kc=kc: e.matmul(ps_ap, lhsT=hnT[:, kc, j * 128:(j + 1) * 128], rhs=W[:, kc, c0:c0 + ncols],
                                                  start=(kc == 0), stop=(kc == 7)),
                 reads=[wbuf, BhnT[j]], writes=[psbuf])

    def norm_to_hnT(xt, Bxt, j, sq, Bsq, hb, Bhb, psi):
        p.op("pool", lambda e: e.memset(sq, 0.0), writes=[Bsq])
        p.op("act", lambda e: e.activation(out=hb, in_=xt, func=AF.Square, accum_out=sq[:, 0:1]), reads=[Bxt], writes=[Bsq, Bhb])
        p.op("act", lambda e: e.activation(out=sq[:, 1:2], in_=sq[:, 0:1], func=AF.Ln, bias=EPS, scale=1.0 / D), reads=[Bsq], writes=[Bsq])
        p.op("act", lambda e: e.activation(out=sq[:, 2:3], in_=sq[:, 1:2], func=AF.Exp, scale=-0.5), reads=[Bsq], writes=[Bsq])
        p.op("dve", lambda e: e.scalar_tensor_tensor(out=hb, in0=xt, scalar=sq[:, 2:3], in1=npre_bc, op0=ALU.mult, op1=ALU.mult),
             reads=[Bxt, Bsq, Bnpre], writes=[Bhb])
        pst = PS[psi].bitcast(BF16)
        for kc in range(8):
            p.op("pe", lambda e, kc=kc: e.transpose(pst[:, kc * 128:(kc + 1) * 128], hb[:, kc * 128:(kc + 1) * 128], identb),
                 reads=[Bhb, Bc], writes=[BPS[psi]])
        p.op("act", lambda e: e.copy(out=hnT[:, :, j * 128:(j + 1) * 128], in_=pst.rearrange("p (k t) -> p k t", k=8)),
             reads=[BPS[psi]], writes=[BhnT[j]])

    for l in range(DEPTH):
        if l == 0:
            p.dma("sp", lambda e: e.dma_start(out=npre_bc, in_=norm_pre[0].partition_broadcast(128)), writes=[Bnpre])
            p.barrier(); ar.reset()
            xt = [ar.alloc([128, D], F32) for _ in range(2)]; Bxt = bufs("xt", 2)
            hb = [ar.alloc([128, D], BF16) for _ in range(2)]; Bhb = bufs("hb", 2)
            sq = [ar.alloc([128, 4], F32) for _ in range(2)]; Bsq = bufs("sq", 2)
            for j in range(NT):
                k = j % 2
                src = xp[j * 128:(j + 1) * 128, :] if j < 16 else xs
                p.dma("sp", lambda e, k=k, src=src: e.dma_start(out=xt[k], in_=src), writes=[Bxt[k]])
                norm_to_hnT(xt[k], Bxt[k], j, sq[k], Bsq[k], hb[k], Bhb[k], k)
        if stop == "N":
            break

        import os
        SKIPAB = os.environ.get('SKIPAB') == '1'
        p.barrier(); ar.reset()
        wA = [ar.alloc([128, 8, 256], BF16) for _ in range(2)]; BwA = bufs("wA", 2)
        pw = [ar.alloc([128, 128], BF16) for _ in range(2)]; Bpw = bufs("pw", 2)
        psc = ar.alloc([128, 4], F32); Bpsc = Buf("psc")
        ub = ar.alloc([128, 15 + TP], F32); Bub = Buf("ub")
        sA = ar.alloc([128, 15 + TP], F32); BsA = Buf("sA")
        sB = ar.alloc([128, 15 + TP], F32); BsB = Buf("sB")
        zs = ar.alloc([128, NTOK], F32); Bzs = Buf("zs")
        dbf = ar.alloc([128, NTOK], BF16); Bdbf = Buf("dbf")
        us = ar.alloc([128, 16, 23], F32); Bus = Buf("us")
        ssA = ar.alloc([128, 16, 23], F32); BssA = Buf("ssA")
        ssB = ar.alloc([128, 16, 23], F32); BssB = Buf("ssB")
        hist = [ar.alloc([120, 512], F32) for _ in range(2)]; Bhist = bufs("hist", 2)
        tmp15 = ar.alloc([128, 16], F32); Btmp15 = Buf("tmp15")
        utm = [ar.alloc([128, 512], F32) for _ in range(2)]; Butm = bufs("utm", 2)
        wU = ar.alloc([128, 8, 512], BF16); BwU = Buf("wU")

        for g in range(4):
            p.dma("sp", lambda e, l=l, g=g: e.dma_start(out=psc[:, g:g + 1], in_=pool_scale[l, g * 128:(g + 1) * 128].rearrange("(c o) -> c o", o=1)), writes=[Bpsc])
        for h2 in range(2):
            p.dma("sp", lambda e, l=l, h2=h2: e.dma_start(out=hist[h2], in_=spool[l, 8 * h2:8 * h2 + 8].rearrange("b j c -> (b j) c")),
                  writes=[Bhist[h2]])
        p.op("pool", lambda e: e.memset(ub[:, 0:15], 0.0), writes=[Bub])
        load_w(wU, w_in[l][:, C_UA:C_UA + 512], BwU)
        for k, j in enumerate((15, 16)):
            proj_tm(wU, BwU, 0, 512, j, PS[6 + k], BPS[6 + k])
            p.op("act", lambda e, k=k: e.copy(out=utm[k], in_=PS[6 + k]), reads=[BPS[6 + k]], writes=[Butm[k]])
        p.dma("sp", lambda e, l=l: e.dma_start(out=poolp[l], in_=utm[0][113:128, :]), reads=[Butm[0]])
        for b in range(NSEQ):
            p.dma("sp", lambda e, l=l, b=b: e.dma_start(out=pools[l, b, 7:15, :], in_=utm[1][8 * b:8 * b + 8, :]), reads=[Butm[1]])
        p.dma("sp", lambda e, l=l: e.dma_start(out=pools[l, :, 0:7, :], in_=spool[l, :, 8:15, :]), owner=Butm[1])

        for g in range(4):
            w = 2 ** (g + 1)
            k = g % 2
            load_w(wA[k][:, :, 0:128], w_in[l][:, C_UA + g * 128:C_UA + (g + 1) * 128], BwA[k])
            load_w(wA[k][:, :, 128:256], w_in[l][:, C_ZA + g * 128:C_ZA + (g + 1) * 128], BwA[k])
            p.dma("pool", lambda e, l=l, g=g, k=k: e.dma_start(out=pw[k], in_=pool_w[l, g]), writes=[Bpw[k]])
            for tg in range(5):
                tok0 = tg * 512
                ntok = 512 if tg < 4 else 128
                pu, pz = (tg * 2) % 6, (tg * 2 + 1) % 6
                proj_fm(wA[k], BwA[k], 0, 128, tok0, ntok, PS[pu][:, 0:ntok], BPS[pu])
                proj_fm(wA[k], BwA[k], 128, 128, tok0, ntok, PS[pz][:, 0:ntok], BPS[pz])
                if tg < 4:
                    p.op("act", lambda e, pu=pu, tok0=tok0: e.copy(out=ub[:, 15 + tok0:15 + tok0 + 512], in_=PS[pu]), reads=[BPS[pu]], writes=[Bub])
                else:
                    p.op("act", lambda e, pu=pu: e.copy(out=us[:, :, 15:23], in_=PS[pu][:, 0:128].rearrange("p (b t) -> p b t", b=16)),
                         reads=[BPS[pu]], writes=[Bus])
                p.op("act", lambda e, pz=pz, tok0=tok0, ntok=ntok: e.activation(out=zs[:, tok0:tok0 + ntok], in_=PS[pz][:, 0:ntok], func=AF.Silu),
                     reads=[BPS[pz]], writes=[Bzs])
            for h2 in range(2):
                p.op("pe", lambda e, g=g, h2=h2: e.transpose(PS[6][:, h2 * 128:h2 * 128 + 120], hist[h2][:, g * 128:(g + 1) * 128], identf[0:120, 0:120]),
                     reads=[Bhist[h2], Bc], writes=[BPS[6]])
            p.op("act", lambda e: e.copy(out=us[:, :, 0:15].rearrange("p (h b) j -> p h b j", h=2),
                                         in_=PS[6][:, 0:256].rearrange("p (h x) -> p h x", h=2)[:, :, 0:120].rearrange("p h (b j) -> p h b j", b=8)),
                 reads=[BPS[6]], writes=[Bus])
            src_p, Bsrc_p, src_s, Bsrc_s = ub, Bub, us, Bus
            lo = 0
            pp = [(sA, BsA, ssA, BssA), (sB, BsB, ssB, BssB)]
            for step in range(g + 1):
                sh = 2 ** step
                dp, Bdp, ds_, Bds = pp[step % 2]
                n = 15 + TP
                p.op("dve", lambda e, dp=dp, sp_=src_p, lo=lo, sh=sh, n=n: e.tensor_tensor(out=dp[:, lo + sh:n], in0=sp_[:, lo + sh:n], in1=sp_[:, lo:n - sh], op=ALU.add),
                     reads=[Bsrc_p], writes=[Bdp])
                p.op("pool", lambda e, ds_=ds_, ss_=src_s, lo=lo, sh=sh: e.tensor_tensor(out=ds_[:, :, lo + sh:23], in0=ss_[:, :, lo + sh:23], in1=ss_[:, :, lo:23 - sh], op=ALU.add),
                     reads=[Bsrc_s], writes=[Bds])
                src_p, Bsrc_p, src_s, Bsrc_s = dp, Bdp, ds_, Bds
                lo += sh
            p.op("dve", lambda e, sp_=src_p, w=w: e.scalar_tensor_tensor(out=dbf[:, 0:TP], in0=sp_[:, 15:15 + TP], scalar=1.0 / w, in1=ub[:, 15:15 + TP], op0=ALU.mult, op1=ALU.subtract),
                 reads=[Bsrc_p, Bub], writes=[Bdbf])
            p.op("dve", lambda e, sp_=src_p, g=g: e.tensor_tensor(out=tmp15[:, 0:15], in0=sp_[:, 15:30], in1=invc[:, g, 0:15], op=ALU.mult),
                 reads=[Bsrc_p, Bc], writes=[Btmp15])
            p.op("dve", lambda e: e.tensor_tensor(out=dbf[:, 0:15], in0=tmp15[:, 0:15], in1=ub[:, 15:30], op=ALU.subtract),
                 reads=[Btmp15, Bub, Bdbf], writes=[Bdbf])
            p.op("dve", lambda e, ss_=src_s, w=w: e.scalar_tensor_tensor(out=dbf[:, TP:NTOK].rearrange("p (b t) -> p b t", b=16), in0=ss_[:, :, 15:23], scalar=1.0 / w, in1=us[:, :, 15:23], op0=ALU.mult, op1=ALU.subtract),
                 reads=[Bsrc_s, Bus, Bdbf], writes=[Bdbf])
            for tg in range(5):
                tok0 = tg * 512
                ntok = 512 if tg < 4 else 128
                py = tg % 6
                p.op("pe", lambda e, k=k, py=py, tok0=tok0, ntok=ntok: e.matmul(PS[py][:, 0:ntok], lhsT=pw[k], rhs=dbf[:, tok0:tok0 + ntok], start=True, stop=True),
                     reads=[Bpw[k], Bdbf], writes=[BPS[py]])
                p.op("dve", lambda e, g=g, py=py, tok0=tok0, ntok=ntok: e.scalar_tensor_tensor(out=yT[:, g, tok0:tok0 + ntok], in0=PS[py][:, 0:ntok], scalar=psc[:, g:g + 1], in1=zs[:, tok0:tok0 + ntok], op0=ALU.mult, op1=ALU.mult),
                     reads=[BPS[py], Bpsc, Bzs], writes=tile_bufs(ByT[0], tok0, ntok))
        if stop == "A":
            break

        p.barrier(); ar.reset()
        wF = ar.alloc([128, 8, 8], BF16); BwF = Buf("wF")
        fbb = ar.alloc([128, 8], F32); Bfbb = Buf("fbb")
        lf = ar.alloc([128, 17, 8], F32); Blf = Buf("lf")
        qTs = ar.alloc([128, 4, 128], BF16); kTs = ar.alloc([128, 4, 128], BF16); Bqks = Buf("qks")
        vnew = ar.alloc([128, 512], BF16); Bvnew = Buf("vnew")
        zsTs = ar.alloc([128, 4, 128], F32); BzsTs = Buf("zsTs")
        markB = ar.mark()
        tot = ar.alloc([128, 16, 8], F32); carry = ar.alloc([128, 17, 8], F32); ccum = ar.alloc([128, 16, 8], F32); Bcc = Buf("cc")
        btab = ar.alloc([128, 16, 16, 8], F32); Bbtab = Buf("btab")
        wB = [ar.alloc([128, 8, 512], BF16) for _ in range(2)]; BwB = bufs("wB", 2)
        qT = ar.alloc([128, NTOK], BF16); BqT = Buf("qT")
        kT = ar.alloc([128, NTOK], BF16); BkT = Buf("kT")
        vaug = ar.alloc([128, 16, 2, 66], BF16); Bvaug = bufs("vaug", 16)
        kvst = [ar.alloc([128, 256], F32) for _ in range(3)]; Bkvst = bufs("kvst", 3)
        pT = [ar.alloc([128, 128], BF16) for _ in range(4)]; BpT = bufs("pT", 4)
        zsb = [ar.alloc([128, 128], F32) for _ in range(16)]; Bzsb = bufs("zsb", 16)
        rinv = [ar.alloc([128, 1], F32) for _ in range(2)]; Brinv = bufs("rinv", 2)
        ybt = [ar.alloc([128, 128], BF16) for _ in range(2)]; Bybt = bufs("ybt", 2)

        load_w(wF, w_in[l][:, C_F:C_F + 8], BwF)
        p.dma("sp", lambda e, l=l: e.dma_start(out=fbb, in_=f_bias[l].partition_broadcast(128)), writes=[Bfbb])
        for j in range(NT):
            proj_tm(wF, BwF, 0, 8, j, PS[0][:, j * 8:(j + 1) * 8], BPS[0])
        lf_flat = lf.rearrange("p j h -> p (j h)")
        p.op("dve", lambda e: e.tensor_tensor(out=lf, in0=PS[0][:, 0:136].rearrange("p (j h) -> p j h", h=8), in1=fbb.unsqueeze(1).to_broadcast([128, 17, 8]), op=ALU.add),
             reads=[BPS[0], Bfbb], writes=[Blf])
        p.op("act", lambda e: e.activation(out=lf_flat, in_=lf_flat, func=AF.Exp, scale=-1.0), reads=[Blf], writes=[Blf])
        p.op("act", lambda e: e.activation(out=lf_flat, in_=lf_flat, func=AF.Ln, bias=1.0), reads=[Blf], writes=[Blf])
        p.op("dve", lambda e: e.tensor_scalar(out=lf_flat, in0=lf_flat, scalar1=-1.0, scalar2=None, op0=ALU.mult), reads=[Blf], writes=[Blf])
        p.dma("sp", lambda e, l=l: e.dma_start(out=lfp[l].rearrange("(j p) h -> p j h", p=128), in_=lf[:, 0:16, :]), reads=[Blf])
        p.dma("sp", lambda e, l=l: e.dma_start(out=lfs[l], in_=lf[:, 16, :]), reads=[Blf])
        if stop == "B0a":
            break
        p.op("pe", lambda e: e.matmul(PS[1][:, 0:128], lhsT=onesf, rhs=lf_flat[:, 0:128], start=True, stop=True), reads=[Blf, Bc], writes=[BPS[1]])
        p.op("pe", lambda e: e.matmul(PS[2][:, 0:128], lhsT=trif, rhs=lf_flat[:, 0:128], start=True, stop=True), reads=[Blf, Bc], writes=[BPS[2]])
        p.op("dve", lambda e: e.tensor_copy(out=tot.rearrange("p j h -> p (j h)"), in_=PS[1][:, 0:128]), reads=[BPS[1]], writes=[Bcc])
        p.op("dve", lambda e: e.memset(carry[:, 0, :], 0.0), reads=[Bcc], writes=[Bcc])
        for j in range(1, 17):
            p.op("dve", lambda e, j=j: e.tensor_tensor(out=carry[:, j, :], in0=carry[:, j - 1, :], in1=tot[:, j - 1, :], op=ALU.add), reads=[Bcc], writes=[Bcc])
        p.op("dve", lambda e: e.tensor_tensor(out=ccum, in0=PS[2][:, 0:128].rearrange("p (j h) -> p j h", h=8), in1=carry[:, 0:16, :], op=ALU.add), reads=[BPS[2], Bcc], writes=[Bcc])
        if stop == "B0c":
            break
        for Q in range(16):
            p.op("dve", lambda e, Q=Q: e.tensor_tensor(out=btab[:, Q, 0:Q + 1, :], in0=carry[:, Q + 1:Q + 2, :].to_broadcast([128, Q + 1, 8]), in1=ccum[:, 0:Q + 1, :], op=ALU.subtract),
                 reads=[Bcc, Bbtab], writes=[Bbtab])
        p.op("dve", lambda e: e.memset(vaug[:, :, :, 64:66], 1.0), writes=Bvaug)
        if stop == "B0":
            break

        import os
        for pr in range(int(os.environ.get('NPR', 4))):
            k = pr % 2
            for i, c0 in enumerate((C_Q, C_K, C_V, C_ZB)):
                load_w(wB[k][:, :, i * 128:(i + 1) * 128], w_in[l][:, c0 + pr * 128:c0 + (pr + 1) * 128], BwB[k])
            for tg in range(5):
                tok0 = tg * 512
                ntok = 512 if tg < 4 else 128
                pa, pb = 5 + (2 * tg) % 3, 5 + (2 * tg + 1) % 3
                proj_fm(wB[k], BwB[k], 0, 128, tok0, ntok, PS[pa][:, 0:ntok], BPS[pa])
                p.op("dve", lambda e, pa=pa, tok0=tok0, ntok=ntok: e.tensor_scalar(out=qT[:, tok0:tok0 + ntok], in0=PS[pa][:, 0:ntok], scalar1=0.125, scalar2=None, op0=ALU.mult),
                     reads=[BPS[pa]], writes=[BqT])
                proj_fm(wB[k], BwB[k], 128, 128, tok0, ntok, PS[pb][:, 0:ntok], BPS[pb])
                p.op("dve", lambda e, pb=pb, tok0=tok0, ntok=ntok: e.tensor_copy(out=kT[:, tok0:tok0 + ntok], in_=PS[pb][:, 0:ntok]),
                     reads=[BPS[pb]], writes=[BkT])
            if stop == "B0b1":
                continue
            p.op("pool", lambda e, pr=pr: e.tensor_copy(out=qTs[:, pr, :], in_=qT[:, TP:NTOK]), reads=[BqT], writes=[Bqks])
            p.op("pool", lambda e, pr=pr: e.tensor_copy(out=kTs[:, pr, :], in_=kT[:, TP:NTOK]), reads=[BkT], writes=[Bqks])
            proj_fm(wB[k], BwB[k], 384, 128, TP, 128, PS[5][:, 0:128], BPS[5])
            p.op("act", lambda e, pr=pr: e.activation(out=zsTs[:, pr, :], in_=PS[5][:, 0:128], func=AF.Silu), reads=[BPS[5]], writes=[BzsTs])
            if stop == "B0b2":
                continue
            import os
            SK = os.environ.get("SK", "")
            for j in range(int(os.environ.get("NTJ", NT))):
                pb = 5 + j % 3
                m = j % 3
                proj_tm(wB[k], BwB[k], 128, 256, j, PS[pb][:, 0:256], BPS[pb])
                p.op("act", lambda e, pb=pb, m=m: e.copy(out=kvst[m], in_=PS[pb][:, 0:256]), reads=[BPS[pb]], writes=[Bkvst[m]])
                if j < 16:
                    if "v" not in SK:
                        for a in range(2):
                            p.op("dve", lambda e, j=j, a=a, m=m: e.tensor_copy(out=vaug[:, j, a, 0:64], in_=kvst[m][:, 128 + 64 * a:192 + 64 * a]),
                                 reads=[Bkvst[m]], writes=[Bvaug[j]])
                    if "d" not in SK:
                        p.dma("sp", lambda e, l=l, j=j, pr=pr, m=m: e.dma_start(out=kp[l, j * 128:(j + 1) * 128, pr * 128:(pr + 1) * 128], in_=kvst[m][:, 0:128]), reads=[Bkvst[m]])
                        p.dma("sp", lambda e, l=l, j=j, pr=pr, m=m: e.dma_start(out=vp[l, j * 128:(j + 1) * 128, pr * 128:(pr + 1) * 128], in_=kvst[m][:, 128:256]), reads=[Bkvst[m]])
                else:
                    if "n" not in SK:
                        p.op("dve", lambda e, m=m, pr=pr: e.tensor_copy(out=vnew[:, pr * 128:(pr + 1) * 128], in_=kvst[m][:, 128:256]), reads=[Bkvst[m]], writes=[Bvnew])
                    if "e" not in SK:
                        p.dma("sp", lambda e, l=l, pr=pr, m=m: e.dma_start(out=kso[l, :, pr * 128:(pr + 1) * 128], in_=kvst[m][:, 0:128]), reads=[Bkvst[m]])
                        p.dma("sp", lambda e, l=l, pr=pr, m=m: e.dma_start(out=vso[l, :, pr * 128:(pr + 1) * 128], in_=kvst[m][:, 128:256]), reads=[Bkvst[m]])
            if stop == "B0b":
                continue
            it = 0
            for Q in range(16):
                pz_ = 5 + Q % 3
                proj_tm(wB[k], BwB[k], 384, 128, Q, PS[pz_][:, 0:128], BPS[pz_])
                p.op("act", lambda e, Q=Q, pz_=pz_: e.activation(out=zsb[Q], in_=PS[pz_][:, 0:128], func=AF.Silu), reads=[BPS[pz_]], writes=[Bzsb[Q]])
            for Q in range(16):
                zq = Q % 2
                for h2 in range(2):
                    hb = 64 * h2
                    h = 2 * pr + h2
                    po = 3 + h2

                    def qk(S, it):
                        sb_ = it % 3
                        p.op("pe", lambda e, S=S, sb_=sb_, hb=hb, Q=Q: e.matmul(PS[sb_][:, 0:128], lhsT=kT[hb:hb + 64, S * 128:(S + 1) * 128], rhs=qT[hb:hb + 64, Q * 128:(Q + 1) * 128], start=True, stop=(S != Q)),
                             reads=[BkT, BqT], writes=[BPS[sb_]])
                        if S == Q:
                            p.op("pe", lambda e, sb_=sb_: e.matmul(PS[sb_][:, 0:128], lhsT=identb, rhs=maskb, start=False, stop=True), reads=[Bc], writes=[BPS[sb_]])
                    qk(0, it)
                    for S in range(Q + 1):
                        if S + 1 <= Q:
                            qk(S + 1, it + 1)
                        sb_ = it % 3
                        pi = it % 4
                        p.op("act", lambda e, sb_=sb_, pi=pi, Q=Q, S=S, h=h: e.activation(out=pT[pi], in_=PS[sb_][:, 0:128], func=AF.Exp, bias=btab[:, Q, S, h:h + 1], scale=1.0),
                             reads=[BPS[sb_], Bbtab], writes=[BpT[pi]])
                        p.op("pe", lambda e, pi=pi, S=S, h2=h2, po=po, Q=Q: e.matmul(PS[po][:, 0:65], lhsT=pT[pi], rhs=vaug[:, S, h2, 0:65], start=(S == 0), stop=(S == Q)),
                             reads=[BpT[pi], Bvaug[S]], writes=[BPS[po]])
                        it += 1
                    p.op("dve", lambda e, po=po, h2=h2: e.reciprocal(out=rinv[h2], in_=PS[po][:, 64:65]), reads=[BPS[po]], writes=[Brinv[h2]])
                    p.op("dve", lambda e, po=po, h2=h2, hb=hb, zq=zq, Q=Q: e.scalar_tensor_tensor(out=ybt[zq][:, hb:hb + 64], in0=PS[po][:, 0:64], scalar=rinv[h2][:, 0:1], in1=zsb[Q][:, hb:hb + 64], op0=ALU.mult, op1=ALU.mult),
                         reads=[BPS[po], Brinv[h2], Bzsb[Q]], writes=[Bybt[zq]])
                pst = PS[6].bitcast(BF16)
                p.op("pe", lambda e, zq=zq, pst=pst: e.transpose(pst[:, 0:128], ybt[zq], identb), reads=[Bybt[zq], Bc], writes=[BPS[6]])
                p.op("dve", lambda e, pst=pst, pr=pr, Q=Q: e.tensor_copy(out=yT[:, 4 + pr, Q * 128:(Q + 1) * 128], in_=pst[:, 0:128]), reads=[BPS[6]], writes=[ByT[1][Q]])
        if stop == "B1":
            break

        p.barrier(); ar.reset(markB)
        sufm = ar.alloc([128, 128], F32); Bsufm = Buf("sufm")
        lfg = ar.alloc([128, 1024], F32); Blfg = Buf("lfg")
        lft = ar.alloc([128, 8, 128], F32); Blft = Buf("lft")
        totS = ar.alloc([128, 8, 8, 16], F32); later = ar.alloc([128, 8, 8, 16], F32); Blat = Buf("later")
        bpast = ar.alloc([128, 8, 2, 128], F32); Bbpast = Buf("bpast")
        bnew = ar.alloc([128, 8], F32); Bbnew = Buf("bnew")
        qbd = ar.alloc([128, 4, 16, 16], BF16); Bqbd = Buf("qbd")
        kbf = [ar.alloc([128, 512], BF16) for _ in range(4)]; Bkbf = bufs("kbf", 4)
        KT = [ar.alloc([128, 4, 128], BF16) for _ in range(2)]; BKT = bufs("KT", 2)
        Vb = ar.alloc([128, 17, 512], BF16); BVb = bufs("Vb", 17)
        sc = ar.alloc([128, 17, 64], F32); Bsc = Buf("sc")
        pTs = ar.alloc([128, 17, 64], BF16); BpTs = Buf("pTs")
        rs = ar.alloc([128, 16, 64], F32); Brs = Buf("rs")
        ot = ar.alloc([128, 4, 16, 16], F32); Bot = Buf("ot")

        p.op("dve", lambda e: e.tensor_tensor(out=sufm, in0=onesf, in1=trif, op=ALU.subtract), reads=[Bc], writes=[Bsufm])
        kidxL = kidx if l == 0 else kidx1
        lidxL = lidx if l == 0 else lidx1
        BidxL = Bpt
        ck_t = ck.rearrange("l r c -> (l r) c"); cv_t = cv.rearrange("l r c -> (l r) c"); clf_t = clf.rearrange("l r c -> (l r) c")
        p.op("dve", lambda e: e.memset(qbd, 0.0), writes=[Bqbd])
        for pr in range(4):
            p.op("dve", lambda e, pr=pr: e.tensor_copy(out=qbd[0:64, pr, :, 0:8], in_=qTs[0:64, pr, :].rearrange("p (b t) -> p b t", b=16)), reads=[Bqks, Bqbd], writes=[Bqbd])
            p.op("dve", lambda e, pr=pr: e.tensor_copy(out=qbd[64:128, pr, :, 8:16], in_=qTs[64:128, pr, :].rearrange("p (b t) -> p b t", b=16)), reads=[Bqks, Bqbd], writes=[Bqbd])
        p.op("pe", lambda e: e.matmul(PS[0][:, 0:8], lhsT=btrif, rhs=lf[:, 16, :], start=True, stop=True), reads=[Blf, Bc], writes=[BPS[0]])
        p.op("dve", lambda e: e.tensor_scalar(out=bnew, in0=PS[0][:, 0:8], scalar1=-1.0, scalar2=None, op0=ALU.mult), reads=[BPS[0]], writes=[Bbnew])
        for half in range(2):
            p.dma("pool", lambda e, l=l, half=half, lidxL=lidxL: e.indirect_dma_start(out=lfg, out_offset=None, in_=clf_t, in_offset=bass.IndirectOffsetOnAxis(ap=lidxL[:, half:half + 1], axis=0)),
                  reads=[BidxL], writes=[Blfg])
            lfg3 = lfg.rearrange("p (s h) -> p s h", h=8)
            for h in range(8):
                pb_ = 1 + h // 4
                p.op("pe", lambda e, h=h, pb_=pb_: e.transpose(PS[pb_][:, (h % 4) * 128:(h % 4 + 1) * 128], lfg3[:, :, h], identf), reads=[Blfg, Bc], writes=[BPS[pb_]])
            for q4 in range(2):
                p.op("act", lambda e, q4=q4: e.copy(out=lft[:, 4 * q4:4 * q4 + 4, :], in_=PS[1 + q4].rearrange("p (h x) -> p h x", h=4)), reads=[BPS[1 + q4]], writes=[Blft])
            for q4 in range(2):
                p.op("pe", lambda e, q4=q4: e.matmul(PS[3 + q4], lhsT=onesf, rhs=lft[:, 4 * q4:4 * q4 + 4, :].rearrange("p h x -> p (h x)"), start=True, stop=True), reads=[Blft, Bc], writes=[BPS[3 + q4]])
                p.op("act", lambda e, q4=q4: e.copy(out=totS[:, 4 * q4:4 * q4 + 4, :, :].rearrange("p h b j -> p (h b j)"), in_=PS[3 + q4]), reads=[BPS[3 + q4]], writes=[Blat])
            p.op("dve", lambda e: e.memset(later[:, :, :, 15:16], 0.0), reads=[Blat], writes=[Blat])
            for j in range(14, -1, -1):
                p.op("dve", lambda e, j=j: e.tensor_tensor(out=later[:, :, :, j:j + 1], in0=later[:, :, :, j + 1:j + 2], in1=totS[:, :, :, j + 1:j + 2], op=ALU.add), reads=[Blat], writes=[Blat])
            for q4 in range(2):
                p.op("pe", lambda e, q4=q4: e.matmul(PS[5 + q4], lhsT=sufm, rhs=lft[:, 4 * q4:4 * q4 + 4, :].rearrange("p h x -> p (h x)"), start=True, stop=True), reads=[Blft, Bsufm], writes=[BPS[5 + q4]])
                p.op("dve", lambda e, q4=q4, half=half: e.tensor_tensor(out=bpast[:, 4 * q4:4 * q4 + 4, half, :], in0=PS[5 + q4].rearrange("p (h x) -> p h x", h=4),
                                                                    in1=later[:, 4 * q4:4 * q4 + 4, :, :].rearrange("p h b j -> p h (b j)"), op=ALU.add),
                     reads=[BPS[5 + q4], Blat], writes=[Bbpast])
        p.op("dve", lambda e: e.tensor_copy(out=Vb[:, 16, :], in_=vnew), reads=[Bvnew], writes=[BVb[16]])
        for b in range(NSEQ):
            half, b8 = b // 8, b % 8
            for j in range(NPG):
                m = j % 4
                col = b * 16 + j
                p.dma("pool", lambda e, m=m, col=col, kidxL=kidxL: e.indirect_dma_start(out=kbf[m], out_offset=None, in_=ck_t, in_offset=bass.IndirectOffsetOnAxis(ap=kidxL[:, col:col + 1], axis=0)),
                      reads=[BidxL], writes=[Bkbf[m]])
                p.dma("pool", lambda e, j=j, col=col, kidxL=kidxL: e.indirect_dma_start(out=Vb[:, j, :], out_offset=None, in_=cv_t, in_offset=bass.IndirectOffsetOnAxis(ap=kidxL[:, col:col + 1], axis=0)),
                      reads=[BidxL], writes=[BVb[j]])
            for j in range(NPG):
                m = j % 4
                m2 = j % 2
                pst = PS[3].bitcast(BF16)
                for pr in range(4):
                    p.op("pe", lambda e, m=m, pr=pr, pst=pst: e.transpose(pst[:, pr * 128:(pr + 1) * 128], kbf[m][:, pr * 128:(pr + 1) * 128], identb), reads=[Bkbf[m], Bc], writes=[BPS[3]])
                p.op("dve", lambda e, m2=m2, pst=pst: e.tensor_copy(out=KT[m2], in_=pst[:, 0:512].rearrange("p (r s) -> p r s", r=4)), reads=[BPS[3]], writes=[BKT[m2]])
                bank = j // 8
                for pr in range(4):
                    c0 = (j % 8) * 64 + pr * 16
                    p.op("pe", lambda e, m2=m2, pr=pr, bank=bank, c0=c0, b=b: e.matmul(PS[bank][:, c0:c0 + 16], lhsT=KT[m2][:, pr, :], rhs=qbd[:, pr, b, :], start=True, stop=True),
                         reads=[BKT[m2], Bqbd], writes=[BPS[bank]])
            for pr in range(4):
                p.op("pe", lambda e, pr=pr, b=b: e.matmul(PS[2][:, pr * 16:(pr + 1) * 16], lhsT=kTs[:, pr, :], rhs=qbd[:, pr, b, :], start=True, stop=True),
                     reads=[Bqks, Bqbd], writes=[BPS[2]])
            for bank in range(2):
                bias_v = bpast[:, :, half, b8 * 16 + bank * 8:b8 * 16 + bank * 8 + 8].rearrange("p h j -> p j h").unsqueeze(3).to_broadcast([128, 8, 8, 8])
                p.op("dve", lambda e, bank=bank, bias_v=bias_v: e.tensor_tensor(out=sc[:, bank * 8:(bank + 1) * 8, :].rearrange("p j (h t) -> p j h t", h=8),
                                                                           in0=PS[bank].rearrange("p (j h t) -> p j h t", j=8, h=8), in1=bias_v, op=ALU.add),
                     reads=[BPS[bank], Bbpast], writes=[Bsc])
            p.op("dve", lambda e, b=b: e.tensor_tensor(out=sc[:, 16, :].rearrange("p (h t) -> p h t", h=8), in0=PS[2][:, 0:64].rearrange("p (h t) -> p h t", h=8),
                                                  in1=negm_s[:, b * 8:(b + 1) * 8].unsqueeze(1).to_broadcast([128, 8, 8]), op=ALU.add),
                 reads=[BPS[2], Bc, Bsc], writes=[Bsc])
            p.op("dve", lambda e: e.tensor_tensor(out=sc[:, 16, :].rearrange("p (h t) -> p h t", h=8), in0=sc[:, 16, :].rearrange("p (h t) -> p h t", h=8),
                                             in1=bnew.unsqueeze(2).to_broadcast([128, 8, 8]), op=ALU.add),
                 reads=[Bbnew, Bsc], writes=[Bsc])
            p.op("act", lambda e: e.activation(out=pTs.rearrange("p j x -> p (j x)"), in_=sc.rearrange("p j x -> p (j x)"), func=AF.Exp), reads=[Bsc], writes=[BpTs])
            for pr in range(4):
                ob = 4 + pr // 2
                c0 = (pr % 2) * 256 + b * 16
                for j in range(17):
                    p.op("pe", lambda e, pr=pr, j=j, ob=ob, c0=c0: e.matmul(PS[ob][:, c0:c0 + 16], lhsT=Vb[:, j, pr * 128:(pr + 1) * 128], rhs=pTs[:, j, pr * 16:(pr + 1) * 16], start=(j == 0), stop=(j == 16)),
                         reads=[BVb[j], BpTs], writes=[BPS[ob]])
            sbk = 6 + b // 8
            for j in range(17):
                p.op("pe", lambda e, j=j, sbk=sbk, b8=b8: e.matmul(PS[sbk][:, b8 * 64:(b8 + 1) * 64], lhsT=onesb, rhs=pTs[:, j, :], start=(j == 0), stop=(j == 16)),
                     reads=[Bc, BpTs], writes=[BPS[sbk]])
        for hf in range(2):
            p.op("dve", lambda e, hf=hf: e.reciprocal(out=rs[:, 8 * hf:8 * hf + 8, :].rearrange("p b x -> p (b x)"), in_=PS[6 + hf]), reads=[BPS[6 + hf]], writes=[Brs])
        for q2 in range(2):
            p.op("dve", lambda e, q2=q2: e.tensor_tensor(out=ot[:, 2 * q2:2 * q2 + 2, :, :], in0=PS[4 + q2].rearrange("p (r b x) -> p r b x", r=2, b=16),
                                                    in1=rs.rearrange("p b (r x) -> p r b x", r=4)[:, 2 * q2:2 * q2 + 2, :, :], op=ALU.mult),
                 reads=[BPS[4 + q2], Brs], writes=[Bot])
        for h2 in range(2):
            hb = 64 * h2
            p.op("dve", lambda e, h2=h2, hb=hb: e.tensor_tensor(out=yT[hb:hb + 64, 4:8, TP:NTOK].rearrange("p r (b t) -> p r b t", b=16),
                                                           in0=ot[hb:hb + 64, :, :, h2 * 8:(h2 + 1) * 8],
                                                           in1=zsTs[hb:hb + 64, :, :].rearrange("p r (b t) -> p r b t", b=16), op=ALU.mult),
                 reads=[Bot, BzsTs], writes=[ByT[1][16]])
        if stop == "B2":
            break

        p.barrier(); ar.reset()
        cwb = ar.alloc([128, 12, 5], F32); Bcw = Buf("cw")
        Ebc = ar.alloc([128, 16, 128], BF16); blkrow = ar.alloc([128, 16], F32); BE = Buf("Ebc")
        markC = ar.mark()
        cwT = ar.alloc([5, 1536], F32)
        p.dma("sp", lambda e, l=l: e.dma_start(out=cwT[0:4, :], in_=conv_w[l]), writes=[Bcw])
        p.dma("sp", lambda e, l=l: e.dma_start(out=cwT[4:5, :], in_=conv_b[l].rearrange("(o c) -> o c", o=1)), writes=[Bcw])
        for cc in range(12):
            p.op("pe", lambda e, cc=cc: e.transpose(PS[0][:, cc * 8:cc * 8 + 5], cwT[:, cc * 128:(cc + 1) * 128], identf[0:5, 0:5]), reads=[Bcw, Bc], writes=[BPS[0]])
        p.op("dve", lambda e: e.tensor_copy(out=cwb, in_=PS[0][:, 0:96].rearrange("p (c x) -> p c x", x=8)[:, :, 0:5]), reads=[BPS[0]], writes=[Bcw])
        p.op("pool", lambda e: e.memset(Ebc, 1.0), writes=[BE])
        p.op("pool", lambda e: e.affine_select(out=Ebc, in_=Ebc, pattern=[[-8, 16], [1, 128]], compare_op=ALU.is_ge, fill=0.0, base=0, channel_multiplier=0), reads=[BE], writes=[BE])
        p.op("pool", lambda e: e.affine_select(out=Ebc, in_=Ebc, pattern=[[8, 16], [-1, 128]], compare_op=ALU.is_ge, fill=0.0, base=7, channel_multiplier=0), reads=[BE], writes=[BE])
        p.op("pool", lambda e: e.affine_select(out=blkrow, in_=onesf[:, 0:16], pattern=[[-8, 16]], compare_op=ALU.is_ge, fill=0.0, base=0, channel_multiplier=1), reads=[BE, Bc], writes=[BE])
        p.op("pool", lambda e: e.affine_select(out=blkrow, in_=blkrow, pattern=[[8, 16]], compare_op=ALU.is_ge, fill=0.0, base=7, channel_multiplier=-1), reads=[BE], writes=[BE])
        p.dma("sp", lambda e, l=l: e.dma_start(out=par[:, 0:16], in_=dt_bias[l].partition_broadcast(128)), writes=[Bpar])
        p.dma("sp", lambda e, l=l: e.dma_start(out=par[:, 16:32], in_=a_log[l].partition_broadcast(128)), writes=[Bpar])
        p.dma("sp", lambda e, l=l: e.dma_start(out=par[:, 32:48], in_=d_skip[l].partition_broadcast(128)), writes=[Bpar])
        p.op("act", lambda e: e.activation(out=par[:, 16:32], in_=par[:, 16:32], func=AF.Exp), reads=[Bpar], writes=[Bpar])
        p.op("dve", lambda e: e.tensor_scalar(out=par[:, 16:32], in0=par[:, 16:32], scalar1=-1.0, scalar2=None, op0=ALU.mult), reads=[Bpar], writes=[Bpar])

        if stop == "C0":
            break
        for g in range(int(os.environ.get("NGC", 2))):
            p.barrier(); ar.reset(markC)
            Wc = ar.alloc([128, 8, 1280], BF16); BWc = Buf("Wc")
            Wdt = ar.alloc([128, 8, 8], BF16); BWdt = Buf("Wdt")
            snb = ar.alloc([128, 512], F32); Bsnb = Buf("snb")
            stT = ar.alloc([128, 512], F32); stTb = ar.alloc([128, 512], BF16); BstT = Buf("stT")
            xr = ar.alloc([128, 6 * 176], F32); Bxraw = Buf("xraw"); Bxraw_s = Bxraw
            xraw = xr[:, 0:786].rearrange("p (c t) -> p c t", c=6)
            xraw_s = xr.rearrange("p (c b t) -> p c b t", c=6, b=16)
            acc = ar.alloc([128, 6, 128], F32); Baccs = bufs("acc", 6)
            xc = acc[:, 0:4, :]
            R3 = ar.alloc([48, 768], F32); BR3 = Buf("R3")
            BCt = ar.alloc([128, 2, 128], BF16); BBCt = Buf("BCt")
            x_tm = ar.alloc([128, 512], F32); xdt = ar.alloc([128, 512], BF16); xw = ar.alloc([128, 512], BF16); Bxtm = Buf("xtm")
            Btm = ar.alloc([128, 128], BF16); BBtm = Buf("Btm")
            sma = ar.alloc([128, 7, 17, 8], F32); Bsm = Buf("sm")
            negcm = ar.alloc([128, 8, 128], F32); Bdtab = Buf("dtab")
            seg0 = ar.alloc([128, 512], F32); seg = [seg0, seg0]; Bseg0 = Buf("seg"); Bseg = [Bseg0, Bseg0]
            MixT = [ar.alloc([128, 4, 128], BF16) for _ in range(2)]; BMix = bufs("Mix", 2)
            cbT = ar.alloc([128, 128], F32); BcbT = Buf("cbT")
            t1 = ar.alloc([128, 512], F32); Bt1 = Buf("t1")
            yc = ar.alloc([128, 512], F32); Byc = Buf("yc")
            zsc = ar.alloc([128, 512], F32); Bzsc = Buf("zsc")
            ycb = ar.alloc([128, 512], BF16); Bycb = Buf("ycb")
            ssq = ar.alloc([128, 4], F32); Bssq = Buf("ssq")
            CzT = ar.alloc([128, 16, 128], BF16); Bz = ar.alloc([128, 16, 128], BF16); BCz = Buf("CzT")
            elb = ar.alloc([128, 16, 8], F32); cumexp = ar.alloc([128, 16, 8], F32); Belb = Buf("elb")
            stin = [ar.alloc([128, 4, 128], F32) for _ in range(2)]; Bstin = bufs("stin", 2)
            sts = [ar.alloc([128, 512], F32) for _ in range(2)]; stsb = [ar.alloc([128, 512], BF16) for _ in range(2)]; Bsts = bufs("sts", 2)
            stout = stin; Bstout = Bstin

            chans = [g * 512 + i * 128 for i in range(4)] + [1024 + g * 128, 1280 + g * 128]
            load_w(Wc[:, :, 0:512], w_in[l][:, C_XBC + g * 512:C_XBC + (g + 1) * 512], BWc)
            load_w(Wc[:, :, 512:640], w_in[l][:, C_XBC + 1024 + g * 128:C_XBC + 1024 + (g + 1) * 128], BWc)
            load_w(Wc[:, :, 640:768], w_in[l][:, C_XBC + 1280 + g * 128:C_XBC + 1280 + (g + 1) * 128], BWc)
            load_w(Wc[:, :, 768:1280], w_in[l][:, C_ZC + g * 512:C_ZC + (g + 1) * 512], BWc)
            load_w(Wdt, w_in[l][:, C_DT + g * 8:C_DT + (g + 1) * 8], BWdt)
            p.dma("sp", lambda e, l=l, g=g: e.dma_start(out=snb, in_=ssm_norm[l, g * 512:(g + 1) * 512].partition_broadcast(128)), writes=[Bsnb])
            p.op("pool", lambda e: e.memset(stT, 0.0), writes=[BstT])
            p.op("dve", lambda e: e.memset(stTb, 0.0), reads=[BstT], writes=[BstT])
            p.op("pool", lambda e: e.memset(xraw[:, :, 0:3], 0.0), writes=[Bxraw])
            dtb = par[:, g * 8:(g + 1) * 8]; a_bc = par[:, 16 + g * 8:16 + (g + 1) * 8]; dsk = par[:, 32 + g * 8:32 + (g + 1) * 8]

            for j in range(NT):
                proj_tm(Wdt, BWdt, 0, 8, j, PS[3][:, j * 8:(j + 1) * 8], BPS[3])
            row = lambda r: sma[:, r, :, :].rearrange("p j h -> p (j h)")
            p.op("dve", lambda e, dtb=dtb: e.tensor_tensor(out=sma[:, 0, :, :], in0=PS[3][:, 0:136].rearrange("p (j h) -> p j h", h=8), in1=dtb.unsqueeze(1).to_broadcast([128, 17, 8]), op=ALU.add),
                 reads=[BPS[3], Bpar], writes=[Bsm])
            p.op("act", lambda e, row=row: e.activation(out=row(0), in_=row(0), func=AF.Exp), reads=[Bsm], writes=[Bsm])
            p.op("act", lambda e, row=row: e.activation(out=row(0), in_=row(0), func=AF.Ln, bias=1.0), reads=[Bsm], writes=[Bsm])
            p.op("dve", lambda e, a_bc=a_bc: e.tensor_tensor(out=sma[:, 1, :, :], in0=sma[:, 0, :, :], in1=a_bc.unsqueeze(1).to_broadcast([128, 17, 8]), op=ALU.mult), reads=[Bsm, Bpar], writes=[Bsm])
            p.op("pe", lambda e, row=row: e.matmul(PS[3][:, 256:384], lhsT=trif, rhs=row(1)[:, 0:128], start=True, stop=True), reads=[Bsm, Bc], writes=[BPS[3]])
            p.op("pe", lambda e, row=row: e.matmul(PS[3][:, 384:392], lhsT=btrif, rhs=row(1)[:, 128:136], start=True, stop=True), reads=[Bsm, Bc], writes=[BPS[3]])
            p.op("dve", lambda e, row=row: e.tensor_copy(out=row(2), in_=PS[3][:, 256:392]), reads=[BPS[3]], writes=[Bsm])
            p.op("pe", lambda e, row=row: e.matmul(PS[3][:, 0:128], lhsT=sel127, rhs=row(2)[:, 0:128], start=True, stop=True), reads=[Bsm, Bc], writes=[BPS[3]])
            p.op("pe", lambda e, row=row: e.matmul(PS[3][:, 128:136], lhsT=L2s, rhs=row(2)[:, 128:136], start=True, stop=True), reads=[Bsm, Bc], writes=[BPS[3]])
            p.op("dve", lambda e, row=row: e.tensor_copy(out=row(6), in_=PS[3][:, 0:136]), reads=[BPS[3]], writes=[Bsm])
            p.op("act", lambda e, row=row: e.activation(out=row(3), in_=row(2), func=AF.Exp), reads=[Bsm], writes=[Bsm])
            p.op("dve", lambda e, row=row: e.tensor_tensor(out=row(4), in0=row(6), in1=row(2), op=ALU.subtract), reads=[Bsm], writes=[Bsm])
            p.op("act", lambda e, row=row: e.activation(out=row(4), in_=row(4), func=AF.Exp), reads=[Bsm], writes=[Bsm])
            p.op("act", lambda e, row=row: e.activation(out=row(5), in_=row(6), func=AF.Exp), reads=[Bsm], writes=[Bsm])
            for j in range(int(os.environ.get("NTC", NT))):
                tok0 = j * 128
                smp = (j == 16)
                TRI = btrif if smp else trif
                LAST = L2s if smp else sel127
                NEGM = negm_s if smp else negm_p
                sm = sma[:, :, j, :]
                for cc in range(6):
                    pb_, c0 = (0, cc * 128) if cc < 4 else (1, (cc - 4) * 128)
                    proj_fm(Wc, BWc, cc * 128, 128, tok0, 128, PS[pb_][:, c0:c0 + 128], BPS[pb_])
                for hf in range(1):
                    proj_tm(Wc, BWc, 768, 512, j, PS[7], BPS[7])
                p.op("act", lambda e: e.activation(out=zsc, in_=PS[7], func=AF.Silu), reads=[BPS[7]], writes=[Bzsc])
                if not smp:
                    p.op("act", lambda e: e.copy(out=xraw[:, 0:4, 3:131], in_=PS[0].rearrange("p (c t) -> p c t", c=4)), reads=[BPS[0]], writes=[Bxraw])
                    p.op("act", lambda e: e.copy(out=xraw[:, 4:6, 3:131], in_=PS[1][:, 0:256].rearrange("p (c t) -> p c t", c=2)), reads=[BPS[1]], writes=[Bxraw])
                    src = lambda cc, tap: xraw[:, cc, tap:tap + 128]
                    accv = lambda cc: acc[:, cc, :]
                    Bsrc = Bxraw
                else:
                    for cc in range(6):
                        p.dma("sp", lambda e, l=l, cc=cc, c0=chans[cc]: e.dma_start(out=R3[:, cc * 128:(cc + 1) * 128], in_=sconv[l][:, :, c0:c0 + 128].rearrange("b j c -> (b j) c")), writes=[BR3])
                    for cc in range(6):
                        p.op("pe", lambda e, cc=cc: e.transpose(PS[3][:, cc * 48:(cc + 1) * 48], R3[:, cc * 128:(cc + 1) * 128], identf[0:48, 0:48]), reads=[BR3, Bc], writes=[BPS[3]])
                    p.op("act", lambda e: e.copy(out=xraw_s[:, :, :, 0:3], in_=PS[3][:, 0:288].rearrange("p (c b j) -> p c b j", c=6, b=16)), reads=[BPS[3]], writes=[Bxraw_s])
                    p.op("act", lambda e: e.copy(out=xraw_s[:, 0:4, :, 3:11], in_=PS[0].rearrange("p (c b t) -> p c b t", c=4, b=16)), reads=[BPS[0]], writes=[Bxraw_s])
                    p.op("act", lambda e: e.copy(out=xraw_s[:, 4:6, :, 3:11], in_=PS[1][:, 0:256].rearrange("p (c b t) -> p c b t", c=2, b=16)), reads=[BPS[1]], writes=[Bxraw_s])
                    src = lambda cc, tap: xraw_s[:, cc, :, tap:tap + 8]
                    accv = lambda cc: acc[:, cc, :].rearrange("p (b t) -> p b t", b=16)
                    Bsrc = Bxraw_s
                for cc in range(6):
                    ch = chans[cc] // 128
                    p.op("dve", lambda e, cc=cc, ch=ch, src=src, accv=accv: e.tensor_scalar(out=accv(cc), in0=src(cc, 3), scalar1=cwb[:, ch, 3:4], scalar2=cwb[:, ch, 4:5], op0=ALU.mult, op1=ALU.add),
                         reads=[Bsrc, Bcw], writes=[Baccs[cc]])
                for tap in range(3):
                    for cc in range(6):
                        ch = chans[cc] // 128
                        p.op("dve", lambda e, cc=cc, ch=ch, tap=tap, src=src, accv=accv: e.scalar_tensor_tensor(out=accv(cc), in0=src(cc, tap), scalar=cwb[:, ch, tap:tap + 1], in1=accv(cc), op0=ALU.mult, op1=ALU.add),
                             reads=[Bsrc, Bcw, Baccs[cc]], writes=[Baccs[cc]])
                p.op("act", lambda e: e.activation(out=xc, in_=acc[:, 0:4, :], func=AF.Silu), reads=Baccs[0:4], writes=Baccs[0:4])
                p.op("act", lambda e: e.activation(out=BCt, in_=acc[:, 4:6, :], func=AF.Silu), reads=Baccs[4:6], writes=[BBCt])
                if not smp:
                    if j < 15:
                        p.op("pool", lambda e: e.tensor_copy(out=xraw[:, :, 0:3], in_=xraw[:, :, 128:131]), reads=[Bxraw], writes=[Bxraw])
                    else:
                        for cc in range(6):
                            p.op("pe", lambda e, cc=cc: e.transpose(PS[3][0:3, cc * 128:(cc + 1) * 128] if cc < 4 else PS[2][0:3, (cc - 4) * 128:(cc - 3) * 128], xraw[:, cc, 128:131], identf), reads=[Bxraw, Bc], writes=[BPS[3] if cc < 4 else BPS[2]])
                        p.op("dve", lambda e: e.tensor_copy(out=R3[0:3, 0:512], in_=PS[3][0:3, 0:512]), reads=[BPS[3]], writes=[BR3])
                        p.op("dve", lambda e: e.tensor_copy(out=R3[0:3, 512:768], in_=PS[2][0:3, 0:256]), reads=[BPS[2]], writes=[BR3])
                        for cc in range(6):
                            p.dma("sp", lambda e, l=l, cc=cc, c0=chans[cc]: e.dma_start(out=convp[l][:, c0:c0 + 128], in_=R3[0:3, cc * 128:(cc + 1) * 128]), reads=[BR3])
                else:
                    p.op("pool", lambda e: e.tensor_copy(out=t1[:, 0:288].rearrange("p (c b t) -> p c b t", c=6, b=16), in_=xraw_s[:, :, :, 8:11]), reads=[Bxraw_s], writes=[Bt1])
                    for cc in range(6):
                        p.op("pe", lambda e, cc=cc: e.transpose(PS[3][0:48, (cc % 4) * 128:(cc % 4 + 1) * 128] if cc < 4 else PS[2][0:48, (cc - 4) * 128:(cc - 3) * 128],
                                                                 t1[:, cc * 48:(cc + 1) * 48], identf), reads=[Bt1, Bc], writes=[BPS[3] if cc < 4 else BPS[2]])
                    p.op("dve", lambda e: e.tensor_copy(out=R3[:, 0:512], in_=PS[3][0:48, 0:512]), reads=[BPS[3]], writes=[BR3])
                    p.op("dve", lambda e: e.tensor_copy(out=R3[:, 512:768], in_=PS[2][0:48, 0:256]), reads=[BPS[2]], writes=[BR3])
                    for cc in range(6):
                        p.dma("sp", lambda e, l=l, cc=cc, c0=chans[cc]: e.dma_start(out=convs[l][:, :, c0:c0 + 128].rearrange("b t c -> (b t) c"), in_=R3[:, cc * 128:(cc + 1) * 128]), reads=[BR3])
                for cc in range(4):
                    p.op("pe", lambda e, cc=cc: e.transpose(PS[2][:, cc * 128:(cc + 1) * 128], xc[:, cc, :], identf), reads=[Baccs[cc], Bc], writes=[BPS[2]])
                p.op("act", lambda e: e.copy(out=x_tm, in_=PS[2]), reads=[BPS[2]], writes=[Bxtm])
                p.op("dve", lambda e, sm=sm: e.tensor_tensor(out=xdt.rearrange("p (h d) -> p h d", h=8), in0=x_tm.rearrange("p (h d) -> p h d", h=8), in1=sm[:, 0, :].unsqueeze(2).to_broadcast([128, 8, 64]), op=ALU.mult),
                     reads=[Bxtm, Bsm], writes=[Bxtm])
                p.op("dve", lambda e, sm=sm: e.tensor_tensor(out=xw.rearrange("p (h d) -> p h d", h=8), in0=xdt.rearrange("p (h d) -> p h d", h=8), in1=sm[:, 4, :].unsqueeze(2).to_broadcast([128, 8, 64]), op=ALU.mult),
                     reads=[Bxtm, Bsm], writes=[Bxtm])
                pst3 = PS[3].bitcast(BF16)
                p.op("pe", lambda e, pst3=pst3: e.transpose(pst3[:, 512:640], BCt[:, 0, :], identb), reads=[BBCt, Bc], writes=[BPS[3]])
                p.op("dve", lambda e, pst3=pst3: e.tensor_copy(out=Btm, in_=pst3[:, 512:640]), reads=[BPS[3]], writes=[BBtm])
                p.op("pe", lambda e: e.matmul(PS[3][:, 128:256], lhsT=BCt[:, 0, :], rhs=BCt[:, 1, :], start=True, stop=True), reads=[BBCt], writes=[BPS[3]])
                p.op("dve", lambda e: e.tensor_copy(out=cbT, in_=PS[3][:, 128:256]), reads=[BPS[3]], writes=[BcbT])
                p.op("pool", lambda e, sm=sm, NEGM=NEGM: e.tensor_tensor(out=negcm, in0=NEGM.unsqueeze(1).to_broadcast([128, 8, 128]), in1=sm[:, 2, :].unsqueeze(2).to_broadcast([128, 8, 128]), op=ALU.subtract),
                     reads=[Bsm, Bc, Bdtab], writes=[Bdtab])
                for hh in range(2):
                    for i in range(4):
                        h = hh * 4 + i
                        p.op("pe", lambda e, i=i, h=h, TRI=TRI, sm=sm: e.matmul(PS[4][:, i * 128:(i + 1) * 128], lhsT=sm[:, 1, h:h + 1].to_broadcast([128, 128]), rhs=TRI, start=True, stop=True), reads=[Bsm, Bc], writes=[BPS[4]])
                    p.op("dve", lambda e, hh=hh: e.tensor_tensor(out=seg[hh], in0=PS[4], in1=negcm[:, hh * 4:(hh + 1) * 4, :].rearrange("p h t -> p (h t)"), op=ALU.add),
                         reads=[BPS[4], Bdtab], writes=[Bseg[hh]])
                    p.op("act", lambda e, hh=hh: e.activation(out=seg[hh], in_=seg[hh], func=AF.Exp), reads=[Bseg[hh]], writes=[Bseg[hh]])
                    p.op("dve", lambda e, hh=hh: e.tensor_tensor(out=MixT[hh], in0=seg[hh].rearrange("p (h t) -> p h t", h=4), in1=cbT.unsqueeze(1).to_broadcast([128, 4, 128]), op=ALU.mult),
                         reads=[Bseg[hh], BcbT], writes=[BMix[hh]])
                    for i in range(4):
                        h = hh * 4 + i
                        p.op("pe", lambda e, i=i, h=h, hh=hh: e.matmul(PS[5][:, h * 64:(h + 1) * 64], lhsT=MixT[hh][:, i, :], rhs=xdt[:, h * 64:(h + 1) * 64], start=True, stop=True),
                             reads=[BMix[hh], Bxtm], writes=[BPS[5]])
                if not smp:
                    p.op("pe", lambda e: e.matmul(PS[6], lhsT=BCt[:, 1, :], rhs=stTb, start=True, stop=True), reads=[BBCt, BstT], writes=[BPS[6]])
                    p.op("pe", lambda e: e.matmul(PS[7], lhsT=Btm, rhs=xw, start=True, stop=True), reads=[BBtm, Bxtm], writes=[BPS[7]])
                    p.op("dve", lambda e, sm=sm: e.tensor_tensor(out=stT.rearrange("p (h d) -> p h d", h=8), in0=stT.rearrange("p (h d) -> p h d", h=8), in1=sm[:, 5, :].unsqueeze(2).to_broadcast([128, 8, 64]), op=ALU.mult),
                         reads=[BstT, Bsm, BPS[6]], writes=[BstT])
                    p.op("dve", lambda e: e.tensor_tensor(out=stT, in0=stT, in1=PS[7], op=ALU.add), reads=[BstT, BPS[7]], writes=[BstT])
                    p.op("act", lambda e: e.copy(out=stTb, in_=stT), reads=[BstT], writes=[BstT])
                    if j == 15:
                        for cc in range(4):
                            p.op("pe", lambda e, cc=cc: e.transpose(PS[7][:, cc * 128:(cc + 1) * 128], stT[:, cc * 128:(cc + 1) * 128], identf), reads=[BstT, Bc], writes=[BPS[7]])
                        p.op("act", lambda e: e.copy(out=stout[0], in_=PS[7].rearrange("p (c n) -> p c n", c=4)), reads=[BPS[7]], writes=[Bstout[0]])
                        p.dma("sp", lambda e, l=l, g=g: e.dma_start(out=ssmp[l, g * 512:(g + 1) * 512, :].rearrange("(c p) n -> p c n", p=128), in_=stout[0]), reads=[Bstout[0]])
                else:
                    p.op("dve", lambda e: e.tensor_tensor(out=CzT, in0=BCt[:, 1, :].unsqueeze(1).to_broadcast([128, 16, 128]), in1=Ebc, op=ALU.mult), reads=[BBCt, BE], writes=[BCz])
                    p.op("dve", lambda e: e.tensor_tensor(out=Bz, in0=Btm.unsqueeze(1).to_broadcast([128, 16, 128]), in1=blkrow.unsqueeze(2).to_broadcast([128, 16, 128]), op=ALU.mult), reads=[BBtm, BE], writes=[BCz])
                    p.op("dve", lambda e, sm=sm: e.tensor_tensor(out=cumexp, in0=sm[:, 2, :].unsqueeze(1).to_broadcast([128, 16, 8]), in1=lastmask.unsqueeze(2).to_broadcast([128, 16, 8]), op=ALU.mult), reads=[Bsm, Bc], writes=[Belb])
                    p.op("pe", lambda e: e.matmul(PS[3][:, 256:384], lhsT=onesf, rhs=cumexp.rearrange("p b h -> p (b h)"), start=True, stop=True), reads=[Belb, Bc], writes=[BPS[3]])
                    p.op("act", lambda e: e.activation(out=elb.rearrange("p b h -> p (b h)"), in_=PS[3][:, 256:384], func=AF.Exp), reads=[BPS[3]], writes=[Belb])
                    for b in range(NSEQ):
                        m = b % 2
                        p.dma("sp", lambda e, l=l, g=g, b=b, m=m: e.dma_start(out=stin[m], in_=sssm[l, b, g * 512:(g + 1) * 512, :].rearrange("(c p) n -> p c n", p=128)), writes=[Bstin[m]])
                        for cc in range(4):
                            p.op("pe", lambda e, cc=cc, m=m: e.transpose(PS[7][:, cc * 128:(cc + 1) * 128], stin[m][:, cc, :], identf), reads=[Bstin[m], Bc], writes=[BPS[7]])
                        p.op("act", lambda e, m=m: e.copy(out=sts[m], in_=PS[7]), reads=[BPS[7]], writes=[Bsts[m]])
                        p.op("dve", lambda e, m=m: e.tensor_copy(out=stsb[m], in_=sts[m]), reads=[Bsts[m]], writes=[Bsts[m]])
                        p.op("pe", lambda e, b=b, m=m: e.matmul(PS[6], lhsT=CzT[:, b, :], rhs=stsb[m], start=(b == 0), stop=(b == NSEQ - 1)), reads=[BCz, Bsts[m]], writes=[BPS[6]])
                        p.op("pe", lambda e, b=b: e.matmul(PS[0], lhsT=Bz[:, b, :], rhs=xw, start=True, stop=True), reads=[BCz, Bxtm], writes=[BPS[0]])
                        p.op("dve", lambda e, b=b, m=m: e.tensor_tensor(out=sts[m].rearrange("p (h d) -> p h d", h=8), in0=sts[m].rearrange("p (h d) -> p h d", h=8), in1=elb[:, b, :].unsqueeze(2).to_broadcast([128, 8, 64]), op=ALU.mult),
                             reads=[Bsts[m], Belb], writes=[Bsts[m]])
                        p.op("dve", lambda e, m=m: e.tensor_tensor(out=sts[m], in0=sts[m], in1=PS[0], op=ALU.add), reads=[Bsts[m], BPS[0]], writes=[Bsts[m]])
                        for cc in range(4):
                            p.op("pe", lambda e, cc=cc, m=m: e.transpose(PS[1][:, cc * 128:(cc + 1) * 128], sts[m][:, cc * 128:(cc + 1) * 128], identf), reads=[Bsts[m], Bc], writes=[BPS[1]])
                        p.op("act", lambda e, m=m: e.copy(out=stout[m], in_=PS[1].rearrange("p (c n) -> p c n", c=4)), reads=[BPS[1]], writes=[Bstout[m]])
                        p.dma("sp", lambda e, l=l, g=g, b=b, m=m: e.dma_start(out=ssms[l, b, g * 512:(g + 1) * 512, :].rearrange("(c p) n -> p c n", p=128), in_=stout[m]), reads=[Bstout[m]])
                p.op("dve", lambda e, sm=sm: e.tensor_tensor(out=t1.rearrange("p (h d) -> p h d", h=8), in0=PS[6].rearrange("p (h d) -> p h d", h=8), in1=sm[:, 3, :].unsqueeze(2).to_broadcast([128, 8, 64]), op=ALU.mult),
                     reads=[BPS[6], Bsm], writes=[Bt1])
                p.op("dve", lambda e: e.tensor_tensor(out=yc, in0=PS[5], in1=t1, op=ALU.add), reads=[BPS[5], Bt1], writes=[Byc])
                p.op("pool", lambda e, dsk=dsk: e.tensor_tensor(out=t1.rearrange("p (h d) -> p h d", h=8), in0=x_tm.rearrange("p (h d) -> p h d", h=8), in1=dsk.unsqueeze(2).to_broadcast([128, 8, 64]), op=ALU.mult),
                     reads=[Bxtm, Bpar, Byc], writes=[Bt1])
                p.op("pool", lambda e: e.tensor_tensor(out=yc, in0=yc, in1=t1, op=ALU.add), reads=[Bt1, Byc], writes=[Byc])
                p.op("pool", lambda e: e.tensor_tensor(out=yc, in0=yc, in1=zsc, op=ALU.mult), reads=[Bzsc, Byc], writes=[Byc])
                p.op("pool", lambda e: e.memset(ssq, 0.0), writes=[Bssq])
                p.op("act", lambda e: e.activation(out=zsc, in_=yc, func=AF.Square, accum_out=ssq[:, 0:1]), reads=[Byc, Bssq], writes=[Bzsc, Bssq])
                p.op("act", lambda e: e.activation(out=ssq[:, 1:2], in_=ssq[:, 0:1], func=AF.Ln, bias=EPS, scale=1.0 / 512), reads=[Bssq], writes=[Bssq])
                p.op("act", lambda e: e.activation(out=ssq[:, 2:3], in_=ssq[:, 1:2], func=AF.Exp, scale=-0.5), reads=[Bssq], writes=[Bssq])
                p.op("dve", lambda e: e.scalar_tensor_tensor(out=ycb, in0=yc, scalar=ssq[:, 2:3], in1=snb, op0=ALU.mult, op1=ALU.mult), reads=[Byc, Bssq, Bsnb], writes=[Bycb])
                pst7 = PS[7].bitcast(BF16)
                for cc in range(4):
                    p.op("pe", lambda e, cc=cc, pst7=pst7: e.transpose(pst7[:, cc * 128:(cc + 1) * 128], ycb[:, cc * 128:(cc + 1) * 128], identb), reads=[Bycb, Bc], writes=[BPS[7]])
                p.op("dve", lambda e, pst7=pst7, g=g, tok0=tok0: e.tensor_copy(out=yT[:, 8 + g * 4:12 + g * 4, tok0:tok0 + 128], in_=pst7[:, 0:512].rearrange("p (c t) -> p c t", c=4)),
                     reads=[BPS[7]], writes=[ByT[2][j]])
        if stop == "C":
            break

        p.barrier(); ar.reset()
        mT = ar.alloc([128, 8, NTOK], BF16); BmT = bufs("mT", NT)
        markM = ar.mark()
        wM = [ar.alloc([128, 40, 128], BF16) for _ in range(2)]; BwM = bufs("wM", 2)
        sg = [ar.alloc([128, 512], F32) for _ in range(2)]; Bsg = bufs("sg", 2)
        accm = [ar.alloc([128, 512], F32) for _ in range(2)]; Baccm = bufs("accm", 2)
        tmpm = [ar.alloc([128, 512], F32) for _ in range(2)]; Btmpm = bufs("tmpm", 2)
        KOFF = (0, 4, 8); KCN = (4, 4, 8)
        WSRC = (w_a, w_b, w_c)
        it = 0
        for nj in range(8):
            k = nj % 2
            for br in range(3):
                load_w(wM[k][:, KOFF[br]:KOFF[br] + KCN[br], :], WSRC[br][l][:, nj * 128:(nj + 1) * 128], BwM[k])
                load_w(wM[k][:, 16 + 8 * br:24 + 8 * br, :], w_in[l][:, C_G + br * 1024 + nj * 128:C_G + br * 1024 + (nj + 1) * 128], BwM[k])
            for tg in range(5):
                tok0 = tg * 512
                ntok = 512 if tg < 4 else 128
                a2 = it % 2
                for br in range(3):
                    pp_, pg_ = it % 3, 3 + it % 3
                    s2 = it % 2
                    for kc in range(KCN[br]):
                        p.op("pe", lambda e, k=k, br=br, kc=kc, pp_=pp_, tok0=tok0, ntok=ntok: e.matmul(PS[pp_][:, 0:ntok], lhsT=wM[k][:, KOFF[br] + kc, :], rhs=yT[:, YOFF[br] + kc, tok0:tok0 + ntok],
                                                                                                       start=(kc == 0), stop=(kc == KCN[br] - 1)),
                             reads=[BwM[k]] + tile_bufs(ByT[br], tok0, ntok), writes=[BPS[pp_]])
                    for kc in range(8):
                        p.op("pe", lambda e, k=k, br=br, kc=kc, pg_=pg_, tok0=tok0, ntok=ntok: e.matmul(PS[pg_][:, 0:ntok], lhsT=wM[k][:, 16 + 8 * br + kc, :], rhs=hnT[:, kc, tok0:tok0 + ntok],
                                                                                                       start=(kc == 0), stop=(kc == 7)),
                             reads=[BwM[k]] + tile_bufs(BhnT, tok0, ntok), writes=[BPS[pg_]])
                    p.op("act", lambda e, pg_=pg_, s2=s2, ntok=ntok: e.activation(out=sg[s2][:, 0:ntok], in_=PS[pg_][:, 0:ntok], func=AF.Sigmoid), reads=[BPS[pg_]], writes=[Bsg[s2]])
                    if br == 0:
                        p.op("dve", lambda e, pp_=pp_, s2=s2, a2=a2, ntok=ntok: e.tensor_tensor(out=accm[a2][:, 0:ntok], in0=PS[pp_][:, 0:ntok], in1=sg[s2][:, 0:ntok], op=ALU.mult),
                             reads=[BPS[pp_], Bsg[s2]], writes=[Baccm[a2]])
                    else:
                        p.op("dve", lambda e, pp_=pp_, s2=s2, a2=a2, ntok=ntok: e.tensor_tensor(out=tmpm[a2][:, 0:ntok], in0=PS[pp_][:, 0:ntok], in1=sg[s2][:, 0:ntok], op=ALU.mult),
                             reads=[BPS[pp_], Bsg[s2]], writes=[Btmpm[a2]])
                        if br == 1:
                            p.op("pool", lambda e, a2=a2, ntok=ntok: e.tensor_tensor(out=accm[a2][:, 0:ntok], in0=accm[a2][:, 0:ntok], in1=tmpm[a2][:, 0:ntok], op=ALU.add),
                                 reads=[Btmpm[a2], Baccm[a2]], writes=[Baccm[a2]])
                        else:
                            p.op("pool", lambda e, a2=a2, nj=nj, tok0=tok0, ntok=ntok: e.tensor_tensor(out=mT[:, nj, tok0:tok0 + ntok], in0=accm[a2][:, 0:ntok], in1=tmpm[a2][:, 0:ntok], op=ALU.add),
                                 reads=[Btmpm[a2], Baccm[a2]], writes=tile_bufs(BmT, tok0, ntok))
                    it += 1
        p.barrier(); ar.reset(markM)
        Wo = ar.alloc([128, 8, 1024], BF16); BWo = Buf("Wo")
        npost = ar.alloc([128, D], F32); Bnpost = Buf("npost")
        xt2 = [ar.alloc([128, D], F32) for _ in range(2)]; Bxt2 = bufs("xt2", 2)
        rr = [ar.alloc([128, D], F32) for _ in range(2)]; Brr = bufs("rr", 2)
        hb2 = [ar.alloc([128, D], BF16) for _ in range(2)]; Bhb2 = bufs("hb2", 2)
        sq2 = [ar.alloc([128, 4], F32) for _ in range(2)]; Bsq2 = bufs("sq2", 2)
        ss2 = [ar.alloc([128, 8], F32) for _ in range(2)]; Bss2 = bufs("ss2", 2)
        junk = ar.alloc([128, 512], F32); Bjunk = Buf("junk")
        load_w(Wo[:, :, 0:512], w_out[l][:, 0:512], BWo)
        load_w(Wo[:, :, 512:1024], w_out[l][:, 512:1024], BWo)
        p.dma("sp", lambda e, l=l: e.dma_start(out=npost, in_=norm_post[l].partition_broadcast(128)), writes=[Bnpost])
        if l == 0:
            p.dma("sp", lambda e: e.dma_start(out=npre_bc, in_=norm_pre[1].partition_broadcast(128)), writes=[Bnpre])
        for j in range(NT):
            k = j % 2
            if l == 0:
                srcx = xp[j * 128:(j + 1) * 128, :] if j < 16 else xs
                p.dma("sp", lambda e, k=k, srcx=srcx: e.dma_start(out=xt2[k], in_=srcx), writes=[Bxt2[k]])
            else:
                p.dma("sp", lambda e, k=k, j=j: e.dma_start(out=xt2[k], in_=x1[j * 128:(j + 1) * 128, :]), reads=[Bx1[j]], writes=[Bxt2[k]])
            for hf in range(2):
                for kc in range(8):
                    p.op("pe", lambda e, j=j, hf=hf, kc=kc: e.matmul(PS[6 + hf], lhsT=mT[:, kc, j * 128:(j + 1) * 128], rhs=Wo[:, kc, hf * 512:(hf + 1) * 512], start=(kc == 0), stop=(kc == 7)),
                         reads=[BmT[j], BWo], writes=[BPS[6 + hf]])
            p.op("pool", lambda e, k=k: e.memset(ss2[k], 0.0), writes=[Bss2[k]])
            for hf in range(2):
                p.op("act", lambda e, k=k, hf=hf: e.activation(out=junk, in_=PS[6 + hf], func=AF.Square, accum_out=ss2[k][:, hf:hf + 1]), reads=[BPS[6 + hf], Bss2[k]], writes=[Bjunk, Bss2[k]])
            p.op("dve", lambda e, k=k: e.tensor_tensor(out=ss2[k][:, 2:3], in0=ss2[k][:, 0:1], in1=ss2[k][:, 1:2], op=ALU.add), reads=[Bss2[k]], writes=[Bss2[k]])
            p.op("act", lambda e, k=k: e.activation(out=ss2[k][:, 3:4], in_=ss2[k][:, 2:3], func=AF.Ln, bias=EPS, scale=1.0 / D), reads=[Bss2[k]], writes=[Bss2[k]])
            p.op("act", lambda e, k=k: e.activation(out=ss2[k][:, 4:5], in_=ss2[k][:, 3:4], func=AF.Exp, scale=-0.5), reads=[Bss2[k]], writes=[Bss2[k]])
            for hf in range(2):
                p.op("dve", lambda e, k=k, hf=hf: e.scalar_tensor_tensor(out=rr[k][:, hf * 512:(hf + 1) * 512], in0=PS[6 + hf], scalar=ss2[k][:, 4:5], in1=npost[:, hf * 512:(hf + 1) * 512], op0=ALU.mult, op1=ALU.mult),
                     reads=[BPS[6 + hf], Bss2[k], Bnpost], writes=[Brr[k]])
            p.op("pool", lambda e, k=k: e.tensor_tensor(out=rr[k], in0=rr[k], in1=xt2[k], op=ALU.add), reads=[Bxt2[k], Brr[k]], writes=[Brr[k]])
            if l == 0:
                p.dma("sp", lambda e, k=k, j=j: e.dma_start(out=x1[j * 128:(j + 1) * 128, :], in_=rr[k]), reads=[Brr[k]], writes=[Bx1[j]], owner=Brr[k])
                norm_to_hnT(rr[k], Brr[k], j, sq2[k], Bsq2[k], hb2[k], Bhb2[k], 4 + k)
            else:
                dst = yp[j * 128:(j + 1) * 128, :] if j < 16 else ys
                p.dma("sp", lambda e, k=k, dst=dst: e.dma_start(out=dst, in_=rr[k]), reads=[Brr[k]])

    dbg_dump("hnT", hnT[:, :, :], BhnT)
    dbg_dump("yT", yT[:, :, :], ByT[0] + ByT[1] + ByT[2])
    p.emit()
    return nc


def make_in_maps(inputs, n_cores=8):
    g = {k: np.asarray(v) for k, v in inputs.items()}
    n_pool = g["cache_k"].shape[1]
    ckf = np.ascontiguousarray(g["cache_k"]).reshape(DEPTH, n_pool * 128, 512)
    cvf = np.ascontiguousarray(g["cache_v"]).reshape(DEPTH, n_pool * 128, 512)
    clff = np.ascontiguousarray(g["cache_logf"]).reshape(DEPTH, n_pool, 1024)
    maps = []
    for c in range(n_cores):
        sl = slice(NSEQ * c, NSEQ * (c + 1))
        m = dict(
            xp=np.ascontiguousarray(g["x_prompt"][c]), xs=np.ascontiguousarray(g["x_sample"][sl]).reshape(128, D),
            ck=ckf, cv=cvf, clf=clff,
            spool=np.ascontiguousarray(g["state_pool"][:, sl]), sconv=np.ascontiguousarray(g["state_conv"][:, sl]),
            sssm=np.ascontiguousarray(g["state_ssm"][:, sl]).reshape(DEPTH, NSEQ, 1024, 128),
            pt=np.ascontiguousarray(g["page_table"][sl]).astype(np.int32),
            norm_pre=g["norm_pre"], w_in=g["w_in"], pool_w=g["pool_w"], pool_scale=g["pool_scale"], f_bias=g["f_bias"],
            conv_w=g["conv_w"], conv_b=g["conv_b"], dt_bias=g["dt_bias"], a_log=g["a_log"], d_skip=g["d_skip"],
            ssm_norm=g["ssm_norm"], w_a=g["w_branch_a"], w_b=g["w_branch_b"], w_c=g["w_branch_c"], w_out=g["w_out"],
            norm_post=g["norm_post"])
        maps.append(m)
    return maps, n_pool


def assemble(res, n_cores=8):
    R = res
    cat = lambda k: np.stack([R[c][k] for c in range(n_cores)])
    y_prompt = cat("yp")
    y_sample = cat("ys").reshape(n_cores * NSEQ, 8, D)
    def pl(k, shp):
        return np.stack([R[c][k] for c in range(n_cores)], axis=1).reshape(shp)
    nb = n_cores
    k_p = pl("kp", (DEPTH, nb, TP, 8, 64)); v_p = pl("vp", (DEPTH, nb, TP, 8, 64)); lf_p = pl("lfp", (DEPTH, nb, TP, 8))
    pool_p = pl("poolp", (DEPTH, nb, 15, 512)); conv_p = pl("convp", (DEPTH, nb, 3, 1536)); ssm_p = pl("ssmp", (DEPTH, nb, 16, 64, 128))
    k_s = pl("kso", (DEPTH, nb * NSEQ, 8, 8, 64)); v_s = pl("vso", (DEPTH, nb * NSEQ, 8, 8, 64)); lf_s = pl("lfs", (DEPTH, nb * NSEQ, 8, 8))
    pool_s = pl("pools", (DEPTH, nb * NSEQ, 15, 512)); conv_s = pl("convs", (DEPTH, nb * NSEQ, 3, 1536))
    ssm_s = pl("ssms", (DEPTH, nb * NSEQ, 16, 64, 128))
    return (y_prompt, y_sample, k_p, v_p, lf_p, pool_p, conv_p, ssm_p, k_s, v_s, lf_s, pool_s, conv_s, ssm_s)


def kernel(**inputs):
    maps, n_pool = make_in_maps(inputs)
    nc = build(n_pool=n_pool)
    res = run_bass_kernel_spmd(nc, maps, core_ids=list(range(8)))
    return tuple(np.ascontiguousarray(a, dtype=np.float32) for a in assemble(res.results))
```

```python
import numpy as np
import concourse.bass as bass
import concourse.mybir as mybir
from concourse.bass_utils import run_bass_kernel_spmd
from contextlib import ExitStack

F32 = mybir.dt.float32
BF16 = mybir.dt.bfloat16
I32 = mybir.dt.int32
ALU = mybir.AluOpType
AF = mybir.ActivationFunctionType

DEPTH = 2
D = 1024
DIN = 8728
TP = 2048
NTOK = 2176
NT = 17
NSEQ = 16
NPG = 16
C_UA, C_ZA, C_Q, C_K, C_V, C_F, C_ZB, C_ZC, C_XBC, C_DT, C_G = (
    0, 512, 1024, 1536, 2048, 2560, 2568, 3080, 4104, 5640, 5656)
EPS = 1e-6
NEG = -30000.0


class Buf:
    __slots__ = ("name", "last_w", "reads", "dsem", "dcnt", "excl")

    def __init__(self, name, excl=False):
        self.name = name
        self.excl = excl
        self.last_w = None
        self.reads = {}
        self.dsem = None
        self.dcnt = 0


def bufs(name, n):
    return [Buf("%s%d" % (name, i)) for i in range(n)]


class Prog:
    ENGS = ("pe", "act", "dve", "pool", "sp")

    def __init__(self, nc, strict=True):
        self.nc = nc
        self.ops = {e: [] for e in self.ENGS}
        self.cnt = {e: 0 for e in self.ENGS}
        self.seen = {e: {} for e in self.ENGS}
        self.strict = strict
        self.ndsem = 0
        self.dsem_final = {}
        self.qhist = {e: [] for e in self.ENGS}
        self.dreg = {}
        self.st = ExitStack()

    def sb(self, name, shape, dtype):
        return self.st.enter_context(self.nc.sbuf_tensor(name, list(shape), dtype))

    def ps(self, name, shape, dtype):
        return self.st.enter_context(self.nc.psum_tensor(name, list(shape), dtype))

    def _deps(self, eng, reads, writes):
        need = {}
        for b in reads:
            t = b.last_w
            if t is not None and need.get(t[0], 0) < t[1]:
                need[t[0]] = t[1]
            if b.excl:
                for k, v in b.reads.items():
                    if k != eng and need.get(k, 0) < v:
                        need[k] = v
        for b in writes:
            t = b.last_w
            if t is not None and need.get(t[0], 0) < t[1]:
                need[t[0]] = t[1]
            for k, v in b.reads.items():
                if need.get(k, 0) < v:
                    need[k] = v
        waits = []
        seen = self.seen[eng]
        for k, v in need.items():
            if k == eng and (eng == "pe" or not self.strict):
                continue
            if seen.get(k, 0) >= v:
                continue
            seen[k] = v
            waits.append((k, v))
        return waits

    def _mark(self, tok, reads, writes):
        for b in reads:
            if b.reads.get(tok[0], 0) < tok[1]:
                b.reads[tok[0]] = tok[1]
        for b in writes:
            b.last_w = tok
            b.reads = {}

    def op(self, eng, fn, reads=(), writes=()):
        waits = self._deps(eng, reads, writes)
        self.cnt[eng] += 1
        tok = (eng, self.cnt[eng])
        self.ops[eng].append((waits, fn, (eng, 1)))
        self._mark(tok, reads, writes)

    def dma(self, q, fn, reads=(), writes=(), owner=None):
        if owner is None:
            owner = writes[0] if writes else reads[0]
        reg = self.dreg.get(owner.name)
        if reg is None:
            reg = ["d%d" % self.ndsem, 0]
            self.ndsem += 1
            self.dreg[owner.name] = reg
        owner.dsem = reg[0]
        waits = self._deps(q, reads, writes)
        qh = self.qhist[q]
        if len(qh) >= 24:
            k0, v0 = qh[-24]
            if self.seen[q].get(k0, 0) < v0:
                self.seen[q][k0] = v0
                waits.append((k0, v0))
        reg[1] += 1
        owner.dcnt = reg[1]
        tok = (owner.dsem, 16 * owner.dcnt)
        self.dsem_final[owner.dsem] = 16 * owner.dcnt
        self.ops[q].append((waits, fn, (owner.dsem, 16)))
        qh.append(tok)
        self._mark(tok, reads, writes)

    def barrier(self):
        state = dict(self.cnt)
        for e in self.ENGS:
            waits = []
            for k, v in list(state.items()) + list(self.dsem_final.items()):
                if k == e or v == 0:
                    continue
                if self.seen[e].get(k, 0) >= v:
                    continue
                self.seen[e][k] = v
                waits.append((k, v))
            if waits:
                self.ops[e].append((waits, None, None))

    def emit(self):
        nc = self.nc
        with self.st as st:
            sems = {}
            for e in self.ENGS:
                sems[e] = st.enter_context(nc.semaphore("s_" + e))
            for i in range(self.ndsem):
                sems["d%d" % i] = st.enter_context(nc.semaphore("sd%d" % i))
            block = st.enter_context(nc.Block())

            def run(ename, eng):
                for waits, fn, inc in self.ops[ename]:
                    for k, v in waits:
                        eng.wait_ge(sems[k], v)
                    if fn is None:
                        continue
                    ins = fn(eng)
                    ins.then_inc(sems[inc[0]], inc[1])
                if ename == "sp":
                    for k, v in self.dsem_final.items():
                        eng.wait_ge(sems[k], v)

            @block.tensor
            def _(e):
                run("pe", e)

            @block.scalar
            def _(e):
                run("act", e)

            @block.vector
            def _(e):
                run("dve", e)

            @block.gpsimd
            def _(e):
                run("pool", e)

            @block.sync
            def _(e):
                run("sp", e)


class Arena:
    def __init__(self, p, name, nbytes):
        self.t = p.sb(name, [128, nbytes // 4], F32)
        self.cap = nbytes // 4
        self.off = 0

    def reset(self, off=0):
        self.off = off

    def mark(self):
        return self.off

    def alloc(self, shape, dtype):
        P = shape[0]
        free = list(shape[1:])
        n = 1
        for s in free:
            n *= s
        esz = 4 if dtype in (F32, I32) else 2
        w = (n * esz + 3) // 4
        w = (w + 3) // 4 * 4
        assert self.off + w <= self.cap, ("arena overflow", self.off, w, self.cap)
        v = self.t[0:P, self.off:self.off + w]
        self.off += w
        if dtype != F32:
            v = v.bitcast(dtype)
        v = v[:, 0:n]
        if len(free) == 1:
            return v
        names = " ".join("abcd"[:len(free)])
        kw = {k: s for k, s in zip(names.split(), free)}
        return v.rearrange("p (%s) -> p %s" % (names, names), **kw)


def build(n_pool=2560, dbg=None, stop=None, strict=True):
    nc = bass.Bass("TRN2", target_bir_lowering=False)
    p = Prog(nc, strict=strict)

    def din(name, shape, dt=F32):
        return nc.dram_tensor(name, list(shape), dt, kind="ExternalInput").ap()

    def dout(name, shape, dt=F32):
        return nc.dram_tensor(name, list(shape), dt, kind="ExternalOutput").ap()

    xp = din("xp", [TP, D]); xs = din("xs", [128, D])
    ck = din("ck", [DEPTH, n_pool * 128, 512]); cv = din("cv", [DEPTH, n_pool * 128, 512])
    clf = din("clf", [DEPTH, n_pool, 1024])
    spool = din("spool", [DEPTH, NSEQ, 15, 512]); sconv = din("sconv", [DEPTH, NSEQ, 3, 1536])
    sssm = din("sssm", [DEPTH, NSEQ, 1024, 128])
    pt = din("pt", [NSEQ, NPG], I32)
    norm_pre = din("norm_pre", [DEPTH, D]); w_in = din("w_in", [DEPTH, D, DIN])
    pool_w = din("pool_w", [DEPTH, 4, 128, 128]); pool_scale = din("pool_scale", [DEPTH, 512])
    f_bias = din("f_bias", [DEPTH, 8]); conv_w = din("conv_w", [DEPTH, 4, 1536]); conv_b = din("conv_b", [DEPTH, 1536])
    dt_bias = din("dt_bias", [DEPTH, 16]); a_log = din("a_log", [DEPTH, 16]); d_skip = din("d_skip", [DEPTH, 16])
    ssm_norm = din("ssm_norm", [DEPTH, D])
    w_a = din("w_a", [DEPTH, 512, D]); w_b = din("w_b", [DEPTH, 512, D]); w_c = din("w_c", [DEPTH, D, D])
    w_out = din("w_out", [DEPTH, D, D]); norm_post = din("norm_post", [DEPTH, D])

    yp = dout("yp", [TP, D]); ys = dout("ys", [128, D])
    kp = dout("kp", [DEPTH, TP, 512]); vp = dout("vp", [DEPTH, TP, 512]); lfp = dout("lfp", [DEPTH, TP, 8])
    poolp = dout("poolp", [DEPTH, 15, 512]); convp = dout("convp", [DEPTH, 3, 1536]); ssmp = dout("ssmp", [DEPTH, 1024, 128])
    kso = dout("kso", [DEPTH, 128, 512]); vso = dout("vso", [DEPTH, 128, 512]); lfs = dout("lfs", [DEPTH, 128, 8])
    pools = dout("pools", [DEPTH, NSEQ, 15, 512]); convs = dout("convs", [DEPTH, NSEQ, 3, 1536])
    ssms = dout("ssms", [DEPTH, NSEQ, 1024, 128])
    x1 = nc.dram_tensor("x1", [NTOK, D], F32, kind="Internal").ap()
    Bx1 = bufs("x1_", NT)
    dbg_out = {}
    if dbg:
        for k, shp in dbg.items():
            dbg_out[k] = dout("dbg_" + k, shp, BF16 if k in ("hnT", "yT", "mT") else F32)

    hnT = p.sb("hnT", [128, 8, NTOK], BF16); BhnT = bufs("hnT", NT)
    yT = p.sb("yT", [128, 16, NTOK], BF16)
    ByT = [bufs("yTa", NT), bufs("yTb", NT), bufs("yTc", NT)]
    YOFF = (0, 4, 8)
    PS = [p.ps("ps%d" % i, [128, 512], F32)[:, :] for i in range(8)]
    BPS = [Buf("ps%d" % i, excl=True) for i in range(8)]
    cst = Arena(p, "cst", 15 * 1024)
    ar = Arena(p, "arena", 90 * 1024)

    Bc = Buf("consts")
    onesf = cst.alloc([128, 128], F32); identf = cst.alloc([128, 128], F32); trif = cst.alloc([128, 128], F32)
    zerof = cst.alloc([128, 128], F32)
    identb = cst.alloc([128, 128], BF16); onesb = cst.alloc([128, 128], BF16); maskb = cst.alloc([128, 128], BF16)
    sel127 = cst.alloc([128, 128], F32)
    blk = cst.alloc([128, 128], F32); btrif = cst.alloc([128, 128], F32)
    negm_p = cst.alloc([128, 128], F32); negm_s = cst.alloc([128, 128], F32)
    lastmask = cst.alloc([128, 16], F32); lastrow = cst.alloc([128, 1], F32); L2s = cst.alloc([128, 128], F32)
    Emat = cst.alloc([128, 128], F32)
    invc = cst.alloc([128, 4, 16], F32)
    iot = cst.alloc([128, 16], I32)
    iop = cst.alloc([128, 1], I32); iopf = cst.alloc([128, 1], F32)

    def cop(eng, fn):
        p.op(eng, fn, reads=[Bc], writes=[Bc])

    cop("pool", lambda e: e.memset(onesf, 1.0))
    cop("pool", lambda e: e.memset(zerof, 0.0))
    cop("pool", lambda e: e.affine_select(out=identf, in_=onesf, pattern=[[-1, 128]], compare_op=ALU.is_equal, fill=0.0, base=0, channel_multiplier=1))
    cop("pool", lambda e: e.affine_select(out=trif, in_=onesf, pattern=[[1, 128]], compare_op=ALU.is_ge, fill=0.0, base=0, channel_multiplier=-1))
    cop("pool", lambda e: e.affine_select(out=sel127, in_=onesf, pattern=[[0, 128]], compare_op=ALU.is_equal, fill=0.0, base=-127, channel_multiplier=1))
    cop("pool", lambda e: e.tensor_copy(out=identb, in_=identf))
    cop("pool", lambda e: e.tensor_copy(out=onesb, in_=onesf))
    cop("pool", lambda e: e.affine_select(out=negm_p, in_=zerof, pattern=[[1, 128]], compare_op=ALU.is_ge, fill=NEG, base=0, channel_multiplier=-1))
    cop("pool", lambda e: e.tensor_copy(out=maskb, in_=negm_p))
    cop("pool", lambda e: e.affine_select(out=Emat, in_=onesf, pattern=[[1, 128]], compare_op=ALU.is_ge, fill=0.0, base=0, channel_multiplier=-8))
    cop("pool", lambda e: e.affine_select(out=Emat, in_=Emat, pattern=[[-1, 128]], compare_op=ALU.is_ge, fill=0.0, base=7, channel_multiplier=8))
    cop("pe", lambda e: e.matmul(PS[0][:, 0:128], lhsT=Emat[0:16, :], rhs=Emat[0:16, :], start=True, stop=True))
    cop("dve", lambda e: e.tensor_copy(out=blk, in_=PS[0][:, 0:128]))
    cop("dve", lambda e: e.tensor_tensor(out=btrif, in0=blk, in1=trif, op=ALU.mult))
    cop("dve", lambda e: e.tensor_scalar(out=negm_s, in0=btrif, scalar1=-1.0, scalar2=-NEG, op0=ALU.add, op1=ALU.mult))
    cop("pool", lambda e: e.affine_select(out=lastmask, in_=onesf[:, 0:16], pattern=[[-8, 16]], compare_op=ALU.is_equal, fill=0.0, base=-7, channel_multiplier=1))
    cop("dve", lambda e: e.tensor_reduce(out=lastrow, in_=lastmask, axis=mybir.AxisListType.X, op=ALU.add))
    cop("dve", lambda e: e.tensor_scalar(out=L2s, in0=blk, scalar1=lastrow[:, 0:1], scalar2=None, op0=ALU.mult))
    cop("pool", lambda e: e.iota(iot, pattern=[[1, 16]], base=1, channel_multiplier=0))
    cop("pool", lambda e: e.iota(iop, pattern=[[0, 1]], base=0, channel_multiplier=1))
    cop("dve", lambda e: e.tensor_copy(out=iopf, in_=iop))
    for g in range(4):
        cop("dve", lambda e, g=g: e.tensor_copy(out=invc[:, g, :], in_=iot))
        cop("dve", lambda e, g=g: e.tensor_scalar(out=invc[:, g, :], in0=invc[:, g, :], scalar1=float(2 ** (g + 1)), scalar2=None, op0=ALU.min))
        cop("dve", lambda e, g=g: e.reciprocal(out=invc[:, g, :], in_=invc[:, g, :]))

    ptb = cst.alloc([128, 256], I32); ptf = cst.alloc([128, 256], F32); kidx = cst.alloc([128, 256], I32)
    lidx = cst.alloc([128, 2], I32)
    Bpt = Buf("pt")
    p.dma("sp", lambda e: e.dma_start(out=ptb, in_=pt.rearrange("a b -> (a b)").partition_broadcast(128)), writes=[Bpt])
    for h in range(2):
        p.dma("sp", lambda e, h=h: e.dma_start(out=lidx[:, h:h + 1], in_=pt[8 * h:8 * h + 8, :].rearrange("a (b o) -> (a b) o", o=1)), writes=[Bpt])
    p.op("dve", lambda e: e.tensor_copy(out=ptf, in_=ptb), reads=[Bpt, Bc], writes=[Bpt])
    p.op("dve", lambda e: e.tensor_scalar(out=ptf, in0=ptf, scalar1=128.0, scalar2=iopf[:, 0:1], op0=ALU.mult, op1=ALU.add), reads=[Bpt], writes=[Bpt])
    p.op("dve", lambda e: e.tensor_copy(out=kidx, in_=ptf), reads=[Bpt], writes=[Bpt])
    kidx1 = ptb
    lidx1 = cst.alloc([128, 2], I32); lf2c = cst.alloc([128, 2], F32)
    p.op("dve", lambda e: e.tensor_scalar(out=ptf, in0=ptf, scalar1=float(n_pool * 128), scalar2=None, op0=ALU.add), reads=[Bpt], writes=[Bpt])
    p.op("dve", lambda e: e.tensor_copy(out=kidx1, in_=ptf), reads=[Bpt], writes=[Bpt])
    p.op("dve", lambda e: e.tensor_copy(out=lf2c, in_=lidx), reads=[Bpt], writes=[Bpt])
    p.op("dve", lambda e: e.tensor_scalar(out=lf2c, in0=lf2c, scalar1=float(n_pool), scalar2=None, op0=ALU.add), reads=[Bpt], writes=[Bpt])
    p.op("dve", lambda e: e.tensor_copy(out=lidx1, in_=lf2c), reads=[Bpt], writes=[Bpt])

    npre_bc = cst.alloc([128, D], F32); Bnpre = Buf("npre")
    par = cst.alloc([128, 64], F32); Bpar = Buf("par")

    def dbg_dump(key, src_ap, rbufs, eng="sp"):
        if key in dbg_out:
            p.dma(eng, lambda e: e.dma_start(out=dbg_out[key], in_=src_ap), reads=rbufs)

    def tile_bufs(blist, tok0, ntok):
        return blist[tok0 // 128:(tok0 + ntok + 127) // 128]

    def load_w(dst, src, wbuf, q="pool"):
        p.dma(q, lambda e: e.dma_start(out=dst, in_=src.rearrange("(kc p) n -> p kc n", p=128)), writes=[wbuf])

    def proj_fm(W, wbuf, c0, ncols, tok0, ntok, ps_ap, psbuf):
        for kc in range(8):
            p.op("pe", lambda e, kc=kc: e.matmul(ps_ap, lhsT=W[:, kc, c0:c0 + ncols], rhs=hnT[:, kc, tok0:tok0 + ntok],
                                                  start=(kc == 0), stop=(kc == 7)),
                 reads=[wbuf] + tile_bufs(BhnT, tok0, ntok), writes=[psbuf])

    def proj_tm(W, wbuf, c0, ncols, j, ps_ap, psbuf):
        for kc in range(8):
            p.op("pe", lambda e, kc=kc: e.matmul(ps_ap, lhsT=hnT[:, kc, j * 128:(j + 1) * 128], rhs=W[:, kc, c0:c0 + ncols],
                                                  start=(kc == 0), stop=(kc == 7)),
                 reads=[wbuf, BhnT[j]], writes=[psbuf])

    def norm_to_hnT(xt, Bxt, j, sq, Bsq, hb, Bhb, psi):
        p.op("pool", lambda e: e.memset(sq, 0.0), writes=[Bsq])
        p.op("act", lambda e: e.activation(out=hb, in_=xt, func=AF.Square, accum_out=sq[:, 0:1]), reads=[Bxt], writes=[Bsq, Bhb])
        p.op("act", lambda e: e.activation(out=sq[:, 1:2], in_=sq[:, 0:1], func=AF.Ln, bias=EPS, scale=1.0 / D), reads=[Bsq], writes=[Bsq])
        p.op("act", lambda e: e.activation(out=sq[:, 2:3], in_=sq[:, 1:2], func=AF.Exp, scale=-0.5), reads=[Bsq], writes=[Bsq])
        p.op("dve", lambda e: e.scalar_tensor_tensor(out=hb, in0=xt, scalar=sq[:, 2:3], in1=npre_bc, op0=ALU.mult, op1=ALU.mult),
             reads=[Bxt, Bsq, Bnpre], writes=[Bhb])
        pst = PS[psi].bitcast(BF16)
        for kc in range(8):
            p.op("pe", lambda e, kc=kc: e.transpose(pst[:, kc * 128:(kc + 1) * 128], hb[:, kc * 128:(kc + 1) * 128], identb),
                 reads=[Bhb, Bc], writes=[BPS[psi]])
        p.op("act", lambda e: e.copy(out=hnT[:, :, j * 128:(j + 1) * 128], in_=pst.rearrange("p (k t) -> p k t", k=8)),
             reads=[BPS[psi]], writes=[BhnT[j]])

    for l in range(DEPTH):
        if l == 0:
            p.dma("sp", lambda e: e.dma_start(out=npre_bc, in_=norm_pre[0].partition_broadcast(128)), writes=[Bnpre])
            p.barrier(); ar.reset()
            xt = [ar.alloc([128, D], F32) for _ in range(2)]; Bxt = bufs("xt", 2)
            hb = [ar.alloc([128, D], BF16) for _ in range(2)]; Bhb = bufs("hb", 2)
            sq = [ar.alloc([128, 4], F32) for _ in range(2)]; Bsq = bufs("sq", 2)
            for j in range(NT):
                k = j % 2
                src = xp[j * 128:(j + 1) * 128, :] if j < 16 else xs
                p.dma("sp", lambda e, k=k, src=src: e.dma_start(out=xt[k], in_=src), writes=[Bxt[k]])
                norm_to_hnT(xt[k], Bxt[k], j, sq[k], Bsq[k], hb[k], Bhb[k], k)
        if stop == "N":
            break

        import os
        SKIPAB = os.environ.get('SKIPAB') == '1'
        p.barrier(); ar.reset()
        wA = [ar.alloc([128, 8, 256], BF16) for _ in range(2)]; BwA = bufs("wA", 2)
        pw = [ar.alloc([128, 128], BF16) for _ in range(2)]; Bpw = bufs("pw", 2)
        psc = ar.alloc([128, 4], F32); Bpsc = Buf("psc")
        ub = ar.alloc([128, 15 + TP], F32); Bub = Buf("ub")
        sA = ar.alloc([128, 15 + TP], F32); BsA = Buf("sA")
        sB = ar.alloc([128, 15 + TP], F32); BsB = Buf("sB")
        zs = ar.alloc([128, NTOK], F32); Bzs = Buf("zs")
        dbf = ar.alloc([128, NTOK], BF16); Bdbf = Buf("dbf")
        us = ar.alloc([128, 16, 23], F32); Bus = Buf("us")
        ssA = ar.alloc([128, 16, 23], F32); BssA = Buf("ssA")
        ssB = ar.alloc([128, 16, 23], F32); BssB = Buf("ssB")
        hist = [ar.alloc([120, 512], F32) for _ in range(2)]; Bhist = bufs("hist", 2)
        tmp15 = ar.alloc([128, 16], F32); Btmp15 = Buf("tmp15")
        utm = [ar.alloc([128, 512], F32) for _ in range(2)]; Butm = bufs("utm", 2)
        wU = ar.alloc([128, 8, 512], BF16); BwU = Buf("wU")

        for g in range(4):
            p.dma("sp", lambda e, l=l, g=g: e.dma_start(out=psc[:, g:g + 1], in_=pool_scale[l, g * 128:(g + 1) * 128].rearrange("(c o) -> c o", o=1)), writes=[Bpsc])
        for h2 in range(2):
            p.dma("sp", lambda e, l=l, h2=h2: e.dma_start(out=hist[h2], in_=spool[l, 8 * h2:8 * h2 + 8].rearrange("b j c -> (b j) c")),
                  writes=[Bhist[h2]])
        p.op("pool", lambda e: e.memset(ub[:, 0:15], 0.0), writes=[Bub])
        load_w(wU, w_in[l][:, C_UA:C_UA + 512], BwU)
        for k, j in enumerate((15, 16)):
            proj_tm(wU, BwU, 0, 512, j, PS[6 + k], BPS[6 + k])
            p.op("act", lambda e, k=k: e.copy(out=utm[k], in_=PS[6 + k]), reads=[BPS[6 + k]], writes=[Butm[k]])
        p.dma("sp", lambda e, l=l: e.dma_start(out=poolp[l], in_=utm[0][113:128, :]), reads=[Butm[0]])
        for b in range(NSEQ):
            p.dma("sp", lambda e, l=l, b=b: e.dma_start(out=pools[l, b, 7:15, :], in_=utm[1][8 * b:8 * b + 8, :]), reads=[Butm[1]])
        p.dma("sp", lambda e, l=l: e.dma_start(out=pools[l, :, 0:7, :], in_=spool[l, :, 8:15, :]), owner=Butm[1])

        for g in range(4):
            w = 2 ** (g + 1)
            k = g % 2
            load_w(wA[k][:, :, 0:128], w_in[l][:, C_UA + g * 128:C_UA + (g + 1) * 128], BwA[k])
            load_w(wA[k][:, :, 128:256], w_in[l][:, C_ZA + g * 128:C_ZA + (g + 1) * 128], BwA[k])
            p.dma("pool", lambda e, l=l, g=g, k=k: e.dma_start(out=pw[k], in_=pool_w[l, g]), writes=[Bpw[k]])
            for tg in range(5):
                tok0 = tg * 512
                ntok = 512 if tg < 4 else 128
                pu, pz = (tg * 2) % 6, (tg * 2 + 1) % 6
                proj_fm(wA[k], BwA[k], 0, 128, tok0, ntok, PS[pu][:, 0:ntok], BPS[pu])
                proj_fm(wA[k], BwA[k], 128, 128, tok0, ntok, PS[pz][:, 0:ntok], BPS[pz])
                if tg < 4:
                    p.op("act", lambda e, pu=pu, tok0=tok0: e.copy(out=ub[:, 15 + tok0:15 + tok0 + 512], in_=PS[pu]), reads=[BPS[pu]], writes=[Bub])
                else:
                    p.op("act", lambda e, pu=pu: e.copy(out=us[:, :, 15:23], in_=PS[pu][:, 0:128].rearrange("p (b t) -> p b t", b=16)),
                         reads=[BPS[pu]], writes=[Bus])
                p.op("act", lambda e, pz=pz, tok0=tok0, ntok=ntok: e.activation(out=zs[:, tok0:tok0 + ntok], in_=PS[pz][:, 0:ntok], func=AF.Silu),
                     reads=[BPS[pz]], writes=[Bzs])
            for h2 in range(2):
                p.op("pe", lambda e, g=g, h2=h2: e.transpose(PS[6][:, h2 * 128:h2 * 128 + 120], hist[h2][:, g * 128:(g + 1) * 128], identf[0:120, 0:120]),
                     reads=[Bhist[h2], Bc], writes=[BPS[6]])
            p.op("act", lambda e: e.copy(out=us[:, :, 0:15].rearrange("p (h b) j -> p h b j", h=2),
                                         in_=PS[6][:, 0:256].rearrange("p (h x) -> p h x", h=2)[:, :, 0:120].rearrange("p h (b j) -> p h b j", b=8)),
                 reads=[BPS[6]], writes=[Bus])
            src_p, Bsrc_p, src_s, Bsrc_s = ub, Bub, us, Bus
            lo = 0
            pp = [(sA, BsA, ssA, BssA), (sB, BsB, ssB, BssB)]
            for step in range(g + 1):
                sh = 2 ** step
                dp, Bdp, ds_, Bds = pp[step % 2]
                n = 15 + TP
                p.op("dve", lambda e, dp=dp, sp_=src_p, lo=lo, sh=sh, n=n: e.tensor_tensor(out=dp[:, lo + sh:n], in0=sp_[:, lo + sh:n], in1=sp_[:, lo:n - sh], op=ALU.add),
                     reads=[Bsrc_p], writes=[Bdp])
                p.op("pool", lambda e, ds_=ds_, ss_=src_s, lo=lo, sh=sh: e.tensor_tensor(out=ds_[:, :, lo + sh:23], in0=ss_[:, :, lo + sh:23], in1=ss_[:, :, lo:23 - sh], op=ALU.add),
                     reads=[Bsrc_s], writes=[Bds])
                src_p, Bsrc_p, src_s, Bsrc_s = dp, Bdp, ds_, Bds
                lo += sh
            p.op("dve", lambda e, sp_=src_p, w=w: e.scalar_tensor_tensor(out=dbf[:, 0:TP], in0=sp_[:, 15:15 + TP], scalar=1.0 / w, in1=ub[:, 15:15 + TP], op0=ALU.mult, op1=ALU.subtract),
                 reads=[Bsrc_p, Bub], writes=[Bdbf])
            p.op("dve", lambda e, sp_=src_p, g=g: e.tensor_tensor(out=tmp15[:, 0:15], in0=sp_[:, 15:30], in1=invc[:, g, 0:15], op=ALU.mult),
                 reads=[Bsrc_p, Bc], writes=[Btmp15])
            p.op("dve", lambda e: e.tensor_tensor(out=dbf[:, 0:15], in0=tmp15[:, 0:15], in1=ub[:, 15:30], op=ALU.subtract),
                 reads=[Btmp15, Bub, Bdbf], writes=[Bdbf])
            p.op("dve", lambda e, ss_=src_s, w=w: e.scalar_tensor_tensor(out=dbf[:, TP:NTOK].rearrange("p (b t) -> p b t", b=16), in0=ss_[:, :, 15:23], scalar=1.0 / w, in1=us[:, :, 15:23], op0=ALU.mult, op1=ALU.subtract),
                 reads=[Bsrc_s, Bus, Bdbf], writes=[Bdbf])
            for tg in range(5):
                tok0 = tg * 512
                ntok = 512 if tg < 4 else 128
                py = tg % 6
                p.op("pe", lambda e, k=k, py=py, tok0=tok0, ntok=ntok: e.matmul(PS[py][:, 0:ntok], lhsT=pw[k], rhs=dbf[:, tok0:tok0 + ntok], start=True, stop=True),
                     reads=[Bpw[k], Bdbf], writes=[BPS[py]])
                p.op("dve", lambda e, g=g, py=py, tok0=tok0, ntok=ntok: e.scalar_tensor_tensor(out=yT[:, g, tok0:tok0 + ntok], in0=PS[py][:, 0:ntok], scalar=psc[:, g:g + 1], in1=zs[:, tok0:tok0 + ntok], op0=ALU.mult, op1=ALU.mult),
                     reads=[BPS[py], Bpsc, Bzs], writes=tile_bufs(ByT[0], tok0, ntok))
        if stop == "A":
            break

        p.barrier(); ar.reset()
        wF = ar.alloc([128, 8, 8], BF16); BwF = Buf("wF")
        fbb = ar.alloc([128, 8], F32); Bfbb = Buf("fbb")
        lf = ar.alloc([128, 17, 8], F32); Blf = Buf("lf")
        qTs = ar.alloc([128, 4, 128], BF16); kTs = ar.alloc([128, 4, 128], BF16); Bqks = Buf("qks")
        vnew = ar.alloc([128, 512], BF16); Bvnew = Buf("vnew")
        zsTs = ar.alloc([128, 4, 128], F32); BzsTs = Buf("zsTs")
        markB = ar.mark()
        tot = ar.alloc([128, 16, 8], F32); carry = ar.alloc([128, 17, 8], F32); ccum = ar.alloc([128, 16, 8], F32); Bcc = Buf("cc")
        btab = ar.alloc([128, 16, 16, 8], F32); Bbtab = Buf("btab")
        wB = [ar.alloc([128, 8, 512], BF16) for _ in range(2)]; BwB = bufs("wB", 2)
        qT = ar.alloc([128, NTOK], BF16); BqT = Buf("qT")
        kT = ar.alloc([128, NTOK], BF16); BkT = Buf("kT")
        vaug = ar.alloc([128, 16, 2, 66], BF16); Bvaug = bufs("vaug", 16)
        kvst = [ar.alloc([128, 256], F32) for _ in range(3)]; Bkvst = bufs("kvst", 3)
        pT = [ar.alloc([128, 128], BF16) for _ in range(4)]; BpT = bufs("pT", 4)
        zsb = [ar.alloc([128, 128], F32) for _ in range(16)]; Bzsb = bufs("zsb", 16)
        rinv = [ar.alloc([128, 1], F32) for _ in range(2)]; Brinv = bufs("rinv", 2)
        ybt = [ar.alloc([128, 128], BF16) for _ in range(2)]; Bybt = bufs("ybt", 2)

        load_w(wF, w_in[l][:, C_F:C_F + 8], BwF)
        p.dma("sp", lambda e, l=l: e.dma_start(out=fbb, in_=f_bias[l].partition_broadcast(128)), writes=[Bfbb])
        for j in range(NT):
            proj_tm(wF, BwF, 0, 8, j, PS[0][:, j * 8:(j + 1) * 8], BPS[0])
        lf_flat = lf.rearrange("p j h -> p (j h)")
        p.op("dve", lambda e: e.tensor_tensor(out=lf, in0=PS[0][:, 0:136].rearrange("p (j h) -> p j h", h=8), in1=fbb.unsqueeze(1).to_broadcast([128, 17, 8]), op=ALU.add),
             reads=[BPS[0], Bfbb], writes=[Blf])
        p.op("act", lambda e: e.activation(out=lf_flat, in_=lf_flat, func=AF.Exp, scale=-1.0), reads=[Blf], writes=[Blf])
        p.op("act", lambda e: e.activation(out=lf_flat, in_=lf_flat, func=AF.Ln, bias=1.0), reads=[Blf], writes=[Blf])
        p.op("dve", lambda e: e.tensor_scalar(out=lf_flat, in0=lf_flat, scalar1=-1.0, scalar2=None, op0=ALU.mult), reads=[Blf], writes=[Blf])
        p.dma("sp", lambda e, l=l: e.dma_start(out=lfp[l].rearrange("(j p) h -> p j h", p=128), in_=lf[:, 0:16, :]), reads=[Blf])
        p.dma("sp", lambda e, l=l: e.dma_start(out=lfs[l], in_=lf[:, 16, :]), reads=[Blf])
        if stop == "B0a":
            break
        p.op("pe", lambda e: e.matmul(PS[1][:, 0:128], lhsT=onesf, rhs=lf_flat[:, 0:128], start=True, stop=True), reads=[Blf, Bc], writes=[BPS[1]])
        p.op("pe", lambda e: e.matmul(PS[2][:, 0:128], lhsT=trif, rhs=lf_flat[:, 0:128], start=True, stop=True), reads=[Blf, Bc], writes=[BPS[2]])
        p.op("dve", lambda e: e.tensor_copy(out=tot.rearrange("p j h -> p (j h)"), in_=PS[1][:, 0:128]), reads=[BPS[1]], writes=[Bcc])
        p.op("dve", lambda e: e.memset(carry[:, 0, :], 0.0), reads=[Bcc], writes=[Bcc])
        for j in range(1, 17):
            p.op("dve", lambda e, j=j: e.tensor_tensor(out=carry[:, j, :], in0=carry[:, j - 1, :], in1=tot[:, j - 1, :], op=ALU.add), reads=[Bcc], writes=[Bcc])
        p.op("dve", lambda e: e.tensor_tensor(out=ccum, in0=PS[2][:, 0:128].rearrange("p (j h) -> p j h", h=8), in1=carry[:, 0:16, :], op=ALU.add), reads=[BPS[2], Bcc], writes=[Bcc])
        if stop == "B0c":
            break
        for Q in range(16):
            p.op("dve", lambda e, Q=Q: e.tensor_tensor(out=btab[:, Q, 0:Q + 1, :], in0=carry[:, Q + 1:Q + 2, :].to_broadcast([128, Q + 1, 8]), in1=ccum[:, 0:Q + 1, :], op=ALU.subtract),
                 reads=[Bcc, Bbtab], writes=[Bbtab])
        p.op("dve", lambda e: e.memset(vaug[:, :, :, 64:66], 1.0), writes=Bvaug)
        if stop == "B0":
            break

        import os
        for pr in range(int(os.environ.get('NPR', 4))):
            k = pr % 2
            for i, c0 in enumerate((C_Q, C_K, C_V, C_ZB)):
                load_w(wB[k][:, :, i * 128:(i + 1) * 128], w_in[l][:, c0 + pr * 128:c0 + (pr + 1) * 128], BwB[k])
            for tg in range(5):
                tok0 = tg * 512
                ntok = 512 if tg < 4 else 128
                pa, pb = 5 + (2 * tg) % 3, 5 + (2 * tg + 1) % 3
                proj_fm(wB[k], BwB[k], 0, 128, tok0, ntok, PS[pa][:, 0:ntok], BPS[pa])
                p.op("dve", lambda e, pa=pa, tok0=tok0, ntok=ntok: e.tensor_scalar(out=qT[:, tok0:tok0 + ntok], in0=PS[pa][:, 0:ntok], scalar1=0.125, scalar2=None, op0=ALU.mult),
                     reads=[BPS[pa]], writes=[BqT])
                proj_fm(wB[k], BwB[k], 128, 128, tok0, ntok, PS[pb][:, 0:ntok], BPS[pb])
                p.op("dve", lambda e, pb=pb, tok0=tok0, ntok=ntok: e.tensor_copy(out=kT[:, tok0:tok0 + ntok], in_=PS[pb][:, 0:ntok]),
                     reads=[BPS[pb]], writes=[BkT])
            if stop == "B0b1":
                continue
            p.op("pool", lambda e, pr=pr: e.tensor_copy(out=qTs[:, pr, :], in_=qT[:, TP:NTOK]), reads=[BqT], writes=[Bqks])
            p.op("pool", lambda e, pr=pr: e.tensor_copy(out=kTs[:, pr, :], in_=kT[:, TP:NTOK]), reads=[BkT], writes=[Bqks])
            proj_fm(wB[k], BwB[k], 384, 128, TP, 128, PS[5][:, 0:128], BPS[5])
            p.op("act", lambda e, pr=pr: e.activation(out=zsTs[:, pr, :], in_=PS[5][:, 0:128], func=AF.Silu), reads=[BPS[5]], writes=[BzsTs])
            if stop == "B0b2":
                continue
            import os
            SK = os.environ.get("SK", "")
            for j in range(int(os.environ.get("NTJ", NT))):
                pb = 5 + j % 3
                m = j % 3
                proj_tm(wB[k], BwB[k], 128, 256, j, PS[pb][:, 0:256], BPS[pb])
                p.op("act", lambda e, pb=pb, m=m: e.copy(out=kvst[m], in_=PS[pb][:, 0:256]), reads=[BPS[pb]], writes=[Bkvst[m]])
                if j < 16:
                    if "v" not in SK:
                        for a in range(2):
                            p.op("dve", lambda e, j=j, a=a, m=m: e.tensor_copy(out=vaug[:, j, a, 0:64], in_=kvst[m][:, 128 + 64 * a:192 + 64 * a]),
                                 reads=[Bkvst[m]], writes=[Bvaug[j]])
                    if "d" not in SK:
                        p.dma("sp", lambda e, l=l, j=j, pr=pr, m=m: e.dma_start(out=kp[l, j * 128:(j + 1) * 128, pr * 128:(pr + 1) * 128], in_=kvst[m][:, 0:128]), reads=[Bkvst[m]])
                        p.dma("sp", lambda e, l=l, j=j, pr=pr, m=m: e.dma_start(out=vp[l, j * 128:(j + 1) * 128, pr * 128:(pr + 1) * 128], in_=kvst[m][:, 128:256]), reads=[Bkvst[m]])
                else:
                    if "n" not in SK:
                        p.op("dve", lambda e, m=m, pr=pr: e.tensor_copy(out=vnew[:, pr * 128:(pr + 1) * 128], in_=kvst[m][:, 128:256]), reads=[Bkvst[m]], writes=[Bvnew])
                    if "e" not in SK:
                        p.dma("sp", lambda e, l=l, pr=pr, m=m: e.dma_start(out=kso[l, :, pr * 128:(pr + 1) * 128], in_=kvst[m][:, 0:128]), reads=[Bkvst[m]])
                        p.dma("sp", lambda e, l=l, pr=pr, m=m: e.dma_start(out=vso[l, :, pr * 128:(pr + 1) * 128], in_=kvst[m][:, 128:256]), reads=[Bkvst[m]])
            if stop == "B0b":
                continue
            it = 0
            for Q in range(16):
                pz_ = 5 + Q % 3
                proj_tm(wB[k], BwB[k], 384, 128, Q, PS[pz_][:, 0:128], BPS[pz_])
                p.op("act", lambda e, Q=Q, pz_=pz_: e.activation(out=zsb[Q], in_=PS[pz_][:, 0:128], func=AF.Silu), reads=[BPS[pz_]], writes=[Bzsb[Q]])
            for Q in range(16):
                zq = Q % 2
                for h2 in range(2):
                    hb = 64 * h2
                    h = 2 * pr + h2
                    po = 3 + h2

                    def qk(S, it):
                        sb_ = (0, 1, 2, 5)[it % 4]
                        p.op("pe", lambda e, S=S, sb_=sb_, hb=hb, Q=Q: e.matmul(PS[sb_][:, 0:128], lhsT=kT[hb:hb + 64, S * 128:(S + 1) * 128], rhs=qT[hb:hb + 64, Q * 128:(Q + 1) * 128], start=True, stop=(S != Q)),
                             reads=[BkT, BqT], writes=[BPS[sb_]])
                        if S == Q:
                            p.op("pe", lambda e, sb_=sb_: e.matmul(PS[sb_][:, 0:128], lhsT=identb, rhs=maskb, start=False, stop=True), reads=[Bc], writes=[BPS[sb_]])
                    qk(0, it)
                    if Q >= 1:
                        qk(1, it + 1)
                    for S in range(Q + 1):
                        if S + 2 <= Q:
                            qk(S + 2, it + 2)
                        sb_ = (0, 1, 2, 5)[it % 4]
                        pi = it % 4
                        p.op("act", lambda e, sb_=sb_, pi=pi, Q=Q, S=S, h=h: e.activation(out=pT[pi], in_=PS[sb_][:, 0:128], func=AF.Exp, bias=btab[:, Q, S, h:h + 1], scale=1.0),
                             reads=[BPS[sb_], Bbtab], writes=[BpT[pi]])
                        p.op("pe", lambda e, pi=pi, S=S, h2=h2, po=po, Q=Q: e.matmul(PS[po][:, 0:65], lhsT=pT[pi], rhs=vaug[:, S, h2, 0:65], start=(S == 0), stop=(S == Q)),
                             reads=[BpT[pi], Bvaug[S]], writes=[BPS[po]])
                        it += 1
                    p.op("dve", lambda e, po=po, h2=h2: e.reciprocal(out=rinv[h2], in_=PS[po][:, 64:65]), reads=[BPS[po]], writes=[Brinv[h2]])
                    p.op("dve", lambda e, po=po, h2=h2, hb=hb, zq=zq, Q=Q: e.scalar_tensor_tensor(out=ybt[zq][:, hb:hb + 64], in0=PS[po][:, 0:64], scalar=rinv[h2][:, 0:1], in1=zsb[Q][:, hb:hb + 64], op0=ALU.mult, op1=ALU.mult),
                         reads=[BPS[po], Brinv[h2], Bzsb[Q]], writes=[Bybt[zq]])
                pst = PS[6].bitcast(BF16)
                p.op("pe", lambda e, zq=zq, pst=pst: e.transpose(pst[:, 0:128], ybt[zq], identb), reads=[Bybt[zq], Bc], writes=[BPS[6]])
                p.op("dve", lambda e, pst=pst, pr=pr, Q=Q: e.tensor_copy(out=yT[:, 4 + pr, Q * 128:(Q + 1) * 128], in_=pst[:, 0:128]), reads=[BPS[6]], writes=[ByT[1][Q]])
        if stop == "B1":
            break

        p.barrier(); ar.reset(markB)
        sufm = ar.alloc([128, 128], F32); Bsufm = Buf("sufm")
        lfg = ar.alloc([128, 1024], F32); Blfg = Buf("lfg")
        lft = ar.alloc([128, 8, 128], F32); Blft = Buf("lft")
        totS = ar.alloc([128, 8, 8, 16], F32); later = ar.alloc([128, 8, 8, 16], F32); Blat = Buf("later")
        bpast = ar.alloc([128, 8, 2, 128], F32); Bbpast = Buf("bpast")
        bnew = ar.alloc([128, 8], F32); Bbnew = Buf("bnew")
        qbd = ar.alloc([128, 4, 16, 16], BF16); Bqbd = Buf("qbd")
        kbf = [ar.alloc([128, 512], BF16) for _ in range(4)]; Bkbf = bufs("kbf", 4)
        KT = [ar.alloc([128, 4, 128], BF16) for _ in range(2)]; BKT = bufs("KT", 2)
        Vb = ar.alloc([128, 17, 512], BF16); BVb = bufs("Vb", 17)
        sc = ar.alloc([128, 17, 64], F32); Bsc = Buf("sc")
        pTs = ar.alloc([128, 17, 64], BF16); BpTs = Buf("pTs")
        rs = ar.alloc([128, 16, 64], F32); Brs = Buf("rs")
        ot = ar.alloc([128, 4, 16, 16], F32); Bot = Buf("ot")

        p.op("dve", lambda e: e.tensor_tensor(out=sufm, in0=onesf, in1=trif, op=ALU.subtract), reads=[Bc], writes=[Bsufm])
        kidxL = kidx if l == 0 else kidx1
        lidxL = lidx if l == 0 else lidx1
        BidxL = Bpt
        ck_t = ck.rearrange("l r c -> (l r) c"); cv_t = cv.rearrange("l r c -> (l r) c"); clf_t = clf.rearrange("l r c -> (l r) c")
        p.op("dve", lambda e: e.memset(qbd, 0.0), writes=[Bqbd])
        for pr in range(4):
            p.op("dve", lambda e, pr=pr: e.tensor_copy(out=qbd[0:64, pr, :, 0:8], in_=qTs[0:64, pr, :].rearrange("p (b t) -> p b t", b=16)), reads=[Bqks, Bqbd], writes=[Bqbd])
            p.op("dve", lambda e, pr=pr: e.tensor_copy(out=qbd[64:128, pr, :, 8:16], in_=qTs[64:128, pr, :].rearrange("p (b t) -> p b t", b=16)), reads=[Bqks, Bqbd], writes=[Bqbd])
        p.op("pe", lambda e: e.matmul(PS[0][:, 0:8], lhsT=btrif, rhs=lf[:, 16, :], start=True, stop=True), reads=[Blf, Bc], writes=[BPS[0]])
        p.op("dve", lambda e: e.tensor_scalar(out=bnew, in0=PS[0][:, 0:8], scalar1=-1.0, scalar2=None, op0=ALU.mult), reads=[BPS[0]], writes=[Bbnew])
        for half in range(2):
            p.dma("pool", lambda e, l=l, half=half, lidxL=lidxL: e.indirect_dma_start(out=lfg, out_offset=None, in_=clf_t, in_offset=bass.IndirectOffsetOnAxis(ap=lidxL[:, half:half + 1], axis=0)),
                  reads=[BidxL], writes=[Blfg])
            lfg3 = lfg.rearrange("p (s h) -> p s h", h=8)
            for h in range(8):
                pb_ = 1 + h // 4
                p.op("pe", lambda e, h=h, pb_=pb_: e.transpose(PS[pb_][:, (h % 4) * 128:(h % 4 + 1) * 128], lfg3[:, :, h], identf), reads=[Blfg, Bc], writes=[BPS[pb_]])
            for q4 in range(2):
                p.op("act", lambda e, q4=q4: e.copy(out=lft[:, 4 * q4:4 * q4 + 4, :], in_=PS[1 + q4].rearrange("p (h x) -> p h x", h=4)), reads=[BPS[1 + q4]], writes=[Blft])
            for q4 in range(2):
                p.op("pe", lambda e, q4=q4: e.matmul(PS[3 + q4], lhsT=onesf, rhs=lft[:, 4 * q4:4 * q4 + 4, :].rearrange("p h x -> p (h x)"), start=True, stop=True), reads=[Blft, Bc], writes=[BPS[3 + q4]])
                p.op("act", lambda e, q4=q4: e.copy(out=totS[:, 4 * q4:4 * q4 + 4, :, :].rearrange("p h b j -> p (h b j)"), in_=PS[3 + q4]), reads=[BPS[3 + q4]], writes=[Blat])
            p.op("dve", lambda e: e.memset(later[:, :, :, 15:16], 0.0), reads=[Blat], writes=[Blat])
            for j in range(14, -1, -1):
                p.op("dve", lambda e, j=j: e.tensor_tensor(out=later[:, :, :, j:j + 1], in0=later[:, :, :, j + 1:j + 2], in1=totS[:, :, :, j + 1:j + 2], op=ALU.add), reads=[Blat], writes=[Blat])
            for q4 in range(2):
                p.op("pe", lambda e, q4=q4: e.matmul(PS[5 + q4], lhsT=sufm, rhs=lft[:, 4 * q4:4 * q4 + 4, :].rearrange("p h x -> p (h x)"), start=True, stop=True), reads=[Blft, Bsufm], writes=[BPS[5 + q4]])
                p.op("dve", lambda e, q4=q4, half=half: e.tensor_tensor(out=bpast[:, 4 * q4:4 * q4 + 4, half, :], in0=PS[5 + q4].rearrange("p (h x) -> p h x", h=4),
                                                                    in1=later[:, 4 * q4:4 * q4 + 4, :, :].rearrange("p h b j -> p h (b j)"), op=ALU.add),
                     reads=[BPS[5 + q4], Blat], writes=[Bbpast])
        p.op("dve", lambda e: e.tensor_copy(out=Vb[:, 16, :], in_=vnew), reads=[Bvnew], writes=[BVb[16]])
        for b in range(NSEQ):
            half, b8 = b // 8, b % 8
            for j in range(NPG):
                m = j % 4
                col = b * 16 + j
                p.dma("pool", lambda e, m=m, col=col, kidxL=kidxL: e.indirect_dma_start(out=kbf[m], out_offset=None, in_=ck_t, in_offset=bass.IndirectOffsetOnAxis(ap=kidxL[:, col:col + 1], axis=0)),
                      reads=[BidxL], writes=[Bkbf[m]])
                p.dma("pool", lambda e, j=j, col=col, kidxL=kidxL: e.indirect_dma_start(out=Vb[:, j, :], out_offset=None, in_=cv_t, in_offset=bass.IndirectOffsetOnAxis(ap=kidxL[:, col:col + 1], axis=0)),
                      reads=[BidxL], writes=[BVb[j]])
            for j in range(NPG):
                m = j % 4
                m2 = j % 2
                pst = PS[3].bitcast(BF16)
                for pr in range(4):
                    p.op("pe", lambda e, m=m, pr=pr, pst=pst: e.transpose(pst[:, pr * 128:(pr + 1) * 128], kbf[m][:, pr * 128:(pr + 1) * 128], identb), reads=[Bkbf[m], Bc], writes=[BPS[3]])
                p.op("dve", lambda e, m2=m2, pst=pst: e.tensor_copy(out=KT[m2], in_=pst[:, 0:512].rearrange("p (r s) -> p r s", r=4)), reads=[BPS[3]], writes=[BKT[m2]])
                bank = j // 8
                for pr in range(4):
                    c0 = (j % 8) * 64 + pr * 16
                    p.op("pe", lambda e, m2=m2, pr=pr, bank=bank, c0=c0, b=b: e.matmul(PS[bank][:, c0:c0 + 16], lhsT=KT[m2][:, pr, :], rhs=qbd[:, pr, b, :], start=True, stop=True),
                         reads=[BKT[m2], Bqbd], writes=[BPS[bank]])
            for pr in range(4):
                p.op("pe", lambda e, pr=pr, b=b: e.matmul(PS[2][:, pr * 16:(pr + 1) * 16], lhsT=kTs[:, pr, :], rhs=qbd[:, pr, b, :], start=True, stop=True),
                     reads=[Bqks, Bqbd], writes=[BPS[2]])
            for bank in range(2):
                bias_v = bpast[:, :, half, b8 * 16 + bank * 8:b8 * 16 + bank * 8 + 8].rearrange("p h j -> p j h").unsqueeze(3).to_broadcast([128, 8, 8, 8])
                p.op("dve", lambda e, bank=bank, bias_v=bias_v: e.tensor_tensor(out=sc[:, bank * 8:(bank + 1) * 8, :].rearrange("p j (h t) -> p j h t", h=8),
                                                                           in0=PS[bank].rearrange("p (j h t) -> p j h t", j=8, h=8), in1=bias_v, op=ALU.add),
                     reads=[BPS[bank], Bbpast], writes=[Bsc])
            p.op("dve", lambda e, b=b: e.tensor_tensor(out=sc[:, 16, :].rearrange("p (h t) -> p h t", h=8), in0=PS[2][:, 0:64].rearrange("p (h t) -> p h t", h=8),
                                                  in1=negm_s[:, b * 8:(b + 1) * 8].unsqueeze(1).to_broadcast([128, 8, 8]), op=ALU.add),
                 reads=[BPS[2], Bc, Bsc], writes=[Bsc])
            p.op("dve", lambda e: e.tensor_tensor(out=sc[:, 16, :].rearrange("p (h t) -> p h t", h=8), in0=sc[:, 16, :].rearrange("p (h t) -> p h t", h=8),
                                             in1=bnew.unsqueeze(2).to_broadcast([128, 8, 8]), op=ALU.add),
                 reads=[Bbnew, Bsc], writes=[Bsc])
            p.op("act", lambda e: e.activation(out=pTs.rearrange("p j x -> p (j x)"), in_=sc.rearrange("p j x -> p (j x)"), func=AF.Exp), reads=[Bsc], writes=[BpTs])
            for pr in range(4):
                ob = 4 + pr // 2
                c0 = (pr % 2) * 256 + b * 16
                for j in range(17):
                    p.op("pe", lambda e, pr=pr, j=j, ob=ob, c0=c0: e.matmul(PS[ob][:, c0:c0 + 16], lhsT=Vb[:, j, pr * 128:(pr + 1) * 128], rhs=pTs[:, j, pr * 16:(pr + 1) * 16], start=(j == 0), stop=(j == 16)),
                         reads=[BVb[j], BpTs], writes=[BPS[ob]])
            sbk = 6 + b // 8
            for j in range(17):
                p.op("pe", lambda e, j=j, sbk=sbk, b8=b8: e.matmul(PS[sbk][:, b8 * 64:(b8 + 1) * 64], lhsT=onesb, rhs=pTs[:, j, :], start=(j == 0), stop=(j == 16)),
                     reads=[Bc, BpTs], writes=[BPS[sbk]])
        for hf in range(2):
            p.op("dve", lambda e, hf=hf: e.reciprocal(out=rs[:, 8 * hf:8 * hf + 8, :].rearrange("p b x -> p (b x)"), in_=PS[6 + hf]), reads=[BPS[6 + hf]], writes=[Brs])
        for q2 in range(2):
            p.op("dve", lambda e, q2=q2: e.tensor_tensor(out=ot[:, 2 * q2:2 * q2 + 2, :, :], in0=PS[4 + q2].rearrange("p (r b x) -> p r b x", r=2, b=16),
                                                    in1=rs.rearrange("p b (r x) -> p r b x", r=4)[:, 2 * q2:2 * q2 + 2, :, :], op=ALU.mult),
                 reads=[BPS[4 + q2], Brs], writes=[Bot])
        for h2 in range(2):
            hb = 64 * h2
            p.op("dve", lambda e, h2=h2, hb=hb: e.tensor_tensor(out=yT[hb:hb + 64, 4:8, TP:NTOK].rearrange("p r (b t) -> p r b t", b=16),
                                                           in0=ot[hb:hb + 64, :, :, h2 * 8:(h2 + 1) * 8],
                                                           in1=zsTs[hb:hb + 64, :, :].rearrange("p r (b t) -> p r b t", b=16), op=ALU.mult),
                 reads=[Bot, BzsTs], writes=[ByT[1][16]])
        if stop == "B2":
            break

        p.barrier(); ar.reset()
        cwb = ar.alloc([128, 12, 5], F32); Bcw = Buf("cw")
        Ebc = ar.alloc([128, 16, 128], BF16); blkrow = ar.alloc([128, 16], F32); BE = Buf("Ebc")
        markC = ar.mark()
        cwT = ar.alloc([5, 1536], F32)
        p.dma("sp", lambda e, l=l: e.dma_start(out=cwT[0:4, :], in_=conv_w[l]), writes=[Bcw])
        p.dma("sp", lambda e, l=l: e.dma_start(out=cwT[4:5, :], in_=conv_b[l].rearrange("(o c) -> o c", o=1)), writes=[Bcw])
        for cc in range(12):
            p.op("pe", lambda e, cc=cc: e.transpose(PS[0][:, cc * 8:cc * 8 + 5], cwT[:, cc * 128:(cc + 1) * 128], identf[0:5, 0:5]), reads=[Bcw, Bc], writes=[BPS[0]])
        p.op("dve", lambda e: e.tensor_copy(out=cwb, in_=PS[0][:, 0:96].rearrange("p (c x) -> p c x", x=8)[:, :, 0:5]), reads=[BPS[0]], writes=[Bcw])
        p.op("pool", lambda e: e.memset(Ebc, 1.0), writes=[BE])
        p.op("pool", lambda e: e.affine_select(out=Ebc, in_=Ebc, pattern=[[-8, 16], [1, 128]], compare_op=ALU.is_ge, fill=0.0, base=0, channel_multiplier=0), reads=[BE], writes=[BE])
        p.op("pool", lambda e: e.affine_select(out=Ebc, in_=Ebc, pattern=[[8, 16], [-1, 128]], compare_op=ALU.is_ge, fill=0.0, base=7, channel_multiplier=0), reads=[BE], writes=[BE])
        p.op("pool", lambda e: e.affine_select(out=blkrow, in_=onesf[:, 0:16], pattern=[[-8, 16]], compare_op=ALU.is_ge, fill=0.0, base=0, channel_multiplier=1), reads=[BE, Bc], writes=[BE])
        p.op("pool", lambda e: e.affine_select(out=blkrow, in_=blkrow, pattern=[[8, 16]], compare_op=ALU.is_ge, fill=0.0, base=7, channel_multiplier=-1), reads=[BE], writes=[BE])
        p.dma("sp", lambda e, l=l: e.dma_start(out=par[:, 0:16], in_=dt_bias[l].partition_broadcast(128)), writes=[Bpar])
        p.dma("sp", lambda e, l=l: e.dma_start(out=par[:, 16:32], in_=a_log[l].partition_broadcast(128)), writes=[Bpar])
        p.dma("sp", lambda e, l=l: e.dma_start(out=par[:, 32:48], in_=d_skip[l].partition_broadcast(128)), writes=[Bpar])
        p.op("act", lambda e: e.activation(out=par[:, 16:32], in_=par[:, 16:32], func=AF.Exp), reads=[Bpar], writes=[Bpar])
        p.op("dve", lambda e: e.tensor_scalar(out=par[:, 16:32], in0=par[:, 16:32], scalar1=-1.0, scalar2=None, op0=ALU.mult), reads=[Bpar], writes=[Bpar])

        if stop == "C0":
            break
        for g in range(int(os.environ.get("NGC", 2))):
            p.barrier(); ar.reset(markC)
            Wc = ar.alloc([128, 8, 1280], BF16); BWc = Buf("Wc")
            Wdt = ar.alloc([128, 8, 8], BF16); BWdt = Buf("Wdt")
            snb = ar.alloc([128, 512], F32); Bsnb = Buf("snb")
            stT = ar.alloc([128, 512], F32); stTb = ar.alloc([128, 512], BF16); BstT = Buf("stT")
            xr = ar.alloc([128, 6 * 176], F32); Bxraw = Buf("xraw"); Bxraw_s = Bxraw
            xraw = xr[:, 0:786].rearrange("p (c t) -> p c t", c=6)
            xraw_s = xr.rearrange("p (c b t) -> p c b t", c=6, b=16)
            acc = ar.alloc([128, 6, 128], F32); Baccs = bufs("acc", 6)
            xc = acc[:, 0:4, :]
            R3 = ar.alloc([48, 768], F32); BR3 = Buf("R3")
            BCt = ar.alloc([128, 2, 128], BF16); BBCt = Buf("BCt")
            x_tm = ar.alloc([128, 512], F32); xdt = ar.alloc([128, 512], BF16); xw = ar.alloc([128, 512], BF16); Bxtm = Buf("xtm")
            Btm = ar.alloc([128, 128], BF16); BBtm = Buf("Btm")
            sma = ar.alloc([128, 7, 17, 8], F32); Bsm = Buf("sm")
            negcm = ar.alloc([128, 8, 128], F32); Bdtab = Buf("dtab")
            seg0 = ar.alloc([128, 512], F32); seg = [seg0, seg0]; Bseg0 = Buf("seg"); Bseg = [Bseg0, Bseg0]
            MixT = [ar.alloc([128, 4, 128], BF16) for _ in range(2)]; BMix = bufs("Mix", 2)
            cbT = ar.alloc([128, 128], F32); BcbT = Buf("cbT")
            t1 = ar.alloc([128, 512], F32); Bt1 = Buf("t1")
            yc = ar.alloc([128, 512], F32); Byc = Buf("yc")
            zsc = ar.alloc([128, 512], F32); Bzsc = Buf("zsc")
            ycb = ar.alloc([128, 512], BF16); Bycb = Buf("ycb")
            ssq = ar.alloc([128, 4], F32); Bssq = Buf("ssq")
            CzT = ar.alloc([128, 16, 128], BF16); Bz = ar.alloc([128, 16, 128], BF16); BCz = Buf("CzT")
            elb = ar.alloc([128, 16, 8], F32); cumexp = ar.alloc([128, 16, 8], F32); Belb = Buf("elb")
            stin = [ar.alloc([128, 4, 128], F32) for _ in range(2)]; Bstin = bufs("stin", 2)
            sts = [ar.alloc([128, 512], F32) for _ in range(2)]; stsb = [ar.alloc([128, 512], BF16) for _ in range(2)]; Bsts = bufs("sts", 2)
            stout = stin; Bstout = Bstin

            chans = [g * 512 + i * 128 for i in range(4)] + [1024 + g * 128, 1280 + g * 128]
            load_w(Wc[:, :, 0:512], w_in[l][:, C_XBC + g * 512:C_XBC + (g + 1) * 512], BWc)
            load_w(Wc[:, :, 512:640], w_in[l][:, C_XBC + 1024 + g * 128:C_XBC + 1024 + (g + 1) * 128], BWc)
            load_w(Wc[:, :, 640:768], w_in[l][:, C_XBC + 1280 + g * 128:C_XBC + 1280 + (g + 1) * 128], BWc)
            load_w(Wc[:, :, 768:1280], w_in[l][:, C_ZC + g * 512:C_ZC + (g + 1) * 512], BWc)
            load_w(Wdt, w_in[l][:, C_DT + g * 8:C_DT + (g + 1) * 8], BWdt)
            p.dma("sp", lambda e, l=l, g=g: e.dma_start(out=snb, in_=ssm_norm[l, g * 512:(g + 1) * 512].partition_broadcast(128)), writes=[Bsnb])
            p.op("pool", lambda e: e.memset(stT, 0.0), writes=[BstT])
            p.op("dve", lambda e: e.memset(stTb, 0.0), reads=[BstT], writes=[BstT])
            p.op("pool", lambda e: e.memset(xraw[:, :, 0:3], 0.0), writes=[Bxraw])
            dtb = par[:, g * 8:(g + 1) * 8]; a_bc = par[:, 16 + g * 8:16 + (g + 1) * 8]; dsk = par[:, 32 + g * 8:32 + (g + 1) * 8]

            for j in range(NT):
                proj_tm(Wdt, BWdt, 0, 8, j, PS[3][:, j * 8:(j + 1) * 8], BPS[3])
            row = lambda r: sma[:, r, :, :].rearrange("p j h -> p (j h)")
            p.op("dve", lambda e, dtb=dtb: e.tensor_tensor(out=sma[:, 0, :, :], in0=PS[3][:, 0:136].rearrange("p (j h) -> p j h", h=8), in1=dtb.unsqueeze(1).to_broadcast([128, 17, 8]), op=ALU.add),
                 reads=[BPS[3], Bpar], writes=[Bsm])
            p.op("act", lambda e, row=row: e.activation(out=row(0), in_=row(0), func=AF.Exp), reads=[Bsm], writes=[Bsm])
            p.op("act", lambda e, row=row: e.activation(out=row(0), in_=row(0), func=AF.Ln, bias=1.0), reads=[Bsm], writes=[Bsm])
            p.op("dve", lambda e, a_bc=a_bc: e.tensor_tensor(out=sma[:, 1, :, :], in0=sma[:, 0, :, :], in1=a_bc.unsqueeze(1).to_broadcast([128, 17, 8]), op=ALU.mult), reads=[Bsm, Bpar], writes=[Bsm])
            p.op("pe", lambda e, row=row: e.matmul(PS[3][:, 256:384], lhsT=trif, rhs=row(1)[:, 0:128], start=True, stop=True), reads=[Bsm, Bc], writes=[BPS[3]])
            p.op("pe", lambda e, row=row: e.matmul(PS[3][:, 384:392], lhsT=btrif, rhs=row(1)[:, 128:136], start=True, stop=True), reads=[Bsm, Bc], writes=[BPS[3]])
            p.op("dve", lambda e, row=row: e.tensor_copy(out=row(2), in_=PS[3][:, 256:392]), reads=[BPS[3]], writes=[Bsm])
            p.op("pe", lambda e, row=row: e.matmul(PS[3][:, 0:128], lhsT=sel127, rhs=row(2)[:, 0:128], start=True, stop=True), reads=[Bsm, Bc], writes=[BPS[3]])
            p.op("pe", lambda e, row=row: e.matmul(PS[3][:, 128:136], lhsT=L2s, rhs=row(2)[:, 128:136], start=True, stop=True), reads=[Bsm, Bc], writes=[BPS[3]])
            p.op("dve", lambda e, row=row: e.tensor_copy(out=row(6), in_=PS[3][:, 0:136]), reads=[BPS[3]], writes=[Bsm])
            p.op("act", lambda e, row=row: e.activation(out=row(3), in_=row(2), func=AF.Exp), reads=[Bsm], writes=[Bsm])
            p.op("dve", lambda e, row=row: e.tensor_tensor(out=row(4), in0=row(6), in1=row(2), op=ALU.subtract), reads=[Bsm], writes=[Bsm])
            p.op("act", lambda e, row=row: e.activation(out=row(4), in_=row(4), func=AF.Exp), reads=[Bsm], writes=[Bsm])
            p.op("act", lambda e, row=row: e.activation(out=row(5), in_=row(6), func=AF.Exp), reads=[Bsm], writes=[Bsm])
            for j in range(int(os.environ.get("NTC", NT))):
                tok0 = j * 128
                smp = (j == 16)
                TRI = btrif if smp else trif
                LAST = L2s if smp else sel127
                NEGM = negm_s if smp else negm_p
                sm = sma[:, :, j, :]
                for cc in range(6):
                    pb_, c0 = (0, cc * 128) if cc < 4 else (1, (cc - 4) * 128)
                    proj_fm(Wc, BWc, cc * 128, 128, tok0, 128, PS[pb_][:, c0:c0 + 128], BPS[pb_])
                for hf in range(1):
                    proj_tm(Wc, BWc, 768, 512, j, PS[7], BPS[7])
                p.op("act", lambda e: e.activation(out=zsc, in_=PS[7], func=AF.Silu), reads=[BPS[7]], writes=[Bzsc])
                if not smp:
                    p.op("act", lambda e: e.copy(out=xraw[:, 0:4, 3:131], in_=PS[0].rearrange("p (c t) -> p c t", c=4)), reads=[BPS[0]], writes=[Bxraw])
                    p.op("act", lambda e: e.copy(out=xraw[:, 4:6, 3:131], in_=PS[1][:, 0:256].rearrange("p (c t) -> p c t", c=2)), reads=[BPS[1]], writes=[Bxraw])
                    src = lambda cc, tap: xraw[:, cc, tap:tap + 128]
                    accv = lambda cc: acc[:, cc, :]
                    Bsrc = Bxraw
                else:
                    for cc in range(6):
                        p.dma("sp", lambda e, l=l, cc=cc, c0=chans[cc]: e.dma_start(out=R3[:, cc * 128:(cc + 1) * 128], in_=sconv[l][:, :, c0:c0 + 128].rearrange("b j c -> (b j) c")), writes=[BR3])
                    for cc in range(6):
                        p.op("pe", lambda e, cc=cc: e.transpose(PS[3][:, cc * 48:(cc + 1) * 48], R3[:, cc * 128:(cc + 1) * 128], identf[0:48, 0:48]), reads=[BR3, Bc], writes=[BPS[3]])
                    p.op("act", lambda e: e.copy(out=xraw_s[:, :, :, 0:3], in_=PS[3][:, 0:288].rearrange("p (c b j) -> p c b j", c=6, b=16)), reads=[BPS[3]], writes=[Bxraw_s])
                    p.op("act", lambda e: e.copy(out=xraw_s[:, 0:4, :, 3:11], in_=PS[0].rearrange("p (c b t) -> p c b t", c=4, b=16)), reads=[BPS[0]], writes=[Bxraw_s])
                    p.op("act", lambda e: e.copy(out=xraw_s[:, 4:6, :, 3:11], in_=PS[1][:, 0:256].rearrange("p (c b t) -> p c b t", c=2, b=16)), reads=[BPS[1]], writes=[Bxraw_s])
                    src = lambda cc, tap: xraw_s[:, cc, :, tap:tap + 8]
                    accv = lambda cc: acc[:, cc, :].rearrange("p (b t) -> p b t", b=16)
                    Bsrc = Bxraw_s
                for cc in range(6):
                    ch = chans[cc] // 128
                    p.op("dve", lambda e, cc=cc, ch=ch, src=src, accv=accv: e.tensor_scalar(out=accv(cc), in0=src(cc, 3), scalar1=cwb[:, ch, 3:4], scalar2=cwb[:, ch, 4:5], op0=ALU.mult, op1=ALU.add),
                         reads=[Bsrc, Bcw], writes=[Baccs[cc]])
                for tap in range(3):
                    for cc in range(6):
                        ch = chans[cc] // 128
                        p.op("dve", lambda e, cc=cc, ch=ch, tap=tap, src=src, accv=accv: e.scalar_tensor_tensor(out=accv(cc), in0=src(cc, tap), scalar=cwb[:, ch, tap:tap + 1], in1=accv(cc), op0=ALU.mult, op1=ALU.add),
                             reads=[Bsrc, Bcw, Baccs[cc]], writes=[Baccs[cc]])
                p.op("act", lambda e: e.activation(out=xc, in_=acc[:, 0:4, :], func=AF.Silu), reads=Baccs[0:4], writes=Baccs[0:4])
                p.op("act", lambda e: e.activation(out=BCt, in_=acc[:, 4:6, :], func=AF.Silu), reads=Baccs[4:6], writes=[BBCt])
                if not smp:
                    if j < 15:
                        p.op("pool", lambda e: e.tensor_copy(out=xraw[:, :, 0:3], in_=xraw[:, :, 128:131]), reads=[Bxraw], writes=[Bxraw])
                    else:
                        for cc in range(6):
                            p.op("pe", lambda e, cc=cc: e.transpose(PS[3][0:3, cc * 128:(cc + 1) * 128] if cc < 4 else PS[2][0:3, (cc - 4) * 128:(cc - 3) * 128], xraw[:, cc, 128:131], identf), reads=[Bxraw, Bc], writes=[BPS[3] if cc < 4 else BPS[2]])
                        p.op("dve", lambda e: e.tensor_copy(out=R3[0:3, 0:512], in_=PS[3][0:3, 0:512]), reads=[BPS[3]], writes=[BR3])
                        p.op("dve", lambda e: e.tensor_copy(out=R3[0:3, 512:768], in_=PS[2][0:3, 0:256]), reads=[BPS[2]], writes=[BR3])
                        for cc in range(6):
                            p.dma("sp", lambda e, l=l, cc=cc, c0=chans[cc]: e.dma_start(out=convp[l][:, c0:c0 + 128], in_=R3[0:3, cc * 128:(cc + 1) * 128]), reads=[BR3])
                else:
                    p.op("pool", lambda e: e.tensor_copy(out=t1[:, 0:288].rearrange("p (c b t) -> p c b t", c=6, b=16), in_=xraw_s[:, :, :, 8:11]), reads=[Bxraw_s], writes=[Bt1])
                    for cc in range(6):
                        p.op("pe", lambda e, cc=cc: e.transpose(PS[3][0:48, (cc % 4) * 128:(cc % 4 + 1) * 128] if cc < 4 else PS[2][0:48, (cc - 4) * 128:(cc - 3) * 128],
                                                                 t1[:, cc * 48:(cc + 1) * 48], identf), reads=[Bt1, Bc], writes=[BPS[3] if cc < 4 else BPS[2]])
                    p.op("dve", lambda e: e.tensor_copy(out=R3[:, 0:512], in_=PS[3][0:48, 0:512]), reads=[BPS[3]], writes=[BR3])
                    p.op("dve", lambda e: e.tensor_copy(out=R3[:, 512:768], in_=PS[2][0:48, 0:256]), reads=[BPS[2]], writes=[BR3])
                    for cc in range(6):
                        p.dma("sp", lambda e, l=l, cc=cc, c0=chans[cc]: e.dma_start(out=convs[l][:, :, c0:c0 + 128].rearrange("b t c -> (b t) c"), in_=R3[:, cc * 128:(cc + 1) * 128]), reads=[BR3])
                for cc in range(4):
                    p.op("pe", lambda e, cc=cc: e.transpose(PS[2][:, cc * 128:(cc + 1) * 128], xc[:, cc, :], identf), reads=[Baccs[cc], Bc], writes=[BPS[2]])
                p.op("act", lambda e: e.copy(out=x_tm, in_=PS[2]), reads=[BPS[2]], writes=[Bxtm])
                p.op("dve", lambda e, sm=sm: e.tensor_tensor(out=xdt.rearrange("p (h d) -> p h d", h=8), in0=x_tm.rearrange("p (h d) -> p h d", h=8), in1=sm[:, 0, :].unsqueeze(2).to_broadcast([128, 8, 64]), op=ALU.mult),
                     reads=[Bxtm, Bsm], writes=[Bxtm])
                p.op("dve", lambda e, sm=sm: e.tensor_tensor(out=xw.rearrange("p (h d) -> p h d", h=8), in0=xdt.rearrange("p (h d) -> p h d", h=8), in1=sm[:, 4, :].unsqueeze(2).to_broadcast([128, 8, 64]), op=ALU.mult),
                     reads=[Bxtm, Bsm], writes=[Bxtm])
                pst3 = PS[3].bitcast(BF16)
                p.op("pe", lambda e, pst3=pst3: e.transpose(pst3[:, 512:640], BCt[:, 0, :], identb), reads=[BBCt, Bc], writes=[BPS[3]])
                p.op("dve", lambda e, pst3=pst3: e.tensor_copy(out=Btm, in_=pst3[:, 512:640]), reads=[BPS[3]], writes=[BBtm])
                p.op("pe", lambda e: e.matmul(PS[3][:, 128:256], lhsT=BCt[:, 0, :], rhs=BCt[:, 1, :], start=True, stop=True), reads=[BBCt], writes=[BPS[3]])
                p.op("dve", lambda e: e.tensor_copy(out=cbT, in_=PS[3][:, 128:256]), reads=[BPS[3]], writes=[BcbT])
                p.op("pool", lambda e, sm=sm, NEGM=NEGM: e.tensor_tensor(out=negcm, in0=NEGM.unsqueeze(1).to_broadcast([128, 8, 128]), in1=sm[:, 2, :].unsqueeze(2).to_broadcast([128, 8, 128]), op=ALU.subtract),
                     reads=[Bsm, Bc, Bdtab], writes=[Bdtab])
                for hh in range(2):
                    for i in range(4):
                        h = hh * 4 + i
                        p.op("pe", lambda e, i=i, h=h, TRI=TRI, sm=sm: e.matmul(PS[4][:, i * 128:(i + 1) * 128], lhsT=sm[:, 1, h:h + 1].to_broadcast([128, 128]), rhs=TRI, start=True, stop=True), reads=[Bsm, Bc], writes=[BPS[4]])
                    p.op("dve", lambda e, hh=hh: e.tensor_tensor(out=seg[hh], in0=PS[4], in1=negcm[:, hh * 4:(hh + 1) * 4, :].rearrange("p h t -> p (h t)"), op=ALU.add),
                         reads=[BPS[4], Bdtab], writes=[Bseg[hh]])
                    p.op("act", lambda e, hh=hh: e.activation(out=seg[hh], in_=seg[hh], func=AF.Exp), reads=[Bseg[hh]], writes=[Bseg[hh]])
                    p.op("dve", lambda e, hh=hh: e.tensor_tensor(out=MixT[hh], in0=seg[hh].rearrange("p (h t) -> p h t", h=4), in1=cbT.unsqueeze(1).to_broadcast([128, 4, 128]), op=ALU.mult),
                         reads=[Bseg[hh], BcbT], writes=[BMix[hh]])
                    for i in range(4):
                        h = hh * 4 + i
                        p.op("pe", lambda e, i=i, h=h, hh=hh: e.matmul(PS[5][:, h * 64:(h + 1) * 64], lhsT=MixT[hh][:, i, :], rhs=xdt[:, h * 64:(h + 1) * 64], start=True, stop=True),
                             reads=[BMix[hh], Bxtm], writes=[BPS[5]])
                if not smp:
                    p.op("pe", lambda e: e.matmul(PS[6], lhsT=BCt[:, 1, :], rhs=stTb, start=True, stop=True), reads=[BBCt, BstT], writes=[BPS[6]])
                    p.op("pe", lambda e: e.matmul(PS[7], lhsT=Btm, rhs=xw, start=True, stop=True), reads=[BBtm, Bxtm], writes=[BPS[7]])
                    p.op("dve", lambda e, sm=sm: e.tensor_tensor(out=stT.rearrange("p (h d) -> p h d", h=8), in0=stT.rearrange("p (h d) -> p h d", h=8), in1=sm[:, 5, :].unsqueeze(2).to_broadcast([128, 8, 64]), op=ALU.mult),
                         reads=[BstT, Bsm, BPS[6]], writes=[BstT])
                    p.op("dve", lambda e: e.tensor_tensor(out=stT, in0=stT, in1=PS[7], op=ALU.add), reads=[BstT, BPS[7]], writes=[BstT])
                    p.op("act", lambda e: e.copy(out=stTb, in_=stT), reads=[BstT], writes=[BstT])
                    if j == 15:
                        for cc in range(4):
                            p.op("pe", lambda e, cc=cc: e.transpose(PS[7][:, cc * 128:(cc + 1) * 128], stT[:, cc * 128:(cc + 1) * 128], identf), reads=[BstT, Bc], writes=[BPS[7]])
                        p.op("act", lambda e: e.copy(out=stout[0], in_=PS[7].rearrange("p (c n) -> p c n", c=4)), reads=[BPS[7]], writes=[Bstout[0]])
                        p.dma("sp", lambda e, l=l, g=g: e.dma_start(out=ssmp[l, g * 512:(g + 1) * 512, :].rearrange("(c p) n -> p c n", p=128), in_=stout[0]), reads=[Bstout[0]])
                else:
                    p.op("dve", lambda e: e.tensor_tensor(out=CzT, in0=BCt[:, 1, :].unsqueeze(1).to_broadcast([128, 16, 128]), in1=Ebc, op=ALU.mult), reads=[BBCt, BE], writes=[BCz])
                    p.op("dve", lambda e: e.tensor_tensor(out=Bz, in0=Btm.unsqueeze(1).to_broadcast([128, 16, 128]), in1=blkrow.unsqueeze(2).to_broadcast([128, 16, 128]), op=ALU.mult), reads=[BBtm, BE], writes=[BCz])
                    p.op("dve", lambda e, sm=sm: e.tensor_tensor(out=cumexp, in0=sm[:, 2, :].unsqueeze(1).to_broadcast([128, 16, 8]), in1=lastmask.unsqueeze(2).to_broadcast([128, 16, 8]), op=ALU.mult), reads=[Bsm, Bc], writes=[Belb])
                    p.op("pe", lambda e: e.matmul(PS[3][:, 256:384], lhsT=onesf, rhs=cumexp.rearrange("p b h -> p (b h)"), start=True, stop=True), reads=[Belb, Bc], writes=[BPS[3]])
                    p.op("act", lambda e: e.activation(out=elb.rearrange("p b h -> p (b h)"), in_=PS[3][:, 256:384], func=AF.Exp), reads=[BPS[3]], writes=[Belb])
                    for b in range(NSEQ):
                        m = b % 2
                        p.dma("sp", lambda e, l=l, g=g, b=b, m=m: e.dma_start(out=stin[m], in_=sssm[l, b, g * 512:(g + 1) * 512, :].rearrange("(c p) n -> p c n", p=128)), writes=[Bstin[m]])
                        for cc in range(4):
                            p.op("pe", lambda e, cc=cc, m=m: e.transpose(PS[7][:, cc * 128:(cc + 1) * 128], stin[m][:, cc, :], identf), reads=[Bstin[m], Bc], writes=[BPS[7]])
                        p.op("act", lambda e, m=m: e.copy(out=sts[m], in_=PS[7]), reads=[BPS[7]], writes=[Bsts[m]])
                        p.op("dve", lambda e, m=m: e.tensor_copy(out=stsb[m], in_=sts[m]), reads=[Bsts[m]], writes=[Bsts[m]])
                        p.op("pe", lambda e, b=b, m=m: e.matmul(PS[6], lhsT=CzT[:, b, :], rhs=stsb[m], start=(b == 0), stop=(b == NSEQ - 1)), reads=[BCz, Bsts[m]], writes=[BPS[6]])
                        p.op("pe", lambda e, b=b: e.matmul(PS[0], lhsT=Bz[:, b, :], rhs=xw, start=True, stop=True), reads=[BCz, Bxtm], writes=[BPS[0]])
                        p.op("dve", lambda e, b=b, m=m: e.tensor_tensor(out=sts[m].rearrange("p (h d) -> p h d", h=8), in0=sts[m].rearrange("p (h d) -> p h d", h=8), in1=elb[:, b, :].unsqueeze(2).to_broadcast([128, 8, 64]), op=ALU.mult),
                             reads=[Bsts[m], Belb], writes=[Bsts[m]])
                        p.op("dve", lambda e, m=m: e.tensor_tensor(out=sts[m], in0=sts[m], in1=PS[0], op=ALU.add), reads=[Bsts[m], BPS[0]], writes=[Bsts[m]])
                        for cc in range(4):
                            p.op("pe", lambda e, cc=cc, m=m: e.transpose(PS[1][:, cc * 128:(cc + 1) * 128], sts[m][:, cc * 128:(cc + 1) * 128], identf), reads=[Bsts[m], Bc], writes=[BPS[1]])
                        p.op("act", lambda e, m=m: e.copy(out=stout[m], in_=PS[1].rearrange("p (c n) -> p c n", c=4)), reads=[BPS[1]], writes=[Bstout[m]])
                        p.dma("sp", lambda e, l=l, g=g, b=b, m=m: e.dma_start(out=ssms[l, b, g * 512:(g + 1) * 512, :].rearrange("(c p) n -> p c n", p=128), in_=stout[m]), reads=[Bstout[m]])
                p.op("dve", lambda e, sm=sm: e.tensor_tensor(out=t1.rearrange("p (h d) -> p h d", h=8), in0=PS[6].rearrange("p (h d) -> p h d", h=8), in1=sm[:, 3, :].unsqueeze(2).to_broadcast([128, 8, 64]), op=ALU.mult),
                     reads=[BPS[6], Bsm], writes=[Bt1])
                p.op("dve", lambda e: e.tensor_tensor(out=yc, in0=PS[5], in1=t1, op=ALU.add), reads=[BPS[5], Bt1], writes=[Byc])
                p.op("pool", lambda e, dsk=dsk: e.tensor_tensor(out=t1.rearrange("p (h d) -> p h d", h=8), in0=x_tm.rearrange("p (h d) -> p h d", h=8), in1=dsk.unsqueeze(2).to_broadcast([128, 8, 64]), op=ALU.mult),
                     reads=[Bxtm, Bpar, Byc], writes=[Bt1])
                p.op("pool", lambda e: e.tensor_tensor(out=yc, in0=yc, in1=t1, op=ALU.add), reads=[Bt1, Byc], writes=[Byc])
                p.op("pool", lambda e: e.tensor_tensor(out=yc, in0=yc, in1=zsc, op=ALU.mult), reads=[Bzsc, Byc], writes=[Byc])
                p.op("pool", lambda e: e.memset(ssq, 0.0), writes=[Bssq])
                p.op("act", lambda e: e.activation(out=zsc, in_=yc, func=AF.Square, accum_out=ssq[:, 0:1]), reads=[Byc, Bssq], writes=[Bzsc, Bssq])
                p.op("act", lambda e: e.activation(out=ssq[:, 1:2], in_=ssq[:, 0:1], func=AF.Ln, bias=EPS, scale=1.0 / 512), reads=[Bssq], writes=[Bssq])
                p.op("act", lambda e: e.activation(out=ssq[:, 2:3], in_=ssq[:, 1:2], func=AF.Exp, scale=-0.5), reads=[Bssq], writes=[Bssq])
                p.op("dve", lambda e: e.scalar_tensor_tensor(out=ycb, in0=yc, scalar=ssq[:, 2:3], in1=snb, op0=ALU.mult, op1=ALU.mult), reads=[Byc, Bssq, Bsnb], writes=[Bycb])
                pst7 = PS[7].bitcast(BF16)
                for cc in range(4):
                    p.op("pe", lambda e, cc=cc, pst7=pst7: e.transpose(pst7[:, cc * 128:(cc + 1) * 128], ycb[:, cc * 128:(cc + 1) * 128], identb), reads=[Bycb, Bc], writes=[BPS[7]])
                p.op("dve", lambda e, pst7=pst7, g=g, tok0=tok0: e.tensor_copy(out=yT[:, 8 + g * 4:12 + g * 4, tok0:tok0 + 128], in_=pst7[:, 0:512].rearrange("p (c t) -> p c t", c=4)),
                     reads=[BPS[7]], writes=[ByT[2][j]])
        if stop == "C":
            break

        p.barrier(); ar.reset()
        mT = ar.alloc([128, 8, NTOK], BF16); BmT = bufs("mT", NT)
        markM = ar.mark()
        wM = [ar.alloc([128, 40, 128], BF16) for _ in range(2)]; BwM = bufs("wM", 2)
        sg = [ar.alloc([128, 512], F32) for _ in range(2)]; Bsg = bufs("sg", 2)
        accm = [ar.alloc([128, 512], F32) for _ in range(2)]; Baccm = bufs("accm", 2)
        tmpm = [ar.alloc([128, 512], F32) for _ in range(2)]; Btmpm = bufs("tmpm", 2)
        KOFF = (0, 4, 8); KCN = (4, 4, 8)
        WSRC = (w_a, w_b, w_c)
        it = 0
        for nj in range(8):
            k = nj % 2
            for br in range(3):
                load_w(wM[k][:, KOFF[br]:KOFF[br] + KCN[br], :], WSRC[br][l][:, nj * 128:(nj + 1) * 128], BwM[k])
                load_w(wM[k][:, 16 + 8 * br:24 + 8 * br, :], w_in[l][:, C_G + br * 1024 + nj * 128:C_G + br * 1024 + (nj + 1) * 128], BwM[k])
            for tg in range(5):
                tok0 = tg * 512
                ntok = 512 if tg < 4 else 128
                a2 = it % 2
                for br in range(3):
                    pp_, pg_ = it % 3, 3 + it % 3
                    s2 = it % 2
                    for kc in range(KCN[br]):
                        p.op("pe", lambda e, k=k, br=br, kc=kc, pp_=pp_, tok0=tok0, ntok=ntok: e.matmul(PS[pp_][:, 0:ntok], lhsT=wM[k][:, KOFF[br] + kc, :], rhs=yT[:, YOFF[br] + kc, tok0:tok0 + ntok],
                                                                                                       start=(kc == 0), stop=(kc == KCN[br] - 1)),
                             reads=[BwM[k]] + tile_bufs(ByT[br], tok0, ntok), writes=[BPS[pp_]])
                    for kc in range(8):
                        p.op("pe", lambda e, k=k, br=br, kc=kc, pg_=pg_, tok0=tok0, ntok=ntok: e.matmul(PS[pg_][:, 0:ntok], lhsT=wM[k][:, 16 + 8 * br + kc, :], rhs=hnT[:, kc, tok0:tok0 + ntok],
                                                                                                       start=(kc == 0), stop=(kc == 7)),
                             reads=[BwM[k]] + tile_bufs(BhnT, tok0, ntok), writes=[BPS[pg_]])
                    p.op("act", lambda e, pg_=pg_, s2=s2, ntok=ntok: e.activation(out=sg[s2][:, 0:ntok], in_=PS[pg_][:, 0:ntok], func=AF.Sigmoid), reads=[BPS[pg_]], writes=[Bsg[s2]])
                    if br == 0:
                        p.op("dve", lambda e, pp_=pp_, s2=s2, a2=a2, ntok=ntok: e.tensor_tensor(out=accm[a2][:, 0:ntok], in0=PS[pp_][:, 0:ntok], in1=sg[s2][:, 0:ntok], op=ALU.mult),
                             reads=[BPS[pp_], Bsg[s2]], writes=[Baccm[a2]])
                    else:
                        p.op("dve", lambda e, pp_=pp_, s2=s2, a2=a2, ntok=ntok: e.tensor_tensor(out=tmpm[a2][:, 0:ntok], in0=PS[pp_][:, 0:ntok], in1=sg[s2][:, 0:ntok], op=ALU.mult),
                             reads=[BPS[pp_], Bsg[s2]], writes=[Btmpm[a2]])
                        if br == 1:
                            p.op("pool", lambda e, a2=a2, ntok=ntok: e.tensor_tensor(out=accm[a2][:, 0:ntok], in0=accm[a2][:, 0:ntok], in1=tmpm[a2][:, 0:ntok], op=ALU.add),
                                 reads=[Btmpm[a2], Baccm[a2]], writes=[Baccm[a2]])
                        else:
                            p.op("pool", lambda e, a2=a2, nj=nj, tok0=tok0, ntok=ntok: e.tensor_tensor(out=mT[:, nj, tok0:tok0 + ntok], in0=accm[a2][:, 0:ntok], in1=tmpm[a2][:, 0:ntok], op=ALU.add),
                                 reads=[Btmpm[a2], Baccm[a2]], writes=tile_bufs(BmT, tok0, ntok))
                    it += 1
        p.barrier(); ar.reset(markM)
        Wo = ar.alloc([128, 8, 1024], BF16); BWo = Buf("Wo")
        npost = ar.alloc([128, D], F32); Bnpost = Buf("npost")
        xt2 = [ar.alloc([128, D], F32) for _ in range(2)]; Bxt2 = bufs("xt2", 2)
        rr = [ar.alloc([128, D], F32) for _ in range(2)]; Brr = bufs("rr", 2)
        hb2 = [ar.alloc([128, D], BF16) for _ in range(2)]; Bhb2 = bufs("hb2", 2)
        sq2 = [ar.alloc([128, 4], F32) for _ in range(2)]; Bsq2 = bufs("sq2", 2)
        ss2 = [ar.alloc([128, 8], F32) for _ in range(2)]; Bss2 = bufs("ss2", 2)
        junk = ar.alloc([128, 512], F32); Bjunk = Buf("junk")
        load_w(Wo[:, :, 0:512], w_out[l][:, 0:512], BWo)
        load_w(Wo[:, :, 512:1024], w_out[l][:, 512:1024], BWo)
        p.dma("sp", lambda e, l=l: e.dma_start(out=npost, in_=norm_post[l].partition_broadcast(128)), writes=[Bnpost])
        if l == 0:
            p.dma("sp", lambda e: e.dma_start(out=npre_bc, in_=norm_pre[1].partition_broadcast(128)), writes=[Bnpre])
        for j in range(NT):
            k = j % 2
            if l == 0:
                srcx = xp[j * 128:(j + 1) * 128, :] if j < 16 else xs
                p.dma("sp", lambda e, k=k, srcx=srcx: e.dma_start(out=xt2[k], in_=srcx), writes=[Bxt2[k]])
            else:
                p.dma("sp", lambda e, k=k, j=j: e.dma_start(out=xt2[k], in_=x1[j * 128:(j + 1) * 128, :]), reads=[Bx1[j]], writes=[Bxt2[k]])
            for hf in range(2):
                for kc in range(8):
                    p.op("pe", lambda e, j=j, hf=hf, kc=kc: e.matmul(PS[6 + hf], lhsT=mT[:, kc, j * 128:(j + 1) * 128], rhs=Wo[:, kc, hf * 512:(hf + 1) * 512], start=(kc == 0), stop=(kc == 7)),
                         reads=[BmT[j], BWo], writes=[BPS[6 + hf]])
            p.op("pool", lambda e, k=k: e.memset(ss2[k], 0.0), writes=[Bss2[k]])
            for hf in range(2):
                p.op("act", lambda e, k=k, hf=hf: e.activation(out=junk, in_=PS[6 + hf], func=AF.Square, accum_out=ss2[k][:, hf:hf + 1]), reads=[BPS[6 + hf], Bss2[k]], writes=[Bjunk, Bss2[k]])
            p.op("dve", lambda e, k=k: e.tensor_tensor(out=ss2[k][:, 2:3], in0=ss2[k][:, 0:1], in1=ss2[k][:, 1:2], op=ALU.add), reads=[Bss2[k]], writes=[Bss2[k]])
            p.op("act", lambda e, k=k: e.activation(out=ss2[k][:, 3:4], in_=ss2[k][:, 2:3], func=AF.Ln, bias=EPS, scale=1.0 / D), reads=[Bss2[k]], writes=[Bss2[k]])
            p.op("act", lambda e, k=k: e.activation(out=ss2[k][:, 4:5], in_=ss2[k][:, 3:4], func=AF.Exp, scale=-0.5), reads=[Bss2[k]], writes=[Bss2[k]])
            for hf in range(2):
                p.op("dve", lambda e, k=k, hf=hf: e.scalar_tensor_tensor(out=rr[k][:, hf * 512:(hf + 1) * 512], in0=PS[6 + hf], scalar=ss2[k][:, 4:5], in1=npost[:, hf * 512:(hf + 1) * 512], op0=ALU.mult, op1=ALU.mult),
                     reads=[BPS[6 + hf], Bss2[k], Bnpost], writes=[Brr[k]])
            p.op("pool", lambda e, k=k: e.tensor_tensor(out=rr[k], in0=rr[k], in1=xt2[k], op=ALU.add), reads=[Bxt2[k], Brr[k]], writes=[Brr[k]])
            if l == 0:
                p.dma("sp", lambda e, k=k, j=j: e.dma_start(out=x1[j * 128:(j + 1) * 128, :], in_=rr[k]), reads=[Brr[k]], writes=[Bx1[j]], owner=Brr[k])
                norm_to_hnT(rr[k], Brr[k], j, sq2[k], Bsq2[k], hb2[k], Bhb2[k], 4 + k)
            else:
                dst = yp[j * 128:(j + 1) * 128, :] if j < 16 else ys
                p.dma("sp", lambda e, k=k, dst=dst: e.dma_start(out=dst, in_=rr[k]), reads=[Brr[k]])

    dbg_dump("hnT", hnT[:, :, :], BhnT)
    dbg_dump("yT", yT[:, :, :], ByT[0] + ByT[1] + ByT[2])
    p.emit()
    return nc


def make_in_maps(inputs, n_cores=8):
    g = {k: np.asarray(v) for k, v in inputs.items()}
    n_pool = g["cache_k"].shape[1]
    ckf = np.ascontiguousarray(g["cache_k"]).reshape(DEPTH, n_pool * 128, 512)
    cvf = np.ascontiguousarray(g["cache_v"]).reshape(DEPTH, n_pool * 128, 512)
    clff = np.ascontiguousarray(g["cache_logf"]).reshape(DEPTH, n_pool, 1024)
    maps = []
    for c in range(n_cores):
        sl = slice(NSEQ * c, NSEQ * (c + 1))
        m = dict(
            xp=np.ascontiguousarray(g["x_prompt"][c]), xs=np.ascontiguousarray(g["x_sample"][sl]).reshape(128, D),
            ck=ckf, cv=cvf, clf=clff,
            spool=np.ascontiguousarray(g["state_pool"][:, sl]), sconv=np.ascontiguousarray(g["state_conv"][:, sl]),
            sssm=np.ascontiguousarray(g["state_ssm"][:, sl]).reshape(DEPTH, NSEQ, 1024, 128),
            pt=np.ascontiguousarray(g["page_table"][sl]).astype(np.int32),
            norm_pre=g["norm_pre"], w_in=g["w_in"], pool_w=g["pool_w"], pool_scale=g["pool_scale"], f_bias=g["f_bias"],
            conv_w=g["conv_w"], conv_b=g["conv_b"], dt_bias=g["dt_bias"], a_log=g["a_log"], d_skip=g["d_skip"],
            ssm_norm=g["ssm_norm"], w_a=g["w_branch_a"], w_b=g["w_branch_b"], w_c=g["w_branch_c"], w_out=g["w_out"],
            norm_post=g["norm_post"])
        maps.append(m)
    return maps, n_pool


def assemble(res, n_cores=8):
    R = res
    cat = lambda k: np.stack([R[c][k] for c in range(n_cores)])
    y_prompt = cat("yp")
    y_sample = cat("ys").reshape(n_cores * NSEQ, 8, D)
    def pl(k, shp):
        return np.stack([R[c][k] for c in range(n_cores)], axis=1).reshape(shp)
    nb = n_cores
    k_p = pl("kp", (DEPTH, nb, TP, 8, 64)); v_p = pl("vp", (DEPTH, nb, TP, 8, 64)); lf_p = pl("lfp", (DEPTH, nb, TP, 8))
    pool_p = pl("poolp", (DEPTH, nb, 15, 512)); conv_p = pl("convp", (DEPTH, nb, 3, 1536)); ssm_p = pl("ssmp", (DEPTH, nb, 16, 64, 128))
    k_s = pl("kso", (DEPTH, nb * NSEQ, 8, 8, 64)); v_s = pl("vso", (DEPTH, nb * NSEQ, 8, 8, 64)); lf_s = pl("lfs", (DEPTH, nb * NSEQ, 8, 8))
    pool_s = pl("pools", (DEPTH, nb * NSEQ, 15, 512)); conv_s = pl("convs", (DEPTH, nb * NSEQ, 3, 1536))
    ssm_s = pl("ssms", (DEPTH, nb * NSEQ, 16, 64, 128))
    return (y_prompt, y_sample, k_p, v_p, lf_p, pool_p, conv_p, ssm_p, k_s, v_s, lf_s, pool_s, conv_s, ssm_s)


def kernel(**inputs):
    maps, n_pool = make_in_maps(inputs)
    nc = build(n_pool=n_pool)
    res = run_bass_kernel_spmd(nc, maps, core_ids=list(range(8)))
    return tuple(np.ascontiguousarray(a, dtype=np.float32) for a in assemble(res.results))
```
